# Optimizing a Trainium2 kernel written in Bass

```python
import math
import jax, jax.numpy as jnp
from jax import lax
import numpy as np

D_MODEL = 1024
BATCH = 32
SEQ = 2048
DEPTH = 2

EPS = 1e-6
Q_BLOCK = 128
ROPE_THETA = 500000.0
ROPE_FRACTION = 4

A_HEADS = 4
A_QK_DIM = 64
A_V_DIM = 2 * A_QK_DIM
A_QK_W = A_HEADS * 2 * A_QK_DIM
A_V_W = A_HEADS * A_V_DIM

B_HEADS = 4
B_K_DIM = 128
B_V_DIM = 128
B_K_W = B_HEADS * B_K_DIM
B_V_W = B_HEADS * B_V_DIM
B_CONV_CH = 2 * B_K_W + B_V_W
CONV_K = 4
CHUNK = 64

MIX_AB = A_V_W + B_V_W
AB_SIZES = (A_QK_W, A_QK_W, A_V_W, A_V_W, B_CONV_CH, B_V_W, B_HEADS, B_HEADS)
AB_SPLITS = tuple(int(s) for s in np.cumsum(AB_SIZES)[:-1])
IN_AB = int(sum(AB_SIZES))

C_HEADS = 16
C_HEAD_DIM = 64
MIX_C = C_HEADS * C_HEAD_DIM
C_SIZES = (MIX_C, MIX_C, MIX_C, MIX_C, C_HEADS)
C_SPLITS = tuple(int(s) for s in np.cumsum(C_SIZES)[:-1])
IN_C = int(sum(C_SIZES))

kernel_name = "hybrid_diffattn_gdn_fox_block"


def rmsnorm(x, w):
    xf = x.astype(jnp.float32)
    y = xf * lax.rsqrt(jnp.mean(xf * xf, axis=-1, keepdims=True) + EPS)
    return (y * w.astype(jnp.float32)).astype(x.dtype)


def partial_rope(x, positions):
    rot = x.shape[-1] // ROPE_FRACTION
    half = rot // 2
    inv_freq = ROPE_THETA ** (-(jnp.arange(half, dtype=jnp.float32) * 2.0) / rot)
    ang = positions.astype(jnp.float32)[..., None] * inv_freq
    cos = jnp.cos(ang)[:, :, None, :]
    sin = jnp.sin(ang)[:, :, None, :]
    xf = x.astype(jnp.float32)
    x1, x2 = xf[..., :half], xf[..., half:rot]
    out = jnp.concatenate([x1 * cos - x2 * sin, x2 * cos + x1 * sin, xf[..., rot:]], axis=-1)
    return out.astype(x.dtype)


def causal_block_mask(s0, s1):
    qpos = jnp.arange(s0, s1)
    kpos = jnp.arange(s1)
    return kpos[None, :] <= qpos[:, None]


def diff_attention(q, k, v, lam):
    seq = q.shape[1]
    scale = q.shape[-1] ** -0.5
    outs = []
    for s0 in range(0, seq, Q_BLOCK):
        s1 = s0 + Q_BLOCK
        logits = jnp.einsum('bqhcd,bkhcd->bhcqk', q[:, s0:s1], k[:, :s1],
                            preferred_element_type=jnp.float32) * scale
        logits = jnp.where(causal_block_mask(s0, s1), logits, -jnp.inf)
        p = jax.nn.softmax(logits, axis=-1)
        p = p[:, :, 0] - lam * p[:, :, 1]
        outs.append(jnp.einsum('bhqk,bkhe->bqhe', p, v[:, :s1].astype(jnp.float32)))
    return jnp.concatenate(outs, axis=1).astype(v.dtype)


def forgetting_attention(q, k, v, cum_logf):
    seq = q.shape[1]
    scale = q.shape[-1] ** -0.5
    c = jnp.transpose(cum_logf, (0, 2, 1))
    outs = []
    for s0 in range(0, seq, Q_BLOCK):
        s1 = s0 + Q_BLOCK
        logits = jnp.einsum('bqhd,bkhd->bhqk', q[:, s0:s1], k[:, :s1],
                            preferred_element_type=jnp.float32) * scale
        bias = c[:, :, s0:s1, None] - c[:, :, None, :s1]
        logits = jnp.where(causal_block_mask(s0, s1), logits + bias, -jnp.inf)
        p = jax.nn.softmax(logits, axis=-1)
        outs.append(jnp.einsum('bhqk,bkhd->bqhd', p, v[:, :s1].astype(jnp.float32)))
    return jnp.concatenate(outs, axis=1).astype(v.dtype)


def causal_depthwise_conv(x, w):
    kw = w.shape[0]
    return lax.conv_general_dilated(
        x, w[:, None, :], window_strides=(1,), padding=[(kw - 1, 0)],
        dimension_numbers=('NWC', 'WIO', 'NWC'), feature_group_count=x.shape[-1])


def gated_delta_rule_chunked(q, k, v, g, beta):
    out_dtype = v.dtype
    bsz, seq, heads, dk = q.shape
    dv = v.shape[-1]
    n_chunks = seq // CHUNK

    def to_chunks(t):
        return t.astype(jnp.float32).reshape(bsz, n_chunks, CHUNK, heads, -1).transpose(1, 0, 3, 2, 4)

    def to_chunks_s(t):
        return t.reshape(bsz, n_chunks, CHUNK, heads).transpose(1, 0, 3, 2)

    qc = to_chunks(q) * (dk ** -0.5)
    kc = to_chunks(k)
    vc = to_chunks(v)
    bc = to_chunks_s(beta)
    gc = jnp.cumsum(to_chunks_s(g), axis=-1)

    tril = jnp.tril(jnp.ones((CHUNK, CHUNK), dtype=bool))
    strict = jnp.tril(jnp.ones((CHUNK, CHUNK), dtype=bool), k=-1)
    decay = jnp.exp(jnp.where(tril, gc[..., :, None] - gc[..., None, :], -jnp.inf))

    k_beta = kc * bc[..., None]
    v_beta = vc * bc[..., None]
    lower = jnp.where(strict, jnp.einsum('nbhck,nbhsk->nbhcs', k_beta, kc) * decay, 0.0)
    unit_lower = lower + jnp.eye(CHUNK, dtype=jnp.float32)
    rhs = jnp.concatenate([v_beta, k_beta * jnp.exp(gc)[..., None]], axis=-1)
    sol = lax.linalg.triangular_solve(unit_lower, rhs, left_side=True, lower=True,
                                      unit_diagonal=True)
    u, w = sol[..., :dv], sol[..., dv:]
    qk = jnp.where(tril, jnp.einsum('nbhck,nbhsk->nbhcs', qc, kc) * decay, 0.0)

    def step(state, xs):
        q_c, k_c, u_c, w_c, g_c, qk_c = xs
        v_new = u_c - jnp.einsum('bhck,bhkv->bhcv', w_c, state)
        o = (jnp.einsum('bhck,bhkv->bhcv', q_c * jnp.exp(g_c)[..., None], state)
             + jnp.einsum('bhcs,bhsv->bhcv', qk_c, v_new))
        g_last = g_c[..., -1:]
        state = (state * jnp.exp(g_last)[..., None]
                 + jnp.einsum('bhck,bhcv->bhkv', k_c * jnp.exp(g_last - g_c)[..., None], v_new))
        return state, o

    state0 = jnp.zeros((bsz, heads, dk, dv), jnp.float32)
    _, o = lax.scan(step, state0, (qc, kc, u, w, gc, qk))
    return o.transpose(1, 0, 3, 2, 4).reshape(bsz, seq, heads, dv).astype(out_dtype)


def mixer_ab(h, positions, layer, w_in_ab, a_lambda_q1, a_lambda_k1, a_lambda_q2, a_lambda_k2,
             a_subln, b_conv_w, b_a_log, b_dt_bias, b_head_norm, w_out_ab):
    bsz, seq, _ = h.shape
    proj = h @ w_in_ab
    a_q, a_k, a_v, a_z, b_qkv, b_z, b_beta, b_a = jnp.split(proj, AB_SPLITS, axis=-1)

    lambda_init = 0.8 - 0.6 * math.exp(-0.3 * layer)
    f32 = jnp.float32
    lam = (jnp.exp(jnp.sum(a_lambda_q1.astype(f32) * a_lambda_k1.astype(f32)))
           - jnp.exp(jnp.sum(a_lambda_q2.astype(f32) * a_lambda_k2.astype(f32))) + lambda_init)
    q = partial_rope(a_q.reshape(bsz, seq, A_HEADS * 2, A_QK_DIM), positions)
    k = partial_rope(a_k.reshape(bsz, seq, A_HEADS * 2, A_QK_DIM), positions)
    q = q.reshape(bsz, seq, A_HEADS, 2, A_QK_DIM)
    k = k.reshape(bsz, seq, A_HEADS, 2, A_QK_DIM)
    v = a_v.reshape(bsz, seq, A_HEADS, A_V_DIM)
    o_a = diff_attention(q, k, v, lam)
    o_a = rmsnorm(o_a, a_subln) * (1.0 - lambda_init)
    o_a = o_a.reshape(bsz, seq, A_V_W) * jax.nn.silu(a_z)

    qkv = jax.nn.silu(causal_depthwise_conv(b_qkv, b_conv_w))
    bq, bk, bv = jnp.split(qkv, (B_K_W, 2 * B_K_W), axis=-1)
    bq = bq.reshape(bsz, seq, B_HEADS, B_K_DIM).astype(f32)
    bk = bk.reshape(bsz, seq, B_HEADS, B_K_DIM).astype(f32)
    bv = bv.reshape(bsz, seq, B_HEADS, B_V_DIM)
    bq = bq * lax.rsqrt(jnp.sum(bq * bq, axis=-1, keepdims=True) + EPS)
    bk = bk * lax.rsqrt(jnp.sum(bk * bk, axis=-1, keepdims=True) + EPS)
    beta = jax.nn.sigmoid(b_beta.astype(f32))
    g = -jnp.exp(b_a_log.astype(f32)) * jax.nn.softplus(b_a.astype(f32) + b_dt_bias.astype(f32))
    o_b = gated_delta_rule_chunked(bq, bk, bv, g, beta)
    o_b = rmsnorm(o_b, b_head_norm).reshape(bsz, seq, B_V_W) * jax.nn.silu(b_z)

    return jnp.concatenate([o_a, o_b], axis=-1) @ w_out_ab


def mixer_c(h, w_in_c, c_forget_bias, w_out_c):
    bsz, seq, _ = h.shape
    proj = h @ w_in_c
    q, k, v, z, f_logit = jnp.split(proj, C_SPLITS, axis=-1)
    log_f = jax.nn.log_sigmoid(f_logit.astype(jnp.float32) + c_forget_bias.astype(jnp.float32))
    cum_logf = jnp.cumsum(log_f, axis=1)
    o = forgetting_attention(q.reshape(bsz, seq, C_HEADS, C_HEAD_DIM),
                             k.reshape(bsz, seq, C_HEADS, C_HEAD_DIM),
                             v.reshape(bsz, seq, C_HEADS, C_HEAD_DIM), cum_logf)
    return (o.reshape(bsz, seq, MIX_C) * jax.nn.silu(z)) @ w_out_c


def setup_inputs(seed: int = 0) -> dict:
    key = jax.random.key(seed)
    ks = jax.random.split(key, 20)
    f32 = jnp.float32

    def nrm(k, shape, scale):
        return jax.random.normal(k, shape, f32) * scale

    x = nrm(ks[0], (BATCH, SEQ, D_MODEL), 1.0)
    positions = jnp.tile(jnp.arange(SEQ, dtype=jnp.int32)[None, :], (BATCH, 1))
    pre_norm = 1.0 + nrm(ks[1], (DEPTH, D_MODEL), 0.05)
    post_norm = 1.0 + nrm(ks[2], (DEPTH, D_MODEL), 0.05)
    w_in_ab = nrm(ks[3], (D_MODEL, IN_AB), D_MODEL ** -0.5)
    a_lambda_q1 = nrm(ks[4], (A_QK_DIM,), 0.1)
    a_lambda_k1 = nrm(ks[5], (A_QK_DIM,), 0.1)
    a_lambda_q2 = nrm(ks[6], (A_QK_DIM,), 0.1)
    a_lambda_k2 = nrm(ks[7], (A_QK_DIM,), 0.1)
    a_subln = 1.0 + nrm(ks[8], (A_V_DIM,), 0.05)
    b_conv_w = nrm(ks[9], (CONV_K, B_CONV_CH), CONV_K ** -0.5)
    b_a_log = jnp.log(jax.random.uniform(ks[10], (B_HEADS,), f32, 1.0, 16.0))
    dt = jnp.exp(jax.random.uniform(ks[11], (B_HEADS,), f32, math.log(1e-3), math.log(1e-1)))
    b_dt_bias = dt + jnp.log(-jnp.expm1(-dt))
    b_head_norm = 1.0 + nrm(ks[12], (B_V_DIM,), 0.05)
    w_out_ab = nrm(ks[13], (MIX_AB, D_MODEL), MIX_AB ** -0.5)
    w_in_c = nrm(ks[14], (D_MODEL, IN_C), D_MODEL ** -0.5)
    c_forget_bias = jax.random.uniform(ks[15], (C_HEADS,), f32, 1.0, 5.0)
    w_out_c = nrm(ks[16], (MIX_C, D_MODEL), MIX_C ** -0.5)
    return {"x": x, "positions": positions, "pre_norm": pre_norm, "post_norm": post_norm,
            "w_in_ab": w_in_ab, "a_lambda_q1": a_lambda_q1, "a_lambda_k1": a_lambda_k1,
            "a_lambda_q2": a_lambda_q2, "a_lambda_k2": a_lambda_k2, "a_subln": a_subln,
            "b_conv_w": b_conv_w, "b_a_log": b_a_log, "b_dt_bias": b_dt_bias,
            "b_head_norm": b_head_norm, "w_out_ab": w_out_ab,
            "w_in_c": w_in_c, "c_forget_bias": c_forget_bias, "w_out_c": w_out_c}


def reference(x, positions, pre_norm, post_norm, w_in_ab, a_lambda_q1, a_lambda_k1, a_lambda_q2,
              a_lambda_k2, a_subln, b_conv_w, b_a_log, b_dt_bias, b_head_norm, w_out_ab,
              w_in_c, c_forget_bias, w_out_c):
    for layer in range(DEPTH):
        h = rmsnorm(x, pre_norm[layer])
        if layer % 2 == 0:
            y = mixer_ab(h, positions, layer, w_in_ab, a_lambda_q1, a_lambda_k1, a_lambda_q2,
                         a_lambda_k2, a_subln, b_conv_w, b_a_log, b_dt_bias, b_head_norm, w_out_ab)
        else:
            y = mixer_c(h, w_in_c, c_forget_bias, w_out_c)
        x = x + rmsnorm(y, post_norm[layer])
    return x
```

```python
import contextlib
import math
import numpy as np
import concourse.bass as bass
import concourse.mybir as mybir
from concourse.bass_utils import run_bass_kernel_spmd

F32 = mybir.dt.float32
BF16 = mybir.dt.bfloat16
I32 = mybir.dt.int32
AF = mybir.ActivationFunctionType
ALU = mybir.AluOpType
AX = mybir.AxisListType

D = 1024
SEQ = 2048
NT = 16
KC = 8
EPS = 1e-6
NCORES = 8
LAYERS = (0, 1)
STAGE = 99
RUN_KW = {}
LAST = {}
import os
VVAR = int(os.environ.get('VVAR', '3'))


class StopStage(Exception):
    pass


class Buf:
    __slots__ = ("name", "w", "r", "psum")

    def __init__(self, name="", psum=False):
        self.name = name
        self.w = None
        self.r = {}
        self.psum = psum


class Sched:
    ENGS = ("pe", "act", "dve", "pool", "sp")

    def __init__(self, nc, stack, n_dma_sems=6):
        self.nc = nc
        self.e = {"pe": nc.tensor, "act": nc.scalar, "dve": nc.vector,
                  "pool": nc.gpsimd, "sp": nc.sync}
        self.semh = {}
        self.cnt = {}
        for k in self.ENGS:
            self.semh[k] = stack.enter_context(nc.semaphore("s_" + k))
            self.cnt[k] = 0
        self.dq = {}
        for q in ("sp", "act", "pool"):
            slots = []
            for i in range(n_dma_sems):
                key = "d_%s%d" % (q, i)
                self.semh[key] = stack.enter_context(nc.semaphore(key))
                self.cnt[key] = 0
                slots.append(key)
            self.dq[q] = [slots, 0]
        self.seen = {k: {} for k in self.ENGS}
        self.n_ins = {k: 0 for k in self.ENGS}
        self.n_wait = {k: 0 for k in self.ENGS}

    def _wait(self, eng, deps):
        need = {}
        seen = self.seen[eng]
        for (k, v) in deps:
            if eng == "pe" and k == "pe":
                continue
            if seen.get(k, 0) < v and need.get(k, 0) < v:
                need[k] = v
        for k, v in need.items():
            self.e[eng].wait_ge(self.semh[k], v)
            seen[k] = v
            self.n_wait[eng] += 1

    @staticmethod
    def _deps(reads, writes):
        deps = []
        for b in reads:
            if b.w is not None:
                deps.append(b.w)
            if b.psum:
                deps.extend(b.r.items())
        for b in writes:
            if b.w is not None:
                deps.append(b.w)
            deps.extend(b.r.items())
        return deps

    @staticmethod
    def _mark(tok, reads, writes):
        for b in reads:
            if b.r.get(tok[0], 0) < tok[1]:
                b.r[tok[0]] = tok[1]
        for b in writes:
            b.w = tok
            b.r = {}

    def op(self, eng, fn, reads=(), writes=(), inc=True):
        self._wait(eng, self._deps(reads, writes))
        ins = fn(self.e[eng])
        self.n_ins[eng] += 1
        if inc:
            self.cnt[eng] += 1
            ins.then_inc(self.semh[eng], 1)
            tok = (eng, self.cnt[eng])
        else:
            tok = (eng, self.cnt[eng] + 1)
        self._mark(tok, reads, writes)
        return ins

    def dma(self, q, out, in_, reads=(), writes=(), **kw):
        slots, idx = self.dq[q]
        key = slots[idx % len(slots)]
        self.dq[q][1] = idx + 1
        deps = self._deps(reads, writes)
        if self.cnt[key] > 0:
            deps.append((key, self.cnt[key]))
        self._wait(q, deps)
        ins = self.e[q].dma_start(out=out, in_=in_, **kw)
        self.cnt[key] += 16
        ins.then_inc(self.semh[key], 16)
        self._mark((key, self.cnt[key]), reads, writes)
        return ins

    def fence(self):
        allc = [(k, v) for k, v in self.cnt.items() if v > 0]
        for eng in self.ENGS:
            self._wait(eng, allc)

    def finish(self, bufs, eng="sp"):
        deps = []
        for b in bufs:
            if b.w is not None:
                deps.append(b.w)
            deps.extend(b.r.items())
        self._wait(eng, deps)


def host_consts():
    c = {}
    j = np.arange(128)
    U = (j[:, None] <= j[None, :]).astype(np.float32)
    c["c_uext"] = np.concatenate([U, np.ones((128, 1), np.float32)], axis=1)
    c["c_ident"] = np.eye(128, dtype=np.float32)
    p = np.arange(128)
    d = p % 64
    half = 8
    inv_freq = (np.float32(500000.0) ** (-(np.arange(half, dtype=np.float32) * np.float32(2.0)) / np.float32(16.0))).astype(np.float32)
    freq = np.where(d < 16, inv_freq[d % 8], 0.0).astype(np.float32)
    sign = np.where(d < 8, -1.0, np.where(d < 16, 1.0, 0.0)).astype(np.float32)
    c["c_rope"] = np.stack([freq, sign], axis=1).astype(np.float32)
    same = (j[:, None] // 64) == (j[None, :] // 64)
    ublk = (same & (j[:, None] <= j[None, :])).astype(np.float32)
    blk = same.astype(np.float32)
    strictblk = (same & (j[:, None] > j[None, :])).astype(np.float32)
    half0 = np.repeat((j < 64).astype(np.float32)[:, None], 128, axis=1)
    half1 = np.repeat((j >= 64).astype(np.float32)[:, None], 128, axis=1)
    c["c_gdn"] = np.stack([ublk, blk, strictblk, half0, half1, np.eye(128, dtype=np.float32)], axis=1)
    return c


def build(nseq, layers=(0, 1)):
    nc = bass.Bass("TRN2", target_bir_lowering=False)
    dt_in = lambda name, shape, dt=F32: nc.dram_tensor(name, list(shape), dt, kind="ExternalInput").ap()
    x_d = dt_in("x", [nseq, SEQ, D])
    out_d = nc.dram_tensor("out", [nseq, SEQ, D], F32, kind="ExternalOutput").ap()
    pre_d = dt_in("pre_norm", [2, D])
    post_d = dt_in("post_norm", [2, D])
    uext_d = dt_in("c_uext", [128, 129])
    ident_d = dt_in("c_ident", [128, 128])
    wc_d = dt_in("wc", [8, 4, 128, KC, 128])
    wf_d = dt_in("wf", [128, KC, 16])
    woc_d = dt_in("woc", [128, 8, D])
    fb_d = dt_in("c_forget_bias", [1, 16])
    pos_d = dt_in("positions", [nseq, SEQ], I32)
    rope_d = dt_in("c_rope", [128, 2])
    wa_d = dt_in("wa", [4, 6, 128, KC, 128])
    woab_d = dt_in("woab", [128, 8, D])
    lam_d = dt_in("lam4", [4, 64])
    subln_d = dt_in("a_subln", [128, 1])
    gdnc_d = dt_in("c_gdn", [128, 6, 128])
    wbq_d = dt_in("wbq", [12, 128, KC, 128])
    cw_d = dt_in("cw", [128, 12, 4])
    wbz_d = dt_in("wbz", [128, KC, 512])
    wba_d = dt_in("wba", [128, KC, 8])
    alog_d = dt_in("b_a_log", [1, 4])
    dtb_d = dt_in("b_dt_bias", [1, 4])
    hn_d = dt_in("b_head_norm", [1, 128])

    with contextlib.ExitStack() as st:
        S = Sched(nc, st)
        uid = [0]
        def sb(name, shape, dt):
            uid[0] += 1
            return st.enter_context(nc.sbuf_tensor("t%d_%s" % (uid[0], name), list(shape), dt))
        x_sb = sb("x_sb", [128, NT, D], F32)
        hT = sb("hT", [128, KC, SEQ], BF16)
        og = sb("og", [128, 8, SEQ], BF16)
        pre_bc = sb("norm_bc", [128, D], F32)
        post_bc = pre_bc
        ident = sb("ident", [128, 128], BF16)
        uext = sb("uext", [128, 129], F32)
        ones_f = sb("ones_f", [128, 128], F32)
        maskb = sb("maskb", [128, 128], BF16)
        ss = sb("ss", [128, NT], F32)
        rstd = sb("rstd", [128, NT], F32)
        junk = sb("junk", [128, 512], BF16)
        PB = [st.enter_context(nc.psum_tensor("pb%d" % i, [128, 512], F32)) for i in range(8)]
        bPB = [[Buf("pb%d_0" % i, True), Buf("pb%d_1" % i, True)] for i in range(8)]

        b_x = [Buf("x%d" % i) for i in range(NT)]
        b_hT = [Buf("hT%d" % i) for i in range(NT)]
        b_og = [[Buf() for _ in range(4)] for _ in range(8)]
        b_const = Buf("const")
        b_norm = Buf("normbc")
        b_ss = Buf("ss")
        b_rstd = Buf("rstd")
        b_junk = Buf("junk")
        b_xn = [Buf(), Buf()]
        b_wo = Buf("wo")
        b_out = Buf("out")

        S.dma("sp", uext[:], uext_d[:, :], writes=[b_const])
        S.dma("pool", ident[:], ident_d[:, :], writes=[b_const])
        S.dma("pool", maskb[:], uext_d[:, 0:128], writes=[b_const])
        S.op("pool", lambda e: e.memset(ones_f[:], 1.0), writes=[b_const])

        cur_layer = [0]

        def load_norms(layer):
            cur_layer[0] = layer
            S.dma("sp", pre_bc[:], bass.AP(pre_d.tensor, layer * D, [[0, 128], [1, D]]), writes=[b_norm])

        def load_post():
            S.dma("sp", post_bc[:], bass.AP(post_d.tensor, cur_layer[0] * D, [[0, 128], [1, D]]), writes=[b_norm])

        def prenorm():
            xn = [sb_l["tmpf"][k][:].bitcast(BF16) for k in range(2)]
            b_xn = b_l["tmpf"]
            for i in range(NT):
                S.op("act", lambda e: e.activation(out=og[:, 0, 0:D], in_=x_sb[:, i, :], func=AF.Square,
                                                   accum_out=ss[:, i:i + 1]),
                     reads=[b_x[i]], writes=[b_og[0][0], b_og[0][1], b_ss])
            S.op("act", lambda e: e.activation(out=rstd[:], in_=ss[:], func=AF.Ln, scale=1.0 / D, bias=EPS),
                 reads=[b_ss], writes=[b_rstd])
            S.op("act", lambda e: e.activation(out=rstd[:], in_=rstd[:], func=AF.Exp, scale=-0.5),
                 reads=[b_rstd], writes=[b_rstd])
            for i in range(NT):
                xb = xn[i % 2]
                bxb = b_xn[i % 2]
                S.op("dve", lambda e: e.scalar_tensor_tensor(out=xb, in0=x_sb[:, i, :], scalar=rstd[:, i:i + 1],
                                                             in1=pre_bc[:], op0=ALU.mult, op1=ALU.mult),
                     reads=[b_x[i], b_rstd, b_norm], writes=[bxb])
                bank = 6 + (i % 2)
                pv = PB[bank][:].bitcast(BF16)
                for kc in range(KC):
                    S.op("pe", lambda e: e.transpose(pv[:, kc * 128:(kc + 1) * 128], xb[:, kc * 128:(kc + 1) * 128],
                                                     ident[:]),
                         reads=[bxb, b_const], writes=bPB[bank], inc=(kc == KC - 1))
                eng = "act" if i % 2 == 0 else "dve"
                src = pv.rearrange("p (k t) -> p k t", k=KC)
                dst = hT[:, :, i * 128:(i + 1) * 128]
                if eng == "act":
                    S.op("act", lambda e: e.activation(out=dst, in_=src, func=AF.Copy),
                         reads=bPB[bank], writes=[b_hT[i]])
                else:
                    S.op("dve", lambda e: e.tensor_copy(out=dst, in_=src),
                         reads=bPB[bank], writes=[b_hT[i]])

        def outproj_residual(last, seq, wo):
            ss2 = sb_l["ss2"]; rs2 = sb_l["rs2"]; tmpf = sb_l["tmpf"]
            load_post()
            for i in range(NT):
                banks = (4 + 2 * (i % 2), 5 + 2 * (i % 2))
                for hf in range(2):
                    bk = banks[hf]
                    for p in range(8):
                        S.op("pe", lambda e: e.matmul(PB[bk][:, :], lhsT=og[:, p, i * 128:(i + 1) * 128],
                                                      rhs=wo[:, p, hf * 512:(hf + 1) * 512],
                                                      start=(p == 0), stop=(p == 7)),
                             reads=[b_og[p][i // 4], b_wo], writes=bPB[bk], inc=(p == 7))
                    S.op("act", lambda e: e.activation(out=junk[:, 0:512], in_=PB[bk][:, :], func=AF.Square,
                                                       accum_out=ss2[:, 2 * i + hf:2 * i + hf + 1]),
                         reads=bPB[bk], writes=[b_junk, b_l["ss2"]])
                S.op("dve", lambda e: e.tensor_tensor(out=rs2[:, i:i + 1], in0=ss2[:, 2 * i:2 * i + 1],
                                                      in1=ss2[:, 2 * i + 1:2 * i + 2], op=ALU.add),
                     reads=[b_l["ss2"]], writes=[b_l["rs2"]])
                S.op("act", lambda e: e.activation(out=rs2[:, i:i + 1], in_=rs2[:, i:i + 1], func=AF.Ln,
                                                   scale=1.0 / D, bias=EPS),
                     reads=[b_l["rs2"]], writes=[b_l["rs2"]])
                S.op("act", lambda e: e.activation(out=rs2[:, i:i + 1], in_=rs2[:, i:i + 1], func=AF.Exp, scale=-0.5),
                     reads=[b_l["rs2"]], writes=[b_l["rs2"]])
                for hf in range(2):
                    bk = banks[hf]
                    tf = tmpf[hf]
                    S.op("dve", lambda e: e.scalar_tensor_tensor(out=tf[:], in0=PB[bk][:, :], scalar=rs2[:, i:i + 1],
                                                                 in1=post_bc[:, hf * 512:(hf + 1) * 512],
                                                                 op0=ALU.mult, op1=ALU.mult),
                         reads=bPB[bk] + [b_l["rs2"], b_norm], writes=[b_l["tmpf"][hf]])
                    S.op("pool", lambda e: e.tensor_tensor(out=x_sb[:, i, hf * 512:(hf + 1) * 512],
                                                           in0=x_sb[:, i, hf * 512:(hf + 1) * 512], in1=tf[:],
                                                           op=ALU.add),
                         reads=[b_l["tmpf"][hf], b_x[i]], writes=[b_x[i]])
                if last:
                    S.dma("sp", out_d[seq, i * 128:(i + 1) * 128, :], x_sb[:, i, :], reads=[b_x[i]], writes=[b_out])


        PI = 3.141592653589793
        LAMBDA_INIT = 0.8 - 0.6 * math.exp(-0.3 * 0)

        def layer0(seq, last):
            nonlocal sb_l, b_l
            with contextlib.ExitStack() as st1:
                st2 = st1.enter_context(contextlib.ExitStack())
                cur = [st1]

                def sl(name, shape, dt):
                    uid[0] += 1
                    return cur[0].enter_context(nc.sbuf_tensor("t%d_%s" % (uid[0], name), list(shape), dt))
                sb_l = {}
                b_l = {}
                sb_l["ss2"] = sl("ss2", [128, 2 * NT], F32); b_l["ss2"] = Buf()
                sb_l["rs2"] = sl("rs2", [128, NT], F32); b_l["rs2"] = Buf()
                sb_l["tmpf"] = [sl("tmpf%d" % i, [128, 512], F32) for i in range(2)]; b_l["tmpf"] = [Buf(), Buf()]
                load_norms(0)
                prenorm()
                cur[0] = st2
                ropec = sl("ropec", [128, 2], F32)
                lamt = sl("lamt", [128, 4, 64], F32)
                lamp = sl("lamp", [128, 2, 64], F32)
                lams = sl("lams", [128, 2], F32)
                neglam = sl("neglam", [128, 1], F32)
                subcol = sl("subcol", [128, 1], F32)
                ones_b = sl("ones_b", [128, 128], BF16)
                posi = sl("posi", [128, SEQ], I32)
                Ct = sl("Ct", [128, SEQ], F32)
                St = sl("St", [128, SEQ], F32)
                qT = sl("qTa", [128, SEQ], BF16)
                kT = sl("kTa", [128, SEQ], BF16)
                Vh = sl("Vh", [128, NT, 128], BF16)
                wsl = [sl("wa%d" % i, [128, KC, 128], BF16) for i in range(6)]
                pt = [sl("pt%d" % i, [128, 512], BF16) for i in range(3)]
                ta = sb_l["tmpf"][0]; tb = sb_l["tmpf"][1]
                tc = sl("tc", [128, 512], F32); td = sl("td", [128, 512], F32)
                b_ropec, b_lam, b_neglam, b_subcol, b_onesb, b_posi, b_Ct, b_St = [Buf() for _ in range(8)]
                b_qT, b_kT, b_Vh = Buf(), Buf(), Buf()
                b_wsl = [Buf() for _ in range(6)]
                b_pt = [Buf() for _ in range(3)]
                b_ta, b_tb, b_tc, b_td = b_l["tmpf"][0], b_l["tmpf"][1], Buf(), Buf()

                S.dma("sp", ropec[:], rope_d[:, :], writes=[b_ropec])
                S.op("pool", lambda e: e.memset(ones_b[:], 1.0), writes=[b_onesb])
                S.dma("sp", lamt[:], bass.AP(lam_d.tensor, 0, [[0, 128], [64, 4], [1, 64]]), writes=[b_lam])
                S.op("dve", lambda e: e.tensor_tensor(out=lamp[:, 0, :], in0=lamt[:, 0, :], in1=lamt[:, 1, :], op=ALU.mult),
                     reads=[b_lam], writes=[b_lam])
                S.op("dve", lambda e: e.tensor_tensor(out=lamp[:, 1, :], in0=lamt[:, 2, :], in1=lamt[:, 3, :], op=ALU.mult),
                     reads=[b_lam], writes=[b_lam])
                S.op("dve", lambda e: e.tensor_reduce(out=lams[:], in_=lamp[:], axis=AX.X, op=ALU.add),
                     reads=[b_lam], writes=[b_lam])
                S.op("act", lambda e: e.activation(out=lams[:], in_=lams[:], func=AF.Exp), reads=[b_lam], writes=[b_lam])
                S.op("dve", lambda e: e.tensor_tensor(out=neglam[:], in0=lams[:, 1:2], in1=lams[:, 0:1], op=ALU.subtract),
                     reads=[b_lam], writes=[b_neglam])
                S.op("dve", lambda e: e.tensor_scalar(out=neglam[:], in0=neglam[:], scalar1=-LAMBDA_INIT, scalar2=None, op0=ALU.add),
                     reads=[b_neglam], writes=[b_neglam])
                S.dma("sp", subcol[:], subln_d[:, :], writes=[b_subcol])
                S.op("dve", lambda e: e.tensor_scalar(out=subcol[:], in0=subcol[:], scalar1=1.0 - LAMBDA_INIT, scalar2=None, op0=ALU.mult),
                     reads=[b_subcol], writes=[b_subcol])
                S.dma("sp", posi[:], bass.AP(pos_d.tensor, seq * SEQ, [[0, 128], [1, SEQ]]), writes=[b_posi])

                def sin_table(dst, bdst, phase, signed):
                    S.op("dve", lambda e: e.tensor_copy(out=dst[:], in_=posi[:]), reads=[b_posi], writes=[bdst])
                    S.op("dve", lambda e: e.tensor_scalar(out=dst[:], in0=dst[:], scalar1=ropec[:, 0:1], scalar2=phase,
                                                          op0=ALU.mult, op1=ALU.add), reads=[bdst, b_ropec], writes=[bdst])
                    for c4 in range(4):
                        sl_ = slice(c4 * 512, (c4 + 1) * 512)
                        tI = tc[:].bitcast(I32)
                        S.op("dve", lambda e: e.tensor_scalar(out=td[:], in0=dst[:, sl_], scalar1=1.0 / (2 * PI), scalar2=None,
                                                              op0=ALU.mult), reads=[bdst], writes=[b_td])
                        S.op("dve", lambda e: e.tensor_copy(out=tI, in_=td[:]), reads=[b_td], writes=[b_tc])
                        S.op("dve", lambda e: e.tensor_copy(out=td[:], in_=tI), reads=[b_tc], writes=[b_td])
                        S.op("dve", lambda e: e.scalar_tensor_tensor(out=td[:], in0=td[:], scalar=-2 * PI, in1=dst[:, sl_],
                                                                     op0=ALU.mult, op1=ALU.add),
                             reads=[b_td, bdst], writes=[b_td])
                        S.op("dve", lambda e: e.tensor_scalar(out=tc[:], in0=td[:], scalar1=PI, scalar2=-2 * PI,
                                                              op0=ALU.is_gt, op1=ALU.mult), reads=[b_td], writes=[b_tc])
                        S.op("dve", lambda e: e.tensor_tensor(out=td[:], in0=td[:], in1=tc[:], op=ALU.add),
                             reads=[b_td, b_tc], writes=[b_td])
                        S.op("dve", lambda e: e.tensor_scalar(out=td[:], in0=td[:], scalar1=-PI, scalar2=PI,
                                                              op0=ALU.max, op1=ALU.min), reads=[b_td], writes=[b_td])
                        S.op("act", lambda e: e.activation(out=dst[:, sl_], in_=td[:], func=AF.Sin),
                             reads=[b_td], writes=[bdst])
                    if signed:
                        S.op("dve", lambda e: e.tensor_scalar(out=dst[:], in0=dst[:], scalar1=ropec[:, 1:2], scalar2=None,
                                                              op0=ALU.mult), reads=[bdst, b_ropec], writes=[bdst])
                sin_table(Ct, b_Ct, PI / 2, False)
                sin_table(St, b_St, 0.0, True)

                for h in range(4):
                    for i6 in range(6):
                        S.dma("pool", wsl[i6][:], wa_d[h, i6], writes=[b_wsl[i6]])
                    n_ev = 0
                    for (i_w, dstT, bdst) in ((0, qT, b_qT), (2, kT, b_kT)):
                        for t4 in range(4):
                            bks = (6, 7) if n_ev % 2 == 0 else (4, 5)
                            n_ev += 1
                            tsl = slice(t4 * 512, (t4 + 1) * 512)
                            for jj in range(2):
                                for kc in range(KC):
                                    S.op("pe", lambda e: e.matmul(PB[bks[jj]][:, :], lhsT=wsl[i_w + jj][:, kc, :],
                                                                  rhs=hT[:, kc, tsl], start=(kc == 0), stop=(kc == KC - 1)),
                                         reads=[b_wsl[i_w + jj]] + b_hT[4 * t4:4 * t4 + 4], writes=bPB[bks[jj]],
                                         inc=(kc == KC - 1))
                            S.op("dve", lambda e: e.tensor_tensor(out=tc[:], in0=PB[bks[0]][:, :], in1=Ct[:, tsl], op=ALU.mult),
                                 reads=bPB[bks[0]] + [b_Ct], writes=[b_tc])
                            S.op("dve", lambda e: e.tensor_tensor(out=td[:], in0=PB[bks[1]][:, :], in1=St[:, tsl], op=ALU.mult),
                                 reads=bPB[bks[1]] + [b_St], writes=[b_td])
                            S.op("pool", lambda e: e.tensor_tensor(out=dstT[:, tsl], in0=tc[:], in1=td[:], op=ALU.add),
                                 reads=[b_tc, b_td], writes=[bdst])
                    for t4 in range(4):
                        for kc in range(KC):
                            S.op("pe", lambda e: e.matmul(PB[6][:, :], lhsT=wsl[4][:, kc, :],
                                                          rhs=hT[:, kc, t4 * 512:(t4 + 1) * 512],
                                                          start=(kc == 0), stop=(kc == KC - 1)),
                                 reads=[b_wsl[4]] + b_hT[4 * t4:4 * t4 + 4], writes=bPB[6], inc=(kc == KC - 1))
                        S.op("act", lambda e: e.activation(out=pt[0][:], in_=PB[6][:, :], func=AF.Copy),
                             reads=bPB[6], writes=[b_pt[0]])
                        pbf = PB[7][:, :].bitcast(BF16)[:, 0:512].rearrange("p (j c) -> p j c", c=128)
                        for j in range(4):
                            S.op("pe", lambda e: e.transpose(pbf[:, j, :], pt[0][:, j * 128:(j + 1) * 128], ident[:]),
                                 reads=[b_pt[0], b_const], writes=bPB[7], inc=(j == 3))
                        S.op("act", lambda e: e.activation(out=Vh[:, 4 * t4:4 * t4 + 4, :], in_=pbf, func=AF.Copy),
                             reads=bPB[7], writes=[b_Vh])
                    deferred = []
                    jobs = []
                    for Qc in range(4):
                        for kt in range(4 * Qc + 4):
                            for c in range(2):
                                jobs.append((Qc, kt, c))

                    def emit_pv(n):
                        Qc, kt, c = jobs[n]
                        o = max(0, kt - 4 * Qc) * 128
                        N = 512 - o
                        S.op("pe", lambda e: e.matmul(PB[2 + c][:, o:512], lhsT=Vh[:, kt, :], rhs=pt[n % 3][:, 0:N],
                                                      start=(kt == 0), stop=(kt == 4 * Qc + 3)),
                             reads=[b_Vh, b_pt[n % 3]], writes=bPB[2 + c], inc=False)
                        S.op("pe", lambda e: e.matmul(PB[4 + c][:, o:512], lhsT=ones_b[:], rhs=pt[n % 3][:, 0:N],
                                                      start=(kt == 0), stop=(kt == 4 * Qc + 3)),
                             reads=[b_onesb, b_pt[n % 3]], writes=bPB[4 + c])
                        if kt == 4 * Qc + 3 and c == 1:
                            emit_epilogue(Qc)

                    def emit_epilogue(Qc):
                        qsl = slice(Qc * 512, (Qc + 1) * 512)
                        for kc in range(KC):
                            S.op("pe", lambda e: e.matmul(PB[6][:, :], lhsT=wsl[5][:, kc, :], rhs=hT[:, kc, qsl],
                                                          start=(kc == 0), stop=(kc == KC - 1)),
                                 reads=[b_wsl[5]] + b_hT[4 * Qc:4 * Qc + 4], writes=bPB[6], inc=(kc == KC - 1))
                        S.op("dve", lambda e: e.reciprocal(out=ta[:], in_=PB[4][:, :]), reads=bPB[4], writes=[b_ta])
                        S.op("dve", lambda e: e.tensor_tensor(out=tb[:], in0=PB[2][:, :], in1=ta[:], op=ALU.mult),
                             reads=bPB[2] + [b_ta], writes=[b_tb])
                        S.op("dve", lambda e: e.reciprocal(out=ta[:], in_=PB[5][:, :]), reads=bPB[5] + [b_ta], writes=[b_ta])
                        S.op("dve", lambda e: e.tensor_tensor(out=tc[:], in0=PB[3][:, :], in1=ta[:], op=ALU.mult),
                             reads=bPB[3] + [b_ta], writes=[b_tc])
                        S.op("dve", lambda e: e.scalar_tensor_tensor(out=tb[:], in0=tc[:], scalar=neglam[:, 0:1], in1=tb[:],
                                                                      op0=ALU.mult, op1=ALU.add),
                             reads=[b_tc, b_tb, b_neglam], writes=[b_tb])
                        S.op("act", lambda e: e.activation(out=tc[:], in_=tb[:], func=AF.Square), reads=[b_tb], writes=[b_tc])
                        deferred.append([4, lambda: epilogue2(Qc, h)])

                    def epilogue2(Qc, h):
                        qsl = slice(Qc * 512, (Qc + 1) * 512)
                        S.op("pe", lambda e: e.matmul(PB[7][:, :], lhsT=ones_f[:], rhs=tc[:], start=True, stop=True),
                             reads=[b_const, b_tc], writes=bPB[7])
                        S.op("act", lambda e: e.activation(out=td[:], in_=PB[7][:, :], func=AF.Ln, scale=1.0 / 128, bias=EPS),
                             reads=bPB[7], writes=[b_td])
                        S.op("act", lambda e: e.activation(out=td[:], in_=td[:], func=AF.Exp, scale=-0.5), reads=[b_td], writes=[b_td])
                        S.op("act", lambda e: e.activation(out=ta[:], in_=PB[6][:, :], func=AF.Exp, scale=-1.0),
                             reads=bPB[6] + [b_ta], writes=[b_ta])
                        S.op("act", lambda e: e.activation(out=ta[:], in_=ta[:], func=AF.Ln, bias=1.0), reads=[b_ta], writes=[b_ta])
                        S.op("act", lambda e: e.activation(out=ta[:], in_=ta[:], func=AF.Exp, scale=-1.0), reads=[b_ta], writes=[b_ta])
                        S.op("dve", lambda e: e.tensor_tensor(out=ta[:], in0=PB[6][:, :], in1=ta[:], op=ALU.mult),
                             reads=bPB[6] + [b_ta], writes=[b_ta])
                        S.op("pool", lambda e: e.tensor_tensor(out=tb[:], in0=tb[:], in1=td[:], op=ALU.mult),
                             reads=[b_tb, b_td], writes=[b_tb])
                        S.op("dve", lambda e: e.scalar_tensor_tensor(out=og[:, h, qsl], in0=tb[:], scalar=subcol[:, 0:1], in1=ta[:],
                                                                     op0=ALU.mult, op1=ALU.mult),
                             reads=[b_tb, b_ta, b_subcol], writes=[b_og[h][Qc]])

                    for n, (Qc, kt, c) in enumerate(jobs):
                        o = max(0, kt - 4 * Qc) * 128
                        N = 512 - o
                        q0 = Qc * 512 + o
                        sbk = n % 2
                        S.op("pe", lambda e: e.matmul(PB[sbk][:, 0:N], lhsT=kT[c * 64:(c + 1) * 64, kt * 128:(kt + 1) * 128],
                                                      rhs=qT[c * 64:(c + 1) * 64, q0:q0 + N], start=True, stop=True),
                             reads=[b_kT, b_qT], writes=bPB[sbk])
                        S.op("act", lambda e: e.activation(out=pt[n % 3][:, 0:N], in_=PB[sbk][:, 0:N], func=AF.Exp, scale=0.125),
                             reads=bPB[sbk], writes=[b_pt[n % 3]])
                        if kt >= 4 * Qc:
                            S.op("pool", lambda e: e.tensor_tensor(out=pt[n % 3][:, 0:128], in0=pt[n % 3][:, 0:128],
                                                                   in1=maskb[:], op=ALU.mult),
                                 reads=[b_pt[n % 3], b_const], writes=[b_pt[n % 3]])
                        if n >= 1:
                            emit_pv(n - 1)
                        for dfr in list(deferred):
                            dfr[0] -= 1
                            if dfr[0] <= 0:
                                deferred.remove(dfr)
                                dfr[1]()
                    emit_pv(len(jobs) - 1)
                    for dfr in list(deferred):
                        deferred.remove(dfr)
                        dfr[1]()
                S.fence()
                st2.close()
                st3 = st1.enter_context(contextlib.ExitStack())
                cur[0] = st3
                partB(seq, sl)
                S.fence()
                st3.close()
                cur[0] = st1
                wo = sl("wo", [128, 8, D], BF16)
                S.dma("pool", wo[:], woab_d[:, :, :], writes=[b_wo])
                outproj_residual(last, seq, wo)
                S.fence()

        def partB(seq, sl):
            gm = sl("gm", [128, 6, 128], F32)
            ublk, blk, strictblk, ident_f = gm[:, 0, :], gm[:, 1, :], gm[:, 2, :], gm[:, 5, :]
            halfsel = (gm[:, 3, :], gm[:, 4, :])
            ones_b = sl("ones_bB", [128, 128], BF16)
            wbz = sl("wbz", [128, KC, 512], BF16)
            wba = sl("wba", [128, KC, 8], BF16)
            wbq = [sl("wbq%d" % i, [128, KC, 128], BF16) for i in range(2)]
            cw = sl("cw", [128, 12, 4], F32)
            negA = sl("negA", [128, 4], F32)
            dtb = sl("dtb", [128, 4], F32)
            hn_bc = sl("hn_bc", [128, 128], F32)
            gt = {n: sl("g_" + n, [128, NT, 4], F32) for n in ("xa", "g", "beta", "negb", "G", "GL", "eG", "bG", "dG", "eGL0", "eGL1")}
            cr = sl("cr", [128, 12, 3], F32)
            xc = [sl("xc%d" % i, [128, 515], F32) for i in range(2)]
            yc = sl("yc", [128, 512], F32)
            sc = sl("sc", [128, 512], F32)
            t1 = sl("t1", [128, 512], F32)
            sqb = sl("sqb", [128, 512], BF16)
            qnT = sl("qnT", [128, 4, 512], BF16)
            knT = sl("knT", [128, 4, 512], BF16)
            vsT = sl("vsT", [128, 4, 512], BF16)
            qgT = sl("qgT", [128, 4, 128], BF16)
            kbg = sl("kbg", [128, 4, 128], BF16)
            kdec = sl("kdec", [128, 4, 128], BF16)
            vb = sl("vb", [128, 4, 128], BF16)
            gb = yc[:].rearrange("p (h c) -> p h c", c=128)
            gU = sc[:].rearrange("p (h c) -> p h c", c=128)
            Dm = sl("Dm", [128, 4, 128], F32)
            DTm = sl("DTm", [128, 4, 128], F32)
            Pm1 = sl("Pm", [128, 4, 128], F32)
            Qm1 = sl("Qm", [128, 4, 128], F32)
            Pm = [Pm1, Pm1]
            Qm = [Qm1, Qm1]
            TT = sl("TT", [128, 4, 128], F32)
            TTb = sl("TTb", [128, 4, 128], BF16)
            qkDT = sl("qkDT", [128, 4, 128], BF16)
            u_t = Dm[:].rearrange("p h c -> p (h c)")
            wT = sl("wT", [128, 4, 128], BF16)
            vnew = sl("vnew", [128, 512], BF16)
            St = sl("Sst", [128, 4, 128], F32)
            Sdec = sl("Sdec", [128, 4, 128], F32)
            Sb = sl("Sb", [128, 4, 128], BF16)
            zs = DTm[:].rearrange("p h c -> p (h c)")
            sso = sl("sso", [128, 4], F32)
            ot = sl("ot", [128, 512], BF16)
            B = {n: Buf(n) for n in ("gm", "onesb", "wbz", "wba", "cw", "negA", "dtb", "hn", "gates", "cr", "yc", "sc", "t1", "sqb",
                                    "qnT", "knT", "vsT", "qgT", "kbg", "kdec", "vb", "gb", "gU", "Dm", "DTm", "TT", "TTb",
                                    "qkDT", "u", "wT", "vnew", "S", "Sdec", "Sb", "zs", "sso", "ot")}
            b_wbq = [Buf(), Buf()]
            b_xc = [Buf(), Buf()]
            b_Pm = [Buf()] * 2
            b_Qm = [Buf()] * 2
            B["gb"] = B["yc"]; B["gU"] = B["sc"]; B["u"] = B["Dm"]; B["zs"] = B["DTm"]

            S.dma("sp", gm[:], gdnc_d[:, :, :], writes=[B["gm"]])
            S.op("pool", lambda e: e.memset(ones_b[:], 1.0), writes=[B["onesb"]])
            S.dma("pool", wbz[:], wbz_d[:, :, :], writes=[B["wbz"]])
            S.dma("pool", wba[:], wba_d[:, :, :], writes=[B["wba"]])
            S.dma("sp", cw[:], cw_d[:, :, :], writes=[B["cw"]])
            S.dma("sp", negA[:], bass.AP(alog_d.tensor, 0, [[0, 128], [1, 4]]), writes=[B["negA"]])
            S.dma("sp", dtb[:], bass.AP(dtb_d.tensor, 0, [[0, 128], [1, 4]]), writes=[B["dtb"]])
            S.dma("sp", hn_bc[:], bass.AP(hn_d.tensor, 0, [[0, 128], [1, 128]]), writes=[B["hn"]])
            S.op("act", lambda e: e.activation(out=negA[:], in_=negA[:], func=AF.Exp), reads=[B["negA"]], writes=[B["negA"]])
            S.op("dve", lambda e: e.tensor_scalar(out=negA[:], in0=negA[:], scalar1=-1.0, scalar2=None, op0=ALU.mult),
                 reads=[B["negA"]], writes=[B["negA"]])
            S.op("pool", lambda e: e.memset(cr[:], 0.0), writes=[B["cr"]])
            S.op("pool", lambda e: e.memset(St[:], 0.0), writes=[B["S"]])
            S.op("pool", lambda e: e.memset(Sb[:], 0.0), writes=[B["Sb"]])

            ba_ps = PB[0][:, 0:NT * 8].rearrange("p (t c) -> p t c", c=8)
            for i in range(NT):
                for kc in range(KC):
                    S.op("pe", lambda e: e.matmul(ba_ps[:, i, :], lhsT=hT[:, kc, i * 128:(i + 1) * 128], rhs=wba[:, kc, :],
                                                  start=(kc == 0), stop=(kc == KC - 1)),
                         reads=[b_hT[i], B["wba"]], writes=bPB[0], inc=(kc == KC - 1))
            G = gt
            bg = [B["gates"]]
            S.op("dve", lambda e: e.tensor_tensor(out=G["xa"][:], in0=ba_ps[:, :, 4:8], in1=dtb[:, None, :].to_broadcast([128, NT, 4]),
                                                  op=ALU.add), reads=bPB[0] + [B["dtb"]], writes=bg)
            S.op("act", lambda e: e.activation(out=G["xa"][:], in_=G["xa"][:], func=AF.Exp), reads=bg, writes=bg)
            S.op("act", lambda e: e.activation(out=G["xa"][:], in_=G["xa"][:], func=AF.Ln, bias=1.0), reads=bg, writes=bg)
            S.op("dve", lambda e: e.tensor_tensor(out=G["g"][:], in0=G["xa"][:], in1=negA[:, None, :].to_broadcast([128, NT, 4]),
                                                  op=ALU.mult), reads=bg + [B["negA"]], writes=bg)
            S.op("act", lambda e: e.activation(out=G["beta"][:], in_=ba_ps[:, :, 0:4], func=AF.Exp, scale=-1.0),
                 reads=bPB[0] + bg, writes=bg)
            S.op("act", lambda e: e.activation(out=G["beta"][:], in_=G["beta"][:], func=AF.Ln, bias=1.0), reads=bg, writes=bg)
            S.op("act", lambda e: e.activation(out=G["beta"][:], in_=G["beta"][:], func=AF.Exp, scale=-1.0), reads=bg, writes=bg)
            S.op("dve", lambda e: e.tensor_scalar(out=G["negb"][:], in0=G["beta"][:], scalar1=-1.0, scalar2=None, op0=ALU.mult),
                 reads=bg, writes=bg)
            gflat = G["g"][:].rearrange("p t c -> p (t c)")
            S.op("pe", lambda e: e.matmul(PB[1][:, 0:64], lhsT=ublk, rhs=gflat, start=True, stop=True),
                 reads=bg + [B["gm"]], writes=bPB[1], inc=False)
            S.op("pe", lambda e: e.matmul(PB[1][:, 64:128], lhsT=blk, rhs=gflat, start=True, stop=True),
                 reads=bg + [B["gm"]], writes=bPB[1], inc=False)
            S.op("pe", lambda e: e.matmul(PB[1][:, 128:192], lhsT=halfsel[0], rhs=gflat, start=True, stop=True),
                 reads=bg + [B["gm"]], writes=bPB[1], inc=False)
            S.op("pe", lambda e: e.matmul(PB[1][:, 192:256], lhsT=halfsel[1], rhs=gflat, start=True, stop=True),
                 reads=bg + [B["gm"]], writes=bPB[1])
            v3 = lambda ap: ap.rearrange("p (t c) -> p t c", c=4)
            S.op("dve", lambda e: e.tensor_copy(out=G["G"][:], in_=v3(PB[1][:, 0:64])), reads=bPB[1] + bg, writes=bg)
            S.op("dve", lambda e: e.tensor_copy(out=G["GL"][:], in_=v3(PB[1][:, 64:128])), reads=bPB[1] + bg, writes=bg)
            S.op("act", lambda e: e.activation(out=G["eGL0"][:], in_=v3(PB[1][:, 128:192]), func=AF.Exp), reads=bPB[1] + bg, writes=bg)
            S.op("act", lambda e: e.activation(out=G["eGL1"][:], in_=v3(PB[1][:, 192:256]), func=AF.Exp), reads=bPB[1] + bg, writes=bg)
            S.op("act", lambda e: e.activation(out=G["eG"][:], in_=G["G"][:], func=AF.Exp), reads=bg, writes=bg)
            S.op("dve", lambda e: e.tensor_tensor(out=G["bG"][:], in0=G["beta"][:], in1=G["eG"][:], op=ALU.mult), reads=bg, writes=bg)
            S.op("dve", lambda e: e.tensor_tensor(out=G["dG"][:], in0=G["GL"][:], in1=G["G"][:], op=ALU.subtract), reads=bg, writes=bg)
            S.op("act", lambda e: e.activation(out=G["dG"][:], in_=G["dG"][:], func=AF.Exp), reads=bg, writes=bg)
            eGLb = (G["eGL0"], G["eGL1"])

            bc4 = lambda ap2: ap2[:, :, None].to_broadcast([128, 4, 128])
            hb4 = lambda ap2: ap2[:, None, :].to_broadcast([128, 4, 128])
            n_w = 0
            for blkI in range(4):
                bsl = slice(blkI * 512, (blkI + 1) * 512)
                for jc in range(12):
                    wt = wbq[n_w % 2]; bwt = b_wbq[n_w % 2]
                    xcb = xc[n_w % 2]; bxc = b_xc[n_w % 2]
                    bk = n_w % 2
                    n_w += 1
                    S.dma("pool", wt[:], wbq_d[jc], writes=[bwt])
                    for kc in range(KC):
                        S.op("pe", lambda e: e.matmul(PB[bk][:, :], lhsT=wt[:, kc, :], rhs=hT[:, kc, bsl],
                                                      start=(kc == 0), stop=(kc == KC - 1)),
                             reads=[bwt] + b_hT[4 * blkI:4 * blkI + 4], writes=bPB[bk], inc=(kc == KC - 1))
                    S.op("pool", lambda e: e.tensor_copy(out=xcb[:, 0:3], in_=cr[:, jc, :]), reads=[B["cr"]], writes=[bxc])
                    S.op("act", lambda e: e.activation(out=xcb[:, 3:515], in_=PB[bk][:, :], func=AF.Copy),
                         reads=bPB[bk], writes=[bxc])
                    S.op("pool", lambda e: e.tensor_copy(out=cr[:, jc, :], in_=xcb[:, 512:515]), reads=[bxc], writes=[B["cr"]])
                    S.op("dve", lambda e: e.tensor_scalar(out=yc[:], in0=xcb[:, 0:512], scalar1=cw[:, jc, 0:1], scalar2=None,
                                                          op0=ALU.mult), reads=[bxc, B["cw"]], writes=[B["yc"]])
                    for tap in range(1, 4):
                        S.op("dve", lambda e: e.scalar_tensor_tensor(out=yc[:], in0=xcb[:, tap:tap + 512], scalar=cw[:, jc, tap:tap + 1],
                                                                     in1=yc[:], op0=ALU.mult, op1=ALU.add),
                             reads=[bxc, B["cw"], B["yc"]], writes=[B["yc"]])
                    S.op("act", lambda e: e.activation(out=t1[:], in_=yc[:], func=AF.Exp, scale=-1.0), reads=[B["yc"]], writes=[B["t1"]])
                    S.op("act", lambda e: e.activation(out=t1[:], in_=t1[:], func=AF.Ln, bias=1.0), reads=[B["t1"]], writes=[B["t1"]])
                    S.op("act", lambda e: e.activation(out=t1[:], in_=t1[:], func=AF.Exp, scale=-1.0), reads=[B["t1"]], writes=[B["t1"]])
                    if jc >= 8:
                        S.op("dve", lambda e: e.tensor_tensor(out=vsT[:, jc - 8, :], in0=yc[:], in1=t1[:], op=ALU.mult),
                             reads=[B["yc"], B["t1"]], writes=[B["vsT"]])
                        continue
                    S.op("dve", lambda e: e.tensor_tensor(out=sc[:], in0=yc[:], in1=t1[:], op=ALU.mult),
                         reads=[B["yc"], B["t1"]], writes=[B["sc"]])
                    S.op("act", lambda e: e.activation(out=sqb[:], in_=sc[:], func=AF.Square), reads=[B["sc"]], writes=[B["sqb"]])
                    S.op("pe", lambda e: e.matmul(PB[2][:, :], lhsT=ones_b[:], rhs=sqb[:], start=True, stop=True),
                         reads=[B["onesb"], B["sqb"]], writes=bPB[2])
                    S.op("act", lambda e: e.activation(out=t1[:], in_=PB[2][:, :], func=AF.Ln, bias=EPS), reads=bPB[2] + [B["t1"]], writes=[B["t1"]])
                    isq = jc < 4
                    S.op("act", lambda e: e.activation(out=t1[:], in_=t1[:], func=AF.Exp, scale=-0.5,
                                                       bias=(-0.5 * math.log(128.0) if isq else 0.0)),
                         reads=[B["t1"]], writes=[B["t1"]])
                    dstT = qnT if isq else knT
                    bd = B["qnT"] if isq else B["knT"]
                    S.op("dve", lambda e: e.tensor_tensor(out=dstT[:, jc % 4, :], in0=sc[:], in1=t1[:], op=ALU.mult),
                         reads=[B["sc"], B["t1"]], writes=[bd])
                for tl in range(4):
                    i = 4 * blkI + tl
                    csl = slice(tl * 128, (tl + 1) * 128)
                    S.op("dve", lambda e: e.tensor_copy(out=gb[:], in_=bc4(G["g"][:, i, :])), reads=bg, writes=[B["gb"]])
                    p2 = PB[2][:, :].rearrange("p (h c) -> p h c", c=128)
                    for h in range(4):
                        S.op("pe", lambda e: e.matmul(p2[:, h, :], lhsT=gb[:, h, :], rhs=ublk, start=True, stop=True),
                             reads=[B["gb"], B["gm"]], writes=bPB[2], inc=(h == 3))
                    S.op("act", lambda e: e.activation(out=Dm[:], in_=p2, func=AF.Exp), reads=bPB[2], writes=[B["Dm"]])
                    S.op("dve", lambda e: e.tensor_tensor(out=qgT[:], in0=qnT[:, :, csl], in1=Dm[:], op=ALU.mult),
                         reads=[B["qnT"], B["Dm"]], writes=[B["qgT"]])
                    p3b = PB[3][:, :].bitcast(BF16).rearrange("p (a h c) -> p a h c", a=2, h=4)
                    for h in range(4):
                        S.op("pe", lambda e: e.transpose(p3b[:, 0, h, :], knT[:, h, csl], ident[:]),
                             reads=[B["knT"], b_const], writes=bPB[3], inc=False)
                    for h in range(4):
                        S.op("pe", lambda e: e.transpose(p3b[:, 1, h, :], vsT[:, h, csl], ident[:]),
                             reads=[B["vsT"], b_const], writes=bPB[3], inc=(h == 3))
                    S.op("dve", lambda e: e.tensor_tensor(out=kbg[:], in0=p3b[:, 0], in1=bc4(G["bG"][:, i, :]), op=ALU.mult),
                         reads=bPB[3] + bg, writes=[B["kbg"]])
                    S.op("dve", lambda e: e.tensor_tensor(out=kdec[:], in0=p3b[:, 0], in1=bc4(G["dG"][:, i, :]), op=ALU.mult),
                         reads=bPB[3] + bg, writes=[B["kdec"]])
                    S.op("dve", lambda e: e.tensor_tensor(out=vb[:], in0=p3b[:, 1], in1=bc4(G["beta"][:, i, :]), op=ALU.mult),
                         reads=bPB[3] + bg, writes=[B["vb"]])
                    S.op("dve", lambda e: e.tensor_tensor(out=gU[:], in0=hb4(ublk), in1=bc4(G["g"][:, i, :]), op=ALU.mult),
                         reads=bg + [B["gm"]], writes=[B["gU"]])
                    p4 = PB[4][:, :].rearrange("p (h c) -> p h c", c=128)
                    p5 = PB[5][:, :].rearrange("p (h c) -> p h c", c=128)
                    p6 = PB[6][:, :].rearrange("p (h c) -> p h c", c=128)
                    p7 = PB[7][:, :].rearrange("p (h c) -> p h c", c=128)
                    for h in range(4):
                        S.op("pe", lambda e: e.matmul(p4[:, h, :], lhsT=knT[:, h, csl], rhs=knT[:, h, csl], start=True, stop=True),
                             reads=[B["knT"]], writes=bPB[4], inc=(h == 3))
                    for h in range(4):
                        S.op("pe", lambda e: e.matmul(p5[:, h, :], lhsT=knT[:, h, csl], rhs=qnT[:, h, csl], start=True, stop=True),
                             reads=[B["knT"], B["qnT"]], writes=bPB[5], inc=(h == 3))
                    for h in range(4):
                        S.op("pe", lambda e: e.matmul(p6[:, h, :], lhsT=gU[:, h, :], rhs=strictblk, start=True, stop=True),
                             reads=[B["gU"], B["gm"]], writes=bPB[6], inc=(h == 3))
                    for h in range(4):
                        S.op("pe", lambda e: e.matmul(p7[:, h, :], lhsT=strictblk, rhs=gU[:, h, :], start=True, stop=True),
                             reads=[B["gU"], B["gm"]], writes=bPB[7], inc=(h == 3))
                    S.op("act", lambda e: e.activation(out=Dm[:], in_=p6, func=AF.Exp), reads=bPB[6] + [B["Dm"]], writes=[B["Dm"]])
                    S.op("act", lambda e: e.activation(out=DTm[:], in_=p7, func=AF.Exp), reads=bPB[7], writes=[B["DTm"]])
                    S.op("pool", lambda e: e.tensor_tensor(out=Dm[:], in0=Dm[:], in1=hb4(strictblk), op=ALU.mult),
                         reads=[B["Dm"], B["gm"]], writes=[B["Dm"]])
                    S.op("pool", lambda e: e.tensor_tensor(out=Dm[:], in0=Dm[:], in1=bc4(G["negb"][:, i, :]), op=ALU.mult),
                         reads=[B["Dm"]] + bg, writes=[B["Dm"]])
                    S.op("dve", lambda e: e.tensor_tensor(out=Pm[0][:], in0=p4, in1=Dm[:], op=ALU.mult),
                         reads=bPB[4] + [B["Dm"]], writes=[b_Pm[0]])
                    S.op("pool", lambda e: e.tensor_tensor(out=DTm[:], in0=DTm[:], in1=hb4(ublk), op=ALU.mult),
                         reads=[B["DTm"], B["gm"]], writes=[B["DTm"]])
                    S.op("dve", lambda e: e.tensor_tensor(out=qkDT[:], in0=p5, in1=DTm[:], op=ALU.mult),
                         reads=bPB[5] + [B["DTm"]], writes=[B["qkDT"]])
                    for h in range(4):
                        S.op("pe", lambda e: e.matmul(p6[:, h, :], lhsT=Pm[0][:, h, :], rhs=ident_f, start=True, stop=True),
                             reads=[b_Pm[0], B["gm"]], writes=bPB[6], inc=(h == 3))
                    S.op("act", lambda e: e.activation(out=Qm[0][:], in_=p6, func=AF.Copy), reads=bPB[6], writes=[b_Qm[0]])
                    S.op("dve", lambda e: e.tensor_tensor(out=TT[:], in0=p6, in1=hb4(ident_f), op=ALU.add),
                         reads=bPB[6] + [B["gm"]], writes=[B["TT"]])
                    for lvl in range(5):
                        for h in range(4):
                            S.op("pe", lambda e: e.matmul(p4[:, h, :], lhsT=Qm1[:, h, :], rhs=Pm1[:, h, :], start=True, stop=True),
                                 reads=[b_Qm[0], b_Pm[0]], writes=bPB[4], inc=(h == 3))
                        if lvl < 4:
                            for h in range(4):
                                S.op("pe", lambda e: e.matmul(p5[:, h, :], lhsT=Pm1[:, h, :], rhs=Qm1[:, h, :], start=True, stop=True),
                                     reads=[b_Qm[0], b_Pm[0]], writes=bPB[5], inc=(h == 3))
                        S.op("act", lambda e: e.activation(out=Pm1[:], in_=p4, func=AF.Copy), reads=bPB[4], writes=[b_Pm[0]])
                        if lvl < 4:
                            S.op("dve", lambda e: e.tensor_copy(out=Qm1[:], in_=p5), reads=bPB[5], writes=[b_Qm[0]])
                        for h in range(4):
                            S.op("pe", lambda e: e.matmul(p7[:, h, :], lhsT=Pm1[:, h, :], rhs=TT[:, h, :], start=True, stop=True),
                                 reads=[b_Pm[0], B["TT"]], writes=bPB[7], inc=(h == 3))
                        S.op("dve", lambda e: e.tensor_tensor(out=TT[:], in0=p7, in1=TT[:], op=ALU.add),
                             reads=bPB[7] + [B["TT"]], writes=[B["TT"]])
                    S.op("act", lambda e: e.activation(out=TTb[:], in_=TT[:], func=AF.Copy), reads=[B["TT"]], writes=[B["TTb"]])
                    for h in range(4):
                        S.op("pe", lambda e: e.matmul(p4[:, h, :], lhsT=TTb[:, h, :], rhs=vb[:, h, :], start=True, stop=True),
                             reads=[B["TTb"], B["vb"]], writes=bPB[4], inc=(h == 3))
                    for h in range(4):
                        S.op("pe", lambda e: e.matmul(p5[:, h, :], lhsT=kbg[:, h, :], rhs=TTb[:, h, :], start=True, stop=True),
                             reads=[B["TTb"], B["kbg"]], writes=bPB[5], inc=(h == 3))
                    S.op("act", lambda e: e.activation(out=u_t[:], in_=PB[4][:, :], func=AF.Copy), reads=bPB[4], writes=[B["u"]])
                    S.op("dve", lambda e: e.tensor_copy(out=wT[:], in_=p5), reads=bPB[5], writes=[B["wT"]])
                    for ch in range(2):
                        pr = slice(64 * ch, 64 * ch + 64)
                        pc = slice(64 * ch, 64 * ch + 64)
                        S.op("pool", lambda e: e.tensor_tensor(out=Sdec[:], in0=St[:], in1=bc4(eGLb[ch][:, i, :]), op=ALU.mult),
                             reads=[B["S"]] + bg, writes=[B["Sdec"]])
                        for h in range(4):
                            S.op("pe", lambda e: e.matmul(PB[7][pr, h * 128:(h + 1) * 128], lhsT=wT[:, h, pc], rhs=Sb[:, h, :],
                                                          start=True, stop=True),
                                 reads=[B["wT"], B["Sb"]], writes=bPB[7], inc=(h == 3))
                        S.op("dve", lambda e: e.tensor_tensor(out=vnew[pr, :], in0=u_t[pr, :], in1=PB[7][pr, :], op=ALU.subtract),
                             reads=bPB[7] + [B["u"]], writes=[B["vnew"]])
                        for h in range(4):
                            S.op("pe", lambda e: e.matmul(PB[6][pr, h * 128:(h + 1) * 128], lhsT=qgT[:, h, pc], rhs=Sb[:, h, :],
                                                          start=True, stop=False),
                                 reads=[B["qgT"], B["Sb"]], writes=bPB[6], inc=False)
                            S.op("pe", lambda e: e.matmul(PB[6][pr, h * 128:(h + 1) * 128], lhsT=qkDT[pr, h, pc],
                                                          rhs=vnew[pr, h * 128:(h + 1) * 128], start=False, stop=True),
                                 reads=[B["qkDT"], B["vnew"]], writes=bPB[6], inc=(h == 3))
                        for h in range(4):
                            S.op("pe", lambda e: e.matmul(PB[5][:, h * 128:(h + 1) * 128], lhsT=kdec[pr, h, :],
                                                          rhs=vnew[pr, h * 128:(h + 1) * 128], start=True, stop=True),
                                 reads=[B["kdec"], B["vnew"]], writes=bPB[5], inc=(h == 3))
                        Sf = St[:].rearrange("p h c -> p (h c)")
                        Sdf = Sdec[:].rearrange("p h c -> p (h c)")
                        Sbf = Sb[:].rearrange("p h c -> p (h c)")
                        S.op("dve", lambda e: e.tensor_tensor(out=Sbf, in0=PB[5][:, :], in1=Sdf, op=ALU.add),
                             reads=bPB[5] + [B["Sdec"]], writes=[B["Sb"]])
                        S.op("dve", lambda e: e.tensor_tensor(out=Sf, in0=PB[5][:, :], in1=Sdf, op=ALU.add),
                             reads=bPB[5] + [B["Sdec"]], writes=[B["S"]])
                    for kc in range(KC):
                        S.op("pe", lambda e: e.matmul(PB[4][:, :], lhsT=hT[:, kc, i * 128:(i + 1) * 128], rhs=wbz[:, kc, :],
                                                      start=(kc == 0), stop=(kc == KC - 1)),
                             reads=[b_hT[i], B["wbz"]], writes=bPB[4], inc=(kc == KC - 1))
                    S.op("act", lambda e: e.activation(out=zs[:], in_=PB[4][:, :], func=AF.Exp, scale=-1.0), reads=bPB[4], writes=[B["zs"]])
                    S.op("act", lambda e: e.activation(out=zs[:], in_=zs[:], func=AF.Ln, bias=1.0), reads=[B["zs"]], writes=[B["zs"]])
                    S.op("act", lambda e: e.activation(out=zs[:], in_=zs[:], func=AF.Exp, scale=-1.0), reads=[B["zs"]], writes=[B["zs"]])
                    S.op("dve", lambda e: e.tensor_tensor(out=zs[:], in0=PB[4][:, :], in1=zs[:], op=ALU.mult),
                         reads=bPB[4] + [B["zs"]], writes=[B["zs"]])
                    zs3 = zs[:].rearrange("p (h c) -> p h c", c=128)
                    S.op("pool", lambda e: e.tensor_tensor(out=zs3, in0=zs3, in1=hb4(hn_bc[:]), op=ALU.mult),
                         reads=[B["zs"], B["hn"]], writes=[B["zs"]])
                    for h in range(4):
                        S.op("act", lambda e: e.activation(out=sqb[:, 0:128], in_=PB[6][:, h * 128:(h + 1) * 128], func=AF.Square,
                                                           accum_out=sso[:, h:h + 1]),
                             reads=bPB[6], writes=[B["sqb"], B["sso"]])
                    S.op("act", lambda e: e.activation(out=sso[:], in_=sso[:], func=AF.Ln, scale=1.0 / 128, bias=EPS),
                         reads=[B["sso"]], writes=[B["sso"]])
                    S.op("act", lambda e: e.activation(out=sso[:], in_=sso[:], func=AF.Exp, scale=-0.5), reads=[B["sso"]], writes=[B["sso"]])
                    t13 = t1[:].rearrange("p (h c) -> p h c", c=128)
                    S.op("dve", lambda e: e.tensor_tensor(out=t13, in0=PB[6][:, :].rearrange("p (h c) -> p h c", c=128),
                                                          in1=bc4(sso[:]), op=ALU.mult),
                         reads=bPB[6] + [B["sso"], B["t1"]], writes=[B["t1"]])
                    S.op("dve", lambda e: e.tensor_tensor(out=ot[:], in0=t1[:], in1=zs[:], op=ALU.mult),
                         reads=[B["t1"], B["zs"]], writes=[B["ot"]])
                    p3o = PB[3][:, :].bitcast(BF16)[:, 0:512].rearrange("p (h c) -> p h c", c=128)
                    for h in range(4):
                        S.op("pe", lambda e: e.transpose(p3o[:, h, :], ot[:, h * 128:(h + 1) * 128], ident[:]),
                             reads=[B["ot"], b_const], writes=bPB[3], inc=(h == 3))
                    S.op("act", lambda e: e.activation(out=og[:, 4:8, i * 128:(i + 1) * 128], in_=p3o, func=AF.Copy),
                         reads=bPB[3], writes=[b_og[4 + hh][i // 4] for hh in range(4)])

        def layer1(seq, last):
            nonlocal sb_l, b_l
            with contextlib.ExitStack() as st1:
                st2 = st1.enter_context(contextlib.ExitStack())
                cur = [st1]

                def sl(name, shape, dt):
                    uid[0] += 1
                    return cur[0].enter_context(nc.sbuf_tensor("t%d_%s" % (uid[0], name), list(shape), dt))
                sb_l = {}
                b_l = {}
                sb_l["ss2"] = sl("ss2", [128, 2 * NT], F32); b_l["ss2"] = Buf()
                sb_l["rs2"] = sl("rs2", [128, NT], F32); b_l["rs2"] = Buf()
                sb_l["tmpf"] = [sl("tmpf%d" % i, [128, 512], F32) for i in range(2)]; b_l["tmpf"] = [Buf(), Buf()]
                cur[0] = st2
                qT = [sl("qT%d" % i, [128, SEQ], BF16) for i in range(2)]
                kT = [sl("kT%d" % i, [128, SEQ], BF16) for i in range(2)]
                Vx = [sl("Vx%d" % i, [128, NT, 128], BF16) for i in range(2)]
                vT5 = sl("vT5", [128, 512], BF16)
                b_vT5 = Buf()
                wq = sl("wq", [128, KC, 128], BF16)
                wk = sl("wk", [128, KC, 128], BF16)
                wv = sl("wv", [128, KC, 128], BF16)
                wz = [sl("wz%d" % i, [128, KC, 128], BF16) for i in range(2)]
                wf = sl("wf", [128, KC, 16], BF16)
                fb_bc = sl("fb_bc", [128, 16], F32)
                flb = sl("flb", [128, NT, 16], F32)
                nlf = sl("nlf", [128, NT, 16], F32)
                NC_ = sl("NC", [128, NT, 16], F32)
                carry = sl("carry", [128, 16], F32)
                carryT = sl("carryT", [16, 1], F32)
                cT = sl("cT", [16, SEQ], F32)
                cHL = sl("cHL", [16, 2, SEQ], BF16)
                pt = [sl("pt%d" % i, [128, 512], BF16) for i in range(3)]
                e_t = sb_l["tmpf"][0]
                sums = sb_l["tmpf"][1]
                den = sl("den", [128, 512], F32)
                tt = den
                b_qT = [Buf(), Buf()]; b_kT = [Buf(), Buf()]; b_Vx = [Buf(), Buf()]
                b_qaug = [Buf(), Buf()]
                b_wq, b_wk, b_wv = Buf(), Buf(), Buf()
                b_wz = [Buf(), Buf()]
                b_wf, b_fb, b_flb, b_nlf, b_NC, b_carry, b_carryT, b_cT, b_cTt, b_cHL = [Buf() for _ in range(10)]
                b_pt = [Buf() for _ in range(3)]
                b_e, b_sums, b_den = b_l["tmpf"][0], b_l["tmpf"][1], Buf()
                b_tt = b_den

                try:
                    load_norms(1)
                    S.dma("pool", wf[:], wf_d[:, :, :], writes=[b_wf])
                    S.dma("sp", fb_bc[:], bass.AP(fb_d.tensor, 0, [[0, 128], [1, 16]]), writes=[b_fb])
                    for i in range(2):
                        S.op("pool", lambda e: e.memset(kT[i][64:66, :], 1.0), writes=[b_kT[i]])
                    S.op("pool", lambda e: e.memset(Vx[0][:, :, 64:128], 1.0), writes=[b_Vx[0]])
                    S.op("pool", lambda e: e.memset(Vx[1][:, :, 0:64], 1.0), writes=[b_Vx[1]])
                    S.op("pool", lambda e: e.memset(carry[:], 0.0), writes=[b_carry])
                    S.op("pool", lambda e: e.memset(carryT[:], 0.0), writes=[b_carryT])

                    prenorm()
                    if STAGE <= 1:
                        raise StopStage()

                    fl_ps = PB[0][:, 0:NT * 16].rearrange("p (t h) -> p t h", h=16)
                    for i in range(NT):
                        for kc in range(KC):
                            S.op("pe", lambda e: e.matmul(fl_ps[:, i, :], lhsT=hT[:, kc, i * 128:(i + 1) * 128],
                                                          rhs=wf[:, kc, :], start=(kc == 0), stop=(kc == KC - 1)),
                                 reads=[b_hT[i], b_wf], writes=bPB[0], inc=(kc == KC - 1))
                    S.op("dve", lambda e: e.tensor_tensor(out=flb[:], in0=fl_ps, in1=fb_bc[:, None, :].to_broadcast([128, NT, 16]),
                                                          op=ALU.add),
                         reads=bPB[0] + [b_fb], writes=[b_flb])
                    S.op("act", lambda e: e.activation(out=flb[:], in_=flb[:], func=AF.Exp, scale=-1.0),
                         reads=[b_flb], writes=[b_flb])
                    S.op("act", lambda e: e.activation(out=nlf[:], in_=flb[:], func=AF.Ln, bias=1.0),
                         reads=[b_flb], writes=[b_nlf])
                    for i in range(NT):
                        bk = 1 + (i % 2)
                        c1 = PB[bk][:, 0:16]
                        c2 = PB[bk][:, 16:32]
                        c3 = PB[bk][0:16, 32:32 + 129]
                        S.op("pe", lambda e: e.matmul(c1, lhsT=uext[:, 0:128], rhs=nlf[:, i, :], start=True, stop=True),
                             reads=[b_nlf, b_const], writes=bPB[bk], inc=False)
                        S.op("pe", lambda e: e.matmul(c2, lhsT=ones_f[:], rhs=nlf[:, i, :], start=True, stop=True),
                             reads=[b_nlf, b_const], writes=bPB[bk], inc=False)
                        S.op("pe", lambda e: e.matmul(c3, lhsT=nlf[:, i, :], rhs=uext[:, :], start=True, stop=True),
                             reads=[b_nlf, b_const], writes=bPB[bk])
                        S.op("dve", lambda e: e.tensor_tensor(out=NC_[:, i, :], in0=c1, in1=carry[:], op=ALU.add),
                             reads=bPB[bk] + [b_carry], writes=[b_NC])
                        S.op("dve", lambda e: e.tensor_tensor(out=carry[:], in0=c2, in1=carry[:], op=ALU.add),
                             reads=bPB[bk] + [b_carry], writes=[b_carry])
                        S.op("dve", lambda e: e.tensor_scalar(out=cT[:, i * 128:(i + 1) * 128], in0=c3[:, 0:128],
                                                              scalar1=carryT[:, 0:1], scalar2=-8.0,
                                                              op0=ALU.add, op1=ALU.mult),
                             reads=bPB[bk] + [b_carryT], writes=[b_cT])
                        S.op("dve", lambda e: e.tensor_tensor(out=carryT[:], in0=c3[:, 128:129], in1=carryT[:], op=ALU.add),
                             reads=bPB[bk] + [b_carryT], writes=[b_carryT])
                    S.op("dve", lambda e: e.tensor_copy(out=cHL[:, 0, :], in_=cT[:]), reads=[b_cT], writes=[b_cHL])
                    S.op("dve", lambda e: e.tensor_tensor(out=cT[:], in0=cT[:], in1=cHL[:, 0, :], op=ALU.subtract),
                         reads=[b_cT, b_cHL], writes=[b_cT])
                    S.op("dve", lambda e: e.tensor_copy(out=cHL[:, 1, :], in_=cT[:]), reads=[b_cT, b_cHL], writes=[b_cHL])

                    if STAGE <= 2:
                        raise StopStage()
                    for p in range(8):
                        S.dma("pool", wq[:], wc_d[p, 0], writes=[b_wq])
                        S.dma("pool", wk[:], wc_d[p, 1], writes=[b_wk])
                        S.dma("pool", wv[:], wc_d[p, 2], writes=[b_wv])
                        S.dma("pool", wz[p % 2][:], wc_d[p, 3], writes=[b_wz[p % 2]])
                        if STAGE <= 2.2:
                            raise StopStage()
                        for hh in range(2):
                            for r in range(2):
                                S.dma("sp", qT[hh][64 + r:65 + r, :], cHL[2 * p + hh:2 * p + hh + 1, r, :],
                                      reads=[b_cHL], writes=[b_qaug[hh]])
                        if STAGE <= 2.4:
                            raise StopStage()
                        n_ev = 0
                        for (wt, bw, dstT, bdst) in ((wq, b_wq, qT, b_qT), (wk, b_wk, kT, b_kT)):
                            for t4 in range(4):
                                bk = 6 + (n_ev % 2)
                                n_ev += 1
                                for kc in range(KC):
                                    S.op("pe", lambda e: e.matmul(PB[bk][:, :], lhsT=wt[:, kc, :],
                                                                  rhs=hT[:, kc, t4 * 512:(t4 + 1) * 512],
                                                                  start=(kc == 0), stop=(kc == KC - 1)),
                                         reads=[bw] + b_hT[4 * t4:4 * t4 + 4], writes=bPB[bk], inc=(kc == KC - 1))
                                S.op("act", lambda e: e.activation(out=dstT[0][0:64, t4 * 512:(t4 + 1) * 512],
                                                                   in_=PB[bk][0:64, :], func=AF.Copy),
                                     reads=bPB[bk][0:1], writes=[bdst[0]])
                                S.op("dve", lambda e: e.tensor_copy(out=dstT[1][0:64, t4 * 512:(t4 + 1) * 512],
                                                                    in_=PB[bk][64:128, :]),
                                     reads=bPB[bk][1:2], writes=[bdst[1]])
                        if STAGE <= 2.6:
                            raise StopStage()
                        for t4 in range(4):
                            for kc in range(KC):
                                S.op("pe", lambda e: e.matmul(PB[6][:, :], lhsT=wv[:, kc, :],
                                                              rhs=hT[:, kc, t4 * 512:(t4 + 1) * 512],
                                                              start=(kc == 0), stop=(kc == KC - 1)),
                                     reads=[b_wv] + b_hT[4 * t4:4 * t4 + 4], writes=bPB[6], inc=(kc == KC - 1))
                            S.op("act", lambda e: e.activation(out=vT5[:], in_=PB[6][:, :], func=AF.Copy),
                                 reads=bPB[6], writes=[b_vT5])
                            pbf = PB[7][:, :].bitcast(BF16)[:, 0:512].rearrange("p (j c) -> p j c", c=128)
                            for j in range(4):
                                S.op("pe", lambda e: e.transpose(pbf[:, j, :], vT5[:, j * 128:(j + 1) * 128], ident[:]),
                                     reads=[b_vT5, b_const], writes=bPB[7], inc=(j == 3))
                            S.op("act", lambda e: e.activation(out=Vx[0][:, 4 * t4:4 * t4 + 4, 0:64], in_=pbf[:, :, 0:64],
                                                               func=AF.Copy),
                                 reads=bPB[7], writes=[b_Vx[0]])
                            S.op("dve", lambda e: e.tensor_copy(out=Vx[1][:, 4 * t4:4 * t4 + 4, 64:128], in_=pbf[:, :, 64:128]),
                                 reads=bPB[7], writes=[b_Vx[1]])
                        if STAGE <= 3:
                            raise StopStage()
                        jobs = []
                        for Qc in range(4):
                            for kt in range(4 * Qc + 4):
                                for hh in range(2):
                                    jobs.append((Qc, kt, hh))

                        def emit_pv(n):
                            Qc, kt, hh = jobs[n]
                            o = max(0, kt - 4 * Qc) * 128
                            N = 512 - o
                            abk = 2 + 2 * (Qc % 2) + hh
                            S.op("pe", lambda e: e.matmul(PB[abk][:, o:512], lhsT=Vx[hh][:, kt, :], rhs=pt[n % 3][:, 0:N],
                                                          start=(kt == 0), stop=(kt == 4 * Qc + 3)),
                                 reads=[b_Vx[hh], b_pt[n % 3]], writes=bPB[abk])
                            if kt == 4 * Qc + 3 and hh == 1:
                                emit_epilogue(Qc)

                        def emit_epilogue(Qc):
                            zb = 6 + (Qc % 2)
                            a0 = 2 + 2 * (Qc % 2)
                            a1 = a0 + 1
                            for kc in range(KC):
                                S.op("pe", lambda e: e.matmul(PB[zb][:, :], lhsT=wz[p % 2][:, kc, :],
                                                              rhs=hT[:, kc, Qc * 512:(Qc + 1) * 512],
                                                              start=(kc == 0), stop=(kc == KC - 1)),
                                     reads=[b_wz[p % 2]] + b_hT[4 * Qc:4 * Qc + 4], writes=bPB[zb], inc=(kc == KC - 1))
                            S.op("act", lambda e: e.activation(out=e_t[:], in_=PB[zb][:, :], func=AF.Exp, scale=-1.0),
                                 reads=bPB[zb], writes=[b_e])
                            S.op("act", lambda e: e.activation(out=sums[0:64, :], in_=PB[a0][64:128, :], func=AF.Copy),
                                 reads=bPB[a0][1:2], writes=[b_sums])
                            S.op("act", lambda e: e.activation(out=sums[64:128, :], in_=PB[a1][0:64, :], func=AF.Copy),
                                 reads=bPB[a1][0:1], writes=[b_sums])
                            S.op("dve", lambda e: e.scalar_tensor_tensor(out=den[:], in0=e_t[:], scalar=1.0, in1=sums[:],
                                                                         op0=ALU.add, op1=ALU.mult),
                                 reads=[b_e, b_sums], writes=[b_den])
                            S.op("dve", lambda e: e.reciprocal(out=den[:], in_=den[:]), reads=[b_den], writes=[b_den])
                            S.op("dve", lambda e: e.tensor_tensor(out=tt[:], in0=PB[zb][:, :], in1=den[:], op=ALU.mult),
                                 reads=bPB[zb] + [b_den], writes=[b_tt])
                            S.op("dve", lambda e: e.tensor_tensor(out=og[0:64, p, Qc * 512:(Qc + 1) * 512],
                                                                  in0=PB[a0][0:64, :], in1=tt[0:64, :], op=ALU.mult),
                                 reads=bPB[a0][0:1] + [b_tt], writes=[b_og[p][Qc]])
                            S.op("dve", lambda e: e.tensor_tensor(out=og[64:128, p, Qc * 512:(Qc + 1) * 512],
                                                                  in0=PB[a1][64:128, :], in1=tt[64:128, :], op=ALU.mult),
                                 reads=bPB[a1][1:2] + [b_tt], writes=[b_og[p][Qc]])

                        for n, (Qc, kt, hh) in enumerate(jobs):
                            o = max(0, kt - 4 * Qc) * 128
                            N = 512 - o
                            q0 = Qc * 512 + o
                            h = 2 * p + hh
                            sbk = n % 2
                            S.op("pe", lambda e: e.matmul(PB[sbk][:, 0:N], lhsT=kT[hh][0:66, kt * 128:(kt + 1) * 128],
                                                          rhs=qT[hh][0:66, q0:q0 + N], start=True, stop=True),
                                 reads=[b_kT[hh], b_qT[hh], b_qaug[hh]], writes=bPB[sbk])
                            S.op("act", lambda e: e.activation(out=pt[n % 3][:, 0:N], in_=PB[sbk][:, 0:N], func=AF.Exp,
                                                               scale=0.125, bias=NC_[:, kt, h:h + 1]),
                                 reads=bPB[sbk] + [b_NC], writes=[b_pt[n % 3]])
                            if kt >= 4 * Qc:
                                S.op("pool", lambda e: e.tensor_tensor(out=pt[n % 3][:, 0:128], in0=pt[n % 3][:, 0:128],
                                                                       in1=maskb[:], op=ALU.mult),
                                     reads=[b_pt[n % 3], b_const], writes=[b_pt[n % 3]])
                            if n >= 1:
                                emit_pv(n - 1)
                        emit_pv(len(jobs) - 1)
                except StopStage:
                    pass
                S.fence()
                st2.close()
                cur[0] = st1
                wo = sl("wo", [128, 8, D], BF16)
                S.dma("pool", wo[:], woc_d[:, :, :], writes=[b_wo])
                if STAGE <= 4:
                    for i in range(NT):
                        S.dma("sp", out_d[seq, i * 128:(i + 1) * 128, :], x_sb[:, i, :], reads=[b_x[i]], writes=[b_out])
                else:
                    outproj_residual(last, seq, wo)
                S.fence()

        sb_l = None
        b_l = None
        for seq in range(nseq):
            for t4 in range(4):
                S.dma("sp" if t4 % 2 == 0 else "act", x_sb[:, 4 * t4:4 * t4 + 4, :],
                      x_d[seq, t4 * 512:(t4 + 1) * 512, :].rearrange("(i p) d -> p i d", p=128),
                      writes=b_x[4 * t4:4 * t4 + 4])
            if 0 in layers:
                layer0(seq, 1 not in layers)
            if 1 in layers:
                layer1(seq, True)
        S.finish([b_out], "sp")
        print("n_ins", S.n_ins, "n_wait", S.n_wait)
    return nc


def prep_shared(inp):
    m = dict(host_consts())
    m["pre_norm"] = np.ascontiguousarray(inp["pre_norm"], dtype=np.float32)
    m["post_norm"] = np.ascontiguousarray(inp["post_norm"], dtype=np.float32)
    wc = np.asarray(inp["w_in_c"], dtype=np.float32)
    qkvz = wc[:, :4096].reshape(KC, 128, 4, 8, 128)
    m["wc"] = np.ascontiguousarray(qkvz.transpose(3, 2, 1, 0, 4))
    m["wf"] = np.ascontiguousarray(wc[:, 4096:4112].reshape(KC, 128, 16).transpose(1, 0, 2))
    m["woc"] = np.ascontiguousarray(np.asarray(inp["w_out_c"], dtype=np.float32).reshape(8, 128, D).transpose(1, 0, 2))
    m["c_forget_bias"] = np.ascontiguousarray(inp["c_forget_bias"], dtype=np.float32).reshape(1, 16)
    wab = np.asarray(inp["w_in_ab"], dtype=np.float32)
    mcol = np.arange(128)
    dd = mcol % 64
    dperm = np.where(dd < 8, dd + 8, np.where(dd < 16, dd - 8, dd))
    permcol = (mcol // 64) * 64 + dperm
    slabs = []
    for h in range(4):
        qc = wab[:, 0 * 512 + h * 128:0 * 512 + (h + 1) * 128]
        kc_ = wab[:, 1 * 512 + h * 128:1 * 512 + (h + 1) * 128]
        vc = wab[:, 2 * 512 + h * 128:2 * 512 + (h + 1) * 128]
        zc = wab[:, 3 * 512 + h * 128:3 * 512 + (h + 1) * 128]
        slabs.append(np.stack([qc, qc[:, permcol], kc_, kc_[:, permcol], vc, zc], axis=0))
    wa = np.stack(slabs, axis=0).reshape(4, 6, KC, 128, 128).transpose(0, 1, 3, 2, 4)
    m["wa"] = np.ascontiguousarray(wa)
    m["woab"] = np.ascontiguousarray(np.asarray(inp["w_out_ab"], dtype=np.float32).reshape(8, 128, D).transpose(1, 0, 2))
    m["lam4"] = np.ascontiguousarray(np.stack([inp["a_lambda_q1"], inp["a_lambda_k1"], inp["a_lambda_q2"], inp["a_lambda_k2"]]).astype(np.float32))
    m["a_subln"] = np.ascontiguousarray(np.asarray(inp["a_subln"], dtype=np.float32).reshape(128, 1))
    m["wbq"] = np.ascontiguousarray(wab[:, 2048:3584].reshape(KC, 128, 12, 128).transpose(2, 1, 0, 3))
    m["cw"] = np.ascontiguousarray(np.asarray(inp["b_conv_w"], dtype=np.float32).reshape(4, 12, 128).transpose(2, 1, 0))
    m["wbz"] = np.ascontiguousarray(wab[:, 3584:4096].reshape(KC, 128, 512).transpose(1, 0, 2))
    m["wba"] = np.ascontiguousarray(wab[:, 4096:4104].reshape(KC, 128, 8).transpose(1, 0, 2))
    m["b_a_log"] = np.ascontiguousarray(inp["b_a_log"], dtype=np.float32).reshape(1, 4)
    m["b_dt_bias"] = np.ascontiguousarray(inp["b_dt_bias"], dtype=np.float32).reshape(1, 4)
    m["b_head_norm"] = np.ascontiguousarray(inp["b_head_norm"], dtype=np.float32).reshape(1, 128)
    return m


def kernel(**inp):
    x = np.asarray(inp["x"], dtype=np.float32)
    B = x.shape[0]
    nseq = B // NCORES
    shared = prep_shared(inp)
    nc = build(nseq, LAYERS)
    in_maps = []
    for c in range(NCORES):
        m = dict(shared)
        m["x"] = np.ascontiguousarray(x[c * nseq:(c + 1) * nseq])
        m["positions"] = np.ascontiguousarray(np.asarray(inp["positions"], dtype=np.int32)[c * nseq:(c + 1) * nseq])
        in_maps.append(m)
    res = run_bass_kernel_spmd(nc, in_maps, core_ids=list(range(NCORES)), **RUN_KW)
    LAST['res'] = res
    return np.concatenate([r["out"] for r in res.results], axis=0)
```

```python
import contextlib
import math
import numpy as np
import concourse.bass as bass
import concourse.mybir as mybir
from concourse.bass_utils import run_bass_kernel_spmd

F32 = mybir.dt.float32
BF16 = mybir.dt.bfloat16
I32 = mybir.dt.int32
AF = mybir.ActivationFunctionType
ALU = mybir.AluOpType
AX = mybir.AxisListType

D = 1024
SEQ = 2048
NT = 16
KC = 8
EPS = 1e-6
NCORES = 8
LAYERS = (0, 1)
STAGE = 99
BW = (2, 4, 3)
RUN_KW = {}
LAST = {}
import os
VVAR = int(os.environ.get('VVAR', '3'))


class StopStage(Exception):
    pass


class Buf:
    __slots__ = ("name", "w", "r", "psum")

    def __init__(self, name="", psum=False):
        self.name = name
        self.w = None
        self.r = {}
        self.psum = psum


class Sched:
    ENGS = ("pe", "act", "dve", "pool", "sp")

    def __init__(self, nc, stack, n_dma_sems=6):
        self.nc = nc
        self.e = {"pe": nc.tensor, "act": nc.scalar, "dve": nc.vector,
                  "pool": nc.gpsimd, "sp": nc.sync}
        self.semh = {}
        self.cnt = {}
        for k in self.ENGS:
            self.semh[k] = stack.enter_context(nc.semaphore("s_" + k))
            self.cnt[k] = 0
        self.dq = {}
        for q in ("sp", "act", "pool"):
            slots = []
            for i in range(n_dma_sems):
                key = "d_%s%d" % (q, i)
                self.semh[key] = stack.enter_context(nc.semaphore(key))
                self.cnt[key] = 0
                slots.append(key)
            self.dq[q] = [slots, 0]
        self.seen = {k: {} for k in self.ENGS}
        self.n_ins = {k: 0 for k in self.ENGS}
        self.n_wait = {k: 0 for k in self.ENGS}

    def _wait(self, eng, deps):
        need = {}
        seen = self.seen[eng]
        for (k, v) in deps:
            if eng == "pe" and k == "pe":
                continue
            if seen.get(k, 0) < v and need.get(k, 0) < v:
                need[k] = v
        for k, v in need.items():
            self.e[eng].wait_ge(self.semh[k], v)
            seen[k] = v
            self.n_wait[eng] += 1

    @staticmethod
    def _deps(reads, writes):
        deps = []
        for b in reads:
            if b.w is not None:
                deps.append(b.w)
            if b.psum:
                deps.extend(b.r.items())
        for b in writes:
            if b.w is not None:
                deps.append(b.w)
            deps.extend(b.r.items())
        return deps

    @staticmethod
    def _mark(tok, reads, writes):
        for b in reads:
            if b.r.get(tok[0], 0) < tok[1]:
                b.r[tok[0]] = tok[1]
        for b in writes:
            b.w = tok
            b.r = {}

    def op(self, eng, fn, reads=(), writes=(), inc=True):
        self._wait(eng, self._deps(reads, writes))
        ins = fn(self.e[eng])
        self.n_ins[eng] += 1
        if inc:
            self.cnt[eng] += 1
            ins.then_inc(self.semh[eng], 1)
            tok = (eng, self.cnt[eng])
        else:
            tok = (eng, self.cnt[eng] + 1)
        self._mark(tok, reads, writes)
        return ins

    def dma(self, q, out, in_, reads=(), writes=(), **kw):
        slots, idx = self.dq[q]
        key = slots[idx % len(slots)]
        self.dq[q][1] = idx + 1
        deps = self._deps(reads, writes)
        if self.cnt[key] > 0:
            deps.append((key, self.cnt[key]))
        self._wait(q, deps)
        ins = self.e[q].dma_start(out=out, in_=in_, **kw)
        self.cnt[key] += 16
        ins.then_inc(self.semh[key], 16)
        self._mark((key, self.cnt[key]), reads, writes)
        return ins

    def fence(self):
        allc = [(k, v) for k, v in self.cnt.items() if v > 0]
        for eng in self.ENGS:
            self._wait(eng, allc)

    def finish(self, bufs, eng="sp"):
        deps = []
        for b in bufs:
            if b.w is not None:
                deps.append(b.w)
            deps.extend(b.r.items())
        self._wait(eng, deps)


def host_consts():
    c = {}
    j = np.arange(128)
    U = (j[:, None] <= j[None, :]).astype(np.float32)
    c["c_uext"] = np.concatenate([U, np.ones((128, 1), np.float32)], axis=1)
    c["c_ident"] = np.eye(128, dtype=np.float32)
    p = np.arange(128)
    d = p % 64
    half = 8
    inv_freq = (np.float32(500000.0) ** (-(np.arange(half, dtype=np.float32) * np.float32(2.0)) / np.float32(16.0))).astype(np.float32)
    freq = np.where(d < 16, inv_freq[d % 8], 0.0).astype(np.float32)
    sign = np.where(d < 8, -1.0, np.where(d < 16, 1.0, 0.0)).astype(np.float32)
    c["c_rope"] = np.stack([freq, sign], axis=1).astype(np.float32)
    same = (j[:, None] // 64) == (j[None, :] // 64)
    ublk = (same & (j[:, None] <= j[None, :])).astype(np.float32)
    blk = same.astype(np.float32)
    strictblk = (same & (j[:, None] > j[None, :])).astype(np.float32)
    half0 = np.repeat((j < 64).astype(np.float32)[:, None], 128, axis=1)
    half1 = np.repeat((j >= 64).astype(np.float32)[:, None], 128, axis=1)
    c["c_gdn"] = np.stack([ublk, blk, strictblk, half0, half1, np.eye(128, dtype=np.float32)], axis=1)
    return c


def build(nseq, layers=(0, 1)):
    nc = bass.Bass("TRN2", target_bir_lowering=False)
    dt_in = lambda name, shape, dt=F32: nc.dram_tensor(name, list(shape), dt, kind="ExternalInput").ap()
    x_d = dt_in("x", [nseq, SEQ, D])
    out_d = nc.dram_tensor("out", [nseq, SEQ, D], F32, kind="ExternalOutput").ap()
    x1s_d = nc.dram_tensor("x1s", [nseq, SEQ, D], F32, kind="Internal").ap()
    pre_d = dt_in("pre_norm", [2, D])
    post_d = dt_in("post_norm", [2, D])
    uext_d = dt_in("c_uext", [128, 129])
    ident_d = dt_in("c_ident", [128, 128])
    wc_d = dt_in("wc", [8, 4, 128, KC, 128])
    wf_d = dt_in("wf", [128, KC, 16])
    woc_d = dt_in("woc", [128, 8, D])
    fb_d = dt_in("c_forget_bias", [1, 16])
    pos_d = dt_in("positions", [nseq, SEQ], I32)
    rope_d = dt_in("c_rope", [128, 2])
    wa_d = dt_in("wa", [4, 6, 128, KC, 128])
    woab_d = dt_in("woab", [128, 8, D])
    lam_d = dt_in("lam4", [4, 64])
    subln_d = dt_in("a_subln", [128, 1])
    gdnc_d = dt_in("c_gdn", [128, 6, 128])
    wbq_d = dt_in("wbq", [12, 128, KC, 128])
    cw_d = dt_in("cw", [128, 12, 4])
    wbz_d = dt_in("wbz", [128, KC, 512])
    wba_d = dt_in("wba", [128, KC, 8])
    alog_d = dt_in("b_a_log", [1, 4])
    dtb_d = dt_in("b_dt_bias", [1, 4])
    hn_d = dt_in("b_head_norm", [1, 128])

    with contextlib.ExitStack() as st:
        S = Sched(nc, st)
        uid = [0]
        def sb(name, shape, dt):
            uid[0] += 1
            return st.enter_context(nc.sbuf_tensor("t%d_%s" % (uid[0], name), list(shape), dt))
        xt = [sb("xt%d" % i, [128, D], F32) for i in range(3)]
        b_xt = [Buf() for _ in range(3)]
        junk2 = sb("junk2", [128, D], BF16)
        b_junk2 = Buf()
        hT = sb("hT", [128, KC, SEQ], BF16)
        og = sb("og", [128, 8, SEQ], BF16)
        pre_bc = sb("pre_bc", [128, D], F32)
        post_bc = sb("post_bc", [128, D], F32)
        ident = sb("ident", [128, 128], BF16)
        uext = sb("uext", [128, 129], F32)
        ones_f = sb("ones_f", [128, 128], F32)
        maskb = sb("maskb", [128, 128], BF16)
        ss = sb("ss", [128, NT], F32)
        rstd = sb("rstd", [128, NT], F32)
        junk = sb("junk", [128, 512], BF16)
        PB = [st.enter_context(nc.psum_tensor("pb%d" % i, [128, 512], F32)) for i in range(8)]
        bPB = [[Buf("pb%d_0" % i, True), Buf("pb%d_1" % i, True)] for i in range(8)]

        b_xdram = [Buf("xd%d" % i) for i in range(NT)]
        b_x1 = [Buf("x1_%d" % i) for i in range(NT)]
        b_outd = [Buf("od%d" % i) for i in range(NT)]
        b_ssi = [Buf() for _ in range(NT)]
        b_rsi = [Buf() for _ in range(NT)]
        b_hT = [Buf("hT%d" % i) for i in range(NT)]
        b_og = [[Buf() for _ in range(4)] for _ in range(8)]
        b_const = Buf("const")
        b_norm = Buf("normbc")
        b_ss = Buf("ss")
        b_rstd = Buf("rstd")
        b_junk = Buf("junk")
        b_xn = [Buf(), Buf()]
        b_wo = Buf("wo")
        b_out = Buf("out")

        S.dma("sp", uext[:], uext_d[:, :], writes=[b_const])
        S.dma("pool", ident[:], ident_d[:, :], writes=[b_const])
        S.dma("pool", maskb[:], uext_d[:, 0:128], writes=[b_const])
        S.op("pool", lambda e: e.memset(ones_f[:], 1.0), writes=[b_const])

        cur_layer = [0]

        def load_norms(layer):
            cur_layer[0] = layer
            S.dma("sp", pre_bc[:], bass.AP(pre_d.tensor, layer * D, [[0, 128], [1, D]]), writes=[b_norm])
            S.dma("sp", post_bc[:], bass.AP(post_d.tensor, layer * D, [[0, 128], [1, D]]), writes=[b_norm])

        def load_post():
            S.dma("sp", post_bc[:], bass.AP(post_d.tensor, cur_layer[0] * D, [[0, 128], [1, D]]), writes=[b_norm])

        def prenorm(src, bsrc):
            xn = [sb_l["tmpf"][k][:].bitcast(BF16) for k in range(2)]
            b_xn = b_l["tmpf"]
            for i in range(NT):
                xb_ = xt[i % 3]
                bx = b_xt[i % 3]
                S.dma("sp", xb_[:], src[i * 128:(i + 1) * 128, :], reads=[bsrc[i]], writes=[bx])
                S.op("act", lambda e: e.activation(out=junk2[:], in_=xb_[:], func=AF.Square, accum_out=ss[:, i:i + 1]),
                     reads=[bx], writes=[b_junk2, b_ssi[i]])
                S.op("act", lambda e: e.activation(out=rstd[:, i:i + 1], in_=ss[:, i:i + 1], func=AF.Ln, scale=1.0 / D, bias=EPS),
                     reads=[b_ssi[i]], writes=[b_rsi[i]])
                S.op("act", lambda e: e.activation(out=rstd[:, i:i + 1], in_=rstd[:, i:i + 1], func=AF.Exp, scale=-0.5),
                     reads=[b_rsi[i]], writes=[b_rsi[i]])
                xb = xn[i % 2]
                bxb = b_xn[i % 2]
                S.op("dve", lambda e: e.scalar_tensor_tensor(out=xb, in0=xb_[:], scalar=rstd[:, i:i + 1],
                                                             in1=pre_bc[:], op0=ALU.mult, op1=ALU.mult),
                     reads=[bx, b_rsi[i], b_norm], writes=[bxb])
                bank = 6 + (i % 2)
                pv = PB[bank][:].bitcast(BF16)
                for kc in range(KC):
                    S.op("pe", lambda e: e.transpose(pv[:, kc * 128:(kc + 1) * 128], xb[:, kc * 128:(kc + 1) * 128],
                                                     ident[:]),
                         reads=[bxb, b_const], writes=bPB[bank], inc=(kc == KC - 1))
                eng = "act" if i % 2 == 0 else "dve"
                srcp = pv.rearrange("p (k t) -> p k t", k=KC)
                dst = hT[:, :, i * 128:(i + 1) * 128]
                if eng == "act":
                    S.op("act", lambda e: e.activation(out=dst, in_=srcp, func=AF.Copy),
                         reads=bPB[bank], writes=[b_hT[i]])
                else:
                    S.op("dve", lambda e: e.tensor_copy(out=dst, in_=srcp),
                         reads=bPB[bank], writes=[b_hT[i]])

        def outproj_residual(src, bsrc, dst, b_dst, wo):
            ss2 = sb_l["ss2"]; rs2 = sb_l["rs2"]; tmpf = sb_l["tmpf"]
            b_s2 = [Buf() for _ in range(NT)]
            b_r2 = [Buf() for _ in range(NT)]
            for i in range(NT):
                xb_ = xt[i % 3]
                bx = b_xt[i % 3]
                S.dma("sp", xb_[:], src[i * 128:(i + 1) * 128, :], reads=[bsrc[i]], writes=[bx])
                banks = (4 + 2 * (i % 2), 5 + 2 * (i % 2))
                for hf in range(2):
                    bk = banks[hf]
                    for p in range(8):
                        S.op("pe", lambda e: e.matmul(PB[bk][:, :], lhsT=og[:, p, i * 128:(i + 1) * 128],
                                                      rhs=wo[:, p, hf * 512:(hf + 1) * 512],
                                                      start=(p == 0), stop=(p == 7)),
                             reads=[b_og[p][i // 4], b_wo], writes=bPB[bk], inc=(p == 7))
                    S.op("act", lambda e: e.activation(out=junk[:, 0:512], in_=PB[bk][:, :], func=AF.Square,
                                                       accum_out=ss2[:, 2 * i + hf:2 * i + hf + 1]),
                         reads=bPB[bk], writes=[b_junk, b_s2[i]])
                S.op("dve", lambda e: e.tensor_tensor(out=rs2[:, i:i + 1], in0=ss2[:, 2 * i:2 * i + 1],
                                                      in1=ss2[:, 2 * i + 1:2 * i + 2], op=ALU.add),
                     reads=[b_s2[i]], writes=[b_r2[i]])
                S.op("act", lambda e: e.activation(out=rs2[:, i:i + 1], in_=rs2[:, i:i + 1], func=AF.Ln,
                                                   scale=1.0 / D, bias=EPS),
                     reads=[b_r2[i]], writes=[b_r2[i]])
                S.op("act", lambda e: e.activation(out=rs2[:, i:i + 1], in_=rs2[:, i:i + 1], func=AF.Exp, scale=-0.5),
                     reads=[b_r2[i]], writes=[b_r2[i]])
                for hf in range(2):
                    bk = banks[hf]
                    tf = tmpf[hf]
                    S.op("dve", lambda e: e.scalar_tensor_tensor(out=tf[:], in0=PB[bk][:, :], scalar=rs2[:, i:i + 1],
                                                                 in1=post_bc[:, hf * 512:(hf + 1) * 512],
                                                                 op0=ALU.mult, op1=ALU.mult),
                         reads=bPB[bk] + [b_r2[i], b_norm], writes=[b_l["tmpf"][hf]])
                    S.op("pool", lambda e: e.tensor_tensor(out=xb_[:, hf * 512:(hf + 1) * 512],
                                                           in0=xb_[:, hf * 512:(hf + 1) * 512], in1=tf[:],
                                                           op=ALU.add),
                         reads=[b_l["tmpf"][hf], bx], writes=[bx])
                S.dma("sp", dst[i * 128:(i + 1) * 128, :], xb_[:], reads=[bx], writes=[b_dst[i]])

        PI = 3.141592653589793
        LAMBDA_INIT = 0.8 - 0.6 * math.exp(-0.3 * 0)

        def layer0(seq, last):
            nonlocal sb_l, b_l
            with contextlib.ExitStack() as st1:
                st2 = st1.enter_context(contextlib.ExitStack())
                cur = [st1]

                def sl(name, shape, dt):
                    uid[0] += 1
                    return cur[0].enter_context(nc.sbuf_tensor("t%d_%s" % (uid[0], name), list(shape), dt))
                sb_l = {}
                b_l = {}
                sb_l["ss2"] = sl("ss2", [128, 2 * NT], F32); b_l["ss2"] = Buf()
                sb_l["rs2"] = sl("rs2", [128, NT], F32); b_l["rs2"] = Buf()
                sb_l["tmpf"] = [sl("tmpf%d" % i, [128, 512], F32) for i in range(2)]; b_l["tmpf"] = [Buf(), Buf()]
                load_norms(0)
                wo = sl("wo", [128, 8, D], BF16)
                S.dma("pool", wo[:], woab_d[:, :, :], writes=[b_wo])
                prenorm(x_d[seq], b_xdram)
                cur[0] = st2
                ropec = sl("ropec", [128, 2], F32)
                lamt = sl("lamt", [128, 4, 64], F32)
                lamp = sl("lamp", [128, 2, 64], F32)
                lams = sl("lams", [128, 2], F32)
                neglam = sl("neglam", [128, 1], F32)
                subcol = sl("subcol", [128, 1], F32)
                ones_b = sl("ones_b", [128, 128], BF16)
                posi = sl("posi", [128, SEQ], I32)
                Ct = sl("Ct", [128, SEQ], F32)
                St = sl("St", [128, SEQ], F32)
                qT = sl("qTa", [128, SEQ], BF16)
                kT = sl("kTa", [128, SEQ], BF16)
                Vh = sl("Vh", [128, NT, 128], BF16)
                wsl = [sl("wa%d" % i, [128, KC, 128], BF16) for i in range(6)]
                pt = [sl("pt%d" % i, [128, 512], BF16) for i in range(3)]
                ta = sb_l["tmpf"][0]; tb = sb_l["tmpf"][1]
                tc = sl("tc", [128, 512], F32); td = sl("td", [128, 512], F32)
                b_ropec, b_lam, b_neglam, b_subcol, b_onesb, b_posi, b_Ct, b_St = [Buf() for _ in range(8)]
                b_qT, b_kT, b_Vh = Buf(), Buf(), Buf()
                b_wsl = [Buf() for _ in range(6)]
                b_pt = [Buf() for _ in range(3)]
                b_ta, b_tb, b_tc, b_td = b_l["tmpf"][0], b_l["tmpf"][1], Buf(), Buf()

                S.dma("sp", ropec[:], rope_d[:, :], writes=[b_ropec])
                S.op("pool", lambda e: e.memset(ones_b[:], 1.0), writes=[b_onesb])
                S.dma("sp", lamt[:], bass.AP(lam_d.tensor, 0, [[0, 128], [64, 4], [1, 64]]), writes=[b_lam])
                S.op("dve", lambda e: e.tensor_tensor(out=lamp[:, 0, :], in0=lamt[:, 0, :], in1=lamt[:, 1, :], op=ALU.mult),
                     reads=[b_lam], writes=[b_lam])
                S.op("dve", lambda e: e.tensor_tensor(out=lamp[:, 1, :], in0=lamt[:, 2, :], in1=lamt[:, 3, :], op=ALU.mult),
                     reads=[b_lam], writes=[b_lam])
                S.op("dve", lambda e: e.tensor_reduce(out=lams[:], in_=lamp[:], axis=AX.X, op=ALU.add),
                     reads=[b_lam], writes=[b_lam])
                S.op("act", lambda e: e.activation(out=lams[:], in_=lams[:], func=AF.Exp), reads=[b_lam], writes=[b_lam])
                S.op("dve", lambda e: e.tensor_tensor(out=neglam[:], in0=lams[:, 1:2], in1=lams[:, 0:1], op=ALU.subtract),
                     reads=[b_lam], writes=[b_neglam])
                S.op("dve", lambda e: e.tensor_scalar(out=neglam[:], in0=neglam[:], scalar1=-LAMBDA_INIT, scalar2=None, op0=ALU.add),
                     reads=[b_neglam], writes=[b_neglam])
                S.dma("sp", subcol[:], subln_d[:, :], writes=[b_subcol])
                S.op("dve", lambda e: e.tensor_scalar(out=subcol[:], in0=subcol[:], scalar1=1.0 - LAMBDA_INIT, scalar2=None, op0=ALU.mult),
                     reads=[b_subcol], writes=[b_subcol])
                S.dma("sp", posi[:], bass.AP(pos_d.tensor, seq * SEQ, [[0, 128], [1, SEQ]]), writes=[b_posi])

                def sin_table(dst, bdst, phase, signed):
                    S.op("dve", lambda e: e.tensor_copy(out=dst[:], in_=posi[:]), reads=[b_posi], writes=[bdst])
                    S.op("dve", lambda e: e.tensor_scalar(out=dst[:], in0=dst[:], scalar1=ropec[:, 0:1], scalar2=phase,
                                                          op0=ALU.mult, op1=ALU.add), reads=[bdst, b_ropec], writes=[bdst])
                    for c4 in range(4):
                        sl_ = slice(c4 * 512, (c4 + 1) * 512)
                        tI = tc[:].bitcast(I32)
                        S.op("dve", lambda e: e.tensor_scalar(out=td[:], in0=dst[:, sl_], scalar1=1.0 / (2 * PI), scalar2=None,
                                                              op0=ALU.mult), reads=[bdst], writes=[b_td])
                        S.op("dve", lambda e: e.tensor_copy(out=tI, in_=td[:]), reads=[b_td], writes=[b_tc])
                        S.op("dve", lambda e: e.tensor_copy(out=td[:], in_=tI), reads=[b_tc], writes=[b_td])
                        S.op("dve", lambda e: e.scalar_tensor_tensor(out=td[:], in0=td[:], scalar=-2 * PI, in1=dst[:, sl_],
                                                                     op0=ALU.mult, op1=ALU.add),
                             reads=[b_td, bdst], writes=[b_td])
                        S.op("dve", lambda e: e.tensor_scalar(out=tc[:], in0=td[:], scalar1=PI, scalar2=-2 * PI,
                                                              op0=ALU.is_gt, op1=ALU.mult), reads=[b_td], writes=[b_tc])
                        S.op("dve", lambda e: e.tensor_tensor(out=td[:], in0=td[:], in1=tc[:], op=ALU.add),
                             reads=[b_td, b_tc], writes=[b_td])
                        S.op("dve", lambda e: e.tensor_scalar(out=td[:], in0=td[:], scalar1=-PI, scalar2=PI,
                                                              op0=ALU.max, op1=ALU.min), reads=[b_td], writes=[b_td])
                        S.op("act", lambda e: e.activation(out=dst[:, sl_], in_=td[:], func=AF.Sin),
                             reads=[b_td], writes=[bdst])
                    if signed:
                        S.op("dve", lambda e: e.tensor_scalar(out=dst[:], in0=dst[:], scalar1=ropec[:, 1:2], scalar2=None,
                                                              op0=ALU.mult), reads=[bdst, b_ropec], writes=[bdst])
                sin_table(Ct, b_Ct, PI / 2, False)
                sin_table(St, b_St, 0.0, True)

                for h in range(4):
                    for i6 in range(6):
                        S.dma("pool", wsl[i6][:], wa_d[h, i6], writes=[b_wsl[i6]])
                    n_ev = 0
                    for (i_w, dstT, bdst) in ((0, qT, b_qT), (2, kT, b_kT)):
                        for t4 in range(4):
                            bks = (6, 7) if n_ev % 2 == 0 else (4, 5)
                            n_ev += 1
                            tsl = slice(t4 * 512, (t4 + 1) * 512)
                            for jj in range(2):
                                for kc in range(KC):
                                    S.op("pe", lambda e: e.matmul(PB[bks[jj]][:, :], lhsT=wsl[i_w + jj][:, kc, :],
                                                                  rhs=hT[:, kc, tsl], start=(kc == 0), stop=(kc == KC - 1)),
                                         reads=[b_wsl[i_w + jj]] + b_hT[4 * t4:4 * t4 + 4], writes=bPB[bks[jj]],
                                         inc=(kc == KC - 1))
                            S.op("dve", lambda e: e.tensor_tensor(out=tc[:], in0=PB[bks[0]][:, :], in1=Ct[:, tsl], op=ALU.mult),
                                 reads=bPB[bks[0]] + [b_Ct], writes=[b_tc])
                            S.op("dve", lambda e: e.tensor_tensor(out=td[:], in0=PB[bks[1]][:, :], in1=St[:, tsl], op=ALU.mult),
                                 reads=bPB[bks[1]] + [b_St], writes=[b_td])
                            S.op("pool", lambda e: e.tensor_tensor(out=dstT[:, tsl], in0=tc[:], in1=td[:], op=ALU.add),
                                 reads=[b_tc, b_td], writes=[bdst])
                    for t4 in range(4):
                        for kc in range(KC):
                            S.op("pe", lambda e: e.matmul(PB[6][:, :], lhsT=wsl[4][:, kc, :],
                                                          rhs=hT[:, kc, t4 * 512:(t4 + 1) * 512],
                                                          start=(kc == 0), stop=(kc == KC - 1)),
                                 reads=[b_wsl[4]] + b_hT[4 * t4:4 * t4 + 4], writes=bPB[6], inc=(kc == KC - 1))
                        S.op("act", lambda e: e.activation(out=pt[0][:], in_=PB[6][:, :], func=AF.Copy),
                             reads=bPB[6], writes=[b_pt[0]])
                        pbf = PB[7][:, :].bitcast(BF16)[:, 0:512].rearrange("p (j c) -> p j c", c=128)
                        for j in range(4):
                            S.op("pe", lambda e: e.transpose(pbf[:, j, :], pt[0][:, j * 128:(j + 1) * 128], ident[:]),
                                 reads=[b_pt[0], b_const], writes=bPB[7], inc=(j == 3))
                        S.op("act", lambda e: e.activation(out=Vh[:, 4 * t4:4 * t4 + 4, :], in_=pbf, func=AF.Copy),
                             reads=bPB[7], writes=[b_Vh])
                    deferred = []
                    jobs = []
                    for Qc in range(4):
                        for kt in range(4 * Qc + 4):
                            for c in range(2):
                                jobs.append((Qc, kt, c))

                    def emit_pv(n):
                        Qc, kt, c = jobs[n]
                        o = max(0, kt - 4 * Qc) * 128
                        N = 512 - o
                        S.op("pe", lambda e: e.matmul(PB[2 + c][:, o:512], lhsT=Vh[:, kt, :], rhs=pt[n % 3][:, 0:N],
                                                      start=(kt == 0), stop=(kt == 4 * Qc + 3)),
                             reads=[b_Vh, b_pt[n % 3]], writes=bPB[2 + c], inc=False)
                        S.op("pe", lambda e: e.matmul(PB[4 + c][:, o:512], lhsT=ones_b[:], rhs=pt[n % 3][:, 0:N],
                                                      start=(kt == 0), stop=(kt == 4 * Qc + 3)),
                             reads=[b_onesb, b_pt[n % 3]], writes=bPB[4 + c])
                        if kt == 4 * Qc + 3 and c == 1:
                            emit_epilogue(Qc)

                    def emit_epilogue(Qc):
                        qsl = slice(Qc * 512, (Qc + 1) * 512)
                        for kc in range(KC):
                            S.op("pe", lambda e: e.matmul(PB[6][:, :], lhsT=wsl[5][:, kc, :], rhs=hT[:, kc, qsl],
                                                          start=(kc == 0), stop=(kc == KC - 1)),
                                 reads=[b_wsl[5]] + b_hT[4 * Qc:4 * Qc + 4], writes=bPB[6], inc=(kc == KC - 1))
                        S.op("act", lambda e: e.activation(out=ta[:], in_=PB[4][:, :], func=AF.Ln), reads=bPB[4], writes=[b_ta])
                        S.op("act", lambda e: e.activation(out=ta[:], in_=ta[:], func=AF.Exp, scale=-1.0), reads=[b_ta], writes=[b_ta])
                        S.op("dve", lambda e: e.tensor_tensor(out=tb[:], in0=PB[2][:, :], in1=ta[:], op=ALU.mult),
                             reads=bPB[2] + [b_ta], writes=[b_tb])
                        S.op("act", lambda e: e.activation(out=ta[:], in_=PB[5][:, :], func=AF.Ln), reads=bPB[5] + [b_ta], writes=[b_ta])
                        S.op("act", lambda e: e.activation(out=ta[:], in_=ta[:], func=AF.Exp, scale=-1.0), reads=[b_ta], writes=[b_ta])
                        S.op("dve", lambda e: e.tensor_tensor(out=tc[:], in0=PB[3][:, :], in1=ta[:], op=ALU.mult),
                             reads=bPB[3] + [b_ta], writes=[b_tc])
                        S.op("dve", lambda e: e.scalar_tensor_tensor(out=tb[:], in0=tc[:], scalar=neglam[:, 0:1], in1=tb[:],
                                                                      op0=ALU.mult, op1=ALU.add),
                             reads=[b_tc, b_tb, b_neglam], writes=[b_tb])
                        S.op("act", lambda e: e.activation(out=tc[:], in_=tb[:], func=AF.Square), reads=[b_tb], writes=[b_tc])
                        deferred.append([4, lambda: epilogue2(Qc, h)])

                    def epilogue2(Qc, h):
                        qsl = slice(Qc * 512, (Qc + 1) * 512)
                        S.op("pe", lambda e: e.matmul(PB[7][:, :], lhsT=ones_f[:], rhs=tc[:], start=True, stop=True),
                             reads=[b_const, b_tc], writes=bPB[7])
                        S.op("act", lambda e: e.activation(out=td[:], in_=PB[7][:, :], func=AF.Ln, scale=1.0 / 128, bias=EPS),
                             reads=bPB[7], writes=[b_td])
                        S.op("act", lambda e: e.activation(out=td[:], in_=td[:], func=AF.Exp, scale=-0.5), reads=[b_td], writes=[b_td])
                        S.op("act", lambda e: e.activation(out=ta[:], in_=PB[6][:, :], func=AF.Exp, scale=-1.0),
                             reads=bPB[6] + [b_ta], writes=[b_ta])
                        S.op("act", lambda e: e.activation(out=ta[:], in_=ta[:], func=AF.Ln, bias=1.0), reads=[b_ta], writes=[b_ta])
                        S.op("act", lambda e: e.activation(out=ta[:], in_=ta[:], func=AF.Exp, scale=-1.0), reads=[b_ta], writes=[b_ta])
                        S.op("dve", lambda e: e.tensor_tensor(out=ta[:], in0=PB[6][:, :], in1=ta[:], op=ALU.mult),
                             reads=bPB[6] + [b_ta], writes=[b_ta])
                        S.op("pool", lambda e: e.tensor_tensor(out=tb[:], in0=tb[:], in1=td[:], op=ALU.mult),
                             reads=[b_tb, b_td], writes=[b_tb])
                        S.op("dve", lambda e: e.scalar_tensor_tensor(out=og[:, h, qsl], in0=tb[:], scalar=subcol[:, 0:1], in1=ta[:],
                                                                     op0=ALU.mult, op1=ALU.mult),
                             reads=[b_tb, b_ta, b_subcol], writes=[b_og[h][Qc]])

                    for n, (Qc, kt, c) in enumerate(jobs):
                        o = max(0, kt - 4 * Qc) * 128
                        N = 512 - o
                        q0 = Qc * 512 + o
                        sbk = n % 2
                        S.op("pe", lambda e: e.matmul(PB[sbk][:, 0:N], lhsT=kT[c * 64:(c + 1) * 64, kt * 128:(kt + 1) * 128],
                                                      rhs=qT[c * 64:(c + 1) * 64, q0:q0 + N], start=True, stop=True),
                             reads=[b_kT, b_qT], writes=bPB[sbk])
                        S.op("act", lambda e: e.activation(out=pt[n % 3][:, 0:N], in_=PB[sbk][:, 0:N], func=AF.Exp, scale=0.125),
                             reads=bPB[sbk], writes=[b_pt[n % 3]])
                        if kt >= 4 * Qc:
                            S.op("pool", lambda e: e.tensor_tensor(out=pt[n % 3][:, 0:128], in0=pt[n % 3][:, 0:128],
                                                                   in1=maskb[:], op=ALU.mult),
                                 reads=[b_pt[n % 3], b_const], writes=[b_pt[n % 3]])
                        if n >= 1:
                            emit_pv(n - 1)
                        for dfr in list(deferred):
                            dfr[0] -= 1
                            if dfr[0] <= 0:
                                deferred.remove(dfr)
                                dfr[1]()
                    emit_pv(len(jobs) - 1)
                    for dfr in list(deferred):
                        deferred.remove(dfr)
                        dfr[1]()
                S.fence()
                st2.close()
                st3 = st1.enter_context(contextlib.ExitStack())
                cur[0] = st3
                partB(seq, sl)
                cur[0] = st1
                if last:
                    outproj_residual(x_d[seq], b_xdram, out_d[seq], b_outd, wo)
                else:
                    outproj_residual(x_d[seq], b_xdram, x1s_d[seq], b_x1, wo)
                S.fence()
                st3.close()

        def run_threads(threads, weights):
            done = set()
            st_ = [{"g": g, "wait": None, "alive": True} for g in threads]
            while any(t["alive"] for t in st_):
                progressed = False
                for t, w in zip(st_, weights):
                    if not t["alive"]:
                        continue
                    for _ in range(w):
                        if t["wait"] is not None:
                            if t["wait"] in done:
                                t["wait"] = None
                            else:
                                break
                        try:
                            r = next(t["g"])
                        except StopIteration:
                            t["alive"] = False
                            progressed = True
                            break
                        progressed = True
                        if isinstance(r, tuple):
                            if r[0] == "wait":
                                if r[1] not in done:
                                    t["wait"] = r[1]
                                    break
                            elif r[0] == "done":
                                done.add(r[1])
                assert progressed, "emission deadlock"

        def partB(seq, sl):
            gm = sl("gm", [128, 6, 128], F32)
            ublk, blk, strictblk, ident_f = gm[:, 0, :], gm[:, 1, :], gm[:, 2, :], gm[:, 5, :]
            halfsel = (gm[:, 3, :], gm[:, 4, :])
            ones_b = sl("ones_bB", [128, 128], BF16)
            wbz = sl("wbz", [128, KC, 512], BF16)
            wba = sl("wba", [128, KC, 8], BF16)
            wbq = [sl("wbq%d" % i, [128, KC, 128], BF16) for i in range(2)]
            cw = sl("cw", [128, 12, 4], F32)
            negA = sl("negA", [128, 4], F32)
            dtb = sl("dtb", [128, 4], F32)
            hn_bc = sl("hn_bc", [128, 128], F32)
            G = {n: sl("g_" + n, [128, NT, 4], F32) for n in ("xa", "g", "beta", "negb", "G", "GL", "eG", "bG", "dG", "eGL0", "eGL1")}
            cr = sl("cr", [128, 12, 3], F32)
            xc = [sl("xc%d" % i, [128, 515], F32) for i in range(2)]
            yc = sl("yc", [128, 512], F32)
            sc = sl("sc", [128, 512], F32)
            t1 = sl("t1", [128, 512], F32)
            sqb = sl("sqb", [128, 512], BF16)
            qnT = [sl("qnT%d" % i, [128, 4, 512], BF16) for i in range(2)]
            knT = [sl("knT%d" % i, [128, 4, 512], BF16) for i in range(2)]
            vsT = [sl("vsT%d" % i, [128, 4, 512], BF16) for i in range(2)]
            gb = sl("gb", [128, 4, 128], F32)
            gU = sl("gU", [128, 4, 128], F32)
            Dm = sl("Dm", [128, 4, 128], F32)
            DTm = sl("DTm", [128, 4, 128], F32)
            Pm1 = sl("Pm", [128, 4, 128], F32)
            Qm1 = sl("Qm", [128, 4, 128], F32)
            TT = sl("TT", [128, 4, 128], F32)
            TTb = sl("TTb", [128, 4, 128], BF16)
            qgT = [sl("qgT%d" % i, [128, 4, 128], BF16) for i in range(2)]
            kbg = [sl("kbg%d" % i, [128, 4, 128], BF16) for i in range(2)]
            kdec = [sl("kdec%d" % i, [128, 4, 128], BF16) for i in range(2)]
            vb = [sl("vb%d" % i, [128, 4, 128], BF16) for i in range(2)]
            qkDT = [sl("qkDT%d" % i, [128, 4, 128], BF16) for i in range(2)]
            wT = [sl("wT%d" % i, [128, 4, 128], BF16) for i in range(2)]
            u_t = [sl("u_t%d" % i, [128, 512], F32) for i in range(2)]
            vnew = sl("vnew", [128, 512], BF16)
            St = sl("Sst", [128, 4, 128], F32)
            Sdec = sl("Sdec", [128, 4, 128], F32)
            Sb = sl("Sb", [128, 4, 128], BF16)
            zs = sl("zs", [128, 512], F32)
            t1r = sl("t1r", [128, 512], F32)
            sqr = sl("sqr", [128, 128], BF16)
            sso = sl("sso", [128, 4], F32)
            ot = sl("ot", [128, 512], BF16)
            B = {n: Buf(n) for n in ("gm", "onesb", "wbz", "wba", "cw", "negA", "dtb", "hn", "gates", "cr", "yc", "sc", "t1", "sqb",
                                    "gb", "gU", "Dm", "DTm", "Pm", "Qm", "TT", "TTb",
                                    "vnew", "S", "Sdec", "Sb", "zs", "t1r", "sqr", "sso", "ot")}
            D2 = {n: [Buf(n + "0"), Buf(n + "1")] for n in ("qnT", "knT", "vsT", "qgT", "kbg", "kdec", "vb", "qkDT", "wT", "u")}
            b_wbq = [Buf(), Buf()]
            b_xc = [Buf(), Buf()]
            I_A, I_B = 0, 1
            P_A, P_B, P_C = 2, 3, 4
            R_A, R_B, R_C = 5, 6, 7

            S.dma("sp", gm[:], gdnc_d[:, :, :], writes=[B["gm"]])
            S.op("pool", lambda e: e.memset(ones_b[:], 1.0), writes=[B["onesb"]])
            S.dma("pool", wbz[:], wbz_d[:, :, :], writes=[B["wbz"]])
            S.dma("pool", wba[:], wba_d[:, :, :], writes=[B["wba"]])
            S.dma("sp", cw[:], cw_d[:, :, :], writes=[B["cw"]])
            S.dma("sp", negA[:], bass.AP(alog_d.tensor, 0, [[0, 128], [1, 4]]), writes=[B["negA"]])
            S.dma("sp", dtb[:], bass.AP(dtb_d.tensor, 0, [[0, 128], [1, 4]]), writes=[B["dtb"]])
            S.dma("sp", hn_bc[:], bass.AP(hn_d.tensor, 0, [[0, 128], [1, 128]]), writes=[B["hn"]])
            S.op("act", lambda e: e.activation(out=negA[:], in_=negA[:], func=AF.Exp), reads=[B["negA"]], writes=[B["negA"]])
            S.op("dve", lambda e: e.tensor_scalar(out=negA[:], in0=negA[:], scalar1=-1.0, scalar2=None, op0=ALU.mult),
                 reads=[B["negA"]], writes=[B["negA"]])
            S.op("pool", lambda e: e.memset(cr[:], 0.0), writes=[B["cr"]])
            S.op("pool", lambda e: e.memset(St[:], 0.0), writes=[B["S"]])
            S.op("pool", lambda e: e.memset(Sb[:], 0.0), writes=[B["Sb"]])

            ba_ps = PB[0][:, 0:NT * 8].rearrange("p (t c) -> p t c", c=8)
            for i in range(NT):
                for kc in range(KC):
                    S.op("pe", lambda e: e.matmul(ba_ps[:, i, :], lhsT=hT[:, kc, i * 128:(i + 1) * 128], rhs=wba[:, kc, :],
                                                  start=(kc == 0), stop=(kc == KC - 1)),
                         reads=[b_hT[i], B["wba"]], writes=bPB[0], inc=(kc == KC - 1))
            bg = [B["gates"]]
            S.op("dve", lambda e: e.tensor_tensor(out=G["xa"][:], in0=ba_ps[:, :, 4:8], in1=dtb[:, None, :].to_broadcast([128, NT, 4]),
                                                  op=ALU.add), reads=bPB[0] + [B["dtb"]], writes=bg)
            S.op("act", lambda e: e.activation(out=G["xa"][:], in_=G["xa"][:], func=AF.Exp), reads=bg, writes=bg)
            S.op("act", lambda e: e.activation(out=G["xa"][:], in_=G["xa"][:], func=AF.Ln, bias=1.0), reads=bg, writes=bg)
            S.op("dve", lambda e: e.tensor_tensor(out=G["g"][:], in0=G["xa"][:], in1=negA[:, None, :].to_broadcast([128, NT, 4]),
                                                  op=ALU.mult), reads=bg + [B["negA"]], writes=bg)
            S.op("act", lambda e: e.activation(out=G["beta"][:], in_=ba_ps[:, :, 0:4], func=AF.Exp, scale=-1.0),
                 reads=bPB[0] + bg, writes=bg)
            S.op("act", lambda e: e.activation(out=G["beta"][:], in_=G["beta"][:], func=AF.Ln, bias=1.0), reads=bg, writes=bg)
            S.op("act", lambda e: e.activation(out=G["beta"][:], in_=G["beta"][:], func=AF.Exp, scale=-1.0), reads=bg, writes=bg)
            S.op("dve", lambda e: e.tensor_scalar(out=G["negb"][:], in0=G["beta"][:], scalar1=-1.0, scalar2=None, op0=ALU.mult),
                 reads=bg, writes=bg)
            gflat = G["g"][:].rearrange("p t c -> p (t c)")
            S.op("pe", lambda e: e.matmul(PB[1][:, 0:64], lhsT=ublk, rhs=gflat, start=True, stop=True),
                 reads=bg + [B["gm"]], writes=bPB[1], inc=False)
            S.op("pe", lambda e: e.matmul(PB[1][:, 64:128], lhsT=blk, rhs=gflat, start=True, stop=True),
                 reads=bg + [B["gm"]], writes=bPB[1], inc=False)
            S.op("pe", lambda e: e.matmul(PB[1][:, 128:192], lhsT=halfsel[0], rhs=gflat, start=True, stop=True),
                 reads=bg + [B["gm"]], writes=bPB[1], inc=False)
            S.op("pe", lambda e: e.matmul(PB[1][:, 192:256], lhsT=halfsel[1], rhs=gflat, start=True, stop=True),
                 reads=bg + [B["gm"]], writes=bPB[1])
            v3 = lambda ap: ap.rearrange("p (t c) -> p t c", c=4)
            S.op("dve", lambda e: e.tensor_copy(out=G["G"][:], in_=v3(PB[1][:, 0:64])), reads=bPB[1] + bg, writes=bg)
            S.op("dve", lambda e: e.tensor_copy(out=G["GL"][:], in_=v3(PB[1][:, 64:128])), reads=bPB[1] + bg, writes=bg)
            S.op("act", lambda e: e.activation(out=G["eGL0"][:], in_=v3(PB[1][:, 128:192]), func=AF.Exp), reads=bPB[1] + bg, writes=bg)
            S.op("act", lambda e: e.activation(out=G["eGL1"][:], in_=v3(PB[1][:, 192:256]), func=AF.Exp), reads=bPB[1] + bg, writes=bg)
            S.op("act", lambda e: e.activation(out=G["eG"][:], in_=G["G"][:], func=AF.Exp), reads=bg, writes=bg)
            S.op("dve", lambda e: e.tensor_tensor(out=G["bG"][:], in0=G["beta"][:], in1=G["eG"][:], op=ALU.mult), reads=bg, writes=bg)
            S.op("dve", lambda e: e.tensor_tensor(out=G["dG"][:], in0=G["GL"][:], in1=G["G"][:], op=ALU.subtract), reads=bg, writes=bg)
            S.op("act", lambda e: e.activation(out=G["dG"][:], in_=G["dG"][:], func=AF.Exp), reads=bg, writes=bg)
            eGLb = (G["eGL0"], G["eGL1"])
            bgr = [Buf("gates_ro")]
            bgr[0].w = B["gates"].w

            bc4 = lambda ap2: ap2[:, :, None].to_broadcast([128, 4, 128])
            hb4 = lambda ap2: ap2[:, None, :].to_broadcast([128, 4, 128])
            h4 = lambda pb: PB[pb][:, :].rearrange("p (h c) -> p h c", c=128)

            def T_I():
                n_w = 0
                for blkI in range(4):
                    if blkI >= 2:
                        yield ("wait", ("Rblk", blkI - 2))
                    par = blkI % 2
                    bsl = slice(blkI * 512, (blkI + 1) * 512)
                    for jc in range(12):
                        wt = wbq[n_w % 2]; bwt = b_wbq[n_w % 2]
                        xcb = xc[n_w % 2]; bxc = b_xc[n_w % 2]
                        n_w += 1
                        S.dma("pool", wt[:], wbq_d[jc], writes=[bwt])
                        for kc in range(KC):
                            S.op("pe", lambda e: e.matmul(PB[I_A][:, :], lhsT=wt[:, kc, :], rhs=hT[:, kc, bsl],
                                                          start=(kc == 0), stop=(kc == KC - 1)),
                                 reads=[bwt] + b_hT[4 * blkI:4 * blkI + 4], writes=bPB[I_A], inc=(kc == KC - 1))
                        yield
                        S.op("pool", lambda e: e.tensor_copy(out=xcb[:, 0:3], in_=cr[:, jc, :]), reads=[B["cr"]], writes=[bxc])
                        S.op("act", lambda e: e.activation(out=xcb[:, 3:515], in_=PB[I_A][:, :], func=AF.Copy),
                             reads=bPB[I_A], writes=[bxc])
                        S.op("pool", lambda e: e.tensor_copy(out=cr[:, jc, :], in_=xcb[:, 512:515]), reads=[bxc], writes=[B["cr"]])
                        yield
                        S.op("dve", lambda e: e.tensor_scalar(out=yc[:], in0=xcb[:, 0:512], scalar1=cw[:, jc, 0:1], scalar2=None,
                                                              op0=ALU.mult), reads=[bxc, B["cw"]], writes=[B["yc"]])
                        for tap in range(1, 4):
                            S.op("dve", lambda e: e.scalar_tensor_tensor(out=yc[:], in0=xcb[:, tap:tap + 512], scalar=cw[:, jc, tap:tap + 1],
                                                                         in1=yc[:], op0=ALU.mult, op1=ALU.add),
                                 reads=[bxc, B["cw"], B["yc"]], writes=[B["yc"]])
                            yield
                        S.op("act", lambda e: e.activation(out=t1[:], in_=yc[:], func=AF.Exp, scale=-1.0), reads=[B["yc"]], writes=[B["t1"]])
                        S.op("act", lambda e: e.activation(out=t1[:], in_=t1[:], func=AF.Ln, bias=1.0), reads=[B["t1"]], writes=[B["t1"]])
                        S.op("act", lambda e: e.activation(out=t1[:], in_=t1[:], func=AF.Exp, scale=-1.0), reads=[B["t1"]], writes=[B["t1"]])
                        yield
                        if jc >= 8:
                            S.op("dve", lambda e: e.tensor_tensor(out=vsT[par][:, jc - 8, :], in0=yc[:], in1=t1[:], op=ALU.mult),
                                 reads=[B["yc"], B["t1"]], writes=[D2["vsT"][par]])
                            yield
                            continue
                        S.op("dve", lambda e: e.tensor_tensor(out=sc[:], in0=yc[:], in1=t1[:], op=ALU.mult),
                             reads=[B["yc"], B["t1"]], writes=[B["sc"]])
                        S.op("act", lambda e: e.activation(out=sqb[:], in_=sc[:], func=AF.Square), reads=[B["sc"]], writes=[B["sqb"]])
                        yield
                        S.op("pe", lambda e: e.matmul(PB[I_B][:, :], lhsT=ones_b[:], rhs=sqb[:], start=True, stop=True),
                             reads=[B["onesb"], B["sqb"]], writes=bPB[I_B])
                        S.op("act", lambda e: e.activation(out=t1[:], in_=PB[I_B][:, :], func=AF.Ln, bias=EPS),
                             reads=bPB[I_B] + [B["t1"]], writes=[B["t1"]])
                        isq = jc < 4
                        S.op("act", lambda e: e.activation(out=t1[:], in_=t1[:], func=AF.Exp, scale=-0.5,
                                                           bias=(-0.5 * math.log(128.0) if isq else 0.0)),
                             reads=[B["t1"]], writes=[B["t1"]])
                        yield
                        dstT = qnT[par] if isq else knT[par]
                        bd = D2["qnT"][par] if isq else D2["knT"][par]
                        S.op("dve", lambda e: e.tensor_tensor(out=dstT[:, jc % 4, :], in0=sc[:], in1=t1[:], op=ALU.mult),
                             reads=[B["sc"], B["t1"]], writes=[bd])
                        yield
                    yield ("done", ("I", blkI))

            def T_P():
                for i in range(NT):
                    blkI, tl = divmod(i, 4)
                    par = blkI % 2
                    tp = i % 2
                    yield ("wait", ("I", blkI))
                    if i >= 2:
                        yield ("wait", ("R", i - 2))
                    csl = slice(tl * 128, (tl + 1) * 128)
                    qn, kn, vs = qnT[par], knT[par], vsT[par]
                    bqn, bkn, bvs = D2["qnT"][par], D2["knT"][par], D2["vsT"][par]
                    pa, pb_, pc = h4(P_A), h4(P_B), h4(P_C)
                    S.op("dve", lambda e: e.tensor_copy(out=gb[:], in_=bc4(G["g"][:, i, :])), reads=bgr, writes=[B["gb"]])
                    for h in range(4):
                        S.op("pe", lambda e: e.matmul(pa[:, h, :], lhsT=gb[:, h, :], rhs=ublk, start=True, stop=True),
                             reads=[B["gb"], B["gm"]], writes=bPB[P_A], inc=(h == 3))
                    yield
                    S.op("act", lambda e: e.activation(out=Dm[:], in_=pa, func=AF.Exp), reads=bPB[P_A], writes=[B["Dm"]])
                    S.op("dve", lambda e: e.tensor_tensor(out=qgT[tp][:], in0=qn[:, :, csl], in1=Dm[:], op=ALU.mult),
                         reads=[bqn, B["Dm"]], writes=[D2["qgT"][tp]])
                    yield
                    p3b = PB[P_B][:, :].bitcast(BF16).rearrange("p (a h c) -> p a h c", a=2, h=4)
                    for h in range(4):
                        S.op("pe", lambda e: e.transpose(p3b[:, 0, h, :], kn[:, h, csl], ident[:]),
                             reads=[bkn, b_const], writes=bPB[P_B], inc=False)
                    for h in range(4):
                        S.op("pe", lambda e: e.transpose(p3b[:, 1, h, :], vs[:, h, csl], ident[:]),
                             reads=[bvs, b_const], writes=bPB[P_B], inc=(h == 3))
                    yield
                    S.op("dve", lambda e: e.tensor_tensor(out=kbg[tp][:], in0=p3b[:, 0], in1=bc4(G["bG"][:, i, :]), op=ALU.mult),
                         reads=bPB[P_B] + bgr, writes=[D2["kbg"][tp]])
                    S.op("dve", lambda e: e.tensor_tensor(out=kdec[tp][:], in0=p3b[:, 0], in1=bc4(G["dG"][:, i, :]), op=ALU.mult),
                         reads=bPB[P_B] + bgr, writes=[D2["kdec"][tp]])
                    yield
                    S.op("dve", lambda e: e.tensor_tensor(out=vb[tp][:], in0=p3b[:, 1], in1=bc4(G["beta"][:, i, :]), op=ALU.mult),
                         reads=bPB[P_B] + bgr, writes=[D2["vb"][tp]])
                    S.op("pool", lambda e: e.tensor_tensor(out=gU[:], in0=hb4(ublk), in1=bc4(G["g"][:, i, :]), op=ALU.mult),
                         reads=bgr + [B["gm"]], writes=[B["gU"]])
                    yield
                    for h in range(4):
                        S.op("pe", lambda e: e.matmul(pa[:, h, :], lhsT=gU[:, h, :], rhs=strictblk, start=True, stop=True),
                             reads=[B["gU"], B["gm"]], writes=bPB[P_A], inc=(h == 3))
                    for h in range(4):
                        S.op("pe", lambda e: e.matmul(pb_[:, h, :], lhsT=strictblk, rhs=gU[:, h, :], start=True, stop=True),
                             reads=[B["gU"], B["gm"]], writes=bPB[P_B], inc=(h == 3))
                    yield
                    S.op("act", lambda e: e.activation(out=Dm[:], in_=pa, func=AF.Exp), reads=bPB[P_A] + [B["Dm"]], writes=[B["Dm"]])
                    S.op("act", lambda e: e.activation(out=DTm[:], in_=pb_, func=AF.Exp), reads=bPB[P_B], writes=[B["DTm"]])
                    yield
                    for h in range(4):
                        S.op("pe", lambda e: e.matmul(pa[:, h, :], lhsT=kn[:, h, csl], rhs=kn[:, h, csl], start=True, stop=True),
                             reads=[bkn], writes=bPB[P_A], inc=(h == 3))
                    for h in range(4):
                        S.op("pe", lambda e: e.matmul(pb_[:, h, :], lhsT=kn[:, h, csl], rhs=qn[:, h, csl], start=True, stop=True),
                             reads=[bkn, bqn], writes=bPB[P_B], inc=(h == 3))
                    yield
                    S.op("pool", lambda e: e.tensor_tensor(out=Dm[:], in0=Dm[:], in1=hb4(strictblk), op=ALU.mult),
                         reads=[B["Dm"], B["gm"]], writes=[B["Dm"]])
                    S.op("pool", lambda e: e.tensor_tensor(out=Dm[:], in0=Dm[:], in1=bc4(G["negb"][:, i, :]), op=ALU.mult),
                         reads=[B["Dm"]] + bgr, writes=[B["Dm"]])
                    yield
                    S.op("dve", lambda e: e.tensor_tensor(out=Pm1[:], in0=pa, in1=Dm[:], op=ALU.mult),
                         reads=bPB[P_A] + [B["Dm"]], writes=[B["Pm"]])
                    S.op("pool", lambda e: e.tensor_tensor(out=DTm[:], in0=DTm[:], in1=hb4(ublk), op=ALU.mult),
                         reads=[B["DTm"], B["gm"]], writes=[B["DTm"]])
                    yield
                    S.op("dve", lambda e: e.tensor_tensor(out=qkDT[tp][:], in0=pb_, in1=DTm[:], op=ALU.mult),
                         reads=bPB[P_B] + [B["DTm"]], writes=[D2["qkDT"][tp]])
                    for h in range(4):
                        S.op("pe", lambda e: e.matmul(pc[:, h, :], lhsT=Pm1[:, h, :], rhs=ident_f, start=True, stop=True),
                             reads=[B["Pm"], B["gm"]], writes=bPB[P_C], inc=(h == 3))
                    yield
                    S.op("act", lambda e: e.activation(out=Qm1[:], in_=pc, func=AF.Copy), reads=bPB[P_C], writes=[B["Qm"]])
                    S.op("dve", lambda e: e.tensor_tensor(out=TT[:], in0=pc, in1=hb4(ident_f), op=ALU.add),
                         reads=bPB[P_C] + [B["gm"]], writes=[B["TT"]])
                    yield
                    for lvl in range(5):
                        for h in range(4):
                            S.op("pe", lambda e: e.matmul(pa[:, h, :], lhsT=Qm1[:, h, :], rhs=Pm1[:, h, :], start=True, stop=True),
                                 reads=[B["Qm"], B["Pm"]], writes=bPB[P_A], inc=(h == 3))
                        if lvl < 4:
                            for h in range(4):
                                S.op("pe", lambda e: e.matmul(pb_[:, h, :], lhsT=Pm1[:, h, :], rhs=Qm1[:, h, :], start=True, stop=True),
                                     reads=[B["Qm"], B["Pm"]], writes=bPB[P_B], inc=(h == 3))
                        yield
                        S.op("act", lambda e: e.activation(out=Pm1[:], in_=pa, func=AF.Copy), reads=bPB[P_A], writes=[B["Pm"]])
                        if lvl < 4:
                            S.op("dve", lambda e: e.tensor_copy(out=Qm1[:], in_=pb_), reads=bPB[P_B], writes=[B["Qm"]])
                        yield
                        for h in range(4):
                            S.op("pe", lambda e: e.matmul(pc[:, h, :], lhsT=Pm1[:, h, :], rhs=TT[:, h, :], start=True, stop=True),
                                 reads=[B["Pm"], B["TT"]], writes=bPB[P_C], inc=(h == 3))
                        yield
                        S.op("dve", lambda e: e.tensor_tensor(out=TT[:], in0=pc, in1=TT[:], op=ALU.add),
                             reads=bPB[P_C] + [B["TT"]], writes=[B["TT"]])
                        yield
                    S.op("act", lambda e: e.activation(out=TTb[:], in_=TT[:], func=AF.Copy), reads=[B["TT"]], writes=[B["TTb"]])
                    yield
                    for h in range(4):
                        S.op("pe", lambda e: e.matmul(pa[:, h, :], lhsT=TTb[:, h, :], rhs=vb[tp][:, h, :], start=True, stop=True),
                             reads=[B["TTb"], D2["vb"][tp]], writes=bPB[P_A], inc=(h == 3))
                    for h in range(4):
                        S.op("pe", lambda e: e.matmul(pb_[:, h, :], lhsT=kbg[tp][:, h, :], rhs=TTb[:, h, :], start=True, stop=True),
                             reads=[B["TTb"], D2["kbg"][tp]], writes=bPB[P_B], inc=(h == 3))
                    yield
                    S.op("act", lambda e: e.activation(out=u_t[tp][:], in_=PB[P_A][:, :], func=AF.Copy), reads=bPB[P_A], writes=[D2["u"][tp]])
                    S.op("dve", lambda e: e.tensor_copy(out=wT[tp][:], in_=pb_), reads=bPB[P_B], writes=[D2["wT"][tp]])
                    yield ("done", ("P", i))

            def T_R():
                for i in range(NT):
                    tp = i % 2
                    yield ("wait", ("P", i))
                    for ch in range(2):
                        pr = slice(64 * ch, 64 * ch + 64)
                        pc_ = slice(64 * ch, 64 * ch + 64)
                        S.op("pool", lambda e: e.tensor_tensor(out=Sdec[:], in0=St[:], in1=bc4(eGLb[ch][:, i, :]), op=ALU.mult),
                             reads=[B["S"]] + bgr, writes=[B["Sdec"]])
                        for h in range(4):
                            S.op("pe", lambda e: e.matmul(PB[R_A][pr, h * 128:(h + 1) * 128], lhsT=wT[tp][:, h, pc_], rhs=Sb[:, h, :],
                                                          start=True, stop=True),
                                 reads=[D2["wT"][tp], B["Sb"]], writes=bPB[R_A], inc=(h == 3))
                        yield
                        S.op("dve", lambda e: e.tensor_tensor(out=vnew[pr, :], in0=u_t[tp][pr, :], in1=PB[R_A][pr, :], op=ALU.subtract),
                             reads=bPB[R_A] + [D2["u"][tp]], writes=[B["vnew"]])
                        yield
                        for h in range(4):
                            S.op("pe", lambda e: e.matmul(PB[R_B][pr, h * 128:(h + 1) * 128], lhsT=qgT[tp][:, h, pc_], rhs=Sb[:, h, :],
                                                          start=True, stop=False),
                                 reads=[D2["qgT"][tp], B["Sb"]], writes=bPB[R_B], inc=False)
                            S.op("pe", lambda e: e.matmul(PB[R_B][pr, h * 128:(h + 1) * 128], lhsT=qkDT[tp][pr, h, pc_],
                                                          rhs=vnew[pr, h * 128:(h + 1) * 128], start=False, stop=True),
                                 reads=[D2["qkDT"][tp], B["vnew"]], writes=bPB[R_B], inc=(h == 3))
                        yield
                        for h in range(4):
                            S.op("pe", lambda e: e.matmul(PB[R_C][:, h * 128:(h + 1) * 128], lhsT=kdec[tp][pr, h, :],
                                                          rhs=vnew[pr, h * 128:(h + 1) * 128], start=True, stop=True),
                                 reads=[D2["kdec"][tp], B["vnew"]], writes=bPB[R_C], inc=(h == 3))
                        yield
                        Sf = St[:].rearrange("p h c -> p (h c)")
                        Sdf = Sdec[:].rearrange("p h c -> p (h c)")
                        Sbf = Sb[:].rearrange("p h c -> p (h c)")
                        S.op("dve", lambda e: e.tensor_tensor(out=Sbf, in0=PB[R_C][:, :], in1=Sdf, op=ALU.add),
                             reads=bPB[R_C] + [B["Sdec"]], writes=[B["Sb"]])
                        S.op("dve", lambda e: e.tensor_tensor(out=Sf, in0=PB[R_C][:, :], in1=Sdf, op=ALU.add),
                             reads=bPB[R_C] + [B["Sdec"]], writes=[B["S"]])
                        yield
                    for kc in range(KC):
                        S.op("pe", lambda e: e.matmul(PB[R_A][:, :], lhsT=hT[:, kc, i * 128:(i + 1) * 128], rhs=wbz[:, kc, :],
                                                      start=(kc == 0), stop=(kc == KC - 1)),
                             reads=[b_hT[i], B["wbz"]], writes=bPB[R_A], inc=(kc == KC - 1))
                    yield
                    S.op("act", lambda e: e.activation(out=zs[:], in_=PB[R_A][:, :], func=AF.Exp, scale=-1.0), reads=bPB[R_A], writes=[B["zs"]])
                    S.op("act", lambda e: e.activation(out=zs[:], in_=zs[:], func=AF.Ln, bias=1.0), reads=[B["zs"]], writes=[B["zs"]])
                    S.op("act", lambda e: e.activation(out=zs[:], in_=zs[:], func=AF.Exp, scale=-1.0), reads=[B["zs"]], writes=[B["zs"]])
                    yield
                    S.op("dve", lambda e: e.tensor_tensor(out=zs[:], in0=PB[R_A][:, :], in1=zs[:], op=ALU.mult),
                         reads=bPB[R_A] + [B["zs"]], writes=[B["zs"]])
                    zs3 = zs[:].rearrange("p (h c) -> p h c", c=128)
                    S.op("pool", lambda e: e.tensor_tensor(out=zs3, in0=zs3, in1=hb4(hn_bc[:]), op=ALU.mult),
                         reads=[B["zs"], B["hn"]], writes=[B["zs"]])
                    yield
                    for h in range(4):
                        S.op("act", lambda e: e.activation(out=sqr[:], in_=PB[R_B][:, h * 128:(h + 1) * 128], func=AF.Square,
                                                           accum_out=sso[:, h:h + 1]),
                             reads=bPB[R_B], writes=[B["sqr"], B["sso"]])
                    S.op("act", lambda e: e.activation(out=sso[:], in_=sso[:], func=AF.Ln, scale=1.0 / 128, bias=EPS),
                         reads=[B["sso"]], writes=[B["sso"]])
                    S.op("act", lambda e: e.activation(out=sso[:], in_=sso[:], func=AF.Exp, scale=-0.5), reads=[B["sso"]], writes=[B["sso"]])
                    yield
                    t13 = t1r[:].rearrange("p (h c) -> p h c", c=128)
                    S.op("dve", lambda e: e.tensor_tensor(out=t13, in0=h4(R_B), in1=bc4(sso[:]), op=ALU.mult),
                         reads=bPB[R_B] + [B["sso"], B["t1r"]], writes=[B["t1r"]])
                    S.op("dve", lambda e: e.tensor_tensor(out=ot[:], in0=t1r[:], in1=zs[:], op=ALU.mult),
                         reads=[B["t1r"], B["zs"]], writes=[B["ot"]])
                    yield
                    p3o = PB[R_C][:, :].bitcast(BF16)[:, 0:512].rearrange("p (h c) -> p h c", c=128)
                    for h in range(4):
                        S.op("pe", lambda e: e.transpose(p3o[:, h, :], ot[:, h * 128:(h + 1) * 128], ident[:]),
                             reads=[B["ot"], b_const], writes=bPB[R_C], inc=(h == 3))
                    S.op("act", lambda e: e.activation(out=og[:, 4:8, i * 128:(i + 1) * 128], in_=p3o, func=AF.Copy),
                         reads=bPB[R_C], writes=[b_og[4 + hh][i // 4] for hh in range(4)])
                    yield ("done", ("R", i))
                    if i % 4 == 3:
                        yield ("done", ("Rblk", i // 4))

            run_threads([T_I(), T_P(), T_R()], BW)

        def layer1(seq, last):
            nonlocal sb_l, b_l
            with contextlib.ExitStack() as st1:
                st2 = st1.enter_context(contextlib.ExitStack())
                cur = [st1]

                def sl(name, shape, dt):
                    uid[0] += 1
                    return cur[0].enter_context(nc.sbuf_tensor("t%d_%s" % (uid[0], name), list(shape), dt))
                sb_l = {}
                b_l = {}
                sb_l["ss2"] = sl("ss2", [128, 2 * NT], F32); b_l["ss2"] = Buf()
                sb_l["rs2"] = sl("rs2", [128, NT], F32); b_l["rs2"] = Buf()
                sb_l["tmpf"] = [sl("tmpf%d" % i, [128, 512], F32) for i in range(2)]; b_l["tmpf"] = [Buf(), Buf()]
                wo = sl("wo", [128, 8, D], BF16)
                cur[0] = st2
                qT = [sl("qT%d" % i, [128, SEQ], BF16) for i in range(2)]
                kT = [sl("kT%d" % i, [128, SEQ], BF16) for i in range(2)]
                Vx = [sl("Vx%d" % i, [128, NT, 128], BF16) for i in range(2)]
                vT5 = sl("vT5", [128, 512], BF16)
                b_vT5 = Buf()
                wq = sl("wq", [128, KC, 128], BF16)
                wk = sl("wk", [128, KC, 128], BF16)
                wv = sl("wv", [128, KC, 128], BF16)
                wz = [sl("wz%d" % i, [128, KC, 128], BF16) for i in range(2)]
                wf = sl("wf", [128, KC, 16], BF16)
                fb_bc = sl("fb_bc", [128, 16], F32)
                flb = sl("flb", [128, NT, 16], F32)
                nlf = sl("nlf", [128, NT, 16], F32)
                NC_ = sl("NC", [128, NT, 16], F32)
                carry = sl("carry", [128, 16], F32)
                carryT = sl("carryT", [16, 1], F32)
                cT = sl("cT", [16, SEQ], F32)
                cHL = sl("cHL", [16, 2, SEQ], BF16)
                pt = [sl("pt%d" % i, [128, 512], BF16) for i in range(3)]
                e_t = sb_l["tmpf"][0]
                sums = sb_l["tmpf"][1]
                den = sl("den", [128, 512], F32)
                tt = den
                b_qT = [Buf(), Buf()]; b_kT = [Buf(), Buf()]; b_Vx = [Buf(), Buf()]
                b_qaug = [Buf(), Buf()]
                b_wq, b_wk, b_wv = Buf(), Buf(), Buf()
                b_wz = [Buf(), Buf()]
                b_wf, b_fb, b_flb, b_nlf, b_NC, b_carry, b_carryT, b_cT, b_cTt, b_cHL = [Buf() for _ in range(10)]
                b_pt = [Buf() for _ in range(3)]
                b_e, b_sums, b_den = b_l["tmpf"][0], b_l["tmpf"][1], Buf()
                b_tt = b_den

                try:
                    load_norms(1)
                    S.dma("pool", wo[:], woc_d[:, :, :], writes=[b_wo])
                    S.dma("pool", wf[:], wf_d[:, :, :], writes=[b_wf])
                    S.dma("sp", fb_bc[:], bass.AP(fb_d.tensor, 0, [[0, 128], [1, 16]]), writes=[b_fb])
                    for i in range(2):
                        S.op("pool", lambda e: e.memset(kT[i][64:66, :], 1.0), writes=[b_kT[i]])
                    S.op("pool", lambda e: e.memset(Vx[0][:, :, 64:128], 1.0), writes=[b_Vx[0]])
                    S.op("pool", lambda e: e.memset(Vx[1][:, :, 0:64], 1.0), writes=[b_Vx[1]])
                    S.op("pool", lambda e: e.memset(carry[:], 0.0), writes=[b_carry])
                    S.op("pool", lambda e: e.memset(carryT[:], 0.0), writes=[b_carryT])

                    l1src = x1s_d[seq] if 0 in layers else x_d[seq]
                    l1b = b_x1 if 0 in layers else b_xdram
                    prenorm(l1src, l1b)
                    if STAGE <= 1:
                        raise StopStage()

                    fl_ps = PB[0][:, 0:NT * 16].rearrange("p (t h) -> p t h", h=16)
                    for i in range(NT):
                        for kc in range(KC):
                            S.op("pe", lambda e: e.matmul(fl_ps[:, i, :], lhsT=hT[:, kc, i * 128:(i + 1) * 128],
                                                          rhs=wf[:, kc, :], start=(kc == 0), stop=(kc == KC - 1)),
                                 reads=[b_hT[i], b_wf], writes=bPB[0], inc=(kc == KC - 1))
                    S.op("dve", lambda e: e.tensor_tensor(out=flb[:], in0=fl_ps, in1=fb_bc[:, None, :].to_broadcast([128, NT, 16]),
                                                          op=ALU.add),
                         reads=bPB[0] + [b_fb], writes=[b_flb])
                    S.op("act", lambda e: e.activation(out=flb[:], in_=flb[:], func=AF.Exp, scale=-1.0),
                         reads=[b_flb], writes=[b_flb])
                    S.op("act", lambda e: e.activation(out=nlf[:], in_=flb[:], func=AF.Ln, bias=1.0),
                         reads=[b_flb], writes=[b_nlf])
                    for i in range(NT):
                        bk = 1 + (i % 2)
                        c1 = PB[bk][:, 0:16]
                        c2 = PB[bk][:, 16:32]
                        c3 = PB[bk][0:16, 32:32 + 129]
                        S.op("pe", lambda e: e.matmul(c1, lhsT=uext[:, 0:128], rhs=nlf[:, i, :], start=True, stop=True),
                             reads=[b_nlf, b_const], writes=bPB[bk], inc=False)
                        S.op("pe", lambda e: e.matmul(c2, lhsT=ones_f[:], rhs=nlf[:, i, :], start=True, stop=True),
                             reads=[b_nlf, b_const], writes=bPB[bk], inc=False)
                        S.op("pe", lambda e: e.matmul(c3, lhsT=nlf[:, i, :], rhs=uext[:, :], start=True, stop=True),
                             reads=[b_nlf, b_const], writes=bPB[bk])
                        S.op("dve", lambda e: e.tensor_tensor(out=NC_[:, i, :], in0=c1, in1=carry[:], op=ALU.add),
                             reads=bPB[bk] + [b_carry], writes=[b_NC])
                        S.op("dve", lambda e: e.tensor_tensor(out=carry[:], in0=c2, in1=carry[:], op=ALU.add),
                             reads=bPB[bk] + [b_carry], writes=[b_carry])
                        S.op("dve", lambda e: e.tensor_scalar(out=cT[:, i * 128:(i + 1) * 128], in0=c3[:, 0:128],
                                                              scalar1=carryT[:, 0:1], scalar2=-8.0,
                                                              op0=ALU.add, op1=ALU.mult),
                             reads=bPB[bk] + [b_carryT], writes=[b_cT])
                        S.op("dve", lambda e: e.tensor_tensor(out=carryT[:], in0=c3[:, 128:129], in1=carryT[:], op=ALU.add),
                             reads=bPB[bk] + [b_carryT], writes=[b_carryT])
                    S.op("dve", lambda e: e.tensor_copy(out=cHL[:, 0, :], in_=cT[:]), reads=[b_cT], writes=[b_cHL])
                    S.op("dve", lambda e: e.tensor_tensor(out=cT[:], in0=cT[:], in1=cHL[:, 0, :], op=ALU.subtract),
                         reads=[b_cT, b_cHL], writes=[b_cT])
                    S.op("dve", lambda e: e.tensor_copy(out=cHL[:, 1, :], in_=cT[:]), reads=[b_cT, b_cHL], writes=[b_cHL])

                    if STAGE <= 2:
                        raise StopStage()
                    for p in range(8):
                        S.dma("pool", wq[:], wc_d[p, 0], writes=[b_wq])
                        S.dma("pool", wk[:], wc_d[p, 1], writes=[b_wk])
                        S.dma("pool", wv[:], wc_d[p, 2], writes=[b_wv])
                        S.dma("pool", wz[p % 2][:], wc_d[p, 3], writes=[b_wz[p % 2]])
                        if STAGE <= 2.2:
                            raise StopStage()
                        for hh in range(2):
                            for r in range(2):
                                S.dma("sp", qT[hh][64 + r:65 + r, :], cHL[2 * p + hh:2 * p + hh + 1, r, :],
                                      reads=[b_cHL], writes=[b_qaug[hh]])
                        if STAGE <= 2.4:
                            raise StopStage()
                        n_ev = 0
                        for (wt, bw, dstT, bdst) in ((wq, b_wq, qT, b_qT), (wk, b_wk, kT, b_kT)):
                            for t4 in range(4):
                                bk = 6 + (n_ev % 2)
                                n_ev += 1
                                for kc in range(KC):
                                    S.op("pe", lambda e: e.matmul(PB[bk][:, :], lhsT=wt[:, kc, :],
                                                                  rhs=hT[:, kc, t4 * 512:(t4 + 1) * 512],
                                                                  start=(kc == 0), stop=(kc == KC - 1)),
                                         reads=[bw] + b_hT[4 * t4:4 * t4 + 4], writes=bPB[bk], inc=(kc == KC - 1))
                                S.op("act", lambda e: e.activation(out=dstT[0][0:64, t4 * 512:(t4 + 1) * 512],
                                                                   in_=PB[bk][0:64, :], func=AF.Copy),
                                     reads=bPB[bk][0:1], writes=[bdst[0]])
                                S.op("dve", lambda e: e.tensor_copy(out=dstT[1][0:64, t4 * 512:(t4 + 1) * 512],
                                                                    in_=PB[bk][64:128, :]),
                                     reads=bPB[bk][1:2], writes=[bdst[1]])
                        if STAGE <= 2.6:
                            raise StopStage()
                        for t4 in range(4):
                            for kc in range(KC):
                                S.op("pe", lambda e: e.matmul(PB[6][:, :], lhsT=wv[:, kc, :],
                                                              rhs=hT[:, kc, t4 * 512:(t4 + 1) * 512],
                                                              start=(kc == 0), stop=(kc == KC - 1)),
                                     reads=[b_wv] + b_hT[4 * t4:4 * t4 + 4], writes=bPB[6], inc=(kc == KC - 1))
                            S.op("act", lambda e: e.activation(out=vT5[:], in_=PB[6][:, :], func=AF.Copy),
                                 reads=bPB[6], writes=[b_vT5])
                            pbf = PB[7][:, :].bitcast(BF16)[:, 0:512].rearrange("p (j c) -> p j c", c=128)
                            for j in range(4):
                                S.op("pe", lambda e: e.transpose(pbf[:, j, :], vT5[:, j * 128:(j + 1) * 128], ident[:]),
                                     reads=[b_vT5, b_const], writes=bPB[7], inc=(j == 3))
                            S.op("act", lambda e: e.activation(out=Vx[0][:, 4 * t4:4 * t4 + 4, 0:64], in_=pbf[:, :, 0:64],
                                                               func=AF.Copy),
                                 reads=bPB[7], writes=[b_Vx[0]])
                            S.op("dve", lambda e: e.tensor_copy(out=Vx[1][:, 4 * t4:4 * t4 + 4, 64:128], in_=pbf[:, :, 64:128]),
                                 reads=bPB[7], writes=[b_Vx[1]])
                        if STAGE <= 3:
                            raise StopStage()
                        jobs = []
                        for Qc in range(4):
                            for kt in range(4 * Qc + 4):
                                for hh in range(2):
                                    jobs.append((Qc, kt, hh))

                        def emit_pv(n):
                            Qc, kt, hh = jobs[n]
                            o = max(0, kt - 4 * Qc) * 128
                            N = 512 - o
                            abk = 2 + 2 * (Qc % 2) + hh
                            S.op("pe", lambda e: e.matmul(PB[abk][:, o:512], lhsT=Vx[hh][:, kt, :], rhs=pt[n % 3][:, 0:N],
                                                          start=(kt == 0), stop=(kt == 4 * Qc + 3)),
                                 reads=[b_Vx[hh], b_pt[n % 3]], writes=bPB[abk])
                            if kt == 4 * Qc + 3 and hh == 1:
                                emit_epilogue(Qc)

                        def emit_epilogue(Qc):
                            zb = 6 + (Qc % 2)
                            a0 = 2 + 2 * (Qc % 2)
                            a1 = a0 + 1
                            for kc in range(KC):
                                S.op("pe", lambda e: e.matmul(PB[zb][:, :], lhsT=wz[p % 2][:, kc, :],
                                                              rhs=hT[:, kc, Qc * 512:(Qc + 1) * 512],
                                                              start=(kc == 0), stop=(kc == KC - 1)),
                                     reads=[b_wz[p % 2]] + b_hT[4 * Qc:4 * Qc + 4], writes=bPB[zb], inc=(kc == KC - 1))
                            S.op("act", lambda e: e.activation(out=e_t[:], in_=PB[zb][:, :], func=AF.Exp, scale=-1.0),
                                 reads=bPB[zb], writes=[b_e])
                            S.op("act", lambda e: e.activation(out=sums[0:64, :], in_=PB[a0][64:128, :], func=AF.Copy),
                                 reads=bPB[a0][1:2], writes=[b_sums])
                            S.op("act", lambda e: e.activation(out=sums[64:128, :], in_=PB[a1][0:64, :], func=AF.Copy),
                                 reads=bPB[a1][0:1], writes=[b_sums])
                            S.op("dve", lambda e: e.scalar_tensor_tensor(out=den[:], in0=e_t[:], scalar=1.0, in1=sums[:],
                                                                         op0=ALU.add, op1=ALU.mult),
                                 reads=[b_e, b_sums], writes=[b_den])
                            S.op("act", lambda e: e.activation(out=den[:], in_=den[:], func=AF.Ln), reads=[b_den], writes=[b_den])
                            S.op("act", lambda e: e.activation(out=den[:], in_=den[:], func=AF.Exp, scale=-1.0),
                                 reads=[b_den], writes=[b_den])
                            S.op("dve", lambda e: e.tensor_tensor(out=tt[:], in0=PB[zb][:, :], in1=den[:], op=ALU.mult),
                                 reads=bPB[zb] + [b_den], writes=[b_tt])
                            S.op("dve", lambda e: e.tensor_tensor(out=og[0:64, p, Qc * 512:(Qc + 1) * 512],
                                                                  in0=PB[a0][0:64, :], in1=tt[0:64, :], op=ALU.mult),
                                 reads=bPB[a0][0:1] + [b_tt], writes=[b_og[p][Qc]])
                            S.op("dve", lambda e: e.tensor_tensor(out=og[64:128, p, Qc * 512:(Qc + 1) * 512],
                                                                  in0=PB[a1][64:128, :], in1=tt[64:128, :], op=ALU.mult),
                                 reads=bPB[a1][1:2] + [b_tt], writes=[b_og[p][Qc]])

                        for n, (Qc, kt, hh) in enumerate(jobs):
                            o = max(0, kt - 4 * Qc) * 128
                            N = 512 - o
                            q0 = Qc * 512 + o
                            h = 2 * p + hh
                            sbk = n % 2
                            S.op("pe", lambda e: e.matmul(PB[sbk][:, 0:N], lhsT=kT[hh][0:66, kt * 128:(kt + 1) * 128],
                                                          rhs=qT[hh][0:66, q0:q0 + N], start=True, stop=True),
                                 reads=[b_kT[hh], b_qT[hh], b_qaug[hh]], writes=bPB[sbk])
                            S.op("act", lambda e: e.activation(out=pt[n % 3][:, 0:N], in_=PB[sbk][:, 0:N], func=AF.Exp,
                                                               scale=0.125, bias=NC_[:, kt, h:h + 1]),
                                 reads=bPB[sbk] + [b_NC], writes=[b_pt[n % 3]])
                            if kt >= 4 * Qc:
                                S.op("pool", lambda e: e.tensor_tensor(out=pt[n % 3][:, 0:128], in0=pt[n % 3][:, 0:128],
                                                                       in1=maskb[:], op=ALU.mult),
                                     reads=[b_pt[n % 3], b_const], writes=[b_pt[n % 3]])
                            if n >= 1:
                                emit_pv(n - 1)
                        emit_pv(len(jobs) - 1)
                except StopStage:
                    pass
                cur[0] = st1
                l1src = x1s_d[seq] if 0 in layers else x_d[seq]
                l1b = b_x1 if 0 in layers else b_xdram
                outproj_residual(l1src, l1b, out_d[seq], b_outd, wo)
                S.fence()
                st2.close()

        sb_l = None
        b_l = None
        for seq in range(nseq):
            if 0 in layers:
                layer0(seq, 1 not in layers)
            if 1 in layers:
                layer1(seq, True)
        S.finish(b_outd, "sp")
        print("n_ins", S.n_ins, "n_wait", S.n_wait)
    return nc


def prep_shared(inp):
    m = dict(host_consts())
    m["pre_norm"] = np.ascontiguousarray(inp["pre_norm"], dtype=np.float32)
    m["post_norm"] = np.ascontiguousarray(inp["post_norm"], dtype=np.float32)
    wc = np.asarray(inp["w_in_c"], dtype=np.float32)
    qkvz = wc[:, :4096].reshape(KC, 128, 4, 8, 128)
    m["wc"] = np.ascontiguousarray(qkvz.transpose(3, 2, 1, 0, 4))
    m["wf"] = np.ascontiguousarray(wc[:, 4096:4112].reshape(KC, 128, 16).transpose(1, 0, 2))
    m["woc"] = np.ascontiguousarray(np.asarray(inp["w_out_c"], dtype=np.float32).reshape(8, 128, D).transpose(1, 0, 2))
    m["c_forget_bias"] = np.ascontiguousarray(inp["c_forget_bias"], dtype=np.float32).reshape(1, 16)
    wab = np.asarray(inp["w_in_ab"], dtype=np.float32)
    mcol = np.arange(128)
    dd = mcol % 64
    dperm = np.where(dd < 8, dd + 8, np.where(dd < 16, dd - 8, dd))
    permcol = (mcol // 64) * 64 + dperm
    slabs = []
    for h in range(4):
        qc = wab[:, 0 * 512 + h * 128:0 * 512 + (h + 1) * 128]
        kc_ = wab[:, 1 * 512 + h * 128:1 * 512 + (h + 1) * 128]
        vc = wab[:, 2 * 512 + h * 128:2 * 512 + (h + 1) * 128]
        zc = wab[:, 3 * 512 + h * 128:3 * 512 + (h + 1) * 128]
        slabs.append(np.stack([qc, qc[:, permcol], kc_, kc_[:, permcol], vc, zc], axis=0))
    wa = np.stack(slabs, axis=0).reshape(4, 6, KC, 128, 128).transpose(0, 1, 3, 2, 4)
    m["wa"] = np.ascontiguousarray(wa)
    m["woab"] = np.ascontiguousarray(np.asarray(inp["w_out_ab"], dtype=np.float32).reshape(8, 128, D).transpose(1, 0, 2))
    m["lam4"] = np.ascontiguousarray(np.stack([inp["a_lambda_q1"], inp["a_lambda_k1"], inp["a_lambda_q2"], inp["a_lambda_k2"]]).astype(np.float32))
    m["a_subln"] = np.ascontiguousarray(np.asarray(inp["a_subln"], dtype=np.float32).reshape(128, 1))
    m["wbq"] = np.ascontiguousarray(wab[:, 2048:3584].reshape(KC, 128, 12, 128).transpose(2, 1, 0, 3))
    m["cw"] = np.ascontiguousarray(np.asarray(inp["b_conv_w"], dtype=np.float32).reshape(4, 12, 128).transpose(2, 1, 0))
    m["wbz"] = np.ascontiguousarray(wab[:, 3584:4096].reshape(KC, 128, 512).transpose(1, 0, 2))
    m["wba"] = np.ascontiguousarray(wab[:, 4096:4104].reshape(KC, 128, 8).transpose(1, 0, 2))
    m["b_a_log"] = np.ascontiguousarray(inp["b_a_log"], dtype=np.float32).reshape(1, 4)
    m["b_dt_bias"] = np.ascontiguousarray(inp["b_dt_bias"], dtype=np.float32).reshape(1, 4)
    m["b_head_norm"] = np.ascontiguousarray(inp["b_head_norm"], dtype=np.float32).reshape(1, 128)
    return m


def kernel(**inp):
    x = np.asarray(inp["x"], dtype=np.float32)
    B = x.shape[0]
    nseq = B // NCORES
    shared = prep_shared(inp)
    nc = build(nseq, LAYERS)
    in_maps = []
    for c in range(NCORES):
        m = dict(shared)
        m["x"] = np.ascontiguousarray(x[c * nseq:(c + 1) * nseq])
        m["positions"] = np.ascontiguousarray(np.asarray(inp["positions"], dtype=np.int32)[c * nseq:(c + 1) * nseq])
        in_maps.append(m)
    res = run_bass_kernel_spmd(nc, in_maps, core_ids=list(range(NCORES)), **RUN_KW)
    LAST['res'] = res
    return np.concatenate([r["out"] for r in res.results], axis=0)
```

```python
import contextlib
import math
import numpy as np
import concourse.bass as bass
import concourse.mybir as mybir
from concourse.bass_utils import run_bass_kernel_spmd

F32 = mybir.dt.float32
BF16 = mybir.dt.bfloat16
I32 = mybir.dt.int32
AF = mybir.ActivationFunctionType
ALU = mybir.AluOpType
AX = mybir.AxisListType

D = 1024
SEQ = 2048
NT = 16
KC = 8
EPS = 1e-6
NCORES = 8
LAYERS = (0, 1)
STAGE = 99
BW = (2, 4, 3)
L1W = (1, 4)
RUN_KW = {}
LAST = {}
import os
VVAR = int(os.environ.get('VVAR', '3'))


class StopStage(Exception):
    pass


class Buf:
    __slots__ = ("name", "w", "r", "psum")

    def __init__(self, name="", psum=False):
        self.name = name
        self.w = None
        self.r = {}
        self.psum = psum


class Sched:
    ENGS = ("pe", "act", "dve", "pool", "sp")

    def __init__(self, nc, stack, n_dma_sems=6):
        self.nc = nc
        self.e = {"pe": nc.tensor, "act": nc.scalar, "dve": nc.vector,
                  "pool": nc.gpsimd, "sp": nc.sync}
        self.semh = {}
        self.cnt = {}
        for k in self.ENGS:
            self.semh[k] = stack.enter_context(nc.semaphore("s_" + k))
            self.cnt[k] = 0
        self.dq = {}
        for q in ("sp", "act", "pool"):
            slots = []
            for i in range(n_dma_sems):
                key = "d_%s%d" % (q, i)
                self.semh[key] = stack.enter_context(nc.semaphore(key))
                self.cnt[key] = 0
                slots.append(key)
            self.dq[q] = [slots, 0]
        self.seen = {k: {} for k in self.ENGS}
        self.n_ins = {k: 0 for k in self.ENGS}
        self.n_wait = {k: 0 for k in self.ENGS}

    def _wait(self, eng, deps):
        need = {}
        seen = self.seen[eng]
        for (k, v) in deps:
            if eng == "pe" and k == "pe":
                continue
            if seen.get(k, 0) < v and need.get(k, 0) < v:
                need[k] = v
        for k, v in need.items():
            self.e[eng].wait_ge(self.semh[k], v)
            seen[k] = v
            self.n_wait[eng] += 1

    @staticmethod
    def _deps(reads, writes):
        deps = []
        for b in reads:
            if b.w is not None:
                deps.append(b.w)
            if b.psum:
                deps.extend(b.r.items())
        for b in writes:
            if b.w is not None:
                deps.append(b.w)
            deps.extend(b.r.items())
        return deps

    @staticmethod
    def _mark(tok, reads, writes):
        for b in reads:
            if b.r.get(tok[0], 0) < tok[1]:
                b.r[tok[0]] = tok[1]
        for b in writes:
            b.w = tok
            b.r = {}

    def op(self, eng, fn, reads=(), writes=(), inc=True):
        self._wait(eng, self._deps(reads, writes))
        ins = fn(self.e[eng])
        self.n_ins[eng] += 1
        if inc:
            self.cnt[eng] += 1
            ins.then_inc(self.semh[eng], 1)
            tok = (eng, self.cnt[eng])
        else:
            tok = (eng, self.cnt[eng] + 1)
        self._mark(tok, reads, writes)
        return ins

    def dma(self, q, out, in_, reads=(), writes=(), **kw):
        slots, idx = self.dq[q]
        key = slots[idx % len(slots)]
        self.dq[q][1] = idx + 1
        deps = self._deps(reads, writes)
        if self.cnt[key] > 0:
            deps.append((key, self.cnt[key]))
        self._wait(q, deps)
        ins = self.e[q].dma_start(out=out, in_=in_, **kw)
        self.cnt[key] += 16
        ins.then_inc(self.semh[key], 16)
        self._mark((key, self.cnt[key]), reads, writes)
        return ins

    def fence(self):
        allc = [(k, v) for k, v in self.cnt.items() if v > 0]
        for eng in self.ENGS:
            self._wait(eng, allc)

    def finish(self, bufs, eng="sp"):
        deps = []
        for b in bufs:
            if b.w is not None:
                deps.append(b.w)
            deps.extend(b.r.items())
        self._wait(eng, deps)


def host_consts():
    c = {}
    j = np.arange(128)
    U = (j[:, None] <= j[None, :]).astype(np.float32)
    c["c_uext"] = np.concatenate([U, np.ones((128, 1), np.float32)], axis=1)
    c["c_ident"] = np.eye(128, dtype=np.float32)
    p = np.arange(128)
    d = p % 64
    half = 8
    inv_freq = (np.float32(500000.0) ** (-(np.arange(half, dtype=np.float32) * np.float32(2.0)) / np.float32(16.0))).astype(np.float32)
    freq = np.where(d < 16, inv_freq[d % 8], 0.0).astype(np.float32)
    sign = np.where(d < 8, -1.0, np.where(d < 16, 1.0, 0.0)).astype(np.float32)
    c["c_rope"] = np.stack([freq, sign], axis=1).astype(np.float32)
    same = (j[:, None] // 64) == (j[None, :] // 64)
    ublk = (same & (j[:, None] <= j[None, :])).astype(np.float32)
    blk = same.astype(np.float32)
    strictblk = (same & (j[:, None] > j[None, :])).astype(np.float32)
    half0 = np.repeat((j < 64).astype(np.float32)[:, None], 128, axis=1)
    half1 = np.repeat((j >= 64).astype(np.float32)[:, None], 128, axis=1)
    c["c_gdn"] = np.stack([ublk, blk, strictblk, half0, half1, np.eye(128, dtype=np.float32)], axis=1)
    return c


def build(nseq, layers=(0, 1)):
    nc = bass.Bass("TRN2", target_bir_lowering=False)
    dt_in = lambda name, shape, dt=F32: nc.dram_tensor(name, list(shape), dt, kind="ExternalInput").ap()
    x_d = dt_in("x", [nseq, SEQ, D])
    out_d = nc.dram_tensor("out", [nseq, SEQ, D], F32, kind="ExternalOutput").ap()
    x1s_d = nc.dram_tensor("x1s", [nseq, SEQ, D], F32, kind="Internal").ap()
    pre_d = dt_in("pre_norm", [2, D])
    post_d = dt_in("post_norm", [2, D])
    uext_d = dt_in("c_uext", [128, 129])
    ident_d = dt_in("c_ident", [128, 128])
    wc_d = dt_in("wc", [8, 4, 128, KC, 128])
    wf_d = dt_in("wf", [128, KC, 16])
    woc_d = dt_in("woc", [128, 8, D])
    fb_d = dt_in("c_forget_bias", [1, 16])
    pos_d = dt_in("positions", [nseq, SEQ], I32)
    rope_d = dt_in("c_rope", [128, 2])
    wa_d = dt_in("wa", [4, 6, 128, KC, 128])
    woab_d = dt_in("woab", [128, 8, D])
    lam_d = dt_in("lam4", [4, 64])
    subln_d = dt_in("a_subln", [128, 1])
    gdnc_d = dt_in("c_gdn", [128, 6, 128])
    wbq_d = dt_in("wbq", [12, 128, KC, 128])
    cw_d = dt_in("cw", [128, 12, 4])
    wbz_d = dt_in("wbz", [128, KC, 512])
    wba_d = dt_in("wba", [128, KC, 8])
    alog_d = dt_in("b_a_log", [1, 4])
    dtb_d = dt_in("b_dt_bias", [1, 4])
    hn_d = dt_in("b_head_norm", [1, 128])

    with contextlib.ExitStack() as st:
        S = Sched(nc, st)
        uid = [0]
        def sb(name, shape, dt):
            uid[0] += 1
            return st.enter_context(nc.sbuf_tensor("t%d_%s" % (uid[0], name), list(shape), dt))
        xt = [sb("xt%d" % i, [128, D], F32) for i in range(3)]
        b_xt = [Buf() for _ in range(3)]
        junk2 = sb("junk2", [128, D], BF16)
        b_junk2 = Buf()
        hT = sb("hT", [128, KC, SEQ], BF16)
        og = sb("og", [128, 8, SEQ], BF16)
        pre_bc = sb("pre_bc", [128, D], F32)
        post_bc = sb("post_bc", [128, D], F32)
        ident = sb("ident", [128, 128], BF16)
        uext = sb("uext", [128, 129], F32)
        ones_f = sb("ones_f", [128, 128], F32)
        maskb = sb("maskb", [128, 128], BF16)
        ss = sb("ss", [128, NT], F32)
        rstd = sb("rstd", [128, NT], F32)
        junk = sb("junk", [128, 512], BF16)
        PB = [st.enter_context(nc.psum_tensor("pb%d" % i, [128, 512], F32)) for i in range(8)]
        bPB = [[Buf("pb%d_0" % i, True), Buf("pb%d_1" % i, True)] for i in range(8)]

        b_xdram = [Buf("xd%d" % i) for i in range(NT)]
        b_x1 = [Buf("x1_%d" % i) for i in range(NT)]
        b_outd = [Buf("od%d" % i) for i in range(NT)]
        b_ssi = [Buf() for _ in range(NT)]
        b_rsi = [Buf() for _ in range(NT)]
        b_hT = [Buf("hT%d" % i) for i in range(NT)]
        b_og = [[Buf() for _ in range(4)] for _ in range(8)]
        b_const = Buf("const")
        b_norm = Buf("normbc")
        b_ss = Buf("ss")
        b_rstd = Buf("rstd")
        b_junk = Buf("junk")
        b_xn = [Buf(), Buf()]
        b_wo = Buf("wo")
        b_out = Buf("out")

        S.dma("sp", uext[:], uext_d[:, :], writes=[b_const])
        S.dma("pool", ident[:], ident_d[:, :], writes=[b_const])
        S.dma("pool", maskb[:], uext_d[:, 0:128], writes=[b_const])
        S.op("pool", lambda e: e.memset(ones_f[:], 1.0), writes=[b_const])

        cur_layer = [0]

        def load_norms(layer):
            cur_layer[0] = layer
            S.dma("sp", pre_bc[:], bass.AP(pre_d.tensor, layer * D, [[0, 128], [1, D]]), writes=[b_norm])
            S.dma("sp", post_bc[:], bass.AP(post_d.tensor, layer * D, [[0, 128], [1, D]]), writes=[b_norm])

        def load_post():
            S.dma("sp", post_bc[:], bass.AP(post_d.tensor, cur_layer[0] * D, [[0, 128], [1, D]]), writes=[b_norm])

        def prenorm(src, bsrc):
            xn = [sb_l["tmpf"][k][:].bitcast(BF16) for k in range(2)]
            b_xn = b_l["tmpf"]
            for i in range(NT):
                xb_ = xt[i % 3]
                bx = b_xt[i % 3]
                S.dma("sp", xb_[:], src[i * 128:(i + 1) * 128, :], reads=[bsrc[i]], writes=[bx])
                S.op("act", lambda e: e.activation(out=junk2[:], in_=xb_[:], func=AF.Square, accum_out=ss[:, i:i + 1]),
                     reads=[bx], writes=[b_junk2, b_ssi[i]])
                S.op("act", lambda e: e.activation(out=rstd[:, i:i + 1], in_=ss[:, i:i + 1], func=AF.Ln, scale=1.0 / D, bias=EPS),
                     reads=[b_ssi[i]], writes=[b_rsi[i]])
                S.op("act", lambda e: e.activation(out=rstd[:, i:i + 1], in_=rstd[:, i:i + 1], func=AF.Exp, scale=-0.5),
                     reads=[b_rsi[i]], writes=[b_rsi[i]])
                xb = xn[i % 2]
                bxb = b_xn[i % 2]
                S.op("dve", lambda e: e.scalar_tensor_tensor(out=xb, in0=xb_[:], scalar=rstd[:, i:i + 1],
                                                             in1=pre_bc[:], op0=ALU.mult, op1=ALU.mult),
                     reads=[bx, b_rsi[i], b_norm], writes=[bxb])
                bank = 6 + (i % 2)
                pv = PB[bank][:].bitcast(BF16)
                for kc in range(KC):
                    S.op("pe", lambda e: e.transpose(pv[:, kc * 128:(kc + 1) * 128], xb[:, kc * 128:(kc + 1) * 128],
                                                     ident[:]),
                         reads=[bxb, b_const], writes=bPB[bank], inc=(kc == KC - 1))
                eng = "act" if i % 2 == 0 else "dve"
                srcp = pv.rearrange("p (k t) -> p k t", k=KC)
                dst = hT[:, :, i * 128:(i + 1) * 128]
                if eng == "act":
                    S.op("act", lambda e: e.activation(out=dst, in_=srcp, func=AF.Copy),
                         reads=bPB[bank], writes=[b_hT[i]])
                else:
                    S.op("dve", lambda e: e.tensor_copy(out=dst, in_=srcp),
                         reads=bPB[bank], writes=[b_hT[i]])

        def outproj_residual(src, bsrc, dst, b_dst, wo):
            ss2 = sb_l["ss2"]; rs2 = sb_l["rs2"]; tmpf = sb_l["tmpf"]
            b_s2 = [Buf() for _ in range(NT)]
            b_r2 = [Buf() for _ in range(NT)]
            for i in range(NT):
                xb_ = xt[i % 3]
                bx = b_xt[i % 3]
                S.dma("sp", xb_[:], src[i * 128:(i + 1) * 128, :], reads=[bsrc[i]], writes=[bx])
                banks = (4 + 2 * (i % 2), 5 + 2 * (i % 2))
                for hf in range(2):
                    bk = banks[hf]
                    for p in range(8):
                        S.op("pe", lambda e: e.matmul(PB[bk][:, :], lhsT=og[:, p, i * 128:(i + 1) * 128],
                                                      rhs=wo[:, p, hf * 512:(hf + 1) * 512],
                                                      start=(p == 0), stop=(p == 7)),
                             reads=[b_og[p][i // 4], b_wo], writes=bPB[bk], inc=(p == 7))
                    S.op("act", lambda e: e.activation(out=junk[:, 0:512], in_=PB[bk][:, :], func=AF.Square,
                                                       accum_out=ss2[:, 2 * i + hf:2 * i + hf + 1]),
                         reads=bPB[bk], writes=[b_junk, b_s2[i]])
                S.op("dve", lambda e: e.tensor_tensor(out=rs2[:, i:i + 1], in0=ss2[:, 2 * i:2 * i + 1],
                                                      in1=ss2[:, 2 * i + 1:2 * i + 2], op=ALU.add),
                     reads=[b_s2[i]], writes=[b_r2[i]])
                S.op("act", lambda e: e.activation(out=rs2[:, i:i + 1], in_=rs2[:, i:i + 1], func=AF.Ln,
                                                   scale=1.0 / D, bias=EPS),
                     reads=[b_r2[i]], writes=[b_r2[i]])
                S.op("act", lambda e: e.activation(out=rs2[:, i:i + 1], in_=rs2[:, i:i + 1], func=AF.Exp, scale=-0.5),
                     reads=[b_r2[i]], writes=[b_r2[i]])
                for hf in range(2):
                    bk = banks[hf]
                    tf = tmpf[hf]
                    S.op("dve", lambda e: e.scalar_tensor_tensor(out=tf[:], in0=PB[bk][:, :], scalar=rs2[:, i:i + 1],
                                                                 in1=post_bc[:, hf * 512:(hf + 1) * 512],
                                                                 op0=ALU.mult, op1=ALU.mult),
                         reads=bPB[bk] + [b_r2[i], b_norm], writes=[b_l["tmpf"][hf]])
                    S.op("pool", lambda e: e.tensor_tensor(out=xb_[:, hf * 512:(hf + 1) * 512],
                                                           in0=xb_[:, hf * 512:(hf + 1) * 512], in1=tf[:],
                                                           op=ALU.add),
                         reads=[b_l["tmpf"][hf], bx], writes=[bx])
                S.dma("sp", dst[i * 128:(i + 1) * 128, :], xb_[:], reads=[bx], writes=[b_dst[i]])

        PI = 3.141592653589793
        LAMBDA_INIT = 0.8 - 0.6 * math.exp(-0.3 * 0)

        def layer0(seq, last):
            nonlocal sb_l, b_l
            with contextlib.ExitStack() as st1:
                st2 = st1.enter_context(contextlib.ExitStack())
                cur = [st1]

                def sl(name, shape, dt):
                    uid[0] += 1
                    return cur[0].enter_context(nc.sbuf_tensor("t%d_%s" % (uid[0], name), list(shape), dt))
                sb_l = {}
                b_l = {}
                sb_l["ss2"] = sl("ss2", [128, 2 * NT], F32); b_l["ss2"] = Buf()
                sb_l["rs2"] = sl("rs2", [128, NT], F32); b_l["rs2"] = Buf()
                sb_l["tmpf"] = [sl("tmpf%d" % i, [128, 512], F32) for i in range(2)]; b_l["tmpf"] = [Buf(), Buf()]
                load_norms(0)
                wo = sl("wo", [128, 8, D], BF16)
                S.dma("pool", wo[:], woab_d[:, :, :], writes=[b_wo])
                prenorm(x_d[seq], b_xdram)
                cur[0] = st2
                ropec = sl("ropec", [128, 2], F32)
                lamt = sl("lamt", [128, 4, 64], F32)
                lamp = sl("lamp", [128, 2, 64], F32)
                lams = sl("lams", [128, 2], F32)
                neglam = sl("neglam", [128, 1], F32)
                subcol = sl("subcol", [128, 1], F32)
                ones_b = sl("ones_b", [128, 128], BF16)
                posi = sl("posi", [128, SEQ], I32)
                Ct = sl("Ct", [128, SEQ], F32)
                St = sl("St", [128, SEQ], F32)
                qT = sl("qTa", [128, SEQ], BF16)
                kT = sl("kTa", [128, SEQ], BF16)
                Vh = sl("Vh", [128, NT, 128], BF16)
                wsl = [sl("wa%d" % i, [128, KC, 128], BF16) for i in range(6)]
                pt = [sl("pt%d" % i, [128, 512], BF16) for i in range(3)]
                ta = sb_l["tmpf"][0]; tb = sb_l["tmpf"][1]
                tc = sl("tc", [128, 512], F32); td = sl("td", [128, 512], F32)
                b_ropec, b_lam, b_neglam, b_subcol, b_onesb, b_posi, b_Ct, b_St = [Buf() for _ in range(8)]
                b_qT, b_kT, b_Vh = Buf(), Buf(), Buf()
                b_wsl = [Buf() for _ in range(6)]
                b_pt = [Buf() for _ in range(3)]
                b_ta, b_tb, b_tc, b_td = b_l["tmpf"][0], b_l["tmpf"][1], Buf(), Buf()

                S.dma("sp", ropec[:], rope_d[:, :], writes=[b_ropec])
                S.op("pool", lambda e: e.memset(ones_b[:], 1.0), writes=[b_onesb])
                S.dma("sp", lamt[:], bass.AP(lam_d.tensor, 0, [[0, 128], [64, 4], [1, 64]]), writes=[b_lam])
                S.op("dve", lambda e: e.tensor_tensor(out=lamp[:, 0, :], in0=lamt[:, 0, :], in1=lamt[:, 1, :], op=ALU.mult),
                     reads=[b_lam], writes=[b_lam])
                S.op("dve", lambda e: e.tensor_tensor(out=lamp[:, 1, :], in0=lamt[:, 2, :], in1=lamt[:, 3, :], op=ALU.mult),
                     reads=[b_lam], writes=[b_lam])
                S.op("dve", lambda e: e.tensor_reduce(out=lams[:], in_=lamp[:], axis=AX.X, op=ALU.add),
                     reads=[b_lam], writes=[b_lam])
                S.op("act", lambda e: e.activation(out=lams[:], in_=lams[:], func=AF.Exp), reads=[b_lam], writes=[b_lam])
                S.op("dve", lambda e: e.tensor_tensor(out=neglam[:], in0=lams[:, 1:2], in1=lams[:, 0:1], op=ALU.subtract),
                     reads=[b_lam], writes=[b_neglam])
                S.op("dve", lambda e: e.tensor_scalar(out=neglam[:], in0=neglam[:], scalar1=-LAMBDA_INIT, scalar2=None, op0=ALU.add),
                     reads=[b_neglam], writes=[b_neglam])
                S.dma("sp", subcol[:], subln_d[:, :], writes=[b_subcol])
                S.op("dve", lambda e: e.tensor_scalar(out=subcol[:], in0=subcol[:], scalar1=1.0 - LAMBDA_INIT, scalar2=None, op0=ALU.mult),
                     reads=[b_subcol], writes=[b_subcol])
                S.dma("sp", posi[:], bass.AP(pos_d.tensor, seq * SEQ, [[0, 128], [1, SEQ]]), writes=[b_posi])

                def sin_table(dst, bdst, phase, signed):
                    S.op("dve", lambda e: e.tensor_copy(out=dst[:], in_=posi[:]), reads=[b_posi], writes=[bdst])
                    S.op("dve", lambda e: e.tensor_scalar(out=dst[:], in0=dst[:], scalar1=ropec[:, 0:1], scalar2=phase,
                                                          op0=ALU.mult, op1=ALU.add), reads=[bdst, b_ropec], writes=[bdst])
                    for c4 in range(4):
                        sl_ = slice(c4 * 512, (c4 + 1) * 512)
                        tI = tc[:].bitcast(I32)
                        S.op("dve", lambda e: e.tensor_scalar(out=td[:], in0=dst[:, sl_], scalar1=1.0 / (2 * PI), scalar2=None,
                                                              op0=ALU.mult), reads=[bdst], writes=[b_td])
                        S.op("dve", lambda e: e.tensor_copy(out=tI, in_=td[:]), reads=[b_td], writes=[b_tc])
                        S.op("dve", lambda e: e.tensor_copy(out=td[:], in_=tI), reads=[b_tc], writes=[b_td])
                        S.op("dve", lambda e: e.scalar_tensor_tensor(out=td[:], in0=td[:], scalar=-2 * PI, in1=dst[:, sl_],
                                                                     op0=ALU.mult, op1=ALU.add),
                             reads=[b_td, bdst], writes=[b_td])
                        S.op("dve", lambda e: e.tensor_scalar(out=tc[:], in0=td[:], scalar1=PI, scalar2=-2 * PI,
                                                              op0=ALU.is_gt, op1=ALU.mult), reads=[b_td], writes=[b_tc])
                        S.op("dve", lambda e: e.tensor_tensor(out=td[:], in0=td[:], in1=tc[:], op=ALU.add),
                             reads=[b_td, b_tc], writes=[b_td])
                        S.op("dve", lambda e: e.tensor_scalar(out=td[:], in0=td[:], scalar1=-PI, scalar2=PI,
                                                              op0=ALU.max, op1=ALU.min), reads=[b_td], writes=[b_td])
                        S.op("act", lambda e: e.activation(out=dst[:, sl_], in_=td[:], func=AF.Sin),
                             reads=[b_td], writes=[bdst])
                    if signed:
                        S.op("dve", lambda e: e.tensor_scalar(out=dst[:], in0=dst[:], scalar1=ropec[:, 1:2], scalar2=None,
                                                              op0=ALU.mult), reads=[bdst, b_ropec], writes=[bdst])
                sin_table(Ct, b_Ct, PI / 2, False)
                sin_table(St, b_St, 0.0, True)

                for h in range(4):
                    for i6 in range(6):
                        S.dma("pool", wsl[i6][:], wa_d[h, i6], writes=[b_wsl[i6]])
                    n_ev = 0
                    for (i_w, dstT, bdst) in ((0, qT, b_qT), (2, kT, b_kT)):
                        for t4 in range(4):
                            bks = (6, 7) if n_ev % 2 == 0 else (4, 5)
                            n_ev += 1
                            tsl = slice(t4 * 512, (t4 + 1) * 512)
                            for jj in range(2):
                                for kc in range(KC):
                                    S.op("pe", lambda e: e.matmul(PB[bks[jj]][:, :], lhsT=wsl[i_w + jj][:, kc, :],
                                                                  rhs=hT[:, kc, tsl], start=(kc == 0), stop=(kc == KC - 1)),
                                         reads=[b_wsl[i_w + jj]] + b_hT[4 * t4:4 * t4 + 4], writes=bPB[bks[jj]],
                                         inc=(kc == KC - 1))
                            S.op("dve", lambda e: e.tensor_tensor(out=tc[:], in0=PB[bks[0]][:, :], in1=Ct[:, tsl], op=ALU.mult),
                                 reads=bPB[bks[0]] + [b_Ct], writes=[b_tc])
                            S.op("dve", lambda e: e.tensor_tensor(out=td[:], in0=PB[bks[1]][:, :], in1=St[:, tsl], op=ALU.mult),
                                 reads=bPB[bks[1]] + [b_St], writes=[b_td])
                            S.op("pool", lambda e: e.tensor_tensor(out=dstT[:, tsl], in0=tc[:], in1=td[:], op=ALU.add),
                                 reads=[b_tc, b_td], writes=[bdst])
                    for t4 in range(4):
                        for kc in range(KC):
                            S.op("pe", lambda e: e.matmul(PB[6][:, :], lhsT=wsl[4][:, kc, :],
                                                          rhs=hT[:, kc, t4 * 512:(t4 + 1) * 512],
                                                          start=(kc == 0), stop=(kc == KC - 1)),
                                 reads=[b_wsl[4]] + b_hT[4 * t4:4 * t4 + 4], writes=bPB[6], inc=(kc == KC - 1))
                        S.op("act", lambda e: e.activation(out=pt[0][:], in_=PB[6][:, :], func=AF.Copy),
                             reads=bPB[6], writes=[b_pt[0]])
                        pbf = PB[7][:, :].bitcast(BF16)[:, 0:512].rearrange("p (j c) -> p j c", c=128)
                        for j in range(4):
                            S.op("pe", lambda e: e.transpose(pbf[:, j, :], pt[0][:, j * 128:(j + 1) * 128], ident[:]),
                                 reads=[b_pt[0], b_const], writes=bPB[7], inc=(j == 3))
                        S.op("act", lambda e: e.activation(out=Vh[:, 4 * t4:4 * t4 + 4, :], in_=pbf, func=AF.Copy),
                             reads=bPB[7], writes=[b_Vh])
                    deferred = []
                    jobs = []
                    for Qc in range(4):
                        for kt in range(4 * Qc + 4):
                            for c in range(2):
                                jobs.append((Qc, kt, c))

                    def emit_pv(n):
                        Qc, kt, c = jobs[n]
                        o = max(0, kt - 4 * Qc) * 128
                        N = 512 - o
                        S.op("pe", lambda e: e.matmul(PB[2 + c][:, o:512], lhsT=Vh[:, kt, :], rhs=pt[n % 3][:, 0:N],
                                                      start=(kt == 0), stop=(kt == 4 * Qc + 3)),
                             reads=[b_Vh, b_pt[n % 3]], writes=bPB[2 + c], inc=False)
                        S.op("pe", lambda e: e.matmul(PB[4 + c][:, o:512], lhsT=ones_b[:], rhs=pt[n % 3][:, 0:N],
                                                      start=(kt == 0), stop=(kt == 4 * Qc + 3)),
                             reads=[b_onesb, b_pt[n % 3]], writes=bPB[4 + c])
                        if kt == 4 * Qc + 3 and c == 1:
                            emit_epilogue(Qc)

                    def emit_epilogue(Qc):
                        qsl = slice(Qc * 512, (Qc + 1) * 512)
                        for kc in range(KC):
                            S.op("pe", lambda e: e.matmul(PB[6][:, :], lhsT=wsl[5][:, kc, :], rhs=hT[:, kc, qsl],
                                                          start=(kc == 0), stop=(kc == KC - 1)),
                                 reads=[b_wsl[5]] + b_hT[4 * Qc:4 * Qc + 4], writes=bPB[6], inc=(kc == KC - 1))
                        S.op("act", lambda e: e.activation(out=ta[:], in_=PB[4][:, :], func=AF.Ln), reads=bPB[4], writes=[b_ta])
                        S.op("act", lambda e: e.activation(out=ta[:], in_=ta[:], func=AF.Exp, scale=-1.0), reads=[b_ta], writes=[b_ta])
                        S.op("dve", lambda e: e.tensor_tensor(out=tb[:], in0=PB[2][:, :], in1=ta[:], op=ALU.mult),
                             reads=bPB[2] + [b_ta], writes=[b_tb])
                        S.op("act", lambda e: e.activation(out=ta[:], in_=PB[5][:, :], func=AF.Ln), reads=bPB[5] + [b_ta], writes=[b_ta])
                        S.op("act", lambda e: e.activation(out=ta[:], in_=ta[:], func=AF.Exp, scale=-1.0), reads=[b_ta], writes=[b_ta])
                        S.op("dve", lambda e: e.tensor_tensor(out=tc[:], in0=PB[3][:, :], in1=ta[:], op=ALU.mult),
                             reads=bPB[3] + [b_ta], writes=[b_tc])
                        S.op("dve", lambda e: e.scalar_tensor_tensor(out=tb[:], in0=tc[:], scalar=neglam[:, 0:1], in1=tb[:],
                                                                      op0=ALU.mult, op1=ALU.add),
                             reads=[b_tc, b_tb, b_neglam], writes=[b_tb])
                        S.op("act", lambda e: e.activation(out=tc[:], in_=tb[:], func=AF.Square), reads=[b_tb], writes=[b_tc])
                        deferred.append([4, lambda: epilogue2(Qc, h)])

                    def epilogue2(Qc, h):
                        qsl = slice(Qc * 512, (Qc + 1) * 512)
                        S.op("pe", lambda e: e.matmul(PB[7][:, :], lhsT=ones_f[:], rhs=tc[:], start=True, stop=True),
                             reads=[b_const, b_tc], writes=bPB[7])
                        S.op("act", lambda e: e.activation(out=td[:], in_=PB[7][:, :], func=AF.Ln, scale=1.0 / 128, bias=EPS),
                             reads=bPB[7], writes=[b_td])
                        S.op("act", lambda e: e.activation(out=td[:], in_=td[:], func=AF.Exp, scale=-0.5), reads=[b_td], writes=[b_td])
                        S.op("act", lambda e: e.activation(out=ta[:], in_=PB[6][:, :], func=AF.Exp, scale=-1.0),
                             reads=bPB[6] + [b_ta], writes=[b_ta])
                        S.op("act", lambda e: e.activation(out=ta[:], in_=ta[:], func=AF.Ln, bias=1.0), reads=[b_ta], writes=[b_ta])
                        S.op("act", lambda e: e.activation(out=ta[:], in_=ta[:], func=AF.Exp, scale=-1.0), reads=[b_ta], writes=[b_ta])
                        S.op("dve", lambda e: e.tensor_tensor(out=ta[:], in0=PB[6][:, :], in1=ta[:], op=ALU.mult),
                             reads=bPB[6] + [b_ta], writes=[b_ta])
                        S.op("pool", lambda e: e.tensor_tensor(out=tb[:], in0=tb[:], in1=td[:], op=ALU.mult),
                             reads=[b_tb, b_td], writes=[b_tb])
                        S.op("dve", lambda e: e.scalar_tensor_tensor(out=og[:, h, qsl], in0=tb[:], scalar=subcol[:, 0:1], in1=ta[:],
                                                                     op0=ALU.mult, op1=ALU.mult),
                             reads=[b_tb, b_ta, b_subcol], writes=[b_og[h][Qc]])

                    for n, (Qc, kt, c) in enumerate(jobs):
                        o = max(0, kt - 4 * Qc) * 128
                        N = 512 - o
                        q0 = Qc * 512 + o
                        sbk = n % 2
                        S.op("pe", lambda e: e.matmul(PB[sbk][:, 0:N], lhsT=kT[c * 64:(c + 1) * 64, kt * 128:(kt + 1) * 128],
                                                      rhs=qT[c * 64:(c + 1) * 64, q0:q0 + N], start=True, stop=True),
                             reads=[b_kT, b_qT], writes=bPB[sbk])
                        S.op("act", lambda e: e.activation(out=pt[n % 3][:, 0:N], in_=PB[sbk][:, 0:N], func=AF.Exp, scale=0.125),
                             reads=bPB[sbk], writes=[b_pt[n % 3]])
                        if kt >= 4 * Qc:
                            S.op("pool", lambda e: e.tensor_tensor(out=pt[n % 3][:, 0:128], in0=pt[n % 3][:, 0:128],
                                                                   in1=maskb[:], op=ALU.mult),
                                 reads=[b_pt[n % 3], b_const], writes=[b_pt[n % 3]])
                        if n >= 1:
                            emit_pv(n - 1)
                        for dfr in list(deferred):
                            dfr[0] -= 1
                            if dfr[0] <= 0:
                                deferred.remove(dfr)
                                dfr[1]()
                    emit_pv(len(jobs) - 1)
                    for dfr in list(deferred):
                        deferred.remove(dfr)
                        dfr[1]()
                S.fence()
                st2.close()
                st3 = st1.enter_context(contextlib.ExitStack())
                cur[0] = st3
                partB(seq, sl)
                cur[0] = st1
                if last:
                    outproj_residual(x_d[seq], b_xdram, out_d[seq], b_outd, wo)
                else:
                    outproj_residual(x_d[seq], b_xdram, x1s_d[seq], b_x1, wo)
                S.fence()
                st3.close()

        def run_threads(threads, weights):
            done = set()
            st_ = [{"g": g, "wait": None, "alive": True} for g in threads]
            while any(t["alive"] for t in st_):
                progressed = False
                for t, w in zip(st_, weights):
                    if not t["alive"]:
                        continue
                    for _ in range(w):
                        if t["wait"] is not None:
                            if t["wait"] in done:
                                t["wait"] = None
                            else:
                                break
                        try:
                            r = next(t["g"])
                        except StopIteration:
                            t["alive"] = False
                            progressed = True
                            break
                        progressed = True
                        if isinstance(r, tuple):
                            if r[0] == "wait":
                                if r[1] not in done:
                                    t["wait"] = r[1]
                                    break
                            elif r[0] == "done":
                                done.add(r[1])
                assert progressed, "emission deadlock"

        def partB(seq, sl):
            gm = sl("gm", [128, 6, 128], F32)
            ublk, blk, strictblk, ident_f = gm[:, 0, :], gm[:, 1, :], gm[:, 2, :], gm[:, 5, :]
            halfsel = (gm[:, 3, :], gm[:, 4, :])
            ones_b = sl("ones_bB", [128, 128], BF16)
            wbz = sl("wbz", [128, KC, 512], BF16)
            wba = sl("wba", [128, KC, 8], BF16)
            wbq = [sl("wbq%d" % i, [128, KC, 128], BF16) for i in range(2)]
            cw = sl("cw", [128, 12, 4], F32)
            negA = sl("negA", [128, 4], F32)
            dtb = sl("dtb", [128, 4], F32)
            hn_bc = sl("hn_bc", [128, 128], F32)
            G = {n: sl("g_" + n, [128, NT, 4], F32) for n in ("xa", "g", "beta", "negb", "G", "GL", "eG", "bG", "dG", "eGL0", "eGL1")}
            cr = sl("cr", [128, 12, 3], F32)
            xc = [sl("xc%d" % i, [128, 515], F32) for i in range(2)]
            yc = sl("yc", [128, 512], F32)
            sc = sl("sc", [128, 512], F32)
            t1 = sl("t1", [128, 512], F32)
            sqb = sl("sqb", [128, 512], BF16)
            qnT = [sl("qnT%d" % i, [128, 4, 512], BF16) for i in range(2)]
            knT = [sl("knT%d" % i, [128, 4, 512], BF16) for i in range(2)]
            vsT = [sl("vsT%d" % i, [128, 4, 512], BF16) for i in range(2)]
            gb = sl("gb", [128, 4, 128], F32)
            gU = sl("gU", [128, 4, 128], F32)
            Dm = sl("Dm", [128, 4, 128], F32)
            DTm = sl("DTm", [128, 4, 128], F32)
            Pm1 = sl("Pm", [128, 4, 128], F32)
            Qm1 = sl("Qm", [128, 4, 128], F32)
            TT = sl("TT", [128, 4, 128], F32)
            TTb = sl("TTb", [128, 4, 128], BF16)
            qgT = [sl("qgT%d" % i, [128, 4, 128], BF16) for i in range(2)]
            kbg = [sl("kbg%d" % i, [128, 4, 128], BF16) for i in range(2)]
            kdec = [sl("kdec%d" % i, [128, 4, 128], BF16) for i in range(2)]
            vb = [sl("vb%d" % i, [128, 4, 128], BF16) for i in range(2)]
            qkDT = [sl("qkDT%d" % i, [128, 4, 128], BF16) for i in range(2)]
            wT = [sl("wT%d" % i, [128, 4, 128], BF16) for i in range(2)]
            u_t = [sl("u_t%d" % i, [128, 512], F32) for i in range(2)]
            vnew = sl("vnew", [128, 512], BF16)
            St = sl("Sst", [128, 4, 128], F32)
            Sdec = sl("Sdec", [128, 4, 128], F32)
            Sb = sl("Sb", [128, 4, 128], BF16)
            zs = sl("zs", [128, 512], F32)
            t1r = sl("t1r", [128, 512], F32)
            sqr = sl("sqr", [128, 128], BF16)
            sso = sl("sso", [128, 4], F32)
            ot = sl("ot", [128, 512], BF16)
            B = {n: Buf(n) for n in ("gm", "onesb", "wbz", "wba", "cw", "negA", "dtb", "hn", "gates", "cr", "yc", "sc", "t1", "sqb",
                                    "gb", "gU", "Dm", "DTm", "Pm", "Qm", "TT", "TTb",
                                    "vnew", "S", "Sdec", "Sb", "zs", "t1r", "sqr", "sso", "ot")}
            D2 = {n: [Buf(n + "0"), Buf(n + "1")] for n in ("qnT", "knT", "vsT", "qgT", "kbg", "kdec", "vb", "qkDT", "wT", "u")}
            b_wbq = [Buf(), Buf()]
            b_xc = [Buf(), Buf()]
            I_A, I_B = 0, 1
            P_A, P_B, P_C = 2, 3, 4
            R_A, R_B, R_C = 5, 6, 7

            S.dma("sp", gm[:], gdnc_d[:, :, :], writes=[B["gm"]])
            S.op("pool", lambda e: e.memset(ones_b[:], 1.0), writes=[B["onesb"]])
            S.dma("pool", wbz[:], wbz_d[:, :, :], writes=[B["wbz"]])
            S.dma("pool", wba[:], wba_d[:, :, :], writes=[B["wba"]])
            S.dma("sp", cw[:], cw_d[:, :, :], writes=[B["cw"]])
            S.dma("sp", negA[:], bass.AP(alog_d.tensor, 0, [[0, 128], [1, 4]]), writes=[B["negA"]])
            S.dma("sp", dtb[:], bass.AP(dtb_d.tensor, 0, [[0, 128], [1, 4]]), writes=[B["dtb"]])
            S.dma("sp", hn_bc[:], bass.AP(hn_d.tensor, 0, [[0, 128], [1, 128]]), writes=[B["hn"]])
            S.op("act", lambda e: e.activation(out=negA[:], in_=negA[:], func=AF.Exp), reads=[B["negA"]], writes=[B["negA"]])
            S.op("dve", lambda e: e.tensor_scalar(out=negA[:], in0=negA[:], scalar1=-1.0, scalar2=None, op0=ALU.mult),
                 reads=[B["negA"]], writes=[B["negA"]])
            S.op("pool", lambda e: e.memset(cr[:], 0.0), writes=[B["cr"]])
            S.op("pool", lambda e: e.memset(St[:], 0.0), writes=[B["S"]])
            S.op("pool", lambda e: e.memset(Sb[:], 0.0), writes=[B["Sb"]])

            ba_ps = PB[0][:, 0:NT * 8].rearrange("p (t c) -> p t c", c=8)
            for i in range(NT):
                for kc in range(KC):
                    S.op("pe", lambda e: e.matmul(ba_ps[:, i, :], lhsT=hT[:, kc, i * 128:(i + 1) * 128], rhs=wba[:, kc, :],
                                                  start=(kc == 0), stop=(kc == KC - 1)),
                         reads=[b_hT[i], B["wba"]], writes=bPB[0], inc=(kc == KC - 1))
            bg = [B["gates"]]
            S.op("dve", lambda e: e.tensor_tensor(out=G["xa"][:], in0=ba_ps[:, :, 4:8], in1=dtb[:, None, :].to_broadcast([128, NT, 4]),
                                                  op=ALU.add), reads=bPB[0] + [B["dtb"]], writes=bg)
            S.op("act", lambda e: e.activation(out=G["xa"][:], in_=G["xa"][:], func=AF.Exp), reads=bg, writes=bg)
            S.op("act", lambda e: e.activation(out=G["xa"][:], in_=G["xa"][:], func=AF.Ln, bias=1.0), reads=bg, writes=bg)
            S.op("dve", lambda e: e.tensor_tensor(out=G["g"][:], in0=G["xa"][:], in1=negA[:, None, :].to_broadcast([128, NT, 4]),
                                                  op=ALU.mult), reads=bg + [B["negA"]], writes=bg)
            S.op("act", lambda e: e.activation(out=G["beta"][:], in_=ba_ps[:, :, 0:4], func=AF.Exp, scale=-1.0),
                 reads=bPB[0] + bg, writes=bg)
            S.op("act", lambda e: e.activation(out=G["beta"][:], in_=G["beta"][:], func=AF.Ln, bias=1.0), reads=bg, writes=bg)
            S.op("act", lambda e: e.activation(out=G["beta"][:], in_=G["beta"][:], func=AF.Exp, scale=-1.0), reads=bg, writes=bg)
            S.op("dve", lambda e: e.tensor_scalar(out=G["negb"][:], in0=G["beta"][:], scalar1=-1.0, scalar2=None, op0=ALU.mult),
                 reads=bg, writes=bg)
            gflat = G["g"][:].rearrange("p t c -> p (t c)")
            S.op("pe", lambda e: e.matmul(PB[1][:, 0:64], lhsT=ublk, rhs=gflat, start=True, stop=True),
                 reads=bg + [B["gm"]], writes=bPB[1], inc=False)
            S.op("pe", lambda e: e.matmul(PB[1][:, 64:128], lhsT=blk, rhs=gflat, start=True, stop=True),
                 reads=bg + [B["gm"]], writes=bPB[1], inc=False)
            S.op("pe", lambda e: e.matmul(PB[1][:, 128:192], lhsT=halfsel[0], rhs=gflat, start=True, stop=True),
                 reads=bg + [B["gm"]], writes=bPB[1], inc=False)
            S.op("pe", lambda e: e.matmul(PB[1][:, 192:256], lhsT=halfsel[1], rhs=gflat, start=True, stop=True),
                 reads=bg + [B["gm"]], writes=bPB[1])
            v3 = lambda ap: ap.rearrange("p (t c) -> p t c", c=4)
            S.op("dve", lambda e: e.tensor_copy(out=G["G"][:], in_=v3(PB[1][:, 0:64])), reads=bPB[1] + bg, writes=bg)
            S.op("dve", lambda e: e.tensor_copy(out=G["GL"][:], in_=v3(PB[1][:, 64:128])), reads=bPB[1] + bg, writes=bg)
            S.op("act", lambda e: e.activation(out=G["eGL0"][:], in_=v3(PB[1][:, 128:192]), func=AF.Exp), reads=bPB[1] + bg, writes=bg)
            S.op("act", lambda e: e.activation(out=G["eGL1"][:], in_=v3(PB[1][:, 192:256]), func=AF.Exp), reads=bPB[1] + bg, writes=bg)
            S.op("act", lambda e: e.activation(out=G["eG"][:], in_=G["G"][:], func=AF.Exp), reads=bg, writes=bg)
            S.op("dve", lambda e: e.tensor_tensor(out=G["bG"][:], in0=G["beta"][:], in1=G["eG"][:], op=ALU.mult), reads=bg, writes=bg)
            S.op("dve", lambda e: e.tensor_tensor(out=G["dG"][:], in0=G["GL"][:], in1=G["G"][:], op=ALU.subtract), reads=bg, writes=bg)
            S.op("act", lambda e: e.activation(out=G["dG"][:], in_=G["dG"][:], func=AF.Exp), reads=bg, writes=bg)
            eGLb = (G["eGL0"], G["eGL1"])
            bgr = [Buf("gates_ro")]
            bgr[0].w = B["gates"].w

            bc4 = lambda ap2: ap2[:, :, None].to_broadcast([128, 4, 128])
            hb4 = lambda ap2: ap2[:, None, :].to_broadcast([128, 4, 128])
            h4 = lambda pb: PB[pb][:, :].rearrange("p (h c) -> p h c", c=128)

            def T_I():
                n_w = 0
                for blkI in range(4):
                    if blkI >= 2:
                        yield ("wait", ("Rblk", blkI - 2))
                    par = blkI % 2
                    bsl = slice(blkI * 512, (blkI + 1) * 512)
                    for jc in range(12):
                        wt = wbq[n_w % 2]; bwt = b_wbq[n_w % 2]
                        xcb = xc[n_w % 2]; bxc = b_xc[n_w % 2]
                        n_w += 1
                        S.dma("pool", wt[:], wbq_d[jc], writes=[bwt])
                        for kc in range(KC):
                            S.op("pe", lambda e: e.matmul(PB[I_A][:, :], lhsT=wt[:, kc, :], rhs=hT[:, kc, bsl],
                                                          start=(kc == 0), stop=(kc == KC - 1)),
                                 reads=[bwt] + b_hT[4 * blkI:4 * blkI + 4], writes=bPB[I_A], inc=(kc == KC - 1))
                        yield
                        S.op("pool", lambda e: e.tensor_copy(out=xcb[:, 0:3], in_=cr[:, jc, :]), reads=[B["cr"]], writes=[bxc])
                        S.op("act", lambda e: e.activation(out=xcb[:, 3:515], in_=PB[I_A][:, :], func=AF.Copy),
                             reads=bPB[I_A], writes=[bxc])
                        S.op("pool", lambda e: e.tensor_copy(out=cr[:, jc, :], in_=xcb[:, 512:515]), reads=[bxc], writes=[B["cr"]])
                        yield
                        S.op("dve", lambda e: e.tensor_scalar(out=yc[:], in0=xcb[:, 0:512], scalar1=cw[:, jc, 0:1], scalar2=None,
                                                              op0=ALU.mult), reads=[bxc, B["cw"]], writes=[B["yc"]])
                        for tap in range(1, 4):
                            S.op("dve", lambda e: e.scalar_tensor_tensor(out=yc[:], in0=xcb[:, tap:tap + 512], scalar=cw[:, jc, tap:tap + 1],
                                                                         in1=yc[:], op0=ALU.mult, op1=ALU.add),
                                 reads=[bxc, B["cw"], B["yc"]], writes=[B["yc"]])
                            yield
                        S.op("act", lambda e: e.activation(out=t1[:], in_=yc[:], func=AF.Exp, scale=-1.0), reads=[B["yc"]], writes=[B["t1"]])
                        S.op("act", lambda e: e.activation(out=t1[:], in_=t1[:], func=AF.Ln, bias=1.0), reads=[B["t1"]], writes=[B["t1"]])
                        S.op("act", lambda e: e.activation(out=t1[:], in_=t1[:], func=AF.Exp, scale=-1.0), reads=[B["t1"]], writes=[B["t1"]])
                        yield
                        if jc >= 8:
                            S.op("dve", lambda e: e.tensor_tensor(out=vsT[par][:, jc - 8, :], in0=yc[:], in1=t1[:], op=ALU.mult),
                                 reads=[B["yc"], B["t1"]], writes=[D2["vsT"][par]])
                            yield
                            continue
                        S.op("dve", lambda e: e.tensor_tensor(out=sc[:], in0=yc[:], in1=t1[:], op=ALU.mult),
                             reads=[B["yc"], B["t1"]], writes=[B["sc"]])
                        S.op("act", lambda e: e.activation(out=sqb[:], in_=sc[:], func=AF.Square), reads=[B["sc"]], writes=[B["sqb"]])
                        yield
                        S.op("pe", lambda e: e.matmul(PB[I_B][:, :], lhsT=ones_b[:], rhs=sqb[:], start=True, stop=True),
                             reads=[B["onesb"], B["sqb"]], writes=bPB[I_B])
                        S.op("act", lambda e: e.activation(out=t1[:], in_=PB[I_B][:, :], func=AF.Ln, bias=EPS),
                             reads=bPB[I_B] + [B["t1"]], writes=[B["t1"]])
                        isq = jc < 4
                        S.op("act", lambda e: e.activation(out=t1[:], in_=t1[:], func=AF.Exp, scale=-0.5,
                                                           bias=(-0.5 * math.log(128.0) if isq else 0.0)),
                             reads=[B["t1"]], writes=[B["t1"]])
                        yield
                        dstT = qnT[par] if isq else knT[par]
                        bd = D2["qnT"][par] if isq else D2["knT"][par]
                        S.op("dve", lambda e: e.tensor_tensor(out=dstT[:, jc % 4, :], in0=sc[:], in1=t1[:], op=ALU.mult),
                             reads=[B["sc"], B["t1"]], writes=[bd])
                        yield
                    yield ("done", ("I", blkI))

            def T_P():
                for i in range(NT):
                    blkI, tl = divmod(i, 4)
                    par = blkI % 2
                    tp = i % 2
                    yield ("wait", ("I", blkI))
                    if i >= 2:
                        yield ("wait", ("R", i - 2))
                    csl = slice(tl * 128, (tl + 1) * 128)
                    qn, kn, vs = qnT[par], knT[par], vsT[par]
                    bqn, bkn, bvs = D2["qnT"][par], D2["knT"][par], D2["vsT"][par]
                    pa, pb_, pc = h4(P_A), h4(P_B), h4(P_C)
                    S.op("dve", lambda e: e.tensor_copy(out=gb[:], in_=bc4(G["g"][:, i, :])), reads=bgr, writes=[B["gb"]])
                    for h in range(4):
                        S.op("pe", lambda e: e.matmul(pa[:, h, :], lhsT=gb[:, h, :], rhs=ublk, start=True, stop=True),
                             reads=[B["gb"], B["gm"]], writes=bPB[P_A], inc=(h == 3))
                    yield
                    S.op("act", lambda e: e.activation(out=Dm[:], in_=pa, func=AF.Exp), reads=bPB[P_A], writes=[B["Dm"]])
                    S.op("dve", lambda e: e.tensor_tensor(out=qgT[tp][:], in0=qn[:, :, csl], in1=Dm[:], op=ALU.mult),
                         reads=[bqn, B["Dm"]], writes=[D2["qgT"][tp]])
                    yield
                    p3b = PB[P_B][:, :].bitcast(BF16).rearrange("p (a h c) -> p a h c", a=2, h=4)
                    for h in range(4):
                        S.op("pe", lambda e: e.transpose(p3b[:, 0, h, :], kn[:, h, csl], ident[:]),
                             reads=[bkn, b_const], writes=bPB[P_B], inc=False)
                    for h in range(4):
                        S.op("pe", lambda e: e.transpose(p3b[:, 1, h, :], vs[:, h, csl], ident[:]),
                             reads=[bvs, b_const], writes=bPB[P_B], inc=(h == 3))
                    yield
                    S.op("dve", lambda e: e.tensor_tensor(out=kbg[tp][:], in0=p3b[:, 0], in1=bc4(G["bG"][:, i, :]), op=ALU.mult),
                         reads=bPB[P_B] + bgr, writes=[D2["kbg"][tp]])
                    S.op("dve", lambda e: e.tensor_tensor(out=kdec[tp][:], in0=p3b[:, 0], in1=bc4(G["dG"][:, i, :]), op=ALU.mult),
                         reads=bPB[P_B] + bgr, writes=[D2["kdec"][tp]])
                    yield
                    S.op("dve", lambda e: e.tensor_tensor(out=vb[tp][:], in0=p3b[:, 1], in1=bc4(G["beta"][:, i, :]), op=ALU.mult),
                         reads=bPB[P_B] + bgr, writes=[D2["vb"][tp]])
                    S.op("pool", lambda e: e.tensor_tensor(out=gU[:], in0=hb4(ublk), in1=bc4(G["g"][:, i, :]), op=ALU.mult),
                         reads=bgr + [B["gm"]], writes=[B["gU"]])
                    yield
                    for h in range(4):
                        S.op("pe", lambda e: e.matmul(pa[:, h, :], lhsT=gU[:, h, :], rhs=strictblk, start=True, stop=True),
                             reads=[B["gU"], B["gm"]], writes=bPB[P_A], inc=(h == 3))
                    for h in range(4):
                        S.op("pe", lambda e: e.matmul(pb_[:, h, :], lhsT=strictblk, rhs=gU[:, h, :], start=True, stop=True),
                             reads=[B["gU"], B["gm"]], writes=bPB[P_B], inc=(h == 3))
                    yield
                    S.op("act", lambda e: e.activation(out=Dm[:], in_=pa, func=AF.Exp), reads=bPB[P_A] + [B["Dm"]], writes=[B["Dm"]])
                    S.op("act", lambda e: e.activation(out=DTm[:], in_=pb_, func=AF.Exp), reads=bPB[P_B], writes=[B["DTm"]])
                    yield
                    for h in range(4):
                        S.op("pe", lambda e: e.matmul(pa[:, h, :], lhsT=kn[:, h, csl], rhs=kn[:, h, csl], start=True, stop=True),
                             reads=[bkn], writes=bPB[P_A], inc=(h == 3))
                    for h in range(4):
                        S.op("pe", lambda e: e.matmul(pb_[:, h, :], lhsT=kn[:, h, csl], rhs=qn[:, h, csl], start=True, stop=True),
                             reads=[bkn, bqn], writes=bPB[P_B], inc=(h == 3))
                    yield
                    S.op("pool", lambda e: e.tensor_tensor(out=Dm[:], in0=Dm[:], in1=hb4(strictblk), op=ALU.mult),
                         reads=[B["Dm"], B["gm"]], writes=[B["Dm"]])
                    S.op("pool", lambda e: e.tensor_tensor(out=Dm[:], in0=Dm[:], in1=bc4(G["negb"][:, i, :]), op=ALU.mult),
                         reads=[B["Dm"]] + bgr, writes=[B["Dm"]])
                    yield
                    S.op("dve", lambda e: e.tensor_tensor(out=Pm1[:], in0=pa, in1=Dm[:], op=ALU.mult),
                         reads=bPB[P_A] + [B["Dm"]], writes=[B["Pm"]])
                    S.op("pool", lambda e: e.tensor_tensor(out=DTm[:], in0=DTm[:], in1=hb4(ublk), op=ALU.mult),
                         reads=[B["DTm"], B["gm"]], writes=[B["DTm"]])
                    yield
                    S.op("dve", lambda e: e.tensor_tensor(out=qkDT[tp][:], in0=pb_, in1=DTm[:], op=ALU.mult),
                         reads=bPB[P_B] + [B["DTm"]], writes=[D2["qkDT"][tp]])
                    for h in range(4):
                        S.op("pe", lambda e: e.matmul(pc[:, h, :], lhsT=Pm1[:, h, :], rhs=ident_f, start=True, stop=True),
                             reads=[B["Pm"], B["gm"]], writes=bPB[P_C], inc=(h == 3))
                    yield
                    S.op("act", lambda e: e.activation(out=Qm1[:], in_=pc, func=AF.Copy), reads=bPB[P_C], writes=[B["Qm"]])
                    S.op("dve", lambda e: e.tensor_tensor(out=TT[:], in0=pc, in1=hb4(ident_f), op=ALU.add),
                         reads=bPB[P_C] + [B["gm"]], writes=[B["TT"]])
                    yield
                    for lvl in range(5):
                        for h in range(4):
                            S.op("pe", lambda e: e.matmul(pa[:, h, :], lhsT=Qm1[:, h, :], rhs=Pm1[:, h, :], start=True, stop=True),
                                 reads=[B["Qm"], B["Pm"]], writes=bPB[P_A], inc=(h == 3))
                        if lvl < 4:
                            for h in range(4):
                                S.op("pe", lambda e: e.matmul(pb_[:, h, :], lhsT=Pm1[:, h, :], rhs=Qm1[:, h, :], start=True, stop=True),
                                     reads=[B["Qm"], B["Pm"]], writes=bPB[P_B], inc=(h == 3))
                        yield
                        S.op("act", lambda e: e.activation(out=Pm1[:], in_=pa, func=AF.Copy), reads=bPB[P_A], writes=[B["Pm"]])
                        if lvl < 4:
                            S.op("dve", lambda e: e.tensor_copy(out=Qm1[:], in_=pb_), reads=bPB[P_B], writes=[B["Qm"]])
                        yield
                        for h in range(4):
                            S.op("pe", lambda e: e.matmul(pc[:, h, :], lhsT=Pm1[:, h, :], rhs=TT[:, h, :], start=True, stop=True),
                                 reads=[B["Pm"], B["TT"]], writes=bPB[P_C], inc=(h == 3))
                        yield
                        S.op("dve", lambda e: e.tensor_tensor(out=TT[:], in0=pc, in1=TT[:], op=ALU.add),
                             reads=bPB[P_C] + [B["TT"]], writes=[B["TT"]])
                        yield
                    S.op("act", lambda e: e.activation(out=TTb[:], in_=TT[:], func=AF.Copy), reads=[B["TT"]], writes=[B["TTb"]])
                    yield
                    for h in range(4):
                        S.op("pe", lambda e: e.matmul(pa[:, h, :], lhsT=TTb[:, h, :], rhs=vb[tp][:, h, :], start=True, stop=True),
                             reads=[B["TTb"], D2["vb"][tp]], writes=bPB[P_A], inc=(h == 3))
                    for h in range(4):
                        S.op("pe", lambda e: e.matmul(pb_[:, h, :], lhsT=kbg[tp][:, h, :], rhs=TTb[:, h, :], start=True, stop=True),
                             reads=[B["TTb"], D2["kbg"][tp]], writes=bPB[P_B], inc=(h == 3))
                    yield
                    S.op("act", lambda e: e.activation(out=u_t[tp][:], in_=PB[P_A][:, :], func=AF.Copy), reads=bPB[P_A], writes=[D2["u"][tp]])
                    S.op("dve", lambda e: e.tensor_copy(out=wT[tp][:], in_=pb_), reads=bPB[P_B], writes=[D2["wT"][tp]])
                    yield ("done", ("P", i))

            def T_R():
                for i in range(NT):
                    tp = i % 2
                    yield ("wait", ("P", i))
                    for ch in range(2):
                        pr = slice(64 * ch, 64 * ch + 64)
                        pc_ = slice(64 * ch, 64 * ch + 64)
                        S.op("pool", lambda e: e.tensor_tensor(out=Sdec[:], in0=St[:], in1=bc4(eGLb[ch][:, i, :]), op=ALU.mult),
                             reads=[B["S"]] + bgr, writes=[B["Sdec"]])
                        for h in range(4):
                            S.op("pe", lambda e: e.matmul(PB[R_A][pr, h * 128:(h + 1) * 128], lhsT=wT[tp][:, h, pc_], rhs=Sb[:, h, :],
                                                          start=True, stop=True),
                                 reads=[D2["wT"][tp], B["Sb"]], writes=bPB[R_A], inc=(h == 3))
                        yield
                        S.op("dve", lambda e: e.tensor_tensor(out=vnew[pr, :], in0=u_t[tp][pr, :], in1=PB[R_A][pr, :], op=ALU.subtract),
                             reads=bPB[R_A] + [D2["u"][tp]], writes=[B["vnew"]])
                        yield
                        for h in range(4):
                            S.op("pe", lambda e: e.matmul(PB[R_B][pr, h * 128:(h + 1) * 128], lhsT=qgT[tp][:, h, pc_], rhs=Sb[:, h, :],
                                                          start=True, stop=False),
                                 reads=[D2["qgT"][tp], B["Sb"]], writes=bPB[R_B], inc=False)
                            S.op("pe", lambda e: e.matmul(PB[R_B][pr, h * 128:(h + 1) * 128], lhsT=qkDT[tp][pr, h, pc_],
                                                          rhs=vnew[pr, h * 128:(h + 1) * 128], start=False, stop=True),
                                 reads=[D2["qkDT"][tp], B["vnew"]], writes=bPB[R_B], inc=(h == 3))
                        yield
                        for h in range(4):
                            S.op("pe", lambda e: e.matmul(PB[R_C][:, h * 128:(h + 1) * 128], lhsT=kdec[tp][pr, h, :],
                                                          rhs=vnew[pr, h * 128:(h + 1) * 128], start=True, stop=True),
                                 reads=[D2["kdec"][tp], B["vnew"]], writes=bPB[R_C], inc=(h == 3))
                        yield
                        Sf = St[:].rearrange("p h c -> p (h c)")
                        Sdf = Sdec[:].rearrange("p h c -> p (h c)")
                        Sbf = Sb[:].rearrange("p h c -> p (h c)")
                        S.op("dve", lambda e: e.tensor_tensor(out=Sbf, in0=PB[R_C][:, :], in1=Sdf, op=ALU.add),
                             reads=bPB[R_C] + [B["Sdec"]], writes=[B["Sb"]])
                        S.op("dve", lambda e: e.tensor_tensor(out=Sf, in0=PB[R_C][:, :], in1=Sdf, op=ALU.add),
                             reads=bPB[R_C] + [B["Sdec"]], writes=[B["S"]])
                        yield
                    for kc in range(KC):
                        S.op("pe", lambda e: e.matmul(PB[R_A][:, :], lhsT=hT[:, kc, i * 128:(i + 1) * 128], rhs=wbz[:, kc, :],
                                                      start=(kc == 0), stop=(kc == KC - 1)),
                             reads=[b_hT[i], B["wbz"]], writes=bPB[R_A], inc=(kc == KC - 1))
                    yield
                    S.op("act", lambda e: e.activation(out=zs[:], in_=PB[R_A][:, :], func=AF.Exp, scale=-1.0), reads=bPB[R_A], writes=[B["zs"]])
                    S.op("act", lambda e: e.activation(out=zs[:], in_=zs[:], func=AF.Ln, bias=1.0), reads=[B["zs"]], writes=[B["zs"]])
                    S.op("act", lambda e: e.activation(out=zs[:], in_=zs[:], func=AF.Exp, scale=-1.0), reads=[B["zs"]], writes=[B["zs"]])
                    yield
                    S.op("dve", lambda e: e.tensor_tensor(out=zs[:], in0=PB[R_A][:, :], in1=zs[:], op=ALU.mult),
                         reads=bPB[R_A] + [B["zs"]], writes=[B["zs"]])
                    zs3 = zs[:].rearrange("p (h c) -> p h c", c=128)
                    S.op("pool", lambda e: e.tensor_tensor(out=zs3, in0=zs3, in1=hb4(hn_bc[:]), op=ALU.mult),
                         reads=[B["zs"], B["hn"]], writes=[B["zs"]])
                    yield
                    for h in range(4):
                        S.op("act", lambda e: e.activation(out=sqr[:], in_=PB[R_B][:, h * 128:(h + 1) * 128], func=AF.Square,
                                                           accum_out=sso[:, h:h + 1]),
                             reads=bPB[R_B], writes=[B["sqr"], B["sso"]])
                    S.op("act", lambda e: e.activation(out=sso[:], in_=sso[:], func=AF.Ln, scale=1.0 / 128, bias=EPS),
                         reads=[B["sso"]], writes=[B["sso"]])
                    S.op("act", lambda e: e.activation(out=sso[:], in_=sso[:], func=AF.Exp, scale=-0.5), reads=[B["sso"]], writes=[B["sso"]])
                    yield
                    t13 = t1r[:].rearrange("p (h c) -> p h c", c=128)
                    S.op("dve", lambda e: e.tensor_tensor(out=t13, in0=h4(R_B), in1=bc4(sso[:]), op=ALU.mult),
                         reads=bPB[R_B] + [B["sso"], B["t1r"]], writes=[B["t1r"]])
                    S.op("dve", lambda e: e.tensor_tensor(out=ot[:], in0=t1r[:], in1=zs[:], op=ALU.mult),
                         reads=[B["t1r"], B["zs"]], writes=[B["ot"]])
                    yield
                    p3o = PB[R_C][:, :].bitcast(BF16)[:, 0:512].rearrange("p (h c) -> p h c", c=128)
                    for h in range(4):
                        S.op("pe", lambda e: e.transpose(p3o[:, h, :], ot[:, h * 128:(h + 1) * 128], ident[:]),
                             reads=[B["ot"], b_const], writes=bPB[R_C], inc=(h == 3))
                    S.op("act", lambda e: e.activation(out=og[:, 4:8, i * 128:(i + 1) * 128], in_=p3o, func=AF.Copy),
                         reads=bPB[R_C], writes=[b_og[4 + hh][i // 4] for hh in range(4)])
                    yield ("done", ("R", i))
                    if i % 4 == 3:
                        yield ("done", ("Rblk", i // 4))

            run_threads([T_I(), T_P(), T_R()], BW)

        def layer1(seq, last):
            nonlocal sb_l, b_l
            with contextlib.ExitStack() as st1:
                st2 = st1.enter_context(contextlib.ExitStack())
                cur = [st1]

                def sl(name, shape, dt):
                    uid[0] += 1
                    return cur[0].enter_context(nc.sbuf_tensor("t%d_%s" % (uid[0], name), list(shape), dt))
                sb_l = {}
                b_l = {}
                sb_l["ss2"] = sl("ss2", [128, 2 * NT], F32); b_l["ss2"] = Buf()
                sb_l["rs2"] = sl("rs2", [128, NT], F32); b_l["rs2"] = Buf()
                sb_l["tmpf"] = [sl("tmpf%d" % i, [128, 512], F32) for i in range(2)]; b_l["tmpf"] = [Buf(), Buf()]
                wo = sl("wo", [128, 8, D], BF16)
                cur[0] = st2
                qT = [[sl("qT%d_%d" % (s_, i), [128, SEQ], BF16) for i in range(2)] for s_ in range(2)]
                kT = [[sl("kT%d_%d" % (s_, i), [128, SEQ], BF16) for i in range(2)] for s_ in range(2)]
                Vx = [[sl("Vx%d_%d" % (s_, i), [128, NT, 128], BF16) for i in range(2)] for s_ in range(2)]
                vT5 = sl("vT5", [128, 512], BF16)
                b_vT5 = Buf()
                wq = sl("wq", [128, KC, 128], BF16)
                wk = sl("wk", [128, KC, 128], BF16)
                wv = sl("wv", [128, KC, 128], BF16)
                wz = [sl("wz%d" % i, [128, KC, 128], BF16) for i in range(2)]
                wf = sl("wf", [128, KC, 16], BF16)
                fb_bc = sl("fb_bc", [128, 16], F32)
                flb = sl("flb", [128, NT, 16], F32)
                nlf = sl("nlf", [128, NT, 16], F32)
                NC_ = sl("NC", [128, NT, 16], F32)
                carry = sl("carry", [128, 16], F32)
                carryT = sl("carryT", [16, 1], F32)
                cT = sl("cT", [16, SEQ], F32)
                cHL = sl("cHL", [16, 2, SEQ], BF16)
                pt = [sl("pt%d" % i, [128, 512], BF16) for i in range(3)]
                e_t = sb_l["tmpf"][0]
                sums = sb_l["tmpf"][1]
                den = sl("den", [128, 512], F32)
                tt = den
                b_qT = [[Buf(), Buf()] for _ in range(2)]; b_kT = [[Buf(), Buf()] for _ in range(2)]
                b_Vx = [[Buf(), Buf()] for _ in range(2)]
                b_qaug = [[Buf(), Buf()] for _ in range(2)]
                b_wq, b_wk, b_wv = Buf(), Buf(), Buf()
                b_wz = [Buf(), Buf()]
                b_wf, b_fb, b_flb, b_nlf, b_NC, b_carry, b_carryT, b_cT, b_cTt, b_cHL = [Buf() for _ in range(10)]
                b_pt = [Buf() for _ in range(3)]
                b_e, b_sums, b_den = b_l["tmpf"][0], b_l["tmpf"][1], Buf()
                b_tt = b_den

                try:
                    load_norms(1)
                    S.dma("pool", wo[:], woc_d[:, :, :], writes=[b_wo])
                    S.dma("pool", wf[:], wf_d[:, :, :], writes=[b_wf])
                    S.dma("sp", fb_bc[:], bass.AP(fb_d.tensor, 0, [[0, 128], [1, 16]]), writes=[b_fb])
                    for s_ in range(2):
                        for i in range(2):
                            S.op("pool", lambda e: e.memset(kT[s_][i][64:66, :], 1.0), writes=[b_kT[s_][i]])
                        S.op("pool", lambda e: e.memset(Vx[s_][0][:, :, 64:128], 1.0), writes=[b_Vx[s_][0]])
                        S.op("pool", lambda e: e.memset(Vx[s_][1][:, :, 0:64], 1.0), writes=[b_Vx[s_][1]])
                    S.op("pool", lambda e: e.memset(carry[:], 0.0), writes=[b_carry])
                    S.op("pool", lambda e: e.memset(carryT[:], 0.0), writes=[b_carryT])

                    l1src = x1s_d[seq] if 0 in layers else x_d[seq]
                    l1b = b_x1 if 0 in layers else b_xdram
                    prenorm(l1src, l1b)
                    if STAGE <= 1:
                        raise StopStage()

                    fl_ps = PB[0][:, 0:NT * 16].rearrange("p (t h) -> p t h", h=16)
                    for i in range(NT):
                        for kc in range(KC):
                            S.op("pe", lambda e: e.matmul(fl_ps[:, i, :], lhsT=hT[:, kc, i * 128:(i + 1) * 128],
                                                          rhs=wf[:, kc, :], start=(kc == 0), stop=(kc == KC - 1)),
                                 reads=[b_hT[i], b_wf], writes=bPB[0], inc=(kc == KC - 1))
                    S.op("dve", lambda e: e.tensor_tensor(out=flb[:], in0=fl_ps, in1=fb_bc[:, None, :].to_broadcast([128, NT, 16]),
                                                          op=ALU.add),
                         reads=bPB[0] + [b_fb], writes=[b_flb])
                    S.op("act", lambda e: e.activation(out=flb[:], in_=flb[:], func=AF.Exp, scale=-1.0),
                         reads=[b_flb], writes=[b_flb])
                    S.op("act", lambda e: e.activation(out=nlf[:], in_=flb[:], func=AF.Ln, bias=1.0),
                         reads=[b_flb], writes=[b_nlf])
                    for i in range(NT):
                        bk = 1 + (i % 2)
                        c1 = PB[bk][:, 0:16]
                        c2 = PB[bk][:, 16:32]
                        c3 = PB[bk][0:16, 32:32 + 129]
                        S.op("pe", lambda e: e.matmul(c1, lhsT=uext[:, 0:128], rhs=nlf[:, i, :], start=True, stop=True),
                             reads=[b_nlf, b_const], writes=bPB[bk], inc=False)
                        S.op("pe", lambda e: e.matmul(c2, lhsT=ones_f[:], rhs=nlf[:, i, :], start=True, stop=True),
                             reads=[b_nlf, b_const], writes=bPB[bk], inc=False)
                        S.op("pe", lambda e: e.matmul(c3, lhsT=nlf[:, i, :], rhs=uext[:, :], start=True, stop=True),
                             reads=[b_nlf, b_const], writes=bPB[bk])
                        S.op("dve", lambda e: e.tensor_tensor(out=NC_[:, i, :], in0=c1, in1=carry[:], op=ALU.add),
                             reads=bPB[bk] + [b_carry], writes=[b_NC])
                        S.op("dve", lambda e: e.tensor_tensor(out=carry[:], in0=c2, in1=carry[:], op=ALU.add),
                             reads=bPB[bk] + [b_carry], writes=[b_carry])
                        S.op("dve", lambda e: e.tensor_scalar(out=cT[:, i * 128:(i + 1) * 128], in0=c3[:, 0:128],
                                                              scalar1=carryT[:, 0:1], scalar2=-8.0,
                                                              op0=ALU.add, op1=ALU.mult),
                             reads=bPB[bk] + [b_carryT], writes=[b_cT])
                        S.op("dve", lambda e: e.tensor_tensor(out=carryT[:], in0=c3[:, 128:129], in1=carryT[:], op=ALU.add),
                             reads=bPB[bk] + [b_carryT], writes=[b_carryT])
                    S.op("dve", lambda e: e.tensor_copy(out=cHL[:, 0, :], in_=cT[:]), reads=[b_cT], writes=[b_cHL])
                    S.op("dve", lambda e: e.tensor_tensor(out=cT[:], in0=cT[:], in1=cHL[:, 0, :], op=ALU.subtract),
                         reads=[b_cT, b_cHL], writes=[b_cT])
                    S.op("dve", lambda e: e.tensor_copy(out=cHL[:, 1, :], in_=cT[:]), reads=[b_cT, b_cHL], writes=[b_cHL])

                    if STAGE <= 2:
                        raise StopStage()
                    IB = 7

                    def T_in():
                        for p in range(8):
                            if p >= 2:
                                yield ("wait", ("att", p - 2))
                            sp_ = p % 2
                            qTp, kTp, Vxp = qT[sp_], kT[sp_], Vx[sp_]
                            bq, bk_, bV, bqa = b_qT[sp_], b_kT[sp_], b_Vx[sp_], b_qaug[sp_]
                            S.dma("pool", wq[:], wc_d[p, 0], writes=[b_wq])
                            S.dma("pool", wk[:], wc_d[p, 1], writes=[b_wk])
                            S.dma("pool", wv[:], wc_d[p, 2], writes=[b_wv])
                            S.dma("pool", wz[p % 2][:], wc_d[p, 3], writes=[b_wz[p % 2]])
                            for hh in range(2):
                                for r in range(2):
                                    S.dma("sp", qTp[hh][64 + r:65 + r, :], cHL[2 * p + hh:2 * p + hh + 1, r, :],
                                          reads=[b_cHL], writes=[bqa[hh]])
                            yield
                            for (wt, bw, dstT, bdst) in ((wq, b_wq, qTp, bq), (wk, b_wk, kTp, bk_)):
                                for t4 in range(4):
                                    for kc in range(KC):
                                        S.op("pe", lambda e: e.matmul(PB[IB][:, :], lhsT=wt[:, kc, :],
                                                                      rhs=hT[:, kc, t4 * 512:(t4 + 1) * 512],
                                                                      start=(kc == 0), stop=(kc == KC - 1)),
                                             reads=[bw] + b_hT[4 * t4:4 * t4 + 4], writes=bPB[IB], inc=(kc == KC - 1))
                                    yield
                                    S.op("dve", lambda e: e.tensor_copy(out=dstT[0][0:64, t4 * 512:(t4 + 1) * 512],
                                                                        in_=PB[IB][0:64, :]),
                                         reads=bPB[IB][0:1], writes=[bdst[0]])
                                    S.op("dve", lambda e: e.tensor_copy(out=dstT[1][0:64, t4 * 512:(t4 + 1) * 512],
                                                                        in_=PB[IB][64:128, :]),
                                         reads=bPB[IB][1:2], writes=[bdst[1]])
                                    yield
                            for t4 in range(4):
                                for kc in range(KC):
                                    S.op("pe", lambda e: e.matmul(PB[IB][:, :], lhsT=wv[:, kc, :],
                                                                  rhs=hT[:, kc, t4 * 512:(t4 + 1) * 512],
                                                                  start=(kc == 0), stop=(kc == KC - 1)),
                                         reads=[b_wv] + b_hT[4 * t4:4 * t4 + 4], writes=bPB[IB], inc=(kc == KC - 1))
                                yield
                                S.op("dve", lambda e: e.tensor_copy(out=vT5[:], in_=PB[IB][:, :]), reads=bPB[IB], writes=[b_vT5])
                                yield
                                pbf = PB[IB][:, :].bitcast(BF16)[:, 0:512].rearrange("p (j c) -> p j c", c=128)
                                for j in range(4):
                                    S.op("pe", lambda e: e.transpose(pbf[:, j, :], vT5[:, j * 128:(j + 1) * 128], ident[:]),
                                         reads=[b_vT5, b_const], writes=bPB[IB], inc=(j == 3))
                                yield
                                S.op("dve", lambda e: e.tensor_copy(out=Vxp[0][:, 4 * t4:4 * t4 + 4, 0:64], in_=pbf[:, :, 0:64]),
                                     reads=bPB[IB], writes=[bV[0]])
                                S.op("dve", lambda e: e.tensor_copy(out=Vxp[1][:, 4 * t4:4 * t4 + 4, 64:128], in_=pbf[:, :, 64:128]),
                                     reads=bPB[IB], writes=[bV[1]])
                                yield
                            yield ("done", ("in", p))

                    def T_att():
                        deferred = []
                        for p in range(8):
                            yield ("wait", ("in", p))
                            sp_ = p % 2
                            qTp, kTp, Vxp = qT[sp_], kT[sp_], Vx[sp_]
                            bq, bk_, bV, bqa = b_qT[sp_], b_kT[sp_], b_Vx[sp_], b_qaug[sp_]
                            jobs = []
                            for Qc in range(4):
                                for kt in range(4 * Qc + 4):
                                    for hh in range(2):
                                        jobs.append((Qc, kt, hh))

                            def emit_pv(n):
                                Qc, kt, hh = jobs[n]
                                o = max(0, kt - 4 * Qc) * 128
                                N = 512 - o
                                abk = 2 + 2 * (Qc % 2) + hh
                                S.op("pe", lambda e: e.matmul(PB[abk][:, o:512], lhsT=Vxp[hh][:, kt, :], rhs=pt[n % 3][:, 0:N],
                                                              start=(kt == 0), stop=(kt == 4 * Qc + 3)),
                                     reads=[bV[hh], b_pt[n % 3]], writes=bPB[abk])
                                if kt == 4 * Qc + 3 and hh == 1:
                                    emit_epilogue(Qc)

                            def emit_epilogue(Qc, p=p):
                                zb = 6
                                a0 = 2 + 2 * (Qc % 2)
                                a1 = a0 + 1
                                qs = slice(Qc * 512, (Qc + 1) * 512)
                                for kc in range(KC):
                                    S.op("pe", lambda e: e.matmul(PB[zb][:, :], lhsT=wz[p % 2][:, kc, :], rhs=hT[:, kc, qs],
                                                                  start=(kc == 0), stop=(kc == KC - 1)),
                                         reads=[b_wz[p % 2]] + b_hT[4 * Qc:4 * Qc + 4], writes=bPB[zb], inc=(kc == KC - 1))

                                def s1():
                                    S.op("act", lambda e: e.activation(out=e_t[:], in_=PB[zb][:, :], func=AF.Exp, scale=-1.0),
                                         reads=bPB[zb], writes=[b_e])
                                    S.op("dve", lambda e: e.tensor_copy(out=sums[0:64, :], in_=PB[a0][64:128, :]),
                                         reads=bPB[a0][1:2], writes=[b_sums])
                                    S.op("dve", lambda e: e.tensor_copy(out=sums[64:128, :], in_=PB[a1][0:64, :]),
                                         reads=bPB[a1][0:1], writes=[b_sums])

                                def s2():
                                    S.op("dve", lambda e: e.scalar_tensor_tensor(out=den[:], in0=e_t[:], scalar=1.0, in1=sums[:],
                                                                                 op0=ALU.add, op1=ALU.mult),
                                         reads=[b_e, b_sums], writes=[b_den])

                                def s3():
                                    S.op("act", lambda e: e.activation(out=den[:], in_=den[:], func=AF.Ln), reads=[b_den], writes=[b_den])
                                    S.op("act", lambda e: e.activation(out=den[:], in_=den[:], func=AF.Exp, scale=-1.0),
                                         reads=[b_den], writes=[b_den])

                                def s4():
                                    S.op("dve", lambda e: e.tensor_tensor(out=tt[:], in0=PB[zb][:, :], in1=den[:], op=ALU.mult),
                                         reads=bPB[zb] + [b_den], writes=[b_tt])
                                    S.op("dve", lambda e: e.tensor_tensor(out=og[0:64, p, qs], in0=PB[a0][0:64, :], in1=tt[0:64, :],
                                                                          op=ALU.mult),
                                         reads=bPB[a0][0:1] + [b_tt], writes=[b_og[p][Qc]])
                                    S.op("dve", lambda e: e.tensor_tensor(out=og[64:128, p, qs], in0=PB[a1][64:128, :], in1=tt[64:128, :],
                                                                          op=ALU.mult),
                                         reads=bPB[a1][1:2] + [b_tt], writes=[b_og[p][Qc]])
                                for dl, fn in ((2, s1), (4, s2), (6, s3), (8, s4)):
                                    deferred.append([dl, fn])

                            for n, (Qc, kt, hh) in enumerate(jobs):
                                o = max(0, kt - 4 * Qc) * 128
                                N = 512 - o
                                q0 = Qc * 512 + o
                                h = 2 * p + hh
                                sbk = n % 2
                                S.op("pe", lambda e: e.matmul(PB[sbk][:, 0:N], lhsT=kTp[hh][0:66, kt * 128:(kt + 1) * 128],
                                                              rhs=qTp[hh][0:66, q0:q0 + N], start=True, stop=True),
                                     reads=[bk_[hh], bq[hh], bqa[hh]], writes=bPB[sbk])
                                S.op("act", lambda e: e.activation(out=pt[n % 3][:, 0:N], in_=PB[sbk][:, 0:N], func=AF.Exp,
                                                                   scale=0.125, bias=NC_[:, kt, h:h + 1]),
                                     reads=bPB[sbk] + [b_NC], writes=[b_pt[n % 3]])
                                if kt >= 4 * Qc:
                                    S.op("pool", lambda e: e.tensor_tensor(out=pt[n % 3][:, 0:128], in0=pt[n % 3][:, 0:128],
                                                                           in1=maskb[:], op=ALU.mult),
                                         reads=[b_pt[n % 3], b_const], writes=[b_pt[n % 3]])
                                if n >= 1:
                                    emit_pv(n - 1)
                                for dfr in list(deferred):
                                    dfr[0] -= 1
                                    if dfr[0] <= 0:
                                        deferred.remove(dfr)
                                        dfr[1]()
                                yield
                            emit_pv(len(jobs) - 1)
                            yield ("done", ("att", p))
                        while deferred:
                            for dfr in list(deferred):
                                dfr[0] -= 1
                                if dfr[0] <= 0:
                                    deferred.remove(dfr)
                                    dfr[1]()

                    run_threads([T_in(), T_att()], L1W)
                except StopStage:
                    pass
                cur[0] = st1
                l1src = x1s_d[seq] if 0 in layers else x_d[seq]
                l1b = b_x1 if 0 in layers else b_xdram
                outproj_residual(l1src, l1b, out_d[seq], b_outd, wo)
                S.fence()
                st2.close()

        sb_l = None
        b_l = None
        for seq in range(nseq):
            if 0 in layers:
                layer0(seq, 1 not in layers)
            if 1 in layers:
                layer1(seq, True)
        S.finish(b_outd, "sp")
        print("n_ins", S.n_ins, "n_wait", S.n_wait)
    return nc


def prep_shared(inp):
    m = dict(host_consts())
    m["pre_norm"] = np.ascontiguousarray(inp["pre_norm"], dtype=np.float32)
    m["post_norm"] = np.ascontiguousarray(inp["post_norm"], dtype=np.float32)
    wc = np.asarray(inp["w_in_c"], dtype=np.float32)
    qkvz = wc[:, :4096].reshape(KC, 128, 4, 8, 128)
    m["wc"] = np.ascontiguousarray(qkvz.transpose(3, 2, 1, 0, 4))
    m["wf"] = np.ascontiguousarray(wc[:, 4096:4112].reshape(KC, 128, 16).transpose(1, 0, 2))
    m["woc"] = np.ascontiguousarray(np.asarray(inp["w_out_c"], dtype=np.float32).reshape(8, 128, D).transpose(1, 0, 2))
    m["c_forget_bias"] = np.ascontiguousarray(inp["c_forget_bias"], dtype=np.float32).reshape(1, 16)
    wab = np.asarray(inp["w_in_ab"], dtype=np.float32)
    mcol = np.arange(128)
    dd = mcol % 64
    dperm = np.where(dd < 8, dd + 8, np.where(dd < 16, dd - 8, dd))
    permcol = (mcol // 64) * 64 + dperm
    slabs = []
    for h in range(4):
        qc = wab[:, 0 * 512 + h * 128:0 * 512 + (h + 1) * 128]
        kc_ = wab[:, 1 * 512 + h * 128:1 * 512 + (h + 1) * 128]
        vc = wab[:, 2 * 512 + h * 128:2 * 512 + (h + 1) * 128]
        zc = wab[:, 3 * 512 + h * 128:3 * 512 + (h + 1) * 128]
        slabs.append(np.stack([qc, qc[:, permcol], kc_, kc_[:, permcol], vc, zc], axis=0))
    wa = np.stack(slabs, axis=0).reshape(4, 6, KC, 128, 128).transpose(0, 1, 3, 2, 4)
    m["wa"] = np.ascontiguousarray(wa)
    m["woab"] = np.ascontiguousarray(np.asarray(inp["w_out_ab"], dtype=np.float32).reshape(8, 128, D).transpose(1, 0, 2))
    m["lam4"] = np.ascontiguousarray(np.stack([inp["a_lambda_q1"], inp["a_lambda_k1"], inp["a_lambda_q2"], inp["a_lambda_k2"]]).astype(np.float32))
    m["a_subln"] = np.ascontiguousarray(np.asarray(inp["a_subln"], dtype=np.float32).reshape(128, 1))
    m["wbq"] = np.ascontiguousarray(wab[:, 2048:3584].reshape(KC, 128, 12, 128).transpose(2, 1, 0, 3))
    m["cw"] = np.ascontiguousarray(np.asarray(inp["b_conv_w"], dtype=np.float32).reshape(4, 12, 128).transpose(2, 1, 0))
    m["wbz"] = np.ascontiguousarray(wab[:, 3584:4096].reshape(KC, 128, 512).transpose(1, 0, 2))
    m["wba"] = np.ascontiguousarray(wab[:, 4096:4104].reshape(KC, 128, 8).transpose(1, 0, 2))
    m["b_a_log"] = np.ascontiguousarray(inp["b_a_log"], dtype=np.float32).reshape(1, 4)
    m["b_dt_bias"] = np.ascontiguousarray(inp["b_dt_bias"], dtype=np.float32).reshape(1, 4)
    m["b_head_norm"] = np.ascontiguousarray(inp["b_head_norm"], dtype=np.float32).reshape(1, 128)
    return m


def kernel(**inp):
    x = np.asarray(inp["x"], dtype=np.float32)
    B = x.shape[0]
    nseq = B // NCORES
    shared = prep_shared(inp)
    nc = build(nseq, LAYERS)
    in_maps = []
    for c in range(NCORES):
        m = dict(shared)
        m["x"] = np.ascontiguousarray(x[c * nseq:(c + 1) * nseq])
        m["positions"] = np.ascontiguousarray(np.asarray(inp["positions"], dtype=np.int32)[c * nseq:(c + 1) * nseq])
        in_maps.append(m)
    res = run_bass_kernel_spmd(nc, in_maps, core_ids=list(range(NCORES)), **RUN_KW)
    LAST['res'] = res
    return np.concatenate([r["out"] for r in res.results], axis=0)
```

```python
import contextlib
import os
import math
import numpy as np
import concourse.bass as bass
import concourse.mybir as mybir
from concourse.bass_utils import run_bass_kernel_spmd

F32 = mybir.dt.float32
BF16 = mybir.dt.bfloat16
I32 = mybir.dt.int32
AF = mybir.ActivationFunctionType
ALU = mybir.AluOpType
AX = mybir.AxisListType

D = 1024
SEQ = 2048
NT = 16
KC = 8
EPS = 1e-6
NCORES = 8
LAYERS = (0, 1)
STAGE = 99
BW = tuple(int(v) for v in os.environ.get('BW', '2,4,3').split(','))
L1W = (1, 4)
RUN_KW = {}
LAST = {}
VVAR = int(os.environ.get('VVAR', '3'))


class StopStage(Exception):
    pass


class Buf:
    __slots__ = ("name", "w", "r", "psum", "tw", "tr")

    def __init__(self, name="", psum=False):
        self.name = name
        self.w = None
        self.r = {}
        self.psum = psum
        self.tw = 0.0
        self.tr = 0.0


class _Rec:
    def __getattr__(self, name):
        def call(*a, **k):
            return (name, a, k)
        return call


_REC = _Rec()


def _est_us(eng, call):
    name, a, k = call
    out = k.get("out", a[0] if a else None)
    F = 1
    for d in out.shape[1:]:
        F *= d
    if eng == "pe":
        lhsT = k.get("lhsT", a[1] if len(a) > 1 else None)
        passes = 4 if (lhsT is not None and lhsT.dtype == F32) else 1
        return passes * max(F, 64) / 2000.0 + 0.03
    if eng == "act":
        return 0.22 + F / 1400.0
    if eng == "dve":
        return 0.12 + F / 960.0
    return 0.25 + F / 480.0


class Sched:
    ENGS = ("pe", "act", "dve", "pool", "sp")

    def __init__(self, nc, stack, n_dma_sems=6):
        self.nc = nc
        self.e = {"pe": nc.tensor, "act": nc.scalar, "dve": nc.vector,
                  "pool": nc.gpsimd, "sp": nc.sync}
        self.semh = {}
        self.cnt = {}
        for k in self.ENGS:
            self.semh[k] = stack.enter_context(nc.semaphore("s_" + k))
            self.cnt[k] = 0
        self.dq = {}
        for q in ("sp", "act", "pool"):
            slots = []
            for i in range(n_dma_sems):
                key = "d_%s%d" % (q, i)
                self.semh[key] = stack.enter_context(nc.semaphore(key))
                self.cnt[key] = 0
                slots.append(key)
            self.dq[q] = [slots, 0]
        self.seen = {k: {} for k in self.ENGS}
        self.rec = None
        self.n_ins = {k: 0 for k in self.ENGS}
        self.n_wait = {k: 0 for k in self.ENGS}

    def _wait(self, eng, deps):
        need = {}
        seen = self.seen[eng]
        for (k, v) in deps:
            if eng == "pe" and k == "pe":
                continue
            if seen.get(k, 0) < v and need.get(k, 0) < v:
                need[k] = v
        for k, v in need.items():
            self.e[eng].wait_ge(self.semh[k], v)
            seen[k] = v
            self.n_wait[eng] += 1

    @staticmethod
    def _deps(reads, writes):
        deps = []
        for b in reads:
            if b.w is not None:
                deps.append(b.w)
            if b.psum:
                deps.extend(b.r.items())
        for b in writes:
            if b.w is not None:
                deps.append(b.w)
            deps.extend(b.r.items())
        return deps

    @staticmethod
    def _mark(tok, reads, writes):
        for b in reads:
            if b.r.get(tok[0], 0) < tok[1]:
                b.r[tok[0]] = tok[1]
        for b in writes:
            b.w = tok
            b.r = {}

    def op(self, eng, fn, reads=(), writes=(), inc=True):
        if self.rec is not None:
            self.rec.append(("op", eng, fn(_REC), list(reads), list(writes), inc))
            return None
        self._wait(eng, self._deps(reads, writes))
        ins = fn(self.e[eng])
        self.n_ins[eng] += 1
        if inc:
            self.cnt[eng] += 1
            ins.then_inc(self.semh[eng], 1)
            tok = (eng, self.cnt[eng])
        else:
            tok = (eng, self.cnt[eng] + 1)
        self._mark(tok, reads, writes)
        return ins

    def dma(self, q, out, in_, reads=(), writes=(), **kw):
        if self.rec is not None:
            self.rec.append(("dma", q, (out, in_, kw), list(reads), list(writes), True))
            return None
        slots, idx = self.dq[q]
        key = slots[idx % len(slots)]
        self.dq[q][1] = idx + 1
        deps = self._deps(reads, writes)
        if self.cnt[key] > 0:
            deps.append((key, self.cnt[key]))
        self._wait(q, deps)
        ins = self.e[q].dma_start(out=out, in_=in_, **kw)
        self.cnt[key] += 16
        ins.then_inc(self.semh[key], 16)
        self._mark((key, self.cnt[key]), reads, writes)
        return ins

    def fence(self):
        allc = [(k, v) for k, v in self.cnt.items() if v > 0]
        for eng in self.ENGS:
            self._wait(eng, allc)

    def finish(self, bufs, eng="sp"):
        deps = []
        for b in bufs:
            if b.w is not None:
                deps.append(b.w)
            deps.extend(b.r.items())
        self._wait(eng, deps)


def host_consts():
    c = {}
    j = np.arange(128)
    U = (j[:, None] <= j[None, :]).astype(np.float32)
    c["c_uext"] = np.concatenate([U, np.ones((128, 1), np.float32)], axis=1)
    c["c_ident"] = np.eye(128, dtype=np.float32)
    p = np.arange(128)
    d = p % 64
    half = 8
    inv_freq = (np.float32(500000.0) ** (-(np.arange(half, dtype=np.float32) * np.float32(2.0)) / np.float32(16.0))).astype(np.float32)
    freq = np.where(d < 16, inv_freq[d % 8], 0.0).astype(np.float32)
    sign = np.where(d < 8, -1.0, np.where(d < 16, 1.0, 0.0)).astype(np.float32)
    c["c_rope"] = np.stack([freq, sign], axis=1).astype(np.float32)
    same = (j[:, None] // 64) == (j[None, :] // 64)
    ublk = (same & (j[:, None] <= j[None, :])).astype(np.float32)
    blk = same.astype(np.float32)
    strictblk = (same & (j[:, None] > j[None, :])).astype(np.float32)
    half0 = np.repeat((j < 64).astype(np.float32)[:, None], 128, axis=1)
    half1 = np.repeat((j >= 64).astype(np.float32)[:, None], 128, axis=1)
    c["c_gdn"] = np.stack([ublk, blk, strictblk, half0, half1, np.eye(128, dtype=np.float32)], axis=1)
    return c


def build(nseq, layers=(0, 1)):
    nc = bass.Bass("TRN2", target_bir_lowering=False)
    dt_in = lambda name, shape, dt=F32: nc.dram_tensor(name, list(shape), dt, kind="ExternalInput").ap()
    x_d = dt_in("x", [nseq, SEQ, D])
    out_d = nc.dram_tensor("out", [nseq, SEQ, D], F32, kind="ExternalOutput").ap()
    x1s_d = nc.dram_tensor("x1s", [nseq, SEQ, D], F32, kind="Internal").ap()
    pre_d = dt_in("pre_norm", [2, D])
    post_d = dt_in("post_norm", [2, D])
    uext_d = dt_in("c_uext", [128, 129])
    ident_d = dt_in("c_ident", [128, 128])
    wc_d = dt_in("wc", [8, 4, 128, KC, 128])
    wf_d = dt_in("wf", [128, KC, 16])
    woc_d = dt_in("woc", [128, 8, D])
    fb_d = dt_in("c_forget_bias", [1, 16])
    pos_d = dt_in("positions", [nseq, SEQ], I32)
    rope_d = dt_in("c_rope", [128, 2])
    wa_d = dt_in("wa", [4, 6, 128, KC, 128])
    woab_d = dt_in("woab", [128, 8, D])
    lam_d = dt_in("lam4", [4, 64])
    subln_d = dt_in("a_subln", [128, 1])
    gdnc_d = dt_in("c_gdn", [128, 6, 128])
    wbq_d = dt_in("wbq", [12, 128, KC, 128])
    cw_d = dt_in("cw", [128, 12, 4])
    wbz_d = dt_in("wbz", [128, KC, 512])
    wba_d = dt_in("wba", [128, KC, 8])
    alog_d = dt_in("b_a_log", [1, 4])
    dtb_d = dt_in("b_dt_bias", [1, 4])
    hn_d = dt_in("b_head_norm", [1, 128])

    with contextlib.ExitStack() as st:
        S = Sched(nc, st)
        uid = [0]
        def sb(name, shape, dt):
            uid[0] += 1
            return st.enter_context(nc.sbuf_tensor("t%d_%s" % (uid[0], name), list(shape), dt))
        xt = [sb("xt%d" % i, [128, D], F32) for i in range(3)]
        b_xt = [Buf() for _ in range(3)]
        junk2 = sb("junk2", [128, D], BF16)
        b_junk2 = Buf()
        hT = sb("hT", [128, KC, SEQ], BF16)
        og = sb("og", [128, 8, SEQ], BF16)
        pre_bc = sb("pre_bc", [128, D], F32)
        post_bc = sb("post_bc", [128, D], F32)
        ident = sb("ident", [128, 128], BF16)
        uext = sb("uext", [128, 129], F32)
        ones_f = sb("ones_f", [128, 128], F32)
        maskb = sb("maskb", [128, 128], BF16)
        ss = sb("ss", [128, NT], F32)
        rstd = sb("rstd", [128, NT], F32)
        junk = sb("junk", [128, 512], BF16)
        PB = [st.enter_context(nc.psum_tensor("pb%d" % i, [128, 512], F32)) for i in range(8)]
        bPB = [[Buf("pb%d_0" % i, True), Buf("pb%d_1" % i, True)] for i in range(8)]

        b_xdram = [Buf("xd%d" % i) for i in range(NT)]
        b_x1 = [Buf("x1_%d" % i) for i in range(NT)]
        b_outd = [Buf("od%d" % i) for i in range(NT)]
        b_ssi = [Buf() for _ in range(NT)]
        b_rsi = [Buf() for _ in range(NT)]
        b_hT = [Buf("hT%d" % i) for i in range(NT)]
        b_og = [[Buf() for _ in range(4)] for _ in range(8)]
        b_const = Buf("const")
        b_norm = Buf("normbc")
        b_ss = Buf("ss")
        b_rstd = Buf("rstd")
        b_junk = Buf("junk")
        b_xn = [Buf(), Buf()]
        b_wo = Buf("wo")
        b_out = Buf("out")

        S.dma("sp", uext[:], uext_d[:, :], writes=[b_const])
        S.dma("pool", ident[:], ident_d[:, :], writes=[b_const])
        S.dma("pool", maskb[:], uext_d[:, 0:128], writes=[b_const])
        S.op("pool", lambda e: e.memset(ones_f[:], 1.0), writes=[b_const])

        cur_layer = [0]

        def load_norms(layer):
            cur_layer[0] = layer
            S.dma("sp", pre_bc[:], bass.AP(pre_d.tensor, layer * D, [[0, 128], [1, D]]), writes=[b_norm])
            S.dma("sp", post_bc[:], bass.AP(post_d.tensor, layer * D, [[0, 128], [1, D]]), writes=[b_norm])

        def load_post():
            S.dma("sp", post_bc[:], bass.AP(post_d.tensor, cur_layer[0] * D, [[0, 128], [1, D]]), writes=[b_norm])

        def prenorm(src, bsrc):
            xn = [sb_l["tmpf"][k][:].bitcast(BF16) for k in range(2)]
            b_xn = b_l["tmpf"]
            for i in range(NT):
                xb_ = xt[i % 3]
                bx = b_xt[i % 3]
                S.dma("sp", xb_[:], src[i * 128:(i + 1) * 128, :], reads=[bsrc[i]], writes=[bx])
                S.op("act", lambda e: e.activation(out=junk2[:], in_=xb_[:], func=AF.Square, accum_out=ss[:, i:i + 1]),
                     reads=[bx], writes=[b_junk2, b_ssi[i]])
                S.op("act", lambda e: e.activation(out=rstd[:, i:i + 1], in_=ss[:, i:i + 1], func=AF.Ln, scale=1.0 / D, bias=EPS),
                     reads=[b_ssi[i]], writes=[b_rsi[i]])
                S.op("act", lambda e: e.activation(out=rstd[:, i:i + 1], in_=rstd[:, i:i + 1], func=AF.Exp, scale=-0.5),
                     reads=[b_rsi[i]], writes=[b_rsi[i]])
                xb = xn[i % 2]
                bxb = b_xn[i % 2]
                S.op("dve", lambda e: e.scalar_tensor_tensor(out=xb, in0=xb_[:], scalar=rstd[:, i:i + 1],
                                                             in1=pre_bc[:], op0=ALU.mult, op1=ALU.mult),
                     reads=[bx, b_rsi[i], b_norm], writes=[bxb])
                bank = 6 + (i % 2)
                pv = PB[bank][:].bitcast(BF16)
                for kc in range(KC):
                    S.op("pe", lambda e: e.transpose(pv[:, kc * 128:(kc + 1) * 128], xb[:, kc * 128:(kc + 1) * 128],
                                                     ident[:]),
                         reads=[bxb, b_const], writes=bPB[bank], inc=(kc == KC - 1))
                eng = "act" if i % 2 == 0 else "dve"
                srcp = pv.rearrange("p (k t) -> p k t", k=KC)
                dst = hT[:, :, i * 128:(i + 1) * 128]
                if eng == "act":
                    S.op("act", lambda e: e.activation(out=dst, in_=srcp, func=AF.Copy),
                         reads=bPB[bank], writes=[b_hT[i]])
                else:
                    S.op("dve", lambda e: e.tensor_copy(out=dst, in_=srcp),
                         reads=bPB[bank], writes=[b_hT[i]])

        def outproj_residual(src, bsrc, dst, b_dst, wo):
            ss2 = sb_l["ss2"]; rs2 = sb_l["rs2"]; tmpf = sb_l["tmpf"]
            b_s2 = [Buf() for _ in range(NT)]
            b_r2 = [Buf() for _ in range(NT)]
            for i in range(NT):
                xb_ = xt[i % 3]
                bx = b_xt[i % 3]
                S.dma("sp", xb_[:], src[i * 128:(i + 1) * 128, :], reads=[bsrc[i]], writes=[bx])
                banks = (4 + 2 * (i % 2), 5 + 2 * (i % 2))
                for hf in range(2):
                    bk = banks[hf]
                    for p in range(8):
                        S.op("pe", lambda e: e.matmul(PB[bk][:, :], lhsT=og[:, p, i * 128:(i + 1) * 128],
                                                      rhs=wo[:, p, hf * 512:(hf + 1) * 512],
                                                      start=(p == 0), stop=(p == 7)),
                             reads=[b_og[p][i // 4], b_wo], writes=bPB[bk], inc=(p == 7))
                    S.op("act", lambda e: e.activation(out=junk[:, 0:512], in_=PB[bk][:, :], func=AF.Square,
                                                       accum_out=ss2[:, 2 * i + hf:2 * i + hf + 1]),
                         reads=bPB[bk], writes=[b_junk, b_s2[i]])
                S.op("dve", lambda e: e.tensor_tensor(out=rs2[:, i:i + 1], in0=ss2[:, 2 * i:2 * i + 1],
                                                      in1=ss2[:, 2 * i + 1:2 * i + 2], op=ALU.add),
                     reads=[b_s2[i]], writes=[b_r2[i]])
                S.op("act", lambda e: e.activation(out=rs2[:, i:i + 1], in_=rs2[:, i:i + 1], func=AF.Ln,
                                                   scale=1.0 / D, bias=EPS),
                     reads=[b_r2[i]], writes=[b_r2[i]])
                S.op("act", lambda e: e.activation(out=rs2[:, i:i + 1], in_=rs2[:, i:i + 1], func=AF.Exp, scale=-0.5),
                     reads=[b_r2[i]], writes=[b_r2[i]])
                for hf in range(2):
                    bk = banks[hf]
                    tf = tmpf[hf]
                    S.op("dve", lambda e: e.scalar_tensor_tensor(out=tf[:], in0=PB[bk][:, :], scalar=rs2[:, i:i + 1],
                                                                 in1=post_bc[:, hf * 512:(hf + 1) * 512],
                                                                 op0=ALU.mult, op1=ALU.mult),
                         reads=bPB[bk] + [b_r2[i], b_norm], writes=[b_l["tmpf"][hf]])
                    S.op("pool", lambda e: e.tensor_tensor(out=xb_[:, hf * 512:(hf + 1) * 512],
                                                           in0=xb_[:, hf * 512:(hf + 1) * 512], in1=tf[:],
                                                           op=ALU.add),
                         reads=[b_l["tmpf"][hf], bx], writes=[bx])
                S.dma("sp", dst[i * 128:(i + 1) * 128, :], xb_[:], reads=[bx], writes=[b_dst[i]])

        PI = 3.141592653589793
        LAMBDA_INIT = 0.8 - 0.6 * math.exp(-0.3 * 0)

        def layer0(seq, last):
            nonlocal sb_l, b_l
            with contextlib.ExitStack() as st1:
                st2 = st1.enter_context(contextlib.ExitStack())
                cur = [st1]

                def sl(name, shape, dt):
                    uid[0] += 1
                    return cur[0].enter_context(nc.sbuf_tensor("t%d_%s" % (uid[0], name), list(shape), dt))
                sb_l = {}
                b_l = {}
                sb_l["ss2"] = sl("ss2", [128, 2 * NT], F32); b_l["ss2"] = Buf()
                sb_l["rs2"] = sl("rs2", [128, NT], F32); b_l["rs2"] = Buf()
                sb_l["tmpf"] = [sl("tmpf%d" % i, [128, 512], F32) for i in range(2)]; b_l["tmpf"] = [Buf(), Buf()]
                load_norms(0)
                wo = sl("wo", [128, 8, D], BF16)
                S.dma("pool", wo[:], woab_d[:, :, :], writes=[b_wo])
                prenorm(x_d[seq], b_xdram)
                cur[0] = st2
                ropec = sl("ropec", [128, 2], F32)
                lamt = sl("lamt", [128, 4, 64], F32)
                lamp = sl("lamp", [128, 2, 64], F32)
                lams = sl("lams", [128, 2], F32)
                neglam = sl("neglam", [128, 1], F32)
                subcol = sl("subcol", [128, 1], F32)
                ones_b = sl("ones_b", [128, 128], BF16)
                posi = sl("posi", [128, SEQ], I32)
                Ct = sl("Ct", [128, SEQ], F32)
                St = sl("St", [128, SEQ], F32)
                qT = sl("qTa", [128, SEQ], BF16)
                kT = sl("kTa", [128, SEQ], BF16)
                Vh = sl("Vh", [128, NT, 128], BF16)
                wsl = [sl("wa%d" % i, [128, KC, 128], BF16) for i in range(6)]
                pt = [sl("pt%d" % i, [128, 512], BF16) for i in range(3)]
                ta = sb_l["tmpf"][0]; tb = sb_l["tmpf"][1]
                tc = sl("tc", [128, 512], F32); td = sl("td", [128, 512], F32)
                te = sl("te", [128, 512], F32); b_te = Buf()
                b_ropec, b_lam, b_neglam, b_subcol, b_onesb, b_posi, b_Ct, b_St = [Buf() for _ in range(8)]
                b_qT, b_kT, b_Vh = Buf(), Buf(), Buf()
                b_wsl = [Buf() for _ in range(6)]
                b_pt = [Buf() for _ in range(3)]
                b_ta, b_tb, b_tc, b_td = b_l["tmpf"][0], b_l["tmpf"][1], Buf(), Buf()

                S.dma("sp", ropec[:], rope_d[:, :], writes=[b_ropec])
                S.op("pool", lambda e: e.memset(ones_b[:], 1.0), writes=[b_onesb])
                S.dma("sp", lamt[:], bass.AP(lam_d.tensor, 0, [[0, 128], [64, 4], [1, 64]]), writes=[b_lam])
                S.op("dve", lambda e: e.tensor_tensor(out=lamp[:, 0, :], in0=lamt[:, 0, :], in1=lamt[:, 1, :], op=ALU.mult),
                     reads=[b_lam], writes=[b_lam])
                S.op("dve", lambda e: e.tensor_tensor(out=lamp[:, 1, :], in0=lamt[:, 2, :], in1=lamt[:, 3, :], op=ALU.mult),
                     reads=[b_lam], writes=[b_lam])
                S.op("dve", lambda e: e.tensor_reduce(out=lams[:], in_=lamp[:], axis=AX.X, op=ALU.add),
                     reads=[b_lam], writes=[b_lam])
                S.op("act", lambda e: e.activation(out=lams[:], in_=lams[:], func=AF.Exp), reads=[b_lam], writes=[b_lam])
                S.op("dve", lambda e: e.tensor_tensor(out=neglam[:], in0=lams[:, 1:2], in1=lams[:, 0:1], op=ALU.subtract),
                     reads=[b_lam], writes=[b_neglam])
                S.op("dve", lambda e: e.tensor_scalar(out=neglam[:], in0=neglam[:], scalar1=-LAMBDA_INIT, scalar2=None, op0=ALU.add),
                     reads=[b_neglam], writes=[b_neglam])
                S.dma("sp", subcol[:], subln_d[:, :], writes=[b_subcol])
                S.op("dve", lambda e: e.tensor_scalar(out=subcol[:], in0=subcol[:], scalar1=1.0 - LAMBDA_INIT, scalar2=None, op0=ALU.mult),
                     reads=[b_subcol], writes=[b_subcol])
                S.dma("sp", posi[:], bass.AP(pos_d.tensor, seq * SEQ, [[0, 128], [1, SEQ]]), writes=[b_posi])

                def sin_table(dst, bdst, phase, signed):
                    S.op("dve", lambda e: e.tensor_copy(out=dst[:], in_=posi[:]), reads=[b_posi], writes=[bdst])
                    S.op("dve", lambda e: e.tensor_scalar(out=dst[:], in0=dst[:], scalar1=ropec[:, 0:1], scalar2=phase,
                                                          op0=ALU.mult, op1=ALU.add), reads=[bdst, b_ropec], writes=[bdst])
                    for c4 in range(4):
                        sl_ = slice(c4 * 512, (c4 + 1) * 512)
                        tI = tc[:].bitcast(I32)
                        S.op("dve", lambda e: e.tensor_scalar(out=td[:], in0=dst[:, sl_], scalar1=1.0 / (2 * PI), scalar2=None,
                                                              op0=ALU.mult), reads=[bdst], writes=[b_td])
                        S.op("dve", lambda e: e.tensor_copy(out=tI, in_=td[:]), reads=[b_td], writes=[b_tc])
                        S.op("dve", lambda e: e.tensor_copy(out=td[:], in_=tI), reads=[b_tc], writes=[b_td])
                        S.op("dve", lambda e: e.scalar_tensor_tensor(out=td[:], in0=td[:], scalar=-2 * PI, in1=dst[:, sl_],
                                                                     op0=ALU.mult, op1=ALU.add),
                             reads=[b_td, bdst], writes=[b_td])
                        S.op("dve", lambda e: e.tensor_scalar(out=tc[:], in0=td[:], scalar1=PI, scalar2=-2 * PI,
                                                              op0=ALU.is_gt, op1=ALU.mult), reads=[b_td], writes=[b_tc])
                        S.op("dve", lambda e: e.tensor_tensor(out=td[:], in0=td[:], in1=tc[:], op=ALU.add),
                             reads=[b_td, b_tc], writes=[b_td])
                        S.op("dve", lambda e: e.tensor_scalar(out=td[:], in0=td[:], scalar1=-PI, scalar2=PI,
                                                              op0=ALU.max, op1=ALU.min), reads=[b_td], writes=[b_td])
                        S.op("act", lambda e: e.activation(out=dst[:, sl_], in_=td[:], func=AF.Sin),
                             reads=[b_td], writes=[bdst])
                    if signed:
                        S.op("dve", lambda e: e.tensor_scalar(out=dst[:], in0=dst[:], scalar1=ropec[:, 1:2], scalar2=None,
                                                              op0=ALU.mult), reads=[bdst, b_ropec], writes=[bdst])
                sin_table(Ct, b_Ct, PI / 2, False)
                sin_table(St, b_St, 0.0, True)

                for h in range(4):
                    for i6 in range(6):
                        S.dma("pool", wsl[i6][:], wa_d[h, i6], writes=[b_wsl[i6]])
                    n_ev = 0
                    for (i_w, dstT, bdst) in ((0, qT, b_qT), (2, kT, b_kT)):
                        for t4 in range(4):
                            bks = (6, 7) if n_ev % 2 == 0 else (4, 5)
                            n_ev += 1
                            tsl = slice(t4 * 512, (t4 + 1) * 512)
                            for jj in range(2):
                                for kc in range(KC):
                                    S.op("pe", lambda e: e.matmul(PB[bks[jj]][:, :], lhsT=wsl[i_w + jj][:, kc, :],
                                                                  rhs=hT[:, kc, tsl], start=(kc == 0), stop=(kc == KC - 1)),
                                         reads=[b_wsl[i_w + jj]] + b_hT[4 * t4:4 * t4 + 4], writes=bPB[bks[jj]],
                                         inc=(kc == KC - 1))
                            S.op("dve", lambda e: e.tensor_tensor(out=tc[:], in0=PB[bks[0]][:, :], in1=Ct[:, tsl], op=ALU.mult),
                                 reads=bPB[bks[0]] + [b_Ct], writes=[b_tc])
                            S.op("dve", lambda e: e.tensor_tensor(out=td[:], in0=PB[bks[1]][:, :], in1=St[:, tsl], op=ALU.mult),
                                 reads=bPB[bks[1]] + [b_St], writes=[b_td])
                            S.op("pool", lambda e: e.tensor_tensor(out=dstT[:, tsl], in0=tc[:], in1=td[:], op=ALU.add),
                                 reads=[b_tc, b_td], writes=[bdst])
                    for t4 in range(4):
                        for kc in range(KC):
                            S.op("pe", lambda e: e.matmul(PB[6][:, :], lhsT=wsl[4][:, kc, :],
                                                          rhs=hT[:, kc, t4 * 512:(t4 + 1) * 512],
                                                          start=(kc == 0), stop=(kc == KC - 1)),
                                 reads=[b_wsl[4]] + b_hT[4 * t4:4 * t4 + 4], writes=bPB[6], inc=(kc == KC - 1))
                        S.op("act", lambda e: e.activation(out=pt[0][:], in_=PB[6][:, :], func=AF.Copy),
                             reads=bPB[6], writes=[b_pt[0]])
                        pbf = PB[7][:, :].bitcast(BF16)[:, 0:512].rearrange("p (j c) -> p j c", c=128)
                        for j in range(4):
                            S.op("pe", lambda e: e.transpose(pbf[:, j, :], pt[0][:, j * 128:(j + 1) * 128], ident[:]),
                                 reads=[b_pt[0], b_const], writes=bPB[7], inc=(j == 3))
                        S.op("act", lambda e: e.activation(out=Vh[:, 4 * t4:4 * t4 + 4, :], in_=pbf, func=AF.Copy),
                             reads=bPB[7], writes=[b_Vh])
                    deferred = []
                    jobs = []
                    for Qc in range(4):
                        for kt in range(4 * Qc + 4):
                            for c in range(2):
                                jobs.append((Qc, kt, c))

                    def emit_pv(n):
                        Qc, kt, c = jobs[n]
                        o = max(0, kt - 4 * Qc) * 128
                        N = 512 - o
                        S.op("pe", lambda e: e.matmul(PB[2 + c][:, o:512], lhsT=Vh[:, kt, :], rhs=pt[n % 3][:, 0:N],
                                                      start=(kt == 0), stop=(kt == 4 * Qc + 3)),
                             reads=[b_Vh, b_pt[n % 3]], writes=bPB[2 + c], inc=False)
                        S.op("pe", lambda e: e.matmul(PB[4 + c][:, o:512], lhsT=ones_b[:], rhs=pt[n % 3][:, 0:N],
                                                      start=(kt == 0), stop=(kt == 4 * Qc + 3)),
                             reads=[b_onesb, b_pt[n % 3]], writes=bPB[4 + c])
                        if kt == 4 * Qc + 3 and c == 1:
                            emit_epilogue(Qc)

                    def emit_epilogue(Qc, h=h):
                        qsl = slice(Qc * 512, (Qc + 1) * 512)
                        for kc in range(KC):
                            S.op("pe", lambda e: e.matmul(PB[6][:, :], lhsT=wsl[5][:, kc, :], rhs=hT[:, kc, qsl],
                                                          start=(kc == 0), stop=(kc == KC - 1)),
                                 reads=[b_wsl[5]] + b_hT[4 * Qc:4 * Qc + 4], writes=bPB[6], inc=(kc == KC - 1))
                        S.op("act", lambda e: e.activation(out=ta[:], in_=PB[4][:, :], func=AF.Ln), reads=bPB[4], writes=[b_ta])
                        S.op("act", lambda e: e.activation(out=ta[:], in_=ta[:], func=AF.Exp, scale=-1.0), reads=[b_ta], writes=[b_ta])
                        S.op("act", lambda e: e.activation(out=te[:], in_=PB[5][:, :], func=AF.Ln), reads=bPB[5], writes=[b_te])
                        S.op("act", lambda e: e.activation(out=te[:], in_=te[:], func=AF.Exp, scale=-1.0), reads=[b_te], writes=[b_te])
                        S.op("dve", lambda e: e.tensor_tensor(out=tb[:], in0=PB[2][:, :], in1=ta[:], op=ALU.mult),
                             reads=bPB[2] + [b_ta], writes=[b_tb])
                        S.op("dve", lambda e: e.tensor_tensor(out=tc[:], in0=PB[3][:, :], in1=te[:], op=ALU.mult),
                             reads=bPB[3] + [b_te], writes=[b_tc])

                        def s1():
                            S.op("dve", lambda e: e.scalar_tensor_tensor(out=tb[:], in0=tc[:], scalar=neglam[:, 0:1], in1=tb[:],
                                                                          op0=ALU.mult, op1=ALU.add),
                                 reads=[b_tc, b_tb, b_neglam], writes=[b_tb])
                            S.op("act", lambda e: e.activation(out=ta[:], in_=PB[6][:, :], func=AF.Exp, scale=-1.0),
                                 reads=bPB[6] + [b_ta], writes=[b_ta])
                            S.op("act", lambda e: e.activation(out=ta[:], in_=ta[:], func=AF.Ln, bias=1.0), reads=[b_ta], writes=[b_ta])
                            S.op("act", lambda e: e.activation(out=ta[:], in_=ta[:], func=AF.Exp, scale=-1.0), reads=[b_ta], writes=[b_ta])

                        def s2():
                            S.op("act", lambda e: e.activation(out=tc[:], in_=tb[:], func=AF.Square), reads=[b_tb], writes=[b_tc])
                            S.op("dve", lambda e: e.tensor_tensor(out=ta[:], in0=PB[6][:, :], in1=ta[:], op=ALU.mult),
                                 reads=bPB[6] + [b_ta], writes=[b_ta])

                        def s3():
                            S.op("pe", lambda e: e.matmul(PB[7][:, :], lhsT=ones_f[:], rhs=tc[:], start=True, stop=True),
                                 reads=[b_const, b_tc], writes=bPB[7])

                        def s4():
                            S.op("act", lambda e: e.activation(out=td[:], in_=PB[7][:, :], func=AF.Ln, scale=1.0 / 128, bias=EPS),
                                 reads=bPB[7], writes=[b_td])
                            S.op("act", lambda e: e.activation(out=td[:], in_=td[:], func=AF.Exp, scale=-0.5), reads=[b_td], writes=[b_td])

                        def s5():
                            S.op("pool", lambda e: e.tensor_tensor(out=tb[:], in0=tb[:], in1=td[:], op=ALU.mult),
                                 reads=[b_tb, b_td], writes=[b_tb])

                        def s6():
                            S.op("dve", lambda e: e.scalar_tensor_tensor(out=og[:, h, qsl], in0=tb[:], scalar=subcol[:, 0:1], in1=ta[:],
                                                                         op0=ALU.mult, op1=ALU.mult),
                                 reads=[b_tb, b_ta, b_subcol], writes=[b_og[h][Qc]])
                        for dl, fn in ((2, s1), (4, s2), (6, s3), (8, s4), (10, s5), (12, s6)):
                            deferred.append([dl, fn])

                    for n, (Qc, kt, c) in enumerate(jobs):
                        o = max(0, kt - 4 * Qc) * 128
                        N = 512 - o
                        q0 = Qc * 512 + o
                        sbk = n % 2
                        S.op("pe", lambda e: e.matmul(PB[sbk][:, 0:N], lhsT=kT[c * 64:(c + 1) * 64, kt * 128:(kt + 1) * 128],
                                                      rhs=qT[c * 64:(c + 1) * 64, q0:q0 + N], start=True, stop=True),
                             reads=[b_kT, b_qT], writes=bPB[sbk])
                        S.op("act", lambda e: e.activation(out=pt[n % 3][:, 0:N], in_=PB[sbk][:, 0:N], func=AF.Exp, scale=0.125),
                             reads=bPB[sbk], writes=[b_pt[n % 3]])
                        if kt >= 4 * Qc:
                            S.op("pool", lambda e: e.tensor_tensor(out=pt[n % 3][:, 0:128], in0=pt[n % 3][:, 0:128],
                                                                   in1=maskb[:], op=ALU.mult),
                                 reads=[b_pt[n % 3], b_const], writes=[b_pt[n % 3]])
                        if n >= 1:
                            emit_pv(n - 1)
                        for dfr in list(deferred):
                            dfr[0] -= 1
                            if dfr[0] <= 0:
                                deferred.remove(dfr)
                                dfr[1]()
                    emit_pv(len(jobs) - 1)
                    while deferred:
                        for dfr in list(deferred):
                            dfr[0] -= 1
                            if dfr[0] <= 0:
                                deferred.remove(dfr)
                                dfr[1]()
                S.fence()
                st2.close()
                st3 = st1.enter_context(contextlib.ExitStack())
                cur[0] = st3
                partB(seq, sl)
                cur[0] = st1
                if last:
                    outproj_residual(x_d[seq], b_xdram, out_d[seq], b_outd, wo)
                else:
                    outproj_residual(x_d[seq], b_xdram, x1s_d[seq], b_x1, wo)
                S.fence()
                st3.close()

        def run_threads(threads, weights):
            lists = []
            for g in threads:
                S.rec = []
                for r in g:
                    if isinstance(r, tuple):
                        S.rec.append((r[0], r[1]))
                lists.append(S.rec)
            S.rec = None
            ptr = [0] * len(lists)
            done = set()
            eng_free = {}
            rr = 0
            while True:
                best = None
                alive = False
                for ti in range(len(lists)):
                    L = lists[ti]
                    while ptr[ti] < len(L) and L[ptr[ti]][0] in ("wait", "done"):
                        kind, key = L[ptr[ti]]
                        if kind == "done":
                            done.add(key)
                            ptr[ti] += 1
                        elif key in done:
                            ptr[ti] += 1
                        else:
                            break
                    if ptr[ti] >= len(L):
                        continue
                    alive = True
                    it = L[ptr[ti]]
                    if it[0] in ("wait", "done"):
                        continue
                    kind, eng, call, reads, writes, inc = it
                    t0 = eng_free.get(eng, 0.0)
                    for b_ in reads:
                        if b_.tw > t0:
                            t0 = b_.tw
                        if b_.psum and b_.tr > t0:
                            t0 = b_.tr
                    for b_ in writes:
                        if b_.tw > t0:
                            t0 = b_.tw
                        if b_.tr > t0:
                            t0 = b_.tr
                    key2 = (t0, (ti - rr) % len(lists))
                    if best is None or key2 < best[0]:
                        best = (key2, ti, t0)
                if best is None:
                    assert not alive, "emission deadlock"
                    break
                _, ti, t0 = best
                kind, eng, call, reads, writes, inc = lists[ti][ptr[ti]]
                ptr[ti] += 1
                rr = (ti + 1) % len(lists)
                if kind == "op":
                    dur = _est_us(eng, call)
                    S.op(eng, lambda e: getattr(e, call[0])(*call[1], **call[2]), reads=reads, writes=writes, inc=inc)
                    eng_free[eng] = t0 + dur
                    tend = t0 + dur + 0.15
                else:
                    out_, in__, kw_ = call
                    S.dma(eng, out_, in__, reads=reads, writes=writes, **kw_)
                    eng_free[eng] = t0 + 0.1
                    nb = 1
                    for d in out_.shape:
                        nb *= d
                    tend = t0 + 2.0 + nb * 4 / 150e3
                for b_ in reads:
                    if tend > b_.tr:
                        b_.tr = tend
                for b_ in writes:
                    b_.tw = tend
                    b_.tr = 0.0

        def run_threads_rr(threads, weights):
            done = set()
            st_ = [{"g": g, "wait": None, "alive": True} for g in threads]
            while any(t["alive"] for t in st_):
                progressed = False
                for t, w in zip(st_, weights):
                    if not t["alive"]:
                        continue
                    for _ in range(w):
                        if t["wait"] is not None:
                            if t["wait"] in done:
                                t["wait"] = None
                            else:
                                break
                        try:
                            r = next(t["g"])
                        except StopIteration:
                            t["alive"] = False
                            progressed = True
                            break
                        progressed = True
                        if isinstance(r, tuple):
                            if r[0] == "wait":
                                if r[1] not in done:
                                    t["wait"] = r[1]
                                    break
                            elif r[0] == "done":
                                done.add(r[1])
                assert progressed, "emission deadlock"

        def partB(seq, sl):
            gm = sl("gm", [128, 6, 128], F32)
            ublk, blk, strictblk, ident_f = gm[:, 0, :], gm[:, 1, :], gm[:, 2, :], gm[:, 5, :]
            halfsel = (gm[:, 3, :], gm[:, 4, :])
            ones_b = sl("ones_bB", [128, 128], BF16)
            wbz = sl("wbz", [128, KC, 512], BF16)
            wba = sl("wba", [128, KC, 8], BF16)
            wbq = [sl("wbq%d" % i, [128, KC, 128], BF16) for i in range(2)]
            cw = sl("cw", [128, 12, 4], F32)
            negA = sl("negA", [128, 4], F32)
            dtb = sl("dtb", [128, 4], F32)
            hn_bc = sl("hn_bc", [128, 128], F32)
            G = {n: sl("g_" + n, [128, NT, 4], F32) for n in ("xa", "g", "beta", "negb", "G", "GL", "eG", "bG", "dG", "eGL0", "eGL1")}
            cr = sl("cr", [128, 12, 3], F32)
            xc = [sl("xc%d" % i, [128, 515], F32) for i in range(2)]
            yc = sl("yc", [128, 512], F32)
            sc = sl("sc", [128, 512], F32)
            t1 = sl("t1", [128, 512], F32)
            sqb = sl("sqb", [128, 512], BF16)
            qnT = [sl("qnT%d" % i, [128, 4, 512], BF16) for i in range(2)]
            knT = [sl("knT%d" % i, [128, 4, 512], BF16) for i in range(2)]
            vsT = [sl("vsT%d" % i, [128, 4, 512], BF16) for i in range(2)]
            gb = sl("gb", [128, 4, 128], F32)
            gU = sl("gU", [128, 4, 128], F32)
            Dm = sl("Dm", [128, 4, 128], F32)
            DTm = sl("DTm", [128, 4, 128], F32)
            Pm1 = sl("Pm", [128, 4, 128], F32)
            Qm1 = sl("Qm", [128, 4, 128], F32)
            TT = sl("TT", [128, 4, 128], F32)
            TTb = sl("TTb", [128, 4, 128], BF16)
            qgT = [sl("qgT%d" % i, [128, 4, 128], BF16) for i in range(2)]
            kbg = [sl("kbg%d" % i, [128, 4, 128], BF16) for i in range(2)]
            kdec = [sl("kdec%d" % i, [128, 4, 128], BF16) for i in range(2)]
            vb = [sl("vb%d" % i, [128, 4, 128], BF16) for i in range(2)]
            qkDT = [sl("qkDT%d" % i, [128, 4, 128], BF16) for i in range(2)]
            wT = [sl("wT%d" % i, [128, 4, 128], BF16) for i in range(2)]
            u_t = [sl("u_t%d" % i, [128, 512], F32) for i in range(2)]
            vnew = sl("vnew", [128, 512], BF16)
            St = sl("Sst", [128, 4, 128], F32)
            Sdec = sl("Sdec", [128, 4, 128], F32)
            Sb = sl("Sb", [128, 4, 128], BF16)
            zs = sl("zs", [128, 512], F32)
            t1r = sl("t1r", [128, 512], F32)
            sqr = sl("sqr", [128, 128], BF16)
            sso = sl("sso", [128, 4], F32)
            ot = sl("ot", [128, 512], BF16)
            B = {n: Buf(n) for n in ("gm", "onesb", "wbz", "wba", "cw", "negA", "dtb", "hn", "gates", "cr", "yc", "sc", "t1", "sqb",
                                    "gb", "gU", "Dm", "DTm", "Pm", "Qm", "TT", "TTb",
                                    "vnew", "S", "Sdec", "Sb", "zs", "t1r", "sqr", "sso", "ot")}
            D2 = {n: [Buf(n + "0"), Buf(n + "1")] for n in ("qnT", "knT", "vsT", "qgT", "kbg", "kdec", "vb", "qkDT", "wT", "u")}
            b_wbq = [Buf(), Buf()]
            b_xc = [Buf(), Buf()]
            I_A, I_B = 0, 1
            P_A, P_B, P_C = 2, 3, 4
            R_A, R_B, R_C = 5, 6, 7

            S.dma("sp", gm[:], gdnc_d[:, :, :], writes=[B["gm"]])
            S.op("pool", lambda e: e.memset(ones_b[:], 1.0), writes=[B["onesb"]])
            S.dma("pool", wbz[:], wbz_d[:, :, :], writes=[B["wbz"]])
            S.dma("pool", wba[:], wba_d[:, :, :], writes=[B["wba"]])
            S.dma("sp", cw[:], cw_d[:, :, :], writes=[B["cw"]])
            S.dma("sp", negA[:], bass.AP(alog_d.tensor, 0, [[0, 128], [1, 4]]), writes=[B["negA"]])
            S.dma("sp", dtb[:], bass.AP(dtb_d.tensor, 0, [[0, 128], [1, 4]]), writes=[B["dtb"]])
            S.dma("sp", hn_bc[:], bass.AP(hn_d.tensor, 0, [[0, 128], [1, 128]]), writes=[B["hn"]])
            S.op("act", lambda e: e.activation(out=negA[:], in_=negA[:], func=AF.Exp), reads=[B["negA"]], writes=[B["negA"]])
            S.op("dve", lambda e: e.tensor_scalar(out=negA[:], in0=negA[:], scalar1=-1.0, scalar2=None, op0=ALU.mult),
                 reads=[B["negA"]], writes=[B["negA"]])
            S.op("pool", lambda e: e.memset(cr[:], 0.0), writes=[B["cr"]])
            S.op("pool", lambda e: e.memset(St[:], 0.0), writes=[B["S"]])
            S.op("pool", lambda e: e.memset(Sb[:], 0.0), writes=[B["Sb"]])

            ba_ps = PB[0][:, 0:NT * 8].rearrange("p (t c) -> p t c", c=8)
            for i in range(NT):
                for kc in range(KC):
                    S.op("pe", lambda e: e.matmul(ba_ps[:, i, :], lhsT=hT[:, kc, i * 128:(i + 1) * 128], rhs=wba[:, kc, :],
                                                  start=(kc == 0), stop=(kc == KC - 1)),
                         reads=[b_hT[i], B["wba"]], writes=bPB[0], inc=(kc == KC - 1))
            bg = [B["gates"]]
            S.op("dve", lambda e: e.tensor_tensor(out=G["xa"][:], in0=ba_ps[:, :, 4:8], in1=dtb[:, None, :].to_broadcast([128, NT, 4]),
                                                  op=ALU.add), reads=bPB[0] + [B["dtb"]], writes=bg)
            S.op("act", lambda e: e.activation(out=G["xa"][:], in_=G["xa"][:], func=AF.Exp), reads=bg, writes=bg)
            S.op("act", lambda e: e.activation(out=G["xa"][:], in_=G["xa"][:], func=AF.Ln, bias=1.0), reads=bg, writes=bg)
            S.op("dve", lambda e: e.tensor_tensor(out=G["g"][:], in0=G["xa"][:], in1=negA[:, None, :].to_broadcast([128, NT, 4]),
                                                  op=ALU.mult), reads=bg + [B["negA"]], writes=bg)
            S.op("act", lambda e: e.activation(out=G["beta"][:], in_=ba_ps[:, :, 0:4], func=AF.Exp, scale=-1.0),
                 reads=bPB[0] + bg, writes=bg)
            S.op("act", lambda e: e.activation(out=G["beta"][:], in_=G["beta"][:], func=AF.Ln, bias=1.0), reads=bg, writes=bg)
            S.op("act", lambda e: e.activation(out=G["beta"][:], in_=G["beta"][:], func=AF.Exp, scale=-1.0), reads=bg, writes=bg)
            S.op("dve", lambda e: e.tensor_scalar(out=G["negb"][:], in0=G["beta"][:], scalar1=-1.0, scalar2=None, op0=ALU.mult),
                 reads=bg, writes=bg)
            gflat = G["g"][:].rearrange("p t c -> p (t c)")
            S.op("pe", lambda e: e.matmul(PB[1][:, 0:64], lhsT=ublk, rhs=gflat, start=True, stop=True),
                 reads=bg + [B["gm"]], writes=bPB[1], inc=False)
            S.op("pe", lambda e: e.matmul(PB[1][:, 64:128], lhsT=blk, rhs=gflat, start=True, stop=True),
                 reads=bg + [B["gm"]], writes=bPB[1], inc=False)
            S.op("pe", lambda e: e.matmul(PB[1][:, 128:192], lhsT=halfsel[0], rhs=gflat, start=True, stop=True),
                 reads=bg + [B["gm"]], writes=bPB[1], inc=False)
            S.op("pe", lambda e: e.matmul(PB[1][:, 192:256], lhsT=halfsel[1], rhs=gflat, start=True, stop=True),
                 reads=bg + [B["gm"]], writes=bPB[1])
            v3 = lambda ap: ap.rearrange("p (t c) -> p t c", c=4)
            S.op("dve", lambda e: e.tensor_copy(out=G["G"][:], in_=v3(PB[1][:, 0:64])), reads=bPB[1] + bg, writes=bg)
            S.op("dve", lambda e: e.tensor_copy(out=G["GL"][:], in_=v3(PB[1][:, 64:128])), reads=bPB[1] + bg, writes=bg)
            S.op("act", lambda e: e.activation(out=G["eGL0"][:], in_=v3(PB[1][:, 128:192]), func=AF.Exp), reads=bPB[1] + bg, writes=bg)
            S.op("act", lambda e: e.activation(out=G["eGL1"][:], in_=v3(PB[1][:, 192:256]), func=AF.Exp), reads=bPB[1] + bg, writes=bg)
            S.op("act", lambda e: e.activation(out=G["eG"][:], in_=G["G"][:], func=AF.Exp), reads=bg, writes=bg)
            S.op("dve", lambda e: e.tensor_tensor(out=G["bG"][:], in0=G["beta"][:], in1=G["eG"][:], op=ALU.mult), reads=bg, writes=bg)
            S.op("dve", lambda e: e.tensor_tensor(out=G["dG"][:], in0=G["GL"][:], in1=G["G"][:], op=ALU.subtract), reads=bg, writes=bg)
            S.op("act", lambda e: e.activation(out=G["dG"][:], in_=G["dG"][:], func=AF.Exp), reads=bg, writes=bg)
            eGLb = (G["eGL0"], G["eGL1"])
            bgr = [Buf("gates_ro")]
            bgr[0].w = B["gates"].w

            bc4 = lambda ap2: ap2[:, :, None].to_broadcast([128, 4, 128])
            hb4 = lambda ap2: ap2[:, None, :].to_broadcast([128, 4, 128])
            h4 = lambda pb: PB[pb][:, :].rearrange("p (h c) -> p h c", c=128)

            def T_I():
                n_w = 0
                for blkI in range(4):
                    if blkI >= 2:
                        yield ("wait", ("Rblk", blkI - 2))
                    par = blkI % 2
                    bsl = slice(blkI * 512, (blkI + 1) * 512)
                    for jc in range(12):
                        wt = wbq[n_w % 2]; bwt = b_wbq[n_w % 2]
                        xcb = xc[n_w % 2]; bxc = b_xc[n_w % 2]
                        n_w += 1
                        S.dma("pool", wt[:], wbq_d[jc], writes=[bwt])
                        for kc in range(KC):
                            S.op("pe", lambda e: e.matmul(PB[I_A][:, :], lhsT=wt[:, kc, :], rhs=hT[:, kc, bsl],
                                                          start=(kc == 0), stop=(kc == KC - 1)),
                                 reads=[bwt] + b_hT[4 * blkI:4 * blkI + 4], writes=bPB[I_A], inc=(kc == KC - 1))
                        yield
                        S.op("pool", lambda e: e.tensor_copy(out=xcb[:, 0:3], in_=cr[:, jc, :]), reads=[B["cr"]], writes=[bxc])
                        S.op("act", lambda e: e.activation(out=xcb[:, 3:515], in_=PB[I_A][:, :], func=AF.Copy),
                             reads=bPB[I_A], writes=[bxc])
                        S.op("pool", lambda e: e.tensor_copy(out=cr[:, jc, :], in_=xcb[:, 512:515]), reads=[bxc], writes=[B["cr"]])
                        yield
                        S.op("dve", lambda e: e.tensor_scalar(out=yc[:], in0=xcb[:, 0:512], scalar1=cw[:, jc, 0:1], scalar2=None,
                                                              op0=ALU.mult), reads=[bxc, B["cw"]], writes=[B["yc"]])
                        for tap in range(1, 4):
                            S.op("dve", lambda e: e.scalar_tensor_tensor(out=yc[:], in0=xcb[:, tap:tap + 512], scalar=cw[:, jc, tap:tap + 1],
                                                                         in1=yc[:], op0=ALU.mult, op1=ALU.add),
                                 reads=[bxc, B["cw"], B["yc"]], writes=[B["yc"]])
                            yield
                        S.op("act", lambda e: e.activation(out=t1[:], in_=yc[:], func=AF.Exp, scale=-1.0), reads=[B["yc"]], writes=[B["t1"]])
                        S.op("act", lambda e: e.activation(out=t1[:], in_=t1[:], func=AF.Ln, bias=1.0), reads=[B["t1"]], writes=[B["t1"]])
                        S.op("act", lambda e: e.activation(out=t1[:], in_=t1[:], func=AF.Exp, scale=-1.0), reads=[B["t1"]], writes=[B["t1"]])
                        yield
                        if jc >= 8:
                            S.op("dve", lambda e: e.tensor_tensor(out=vsT[par][:, jc - 8, :], in0=yc[:], in1=t1[:], op=ALU.mult),
                                 reads=[B["yc"], B["t1"]], writes=[D2["vsT"][par]])
                            yield
                            continue
                        S.op("dve", lambda e: e.tensor_tensor(out=sc[:], in0=yc[:], in1=t1[:], op=ALU.mult),
                             reads=[B["yc"], B["t1"]], writes=[B["sc"]])
                        S.op("act", lambda e: e.activation(out=sqb[:], in_=sc[:], func=AF.Square), reads=[B["sc"]], writes=[B["sqb"]])
                        yield
                        S.op("pe", lambda e: e.matmul(PB[I_B][:, :], lhsT=ones_b[:], rhs=sqb[:], start=True, stop=True),
                             reads=[B["onesb"], B["sqb"]], writes=bPB[I_B])
                        S.op("act", lambda e: e.activation(out=t1[:], in_=PB[I_B][:, :], func=AF.Ln, bias=EPS),
                             reads=bPB[I_B] + [B["t1"]], writes=[B["t1"]])
                        isq = jc < 4
                        S.op("act", lambda e: e.activation(out=t1[:], in_=t1[:], func=AF.Exp, scale=-0.5,
                                                           bias=(-0.5 * math.log(128.0) if isq else 0.0)),
                             reads=[B["t1"]], writes=[B["t1"]])
                        yield
                        dstT = qnT[par] if isq else knT[par]
                        bd = D2["qnT"][par] if isq else D2["knT"][par]
                        S.op("dve", lambda e: e.tensor_tensor(out=dstT[:, jc % 4, :], in0=sc[:], in1=t1[:], op=ALU.mult),
                             reads=[B["sc"], B["t1"]], writes=[bd])
                        yield
                    yield ("done", ("I", blkI))

            def T_P():
                for i in range(NT):
                    blkI, tl = divmod(i, 4)
                    par = blkI % 2
                    tp = i % 2
                    yield ("wait", ("I", blkI))
                    if i >= 2:
                        yield ("wait", ("R", i - 2))
                    csl = slice(tl * 128, (tl + 1) * 128)
                    qn, kn, vs = qnT[par], knT[par], vsT[par]
                    bqn, bkn, bvs = D2["qnT"][par], D2["knT"][par], D2["vsT"][par]
                    pa, pb_, pc = h4(P_A), h4(P_B), h4(P_C)
                    S.op("dve", lambda e: e.tensor_copy(out=gb[:], in_=bc4(G["g"][:, i, :])), reads=bgr, writes=[B["gb"]])
                    for h in range(4):
                        S.op("pe", lambda e: e.matmul(pa[:, h, :], lhsT=gb[:, h, :], rhs=ublk, start=True, stop=True),
                             reads=[B["gb"], B["gm"]], writes=bPB[P_A], inc=(h == 3))
                    yield
                    S.op("act", lambda e: e.activation(out=Dm[:], in_=pa, func=AF.Exp), reads=bPB[P_A], writes=[B["Dm"]])
                    S.op("dve", lambda e: e.tensor_tensor(out=qgT[tp][:], in0=qn[:, :, csl], in1=Dm[:], op=ALU.mult),
                         reads=[bqn, B["Dm"]], writes=[D2["qgT"][tp]])
                    yield
                    p3b = PB[P_B][:, :].bitcast(BF16).rearrange("p (a h c) -> p a h c", a=2, h=4)
                    for h in range(4):
                        S.op("pe", lambda e: e.transpose(p3b[:, 0, h, :], kn[:, h, csl], ident[:]),
                             reads=[bkn, b_const], writes=bPB[P_B], inc=False)
                    for h in range(4):
                        S.op("pe", lambda e: e.transpose(p3b[:, 1, h, :], vs[:, h, csl], ident[:]),
                             reads=[bvs, b_const], writes=bPB[P_B], inc=(h == 3))
                    yield
                    S.op("dve", lambda e: e.tensor_tensor(out=kbg[tp][:], in0=p3b[:, 0], in1=bc4(G["bG"][:, i, :]), op=ALU.mult),
                         reads=bPB[P_B] + bgr, writes=[D2["kbg"][tp]])
                    S.op("dve", lambda e: e.tensor_tensor(out=kdec[tp][:], in0=p3b[:, 0], in1=bc4(G["dG"][:, i, :]), op=ALU.mult),
                         reads=bPB[P_B] + bgr, writes=[D2["kdec"][tp]])
                    yield
                    S.op("dve", lambda e: e.tensor_tensor(out=vb[tp][:], in0=p3b[:, 1], in1=bc4(G["beta"][:, i, :]), op=ALU.mult),
                         reads=bPB[P_B] + bgr, writes=[D2["vb"][tp]])
                    S.op("pool", lambda e: e.tensor_tensor(out=gU[:], in0=hb4(ublk), in1=bc4(G["g"][:, i, :]), op=ALU.mult),
                         reads=bgr + [B["gm"]], writes=[B["gU"]])
                    yield
                    for h in range(4):
                        S.op("pe", lambda e: e.matmul(pa[:, h, :], lhsT=gU[:, h, :], rhs=strictblk, start=True, stop=True),
                             reads=[B["gU"], B["gm"]], writes=bPB[P_A], inc=(h == 3))
                    for h in range(4):
                        S.op("pe", lambda e: e.matmul(pb_[:, h, :], lhsT=strictblk, rhs=gU[:, h, :], start=True, stop=True),
                             reads=[B["gU"], B["gm"]], writes=bPB[P_B], inc=(h == 3))
                    yield
                    S.op("act", lambda e: e.activation(out=Dm[:], in_=pa, func=AF.Exp), reads=bPB[P_A] + [B["Dm"]], writes=[B["Dm"]])
                    S.op("act", lambda e: e.activation(out=DTm[:], in_=pb_, func=AF.Exp), reads=bPB[P_B], writes=[B["DTm"]])
                    yield
                    for h in range(4):
                        S.op("pe", lambda e: e.matmul(pa[:, h, :], lhsT=kn[:, h, csl], rhs=kn[:, h, csl], start=True, stop=True),
                             reads=[bkn], writes=bPB[P_A], inc=(h == 3))
                    for h in range(4):
                        S.op("pe", lambda e: e.matmul(pb_[:, h, :], lhsT=kn[:, h, csl], rhs=qn[:, h, csl], start=True, stop=True),
                             reads=[bkn, bqn], writes=bPB[P_B], inc=(h == 3))
                    yield
                    S.op("pool", lambda e: e.tensor_tensor(out=Dm[:], in0=Dm[:], in1=hb4(strictblk), op=ALU.mult),
                         reads=[B["Dm"], B["gm"]], writes=[B["Dm"]])
                    S.op("pool", lambda e: e.tensor_tensor(out=Dm[:], in0=Dm[:], in1=bc4(G["negb"][:, i, :]), op=ALU.mult),
                         reads=[B["Dm"]] + bgr, writes=[B["Dm"]])
                    yield
                    S.op("dve", lambda e: e.tensor_tensor(out=Pm1[:], in0=pa, in1=Dm[:], op=ALU.mult),
                         reads=bPB[P_A] + [B["Dm"]], writes=[B["Pm"]])
                    S.op("pool", lambda e: e.tensor_tensor(out=DTm[:], in0=DTm[:], in1=hb4(ublk), op=ALU.mult),
                         reads=[B["DTm"], B["gm"]], writes=[B["DTm"]])
                    yield
                    S.op("dve", lambda e: e.tensor_tensor(out=qkDT[tp][:], in0=pb_, in1=DTm[:], op=ALU.mult),
                         reads=bPB[P_B] + [B["DTm"]], writes=[D2["qkDT"][tp]])
                    for h in range(4):
                        S.op("pe", lambda e: e.matmul(pc[:, h, :], lhsT=Pm1[:, h, :], rhs=ident_f, start=True, stop=True),
                             reads=[B["Pm"], B["gm"]], writes=bPB[P_C], inc=(h == 3))
                    yield
                    S.op("act", lambda e: e.activation(out=Qm1[:], in_=pc, func=AF.Copy), reads=bPB[P_C], writes=[B["Qm"]])
                    S.op("dve", lambda e: e.tensor_tensor(out=TT[:], in0=pc, in1=hb4(ident_f), op=ALU.add),
                         reads=bPB[P_C] + [B["gm"]], writes=[B["TT"]])
                    yield
                    for lvl in range(5):
                        for h in range(4):
                            S.op("pe", lambda e: e.matmul(pa[:, h, :], lhsT=Qm1[:, h, :], rhs=Pm1[:, h, :], start=True, stop=True),
                                 reads=[B["Qm"], B["Pm"]], writes=bPB[P_A], inc=(h == 3))
                        if lvl < 4:
                            for h in range(4):
                                S.op("pe", lambda e: e.matmul(pb_[:, h, :], lhsT=Pm1[:, h, :], rhs=Qm1[:, h, :], start=True, stop=True),
                                     reads=[B["Qm"], B["Pm"]], writes=bPB[P_B], inc=(h == 3))
                        yield
                        S.op("act", lambda e: e.activation(out=Pm1[:], in_=pa, func=AF.Copy), reads=bPB[P_A], writes=[B["Pm"]])
                        if lvl < 4:
                            S.op("dve", lambda e: e.tensor_copy(out=Qm1[:], in_=pb_), reads=bPB[P_B], writes=[B["Qm"]])
                        yield
                        for h in range(4):
                            S.op("pe", lambda e: e.matmul(pc[:, h, :], lhsT=Pm1[:, h, :], rhs=TT[:, h, :], start=True, stop=True),
                                 reads=[B["Pm"], B["TT"]], writes=bPB[P_C], inc=(h == 3))
                        yield
                        S.op("dve", lambda e: e.tensor_tensor(out=TT[:], in0=pc, in1=TT[:], op=ALU.add),
                             reads=bPB[P_C] + [B["TT"]], writes=[B["TT"]])
                        yield
                    S.op("act", lambda e: e.activation(out=TTb[:], in_=TT[:], func=AF.Copy), reads=[B["TT"]], writes=[B["TTb"]])
                    yield
                    for h in range(4):
                        S.op("pe", lambda e: e.matmul(pa[:, h, :], lhsT=TTb[:, h, :], rhs=vb[tp][:, h, :], start=True, stop=True),
                             reads=[B["TTb"], D2["vb"][tp]], writes=bPB[P_A], inc=(h == 3))
                    for h in range(4):
                        S.op("pe", lambda e: e.matmul(pb_[:, h, :], lhsT=kbg[tp][:, h, :], rhs=TTb[:, h, :], start=True, stop=True),
                             reads=[B["TTb"], D2["kbg"][tp]], writes=bPB[P_B], inc=(h == 3))
                    yield
                    S.op("act", lambda e: e.activation(out=u_t[tp][:], in_=PB[P_A][:, :], func=AF.Copy), reads=bPB[P_A], writes=[D2["u"][tp]])
                    S.op("dve", lambda e: e.tensor_copy(out=wT[tp][:], in_=pb_), reads=bPB[P_B], writes=[D2["wT"][tp]])
                    yield ("done", ("P", i))

            def T_R():
                for i in range(NT):
                    tp = i % 2
                    yield ("wait", ("P", i))
                    for ch in range(2):
                        pr = slice(64 * ch, 64 * ch + 64)
                        pc_ = slice(64 * ch, 64 * ch + 64)
                        S.op("pool", lambda e: e.tensor_tensor(out=Sdec[:], in0=St[:], in1=bc4(eGLb[ch][:, i, :]), op=ALU.mult),
                             reads=[B["S"]] + bgr, writes=[B["Sdec"]])
                        for h in range(4):
                            S.op("pe", lambda e: e.matmul(PB[R_A][pr, h * 128:(h + 1) * 128], lhsT=wT[tp][:, h, pc_], rhs=Sb[:, h, :],
                                                          start=True, stop=True),
                                 reads=[D2["wT"][tp], B["Sb"]], writes=bPB[R_A], inc=(h == 3))
                        yield
                        S.op("dve", lambda e: e.tensor_tensor(out=vnew[pr, :], in0=u_t[tp][pr, :], in1=PB[R_A][pr, :], op=ALU.subtract),
                             reads=bPB[R_A] + [D2["u"][tp]], writes=[B["vnew"]])
                        yield
                        for h in range(4):
                            S.op("pe", lambda e: e.matmul(PB[R_B][pr, h * 128:(h + 1) * 128], lhsT=qgT[tp][:, h, pc_], rhs=Sb[:, h, :],
                                                          start=True, stop=False),
                                 reads=[D2["qgT"][tp], B["Sb"]], writes=bPB[R_B], inc=False)
                            S.op("pe", lambda e: e.matmul(PB[R_B][pr, h * 128:(h + 1) * 128], lhsT=qkDT[tp][pr, h, pc_],
                                                          rhs=vnew[pr, h * 128:(h + 1) * 128], start=False, stop=True),
                                 reads=[D2["qkDT"][tp], B["vnew"]], writes=bPB[R_B], inc=(h == 3))
                        yield
                        for h in range(4):
                            S.op("pe", lambda e: e.matmul(PB[R_C][:, h * 128:(h + 1) * 128], lhsT=kdec[tp][pr, h, :],
                                                          rhs=vnew[pr, h * 128:(h + 1) * 128], start=True, stop=True),
                                 reads=[D2["kdec"][tp], B["vnew"]], writes=bPB[R_C], inc=(h == 3))
                        yield
                        Sf = St[:].rearrange("p h c -> p (h c)")
                        Sdf = Sdec[:].rearrange("p h c -> p (h c)")
                        Sbf = Sb[:].rearrange("p h c -> p (h c)")
                        S.op("dve", lambda e: e.tensor_tensor(out=Sbf, in0=PB[R_C][:, :], in1=Sdf, op=ALU.add),
                             reads=bPB[R_C] + [B["Sdec"]], writes=[B["Sb"]])
                        S.op("dve", lambda e: e.tensor_tensor(out=Sf, in0=PB[R_C][:, :], in1=Sdf, op=ALU.add),
                             reads=bPB[R_C] + [B["Sdec"]], writes=[B["S"]])
                        yield
                    for kc in range(KC):
                        S.op("pe", lambda e: e.matmul(PB[R_A][:, :], lhsT=hT[:, kc, i * 128:(i + 1) * 128], rhs=wbz[:, kc, :],
                                                      start=(kc == 0), stop=(kc == KC - 1)),
                             reads=[b_hT[i], B["wbz"]], writes=bPB[R_A], inc=(kc == KC - 1))
                    yield
                    S.op("act", lambda e: e.activation(out=zs[:], in_=PB[R_A][:, :], func=AF.Exp, scale=-1.0), reads=bPB[R_A], writes=[B["zs"]])
                    S.op("act", lambda e: e.activation(out=zs[:], in_=zs[:], func=AF.Ln, bias=1.0), reads=[B["zs"]], writes=[B["zs"]])
                    S.op("act", lambda e: e.activation(out=zs[:], in_=zs[:], func=AF.Exp, scale=-1.0), reads=[B["zs"]], writes=[B["zs"]])
                    yield
                    S.op("dve", lambda e: e.tensor_tensor(out=zs[:], in0=PB[R_A][:, :], in1=zs[:], op=ALU.mult),
                         reads=bPB[R_A] + [B["zs"]], writes=[B["zs"]])
                    zs3 = zs[:].rearrange("p (h c) -> p h c", c=128)
                    S.op("pool", lambda e: e.tensor_tensor(out=zs3, in0=zs3, in1=hb4(hn_bc[:]), op=ALU.mult),
                         reads=[B["zs"], B["hn"]], writes=[B["zs"]])
                    yield
                    for h in range(4):
                        S.op("act", lambda e: e.activation(out=sqr[:], in_=PB[R_B][:, h * 128:(h + 1) * 128], func=AF.Square,
                                                           accum_out=sso[:, h:h + 1]),
                             reads=bPB[R_B], writes=[B["sqr"], B["sso"]])
                    S.op("act", lambda e: e.activation(out=sso[:], in_=sso[:], func=AF.Ln, scale=1.0 / 128, bias=EPS),
                         reads=[B["sso"]], writes=[B["sso"]])
                    S.op("act", lambda e: e.activation(out=sso[:], in_=sso[:], func=AF.Exp, scale=-0.5), reads=[B["sso"]], writes=[B["sso"]])
                    yield
                    t13 = t1r[:].rearrange("p (h c) -> p h c", c=128)
                    S.op("dve", lambda e: e.tensor_tensor(out=t13, in0=h4(R_B), in1=bc4(sso[:]), op=ALU.mult),
                         reads=bPB[R_B] + [B["sso"], B["t1r"]], writes=[B["t1r"]])
                    S.op("dve", lambda e: e.tensor_tensor(out=ot[:], in0=t1r[:], in1=zs[:], op=ALU.mult),
                         reads=[B["t1r"], B["zs"]], writes=[B["ot"]])
                    yield
                    p3o = PB[R_C][:, :].bitcast(BF16)[:, 0:512].rearrange("p (h c) -> p h c", c=128)
                    for h in range(4):
                        S.op("pe", lambda e: e.transpose(p3o[:, h, :], ot[:, h * 128:(h + 1) * 128], ident[:]),
                             reads=[B["ot"], b_const], writes=bPB[R_C], inc=(h == 3))
                    S.op("act", lambda e: e.activation(out=og[:, 4:8, i * 128:(i + 1) * 128], in_=p3o, func=AF.Copy),
                         reads=bPB[R_C], writes=[b_og[4 + hh][i // 4] for hh in range(4)])
                    yield ("done", ("R", i))
                    if i % 4 == 3:
                        yield ("done", ("Rblk", i // 4))

            run_threads([T_I(), T_P(), T_R()], BW)

        def layer1(seq, last):
            nonlocal sb_l, b_l
            with contextlib.ExitStack() as st1:
                st2 = st1.enter_context(contextlib.ExitStack())
                cur = [st1]

                def sl(name, shape, dt):
                    uid[0] += 1
                    return cur[0].enter_context(nc.sbuf_tensor("t%d_%s" % (uid[0], name), list(shape), dt))
                sb_l = {}
                b_l = {}
                sb_l["ss2"] = sl("ss2", [128, 2 * NT], F32); b_l["ss2"] = Buf()
                sb_l["rs2"] = sl("rs2", [128, NT], F32); b_l["rs2"] = Buf()
                sb_l["tmpf"] = [sl("tmpf%d" % i, [128, 512], F32) for i in range(2)]; b_l["tmpf"] = [Buf(), Buf()]
                wo = sl("wo", [128, 8, D], BF16)
                cur[0] = st2
                qT = [[sl("qT%d_%d" % (s_, i), [128, SEQ], BF16) for i in range(2)] for s_ in range(2)]
                kT = [[sl("kT%d_%d" % (s_, i), [128, SEQ], BF16) for i in range(2)] for s_ in range(2)]
                Vx = [[sl("Vx%d_%d" % (s_, i), [128, NT, 128], BF16) for i in range(2)] for s_ in range(2)]
                vT5 = sl("vT5", [128, 512], BF16)
                b_vT5 = Buf()
                wq = sl("wq", [128, KC, 128], BF16)
                wk = sl("wk", [128, KC, 128], BF16)
                wv = sl("wv", [128, KC, 128], BF16)
                wz = [sl("wz%d" % i, [128, KC, 128], BF16) for i in range(2)]
                wf = sl("wf", [128, KC, 16], BF16)
                fb_bc = sl("fb_bc", [128, 16], F32)
                flb = sl("flb", [128, NT, 16], F32)
                nlf = sl("nlf", [128, NT, 16], F32)
                NC_ = sl("NC", [128, NT, 16], F32)
                carry = sl("carry", [128, 16], F32)
                carryT = sl("carryT", [16, 1], F32)
                cT = sl("cT", [16, SEQ], F32)
                cHL = sl("cHL", [16, 2, SEQ], BF16)
                pt = [sl("pt%d" % i, [128, 512], BF16) for i in range(3)]
                e_t = sb_l["tmpf"][0]
                sums = sb_l["tmpf"][1]
                den = sl("den", [128, 512], F32)
                tt = den
                b_qT = [[Buf(), Buf()] for _ in range(2)]; b_kT = [[Buf(), Buf()] for _ in range(2)]
                b_Vx = [[Buf(), Buf()] for _ in range(2)]
                b_qaug = [[Buf(), Buf()] for _ in range(2)]
                b_wq, b_wk, b_wv = Buf(), Buf(), Buf()
                b_wz = [Buf(), Buf()]
                b_wf, b_fb, b_flb, b_nlf, b_NC, b_carry, b_carryT, b_cT, b_cTt, b_cHL = [Buf() for _ in range(10)]
                b_pt = [Buf() for _ in range(3)]
                b_e, b_sums, b_den = b_l["tmpf"][0], b_l["tmpf"][1], Buf()
                b_tt = b_den

                try:
                    load_norms(1)
                    S.dma("pool", wo[:], woc_d[:, :, :], writes=[b_wo])
                    S.dma("pool", wf[:], wf_d[:, :, :], writes=[b_wf])
                    S.dma("sp", fb_bc[:], bass.AP(fb_d.tensor, 0, [[0, 128], [1, 16]]), writes=[b_fb])
                    for s_ in range(2):
                        for i in range(2):
                            S.op("pool", lambda e: e.memset(kT[s_][i][64:66, :], 1.0), writes=[b_kT[s_][i]])
                        S.op("pool", lambda e: e.memset(Vx[s_][0][:, :, 64:128], 1.0), writes=[b_Vx[s_][0]])
                        S.op("pool", lambda e: e.memset(Vx[s_][1][:, :, 0:64], 1.0), writes=[b_Vx[s_][1]])
                    S.op("pool", lambda e: e.memset(carry[:], 0.0), writes=[b_carry])
                    S.op("pool", lambda e: e.memset(carryT[:], 0.0), writes=[b_carryT])

                    l1src = x1s_d[seq] if 0 in layers else x_d[seq]
                    l1b = b_x1 if 0 in layers else b_xdram
                    prenorm(l1src, l1b)
                    if STAGE <= 1:
                        raise StopStage()

                    fl_ps = PB[0][:, 0:NT * 16].rearrange("p (t h) -> p t h", h=16)
                    for i in range(NT):
                        for kc in range(KC):
                            S.op("pe", lambda e: e.matmul(fl_ps[:, i, :], lhsT=hT[:, kc, i * 128:(i + 1) * 128],
                                                          rhs=wf[:, kc, :], start=(kc == 0), stop=(kc == KC - 1)),
                                 reads=[b_hT[i], b_wf], writes=bPB[0], inc=(kc == KC - 1))
                    S.op("dve", lambda e: e.tensor_tensor(out=flb[:], in0=fl_ps, in1=fb_bc[:, None, :].to_broadcast([128, NT, 16]),
                                                          op=ALU.add),
                         reads=bPB[0] + [b_fb], writes=[b_flb])
                    S.op("act", lambda e: e.activation(out=flb[:], in_=flb[:], func=AF.Exp, scale=-1.0),
                         reads=[b_flb], writes=[b_flb])
                    S.op("act", lambda e: e.activation(out=nlf[:], in_=flb[:], func=AF.Ln, bias=1.0),
                         reads=[b_flb], writes=[b_nlf])
                    for i in range(NT):
                        bk = 1 + (i % 2)
                        c1 = PB[bk][:, 0:16]
                        c2 = PB[bk][:, 16:32]
                        c3 = PB[bk][0:16, 32:32 + 129]
                        S.op("pe", lambda e: e.matmul(c1, lhsT=uext[:, 0:128], rhs=nlf[:, i, :], start=True, stop=True),
                             reads=[b_nlf, b_const], writes=bPB[bk], inc=False)
                        S.op("pe", lambda e: e.matmul(c2, lhsT=ones_f[:], rhs=nlf[:, i, :], start=True, stop=True),
                             reads=[b_nlf, b_const], writes=bPB[bk], inc=False)
                        S.op("pe", lambda e: e.matmul(c3, lhsT=nlf[:, i, :], rhs=uext[:, :], start=True, stop=True),
                             reads=[b_nlf, b_const], writes=bPB[bk])
                        S.op("dve", lambda e: e.tensor_tensor(out=NC_[:, i, :], in0=c1, in1=carry[:], op=ALU.add),
                             reads=bPB[bk] + [b_carry], writes=[b_NC])
                        S.op("dve", lambda e: e.tensor_tensor(out=carry[:], in0=c2, in1=carry[:], op=ALU.add),
                             reads=bPB[bk] + [b_carry], writes=[b_carry])
                        S.op("dve", lambda e: e.tensor_scalar(out=cT[:, i * 128:(i + 1) * 128], in0=c3[:, 0:128],
                                                              scalar1=carryT[:, 0:1], scalar2=-8.0,
                                                              op0=ALU.add, op1=ALU.mult),
                             reads=bPB[bk] + [b_carryT], writes=[b_cT])
                        S.op("dve", lambda e: e.tensor_tensor(out=carryT[:], in0=c3[:, 128:129], in1=carryT[:], op=ALU.add),
                             reads=bPB[bk] + [b_carryT], writes=[b_carryT])
                    S.op("dve", lambda e: e.tensor_copy(out=cHL[:, 0, :], in_=cT[:]), reads=[b_cT], writes=[b_cHL])
                    S.op("dve", lambda e: e.tensor_tensor(out=cT[:], in0=cT[:], in1=cHL[:, 0, :], op=ALU.subtract),
                         reads=[b_cT, b_cHL], writes=[b_cT])
                    S.op("dve", lambda e: e.tensor_copy(out=cHL[:, 1, :], in_=cT[:]), reads=[b_cT, b_cHL], writes=[b_cHL])

                    if STAGE <= 2:
                        raise StopStage()
                    IB = 7

                    def T_in():
                        for p in range(8):
                            if p >= 2:
                                yield ("wait", ("att", p - 2))
                            sp_ = p % 2
                            qTp, kTp, Vxp = qT[sp_], kT[sp_], Vx[sp_]
                            bq, bk_, bV, bqa = b_qT[sp_], b_kT[sp_], b_Vx[sp_], b_qaug[sp_]
                            S.dma("pool", wq[:], wc_d[p, 0], writes=[b_wq])
                            S.dma("pool", wk[:], wc_d[p, 1], writes=[b_wk])
                            S.dma("pool", wv[:], wc_d[p, 2], writes=[b_wv])
                            S.dma("pool", wz[p % 2][:], wc_d[p, 3], writes=[b_wz[p % 2]])
                            for hh in range(2):
                                for r in range(2):
                                    S.dma("sp", qTp[hh][64 + r:65 + r, :], cHL[2 * p + hh:2 * p + hh + 1, r, :],
                                          reads=[b_cHL], writes=[bqa[hh]])
                            yield
                            for (wt, bw, dstT, bdst) in ((wq, b_wq, qTp, bq), (wk, b_wk, kTp, bk_)):
                                for t4 in range(4):
                                    for kc in range(KC):
                                        S.op("pe", lambda e: e.matmul(PB[IB][:, :], lhsT=wt[:, kc, :],
                                                                      rhs=hT[:, kc, t4 * 512:(t4 + 1) * 512],
                                                                      start=(kc == 0), stop=(kc == KC - 1)),
                                             reads=[bw] + b_hT[4 * t4:4 * t4 + 4], writes=bPB[IB], inc=(kc == KC - 1))
                                    yield
                                    S.op("dve", lambda e: e.tensor_copy(out=dstT[0][0:64, t4 * 512:(t4 + 1) * 512],
                                                                        in_=PB[IB][0:64, :]),
                                         reads=bPB[IB][0:1], writes=[bdst[0]])
                                    S.op("dve", lambda e: e.tensor_copy(out=dstT[1][0:64, t4 * 512:(t4 + 1) * 512],
                                                                        in_=PB[IB][64:128, :]),
                                         reads=bPB[IB][1:2], writes=[bdst[1]])
                                    yield
                            for t4 in range(4):
                                for kc in range(KC):
                                    S.op("pe", lambda e: e.matmul(PB[IB][:, :], lhsT=wv[:, kc, :],
                                                                  rhs=hT[:, kc, t4 * 512:(t4 + 1) * 512],
                                                                  start=(kc == 0), stop=(kc == KC - 1)),
                                         reads=[b_wv] + b_hT[4 * t4:4 * t4 + 4], writes=bPB[IB], inc=(kc == KC - 1))
                                yield
                                S.op("dve", lambda e: e.tensor_copy(out=vT5[:], in_=PB[IB][:, :]), reads=bPB[IB], writes=[b_vT5])
                                yield
                                pbf = PB[IB][:, :].bitcast(BF16)[:, 0:512].rearrange("p (j c) -> p j c", c=128)
                                for j in range(4):
                                    S.op("pe", lambda e: e.transpose(pbf[:, j, :], vT5[:, j * 128:(j + 1) * 128], ident[:]),
                                         reads=[b_vT5, b_const], writes=bPB[IB], inc=(j == 3))
                                yield
                                S.op("dve", lambda e: e.tensor_copy(out=Vxp[0][:, 4 * t4:4 * t4 + 4, 0:64], in_=pbf[:, :, 0:64]),
                                     reads=bPB[IB], writes=[bV[0]])
                                S.op("dve", lambda e: e.tensor_copy(out=Vxp[1][:, 4 * t4:4 * t4 + 4, 64:128], in_=pbf[:, :, 64:128]),
                                     reads=bPB[IB], writes=[bV[1]])
                                yield
                            yield ("done", ("in", p))

                    def T_att():
                        deferred = []
                        for p in range(8):
                            yield ("wait", ("in", p))
                            sp_ = p % 2
                            qTp, kTp, Vxp = qT[sp_], kT[sp_], Vx[sp_]
                            bq, bk_, bV, bqa = b_qT[sp_], b_kT[sp_], b_Vx[sp_], b_qaug[sp_]
                            jobs = []
                            for Qc in range(4):
                                for kt in range(4 * Qc + 4):
                                    for hh in range(2):
                                        jobs.append((Qc, kt, hh))

                            def emit_pv(n):
                                Qc, kt, hh = jobs[n]
                                o = max(0, kt - 4 * Qc) * 128
                                N = 512 - o
                                abk = 2 + 2 * (Qc % 2) + hh
                                S.op("pe", lambda e: e.matmul(PB[abk][:, o:512], lhsT=Vxp[hh][:, kt, :], rhs=pt[n % 3][:, 0:N],
                                                              start=(kt == 0), stop=(kt == 4 * Qc + 3)),
                                     reads=[bV[hh], b_pt[n % 3]], writes=bPB[abk])
                                if kt == 4 * Qc + 3 and hh == 1:
                                    emit_epilogue(Qc)

                            def emit_epilogue(Qc, p=p):
                                zb = 6
                                a0 = 2 + 2 * (Qc % 2)
                                a1 = a0 + 1
                                qs = slice(Qc * 512, (Qc + 1) * 512)
                                for kc in range(KC):
                                    S.op("pe", lambda e: e.matmul(PB[zb][:, :], lhsT=wz[p % 2][:, kc, :], rhs=hT[:, kc, qs],
                                                                  start=(kc == 0), stop=(kc == KC - 1)),
                                         reads=[b_wz[p % 2]] + b_hT[4 * Qc:4 * Qc + 4], writes=bPB[zb], inc=(kc == KC - 1))

                                def s1():
                                    S.op("act", lambda e: e.activation(out=e_t[:], in_=PB[zb][:, :], func=AF.Exp, scale=-1.0),
                                         reads=bPB[zb], writes=[b_e])
                                    S.op("dve", lambda e: e.tensor_copy(out=sums[0:64, :], in_=PB[a0][64:128, :]),
                                         reads=bPB[a0][1:2], writes=[b_sums])
                                    S.op("dve", lambda e: e.tensor_copy(out=sums[64:128, :], in_=PB[a1][0:64, :]),
                                         reads=bPB[a1][0:1], writes=[b_sums])

                                def s2():
                                    S.op("dve", lambda e: e.scalar_tensor_tensor(out=den[:], in0=e_t[:], scalar=1.0, in1=sums[:],
                                                                                 op0=ALU.add, op1=ALU.mult),
                                         reads=[b_e, b_sums], writes=[b_den])

                                def s3():
                                    S.op("act", lambda e: e.activation(out=den[:], in_=den[:], func=AF.Ln), reads=[b_den], writes=[b_den])
                                    S.op("act", lambda e: e.activation(out=den[:], in_=den[:], func=AF.Exp, scale=-1.0),
                                         reads=[b_den], writes=[b_den])

                                def s4():
                                    S.op("dve", lambda e: e.tensor_tensor(out=tt[:], in0=PB[zb][:, :], in1=den[:], op=ALU.mult),
                                         reads=bPB[zb] + [b_den], writes=[b_tt])
                                    S.op("dve", lambda e: e.tensor_tensor(out=og[0:64, p, qs], in0=PB[a0][0:64, :], in1=tt[0:64, :],
                                                                          op=ALU.mult),
                                         reads=bPB[a0][0:1] + [b_tt], writes=[b_og[p][Qc]])
                                    S.op("dve", lambda e: e.tensor_tensor(out=og[64:128, p, qs], in0=PB[a1][64:128, :], in1=tt[64:128, :],
                                                                          op=ALU.mult),
                                         reads=bPB[a1][1:2] + [b_tt], writes=[b_og[p][Qc]])
                                for dl, fn in ((2, s1), (4, s2), (6, s3), (8, s4)):
                                    deferred.append([dl, fn])

                            for n, (Qc, kt, hh) in enumerate(jobs):
                                o = max(0, kt - 4 * Qc) * 128
                                N = 512 - o
                                q0 = Qc * 512 + o
                                h = 2 * p + hh
                                sbk = n % 2
                                S.op("pe", lambda e: e.matmul(PB[sbk][:, 0:N], lhsT=kTp[hh][0:66, kt * 128:(kt + 1) * 128],
                                                              rhs=qTp[hh][0:66, q0:q0 + N], start=True, stop=True),
                                     reads=[bk_[hh], bq[hh], bqa[hh]], writes=bPB[sbk])
                                S.op("act", lambda e: e.activation(out=pt[n % 3][:, 0:N], in_=PB[sbk][:, 0:N], func=AF.Exp,
                                                                   scale=0.125, bias=NC_[:, kt, h:h + 1]),
                                     reads=bPB[sbk] + [b_NC], writes=[b_pt[n % 3]])
                                if kt >= 4 * Qc:
                                    S.op("pool", lambda e: e.tensor_tensor(out=pt[n % 3][:, 0:128], in0=pt[n % 3][:, 0:128],
                                                                           in1=maskb[:], op=ALU.mult),
                                         reads=[b_pt[n % 3], b_const], writes=[b_pt[n % 3]])
                                if n >= 1:
                                    emit_pv(n - 1)
                                for dfr in list(deferred):
                                    dfr[0] -= 1
                                    if dfr[0] <= 0:
                                        deferred.remove(dfr)
                                        dfr[1]()
                                yield
                            emit_pv(len(jobs) - 1)
                            yield ("done", ("att", p))
                        while deferred:
                            for dfr in list(deferred):
                                dfr[0] -= 1
                                if dfr[0] <= 0:
                                    deferred.remove(dfr)
                                    dfr[1]()

                    run_threads([T_in(), T_att()], L1W)
                except StopStage:
                    pass
                cur[0] = st1
                l1src = x1s_d[seq] if 0 in layers else x_d[seq]
                l1b = b_x1 if 0 in layers else b_xdram
                outproj_residual(l1src, l1b, out_d[seq], b_outd, wo)
                S.fence()
                st2.close()

        sb_l = None
        b_l = None
        for seq in range(nseq):
            if 0 in layers:
                layer0(seq, 1 not in layers)
            if 1 in layers:
                layer1(seq, True)
        S.finish(b_outd, "sp")
        print("n_ins", S.n_ins, "n_wait", S.n_wait)
    return nc


def prep_shared(inp):
    m = dict(host_consts())
    m["pre_norm"] = np.ascontiguousarray(inp["pre_norm"], dtype=np.float32)
    m["post_norm"] = np.ascontiguousarray(inp["post_norm"], dtype=np.float32)
    wc = np.asarray(inp["w_in_c"], dtype=np.float32)
    qkvz = wc[:, :4096].reshape(KC, 128, 4, 8, 128)
    m["wc"] = np.ascontiguousarray(qkvz.transpose(3, 2, 1, 0, 4))
    m["wf"] = np.ascontiguousarray(wc[:, 4096:4112].reshape(KC, 128, 16).transpose(1, 0, 2))
    m["woc"] = np.ascontiguousarray(np.asarray(inp["w_out_c"], dtype=np.float32).reshape(8, 128, D).transpose(1, 0, 2))
    m["c_forget_bias"] = np.ascontiguousarray(inp["c_forget_bias"], dtype=np.float32).reshape(1, 16)
    wab = np.asarray(inp["w_in_ab"], dtype=np.float32)
    mcol = np.arange(128)
    dd = mcol % 64
    dperm = np.where(dd < 8, dd + 8, np.where(dd < 16, dd - 8, dd))
    permcol = (mcol // 64) * 64 + dperm
    slabs = []
    for h in range(4):
        qc = wab[:, 0 * 512 + h * 128:0 * 512 + (h + 1) * 128]
        kc_ = wab[:, 1 * 512 + h * 128:1 * 512 + (h + 1) * 128]
        vc = wab[:, 2 * 512 + h * 128:2 * 512 + (h + 1) * 128]
        zc = wab[:, 3 * 512 + h * 128:3 * 512 + (h + 1) * 128]
        slabs.append(np.stack([qc, qc[:, permcol], kc_, kc_[:, permcol], vc, zc], axis=0))
    wa = np.stack(slabs, axis=0).reshape(4, 6, KC, 128, 128).transpose(0, 1, 3, 2, 4)
    m["wa"] = np.ascontiguousarray(wa)
    m["woab"] = np.ascontiguousarray(np.asarray(inp["w_out_ab"], dtype=np.float32).reshape(8, 128, D).transpose(1, 0, 2))
    m["lam4"] = np.ascontiguousarray(np.stack([inp["a_lambda_q1"], inp["a_lambda_k1"], inp["a_lambda_q2"], inp["a_lambda_k2"]]).astype(np.float32))
    m["a_subln"] = np.ascontiguousarray(np.asarray(inp["a_subln"], dtype=np.float32).reshape(128, 1))
    m["wbq"] = np.ascontiguousarray(wab[:, 2048:3584].reshape(KC, 128, 12, 128).transpose(2, 1, 0, 3))
    m["cw"] = np.ascontiguousarray(np.asarray(inp["b_conv_w"], dtype=np.float32).reshape(4, 12, 128).transpose(2, 1, 0))
    m["wbz"] = np.ascontiguousarray(wab[:, 3584:4096].reshape(KC, 128, 512).transpose(1, 0, 2))
    m["wba"] = np.ascontiguousarray(wab[:, 4096:4104].reshape(KC, 128, 8).transpose(1, 0, 2))
    m["b_a_log"] = np.ascontiguousarray(inp["b_a_log"], dtype=np.float32).reshape(1, 4)
    m["b_dt_bias"] = np.ascontiguousarray(inp["b_dt_bias"], dtype=np.float32).reshape(1, 4)
    m["b_head_norm"] = np.ascontiguousarray(inp["b_head_norm"], dtype=np.float32).reshape(1, 128)
    return m


def kernel(**inp):
    x = np.asarray(inp["x"], dtype=np.float32)
    B = x.shape[0]
    nseq = B // NCORES
    shared = prep_shared(inp)
    nc = build(nseq, LAYERS)
    in_maps = []
    for c in range(NCORES):
        m = dict(shared)
        m["x"] = np.ascontiguousarray(x[c * nseq:(c + 1) * nseq])
        m["positions"] = np.ascontiguousarray(np.asarray(inp["positions"], dtype=np.int32)[c * nseq:(c + 1) * nseq])
        in_maps.append(m)
    res = run_bass_kernel_spmd(nc, in_maps, core_ids=list(range(NCORES)), **RUN_KW)
    LAST['res'] = res
    return np.concatenate([r["out"] for r in res.results], axis=0)
```

```python
import contextlib
import os
import math
import numpy as np
import concourse.bass as bass
import concourse.mybir as mybir
from concourse.bass_utils import run_bass_kernel_spmd

F32 = mybir.dt.float32
BF16 = mybir.dt.bfloat16
I32 = mybir.dt.int32
AF = mybir.ActivationFunctionType
ALU = mybir.AluOpType
AX = mybir.AxisListType

D = 1024
SEQ = 2048
NT = 16
KC = 8
EPS = 1e-6
NCORES = 8
LAYERS = (0, 1)
STAGE = 99
BW = tuple(int(v) for v in os.environ.get('BW', '2,4,3').split(','))
L1W = (1, 4)
RUN_KW = {}
LAST = {}
VVAR = int(os.environ.get('VVAR', '3'))


class StopStage(Exception):
    pass


class Buf:
    __slots__ = ("name", "w", "r", "psum", "tw", "tr")

    def __init__(self, name="", psum=False):
        self.name = name
        self.w = None
        self.r = {}
        self.psum = psum
        self.tw = 0.0
        self.tr = 0.0


class _Rec:
    def __getattr__(self, name):
        def call(*a, **k):
            return (name, a, k)
        return call


_REC = _Rec()


def _est_us(eng, call):
    name, a, k = call
    out = k.get("out", a[0] if a else None)
    F = 1
    for d in out.shape[1:]:
        F *= d
    if eng == "pe":
        lhsT = k.get("lhsT", a[1] if len(a) > 1 else None)
        passes = 4 if (lhsT is not None and lhsT.dtype == F32) else 1
        return passes * max(F, 64) / 2000.0 + 0.03
    if eng == "act":
        return 0.22 + F / 1400.0
    if eng == "dve":
        return 0.12 + F / 960.0
    return 0.25 + F / 480.0


class Sched:
    ENGS = ("pe", "act", "dve", "pool", "sp")

    def __init__(self, nc, stack, n_dma_sems=6):
        self.nc = nc
        self.e = {"pe": nc.tensor, "act": nc.scalar, "dve": nc.vector,
                  "pool": nc.gpsimd, "sp": nc.sync}
        self.semh = {}
        self.cnt = {}
        for k in self.ENGS:
            self.semh[k] = stack.enter_context(nc.semaphore("s_" + k))
            self.cnt[k] = 0
        self.dq = {}
        for q in ("sp", "act", "pool"):
            slots = []
            for i in range(n_dma_sems):
                key = "d_%s%d" % (q, i)
                self.semh[key] = stack.enter_context(nc.semaphore(key))
                self.cnt[key] = 0
                slots.append(key)
            self.dq[q] = [slots, 0]
        self.seen = {k: {} for k in self.ENGS}
        self.rec = None
        self.n_ins = {k: 0 for k in self.ENGS}
        self.n_wait = {k: 0 for k in self.ENGS}

    def _wait(self, eng, deps):
        need = {}
        seen = self.seen[eng]
        for (k, v) in deps:
            if eng == "pe" and k == "pe":
                continue
            if seen.get(k, 0) < v and need.get(k, 0) < v:
                need[k] = v
        for k, v in need.items():
            self.e[eng].wait_ge(self.semh[k], v)
            seen[k] = v
            self.n_wait[eng] += 1

    @staticmethod
    def _deps(reads, writes):
        deps = []
        for b in reads:
            if b.w is not None:
                deps.append(b.w)
            if b.psum:
                deps.extend(b.r.items())
        for b in writes:
            if b.w is not None:
                deps.append(b.w)
            deps.extend(b.r.items())
        return deps

    @staticmethod
    def _mark(tok, reads, writes):
        for b in reads:
            if b.r.get(tok[0], 0) < tok[1]:
                b.r[tok[0]] = tok[1]
        for b in writes:
            b.w = tok
            b.r = {}

    def op(self, eng, fn, reads=(), writes=(), inc=True):
        if self.rec is not None:
            self.rec.append(("op", eng, fn(_REC), list(reads), list(writes), inc))
            return None
        self._wait(eng, self._deps(reads, writes))
        ins = fn(self.e[eng])
        self.n_ins[eng] += 1
        if inc:
            self.cnt[eng] += 1
            ins.then_inc(self.semh[eng], 1)
            tok = (eng, self.cnt[eng])
        else:
            tok = (eng, self.cnt[eng] + 1)
        self._mark(tok, reads, writes)
        return ins

    def dma(self, q, out, in_, reads=(), writes=(), **kw):
        if self.rec is not None:
            self.rec.append(("dma", q, (out, in_, kw), list(reads), list(writes), True))
            return None
        slots, idx = self.dq[q]
        key = slots[idx % len(slots)]
        self.dq[q][1] = idx + 1
        deps = self._deps(reads, writes)
        if self.cnt[key] > 0:
            deps.append((key, self.cnt[key]))
        self._wait(q, deps)
        ins = self.e[q].dma_start(out=out, in_=in_, **kw)
        self.cnt[key] += 16
        ins.then_inc(self.semh[key], 16)
        self._mark((key, self.cnt[key]), reads, writes)
        return ins

    def fence(self):
        allc = [(k, v) for k, v in self.cnt.items() if v > 0]
        for eng in self.ENGS:
            self._wait(eng, allc)

    def finish(self, bufs, eng="sp"):
        deps = []
        for b in bufs:
            if b.w is not None:
                deps.append(b.w)
            deps.extend(b.r.items())
        self._wait(eng, deps)


def host_consts():
    c = {}
    j = np.arange(128)
    U = (j[:, None] <= j[None, :]).astype(np.float32)
    c["c_uext"] = np.concatenate([U, np.ones((128, 1), np.float32)], axis=1)
    c["c_ident"] = np.eye(128, dtype=np.float32)
    p = np.arange(128)
    d = p % 64
    half = 8
    inv_freq = (np.float32(500000.0) ** (-(np.arange(half, dtype=np.float32) * np.float32(2.0)) / np.float32(16.0))).astype(np.float32)
    freq = np.where(d < 16, inv_freq[d % 8], 0.0).astype(np.float32)
    sign = np.where(d < 8, -1.0, np.where(d < 16, 1.0, 0.0)).astype(np.float32)
    c["c_rope"] = np.stack([freq, sign], axis=1).astype(np.float32)
    same = (j[:, None] // 64) == (j[None, :] // 64)
    ublk = (same & (j[:, None] <= j[None, :])).astype(np.float32)
    blk = same.astype(np.float32)
    strictblk = (same & (j[:, None] > j[None, :])).astype(np.float32)
    half0 = np.repeat((j < 64).astype(np.float32)[:, None], 128, axis=1)
    half1 = np.repeat((j >= 64).astype(np.float32)[:, None], 128, axis=1)
    c["c_gdn"] = np.stack([ublk, blk, strictblk, half0, half1, np.eye(128, dtype=np.float32)], axis=1)
    return c


def build(nseq, layers=(0, 1)):
    nc = bass.Bass("TRN2", target_bir_lowering=False)
    dt_in = lambda name, shape, dt=F32: nc.dram_tensor(name, list(shape), dt, kind="ExternalInput").ap()
    x_d = dt_in("x", [nseq, SEQ, D])
    out_d = nc.dram_tensor("out", [nseq, SEQ, D], F32, kind="ExternalOutput").ap()
    x1s_d = nc.dram_tensor("x1s", [nseq, SEQ, D], F32, kind="Internal").ap()
    pre_d = dt_in("pre_norm", [2, D])
    post_d = dt_in("post_norm", [2, D])
    uext_d = dt_in("c_uext", [128, 129])
    ident_d = dt_in("c_ident", [128, 128])
    wc_d = dt_in("wc", [8, 4, 128, KC, 128])
    wf_d = dt_in("wf", [128, KC, 16])
    woc_d = dt_in("woc", [128, 8, D])
    fb_d = dt_in("c_forget_bias", [1, 16])
    pos_d = dt_in("positions", [nseq, SEQ], I32)
    rope_d = dt_in("c_rope", [128, 2])
    wa_d = dt_in("wa", [4, 6, 128, KC, 128])
    woab_d = dt_in("woab", [128, 8, D])
    lam_d = dt_in("lam4", [4, 64])
    subln_d = dt_in("a_subln", [128, 1])
    gdnc_d = dt_in("c_gdn", [128, 6, 128])
    wbq_d = dt_in("wbq", [12, 128, KC, 128])
    cw_d = dt_in("cw", [128, 12, 4])
    wbz_d = dt_in("wbz", [128, KC, 512])
    wba_d = dt_in("wba", [128, KC, 8])
    alog_d = dt_in("b_a_log", [1, 4])
    dtb_d = dt_in("b_dt_bias", [1, 4])
    hn_d = dt_in("b_head_norm", [1, 128])

    with contextlib.ExitStack() as st:
        S = Sched(nc, st)
        uid = [0]
        def sb(name, shape, dt):
            uid[0] += 1
            return st.enter_context(nc.sbuf_tensor("t%d_%s" % (uid[0], name), list(shape), dt))
        xt = [sb("xt%d" % i, [128, D], F32) for i in range(3)]
        b_xt = [Buf() for _ in range(3)]
        junk2_ = sb("junk2", [128, D], BF16)
        junk2 = [junk2_, junk2_]
        b_junk2_ = Buf()
        b_junk2 = [b_junk2_, b_junk2_]
        hT = sb("hT", [128, KC, SEQ], BF16)
        og = sb("og", [128, 8, SEQ], BF16)
        pre_bc = sb("pre_bc", [128, D], F32)
        post_bc = sb("post_bc", [128, D], F32)
        ident = sb("ident", [128, 128], BF16)
        uext = sb("uext", [128, 129], F32)
        ones_f = sb("ones_f", [128, 128], F32)
        maskb = sb("maskb", [128, 128], BF16)
        ss = sb("ss", [128, NT], F32)
        rstd = sb("rstd", [128, NT], F32)
        junk = sb("junk", [128, 512], BF16)
        PB = [st.enter_context(nc.psum_tensor("pb%d" % i, [128, 512], F32)) for i in range(8)]
        bPB = [[Buf("pb%d_0" % i, True), Buf("pb%d_1" % i, True)] for i in range(8)]

        b_xdram = [Buf("xd%d" % i) for i in range(NT)]
        b_x1 = [Buf("x1_%d" % i) for i in range(NT)]
        b_outd = [Buf("od%d" % i) for i in range(NT)]
        b_ssi = [Buf() for _ in range(NT)]
        b_rsi = [Buf() for _ in range(NT)]
        b_hT = [Buf("hT%d" % i) for i in range(NT)]
        b_og = [[Buf() for _ in range(4)] for _ in range(8)]
        b_const = Buf("const")
        b_norm = Buf("normbc")
        b_ss = Buf("ss")
        b_rstd = Buf("rstd")
        b_junk = Buf("junk")
        b_xn = [Buf(), Buf()]
        b_wo = Buf("wo")
        b_out = Buf("out")

        S.dma("sp", uext[:], uext_d[:, :], writes=[b_const])
        S.dma("pool", ident[:], ident_d[:, :], writes=[b_const])
        S.dma("pool", maskb[:], uext_d[:, 0:128], writes=[b_const])
        S.op("pool", lambda e: e.memset(ones_f[:], 1.0), writes=[b_const])

        cur_layer = [0]

        def load_norms(layer):
            cur_layer[0] = layer
            S.dma("sp", pre_bc[:], bass.AP(pre_d.tensor, layer * D, [[0, 128], [1, D]]), writes=[b_norm])
            S.dma("sp", post_bc[:], bass.AP(post_d.tensor, layer * D, [[0, 128], [1, D]]), writes=[b_norm])

        def load_post():
            S.dma("sp", post_bc[:], bass.AP(post_d.tensor, cur_layer[0] * D, [[0, 128], [1, D]]), writes=[b_norm])

        def prenorm(src, bsrc):
            xn = [sb_l["tmpf"][k][:].bitcast(BF16) for k in range(2)]
            b_xn = b_l["tmpf"]

            def T(k):
                for j, i in enumerate(range(k, NT, 2)):
                    xb_ = xt[(j % 2) if k == 0 else 2]
                    bx = b_xt[(j % 2) if k == 0 else 2]
                    S.dma("sp", xb_[:], src[i * 128:(i + 1) * 128, :], reads=[bsrc[i]], writes=[bx])
                    S.op("act", lambda e: e.activation(out=junk2[k][:], in_=xb_[:], func=AF.Square, accum_out=ss[:, i:i + 1]),
                         reads=[bx], writes=[b_junk2[k], b_ssi[i]])
                    S.op("act", lambda e: e.activation(out=rstd[:, i:i + 1], in_=ss[:, i:i + 1], func=AF.Ln, scale=1.0 / D, bias=EPS),
                         reads=[b_ssi[i]], writes=[b_rsi[i]])
                    S.op("act", lambda e: e.activation(out=rstd[:, i:i + 1], in_=rstd[:, i:i + 1], func=AF.Exp, scale=-0.5),
                         reads=[b_rsi[i]], writes=[b_rsi[i]])
                    xb = xn[k]
                    bxb = b_xn[k]
                    S.op("dve", lambda e: e.scalar_tensor_tensor(out=xb, in0=xb_[:], scalar=rstd[:, i:i + 1],
                                                                 in1=pre_bc[:], op0=ALU.mult, op1=ALU.mult),
                         reads=[bx, b_rsi[i], b_norm], writes=[bxb])
                    bank = 6 + k
                    pv = PB[bank][:].bitcast(BF16)
                    for kc in range(KC):
                        S.op("pe", lambda e: e.transpose(pv[:, kc * 128:(kc + 1) * 128], xb[:, kc * 128:(kc + 1) * 128],
                                                         ident[:]),
                             reads=[bxb, b_const], writes=bPB[bank], inc=(kc == KC - 1))
                    srcp = pv.rearrange("p (k t) -> p k t", k=KC)
                    dst = hT[:, :, i * 128:(i + 1) * 128]
                    if k == 0:
                        S.op("act", lambda e: e.activation(out=dst, in_=srcp, func=AF.Copy),
                             reads=bPB[bank], writes=[b_hT[i]])
                    else:
                        S.op("dve", lambda e: e.tensor_copy(out=dst, in_=srcp),
                             reads=bPB[bank], writes=[b_hT[i]])
                    yield
            run_threads([T(0), T(1)], None)

        def outproj_residual(src, bsrc, dst, b_dst, wo):
            ss2 = sb_l["ss2"]; rs2 = sb_l["rs2"]
            b_s2 = [Buf() for _ in range(NT)]
            b_r2 = [Buf() for _ in range(NT)]

            def T(k):
                tmpf = sb_l["tmpf"] if k == 0 else sb_l["tmpf2"]
                b_tmpf = b_l["tmpf"] if k == 0 else b_l["tmpf2"]
                for j, i in enumerate(range(k, NT, 2)):
                    xb_ = xt[(j % 2) if k == 0 else 2]
                    bx = b_xt[(j % 2) if k == 0 else 2]
                    S.dma("sp", xb_[:], src[i * 128:(i + 1) * 128, :], reads=[bsrc[i]], writes=[bx])
                    banks = (4 + 2 * k, 5 + 2 * k)
                    for hf in range(2):
                        bk = banks[hf]
                        for p in range(8):
                            S.op("pe", lambda e: e.matmul(PB[bk][:, :], lhsT=og[:, p, i * 128:(i + 1) * 128],
                                                          rhs=wo[:, p, hf * 512:(hf + 1) * 512],
                                                          start=(p == 0), stop=(p == 7)),
                                 reads=[b_og[p][i // 4], b_wo], writes=bPB[bk], inc=(p == 7))
                        S.op("act", lambda e: e.activation(out=junk2[k][:, 0:512], in_=PB[bk][:, :], func=AF.Square,
                                                           accum_out=ss2[:, 2 * i + hf:2 * i + hf + 1]),
                             reads=bPB[bk], writes=[b_junk2[k], b_s2[i]])
                    yield
                    S.op("dve", lambda e: e.tensor_tensor(out=rs2[:, i:i + 1], in0=ss2[:, 2 * i:2 * i + 1],
                                                          in1=ss2[:, 2 * i + 1:2 * i + 2], op=ALU.add),
                         reads=[b_s2[i]], writes=[b_r2[i]])
                    S.op("act", lambda e: e.activation(out=rs2[:, i:i + 1], in_=rs2[:, i:i + 1], func=AF.Ln,
                                                       scale=1.0 / D, bias=EPS),
                         reads=[b_r2[i]], writes=[b_r2[i]])
                    S.op("act", lambda e: e.activation(out=rs2[:, i:i + 1], in_=rs2[:, i:i + 1], func=AF.Exp, scale=-0.5),
                         reads=[b_r2[i]], writes=[b_r2[i]])
                    for hf in range(2):
                        bk = banks[hf]
                        tf = tmpf[hf]
                        S.op("dve", lambda e: e.scalar_tensor_tensor(out=tf[:], in0=PB[bk][:, :], scalar=rs2[:, i:i + 1],
                                                                     in1=post_bc[:, hf * 512:(hf + 1) * 512],
                                                                     op0=ALU.mult, op1=ALU.mult),
                             reads=bPB[bk] + [b_r2[i], b_norm], writes=[b_tmpf[hf]])
                        S.op("pool", lambda e: e.tensor_tensor(out=xb_[:, hf * 512:(hf + 1) * 512],
                                                               in0=xb_[:, hf * 512:(hf + 1) * 512], in1=tf[:],
                                                               op=ALU.add),
                             reads=[b_tmpf[hf], bx], writes=[bx])
                    S.dma("sp", dst[i * 128:(i + 1) * 128, :], xb_[:], reads=[bx], writes=[b_dst[i]])
                    yield
            run_threads([T(0), T(1)], None)

        PI = 3.141592653589793
        LAMBDA_INIT = 0.8 - 0.6 * math.exp(-0.3 * 0)

        def layer0(seq, last):
            nonlocal sb_l, b_l
            with contextlib.ExitStack() as st1:
                st2 = st1.enter_context(contextlib.ExitStack())
                cur = [st1]

                def sl(name, shape, dt):
                    uid[0] += 1
                    return cur[0].enter_context(nc.sbuf_tensor("t%d_%s" % (uid[0], name), list(shape), dt))
                sb_l = {}
                b_l = {}
                sb_l["ss2"] = sl("ss2", [128, 2 * NT], F32); b_l["ss2"] = Buf()
                sb_l["rs2"] = sl("rs2", [128, NT], F32); b_l["rs2"] = Buf()
                sb_l["tmpf"] = [sl("tmpf%d" % i, [128, 512], F32) for i in range(2)]; b_l["tmpf"] = [Buf(), Buf()]
                sb_l["tmpf2"] = [sl("tmpf2_%d" % i, [128, 512], F32) for i in range(2)]; b_l["tmpf2"] = [Buf(), Buf()]
                load_norms(0)
                wo = sl("wo", [128, 8, D], BF16)
                S.dma("pool", wo[:], woab_d[:, :, :], writes=[b_wo])
                prenorm(x_d[seq], b_xdram)
                cur[0] = st2
                ropec = sl("ropec", [128, 2], F32)
                lamt = sl("lamt", [128, 4, 64], F32)
                lamp = sl("lamp", [128, 2, 64], F32)
                lams = sl("lams", [128, 2], F32)
                neglam = sl("neglam", [128, 1], F32)
                subcol = sl("subcol", [128, 1], F32)
                ones_b = sl("ones_b", [128, 128], BF16)
                posi = sl("posi", [128, SEQ], I32)
                Ct = sl("Ct", [128, SEQ], F32)
                St = sl("St", [128, SEQ], F32)
                qTs = [sl("qTa%d" % i, [128, SEQ], BF16) for i in range(2)]
                kTs = [sl("kTa%d" % i, [128, SEQ], BF16) for i in range(2)]
                Vhs = [sl("Vh%d" % i, [128, NT, 128], BF16) for i in range(2)]
                wsl = [sl("wa%d" % i, [128, KC, 128], BF16) for i in range(5)]
                wzA = [sl("wzA%d" % i, [128, KC, 128], BF16) for i in range(2)]
                rc = sl("rc", [128, 512], F32); rd = sl("rd", [128, 512], F32)
                vT5a = sl("vT5a", [128, 512], BF16)
                b_qTs = [Buf(), Buf()]; b_kTs = [Buf(), Buf()]; b_Vhs = [Buf(), Buf()]
                b_wzA = [Buf(), Buf()]
                b_rc, b_rd, b_vT5a = Buf(), Buf(), Buf()
                pt = [sl("pt%d" % i, [128, 512], BF16) for i in range(3)]
                ta = sb_l["tmpf"][0]; tb = sb_l["tmpf"][1]
                tc = sl("tc", [128, 512], F32); td = sl("td", [128, 512], F32)
                te = sl("te", [128, 512], F32); b_te = Buf()
                b_ropec, b_lam, b_neglam, b_subcol, b_onesb, b_posi, b_Ct, b_St = [Buf() for _ in range(8)]
                b_wsl = [Buf() for _ in range(5)]
                b_pt = [Buf() for _ in range(3)]
                b_ta, b_tb, b_tc, b_td = b_l["tmpf"][0], b_l["tmpf"][1], Buf(), Buf()

                S.dma("sp", ropec[:], rope_d[:, :], writes=[b_ropec])
                S.op("pool", lambda e: e.memset(ones_b[:], 1.0), writes=[b_onesb])
                S.dma("sp", lamt[:], bass.AP(lam_d.tensor, 0, [[0, 128], [64, 4], [1, 64]]), writes=[b_lam])
                S.op("dve", lambda e: e.tensor_tensor(out=lamp[:, 0, :], in0=lamt[:, 0, :], in1=lamt[:, 1, :], op=ALU.mult),
                     reads=[b_lam], writes=[b_lam])
                S.op("dve", lambda e: e.tensor_tensor(out=lamp[:, 1, :], in0=lamt[:, 2, :], in1=lamt[:, 3, :], op=ALU.mult),
                     reads=[b_lam], writes=[b_lam])
                S.op("dve", lambda e: e.tensor_reduce(out=lams[:], in_=lamp[:], axis=AX.X, op=ALU.add),
                     reads=[b_lam], writes=[b_lam])
                S.op("act", lambda e: e.activation(out=lams[:], in_=lams[:], func=AF.Exp), reads=[b_lam], writes=[b_lam])
                S.op("dve", lambda e: e.tensor_tensor(out=neglam[:], in0=lams[:, 1:2], in1=lams[:, 0:1], op=ALU.subtract),
                     reads=[b_lam], writes=[b_neglam])
                S.op("dve", lambda e: e.tensor_scalar(out=neglam[:], in0=neglam[:], scalar1=-LAMBDA_INIT, scalar2=None, op0=ALU.add),
                     reads=[b_neglam], writes=[b_neglam])
                S.dma("sp", subcol[:], subln_d[:, :], writes=[b_subcol])
                S.op("dve", lambda e: e.tensor_scalar(out=subcol[:], in0=subcol[:], scalar1=1.0 - LAMBDA_INIT, scalar2=None, op0=ALU.mult),
                     reads=[b_subcol], writes=[b_subcol])
                S.dma("sp", posi[:], bass.AP(pos_d.tensor, seq * SEQ, [[0, 128], [1, SEQ]]), writes=[b_posi])

                def sin_table(dst, bdst, phase, signed):
                    S.op("dve", lambda e: e.tensor_copy(out=dst[:], in_=posi[:]), reads=[b_posi], writes=[bdst])
                    S.op("dve", lambda e: e.tensor_scalar(out=dst[:], in0=dst[:], scalar1=ropec[:, 0:1], scalar2=phase,
                                                          op0=ALU.mult, op1=ALU.add), reads=[bdst, b_ropec], writes=[bdst])
                    for c4 in range(4):
                        sl_ = slice(c4 * 512, (c4 + 1) * 512)
                        tI = tc[:].bitcast(I32)
                        S.op("dve", lambda e: e.tensor_scalar(out=td[:], in0=dst[:, sl_], scalar1=1.0 / (2 * PI), scalar2=None,
                                                              op0=ALU.mult), reads=[bdst], writes=[b_td])
                        S.op("dve", lambda e: e.tensor_copy(out=tI, in_=td[:]), reads=[b_td], writes=[b_tc])
                        S.op("dve", lambda e: e.tensor_copy(out=td[:], in_=tI), reads=[b_tc], writes=[b_td])
                        S.op("dve", lambda e: e.scalar_tensor_tensor(out=td[:], in0=td[:], scalar=-2 * PI, in1=dst[:, sl_],
                                                                     op0=ALU.mult, op1=ALU.add),
                             reads=[b_td, bdst], writes=[b_td])
                        S.op("dve", lambda e: e.tensor_scalar(out=tc[:], in0=td[:], scalar1=PI, scalar2=-2 * PI,
                                                              op0=ALU.is_gt, op1=ALU.mult), reads=[b_td], writes=[b_tc])
                        S.op("dve", lambda e: e.tensor_tensor(out=td[:], in0=td[:], in1=tc[:], op=ALU.add),
                             reads=[b_td, b_tc], writes=[b_td])
                        S.op("dve", lambda e: e.tensor_scalar(out=td[:], in0=td[:], scalar1=-PI, scalar2=PI,
                                                              op0=ALU.max, op1=ALU.min), reads=[b_td], writes=[b_td])
                        S.op("act", lambda e: e.activation(out=dst[:, sl_], in_=td[:], func=AF.Sin),
                             reads=[b_td], writes=[bdst])
                    if signed:
                        S.op("dve", lambda e: e.tensor_scalar(out=dst[:], in0=dst[:], scalar1=ropec[:, 1:2], scalar2=None,
                                                              op0=ALU.mult), reads=[bdst, b_ropec], writes=[bdst])
                sin_table(Ct, b_Ct, PI / 2, False)
                sin_table(St, b_St, 0.0, True)

                def T_inA():
                    for h in range(4):
                        if h >= 2:
                            yield ("wait", ("attA", h - 2))
                        hs = h % 2
                        for i6 in range(5):
                            S.dma("pool", wsl[i6][:], wa_d[h, i6], writes=[b_wsl[i6]])
                        S.dma("pool", wzA[hs][:], wa_d[h, 5], writes=[b_wzA[hs]])
                        yield
                        for (i_w, dstT, bdst) in ((0, qTs[hs], b_qTs[hs]), (2, kTs[hs], b_kTs[hs])):
                            for t4 in range(4):
                                tsl = slice(t4 * 512, (t4 + 1) * 512)
                                for jj in range(2):
                                    for kc in range(KC):
                                        S.op("pe", lambda e: e.matmul(PB[7][:, :], lhsT=wsl[i_w + jj][:, kc, :],
                                                                      rhs=hT[:, kc, tsl], start=(kc == 0), stop=(kc == KC - 1)),
                                             reads=[b_wsl[i_w + jj]] + b_hT[4 * t4:4 * t4 + 4], writes=bPB[7],
                                             inc=(kc == KC - 1))
                                    yield
                                    tab, tabb, rt = ((Ct, b_Ct, (rc, b_rc)) if jj == 0 else (St, b_St, (rd, b_rd)))
                                    S.op("dve", lambda e: e.tensor_tensor(out=rt[0][:], in0=PB[7][:, :], in1=tab[:, tsl], op=ALU.mult),
                                         reads=bPB[7] + [tabb], writes=[rt[1]])
                                    yield
                                S.op("pool", lambda e: e.tensor_tensor(out=dstT[:, tsl], in0=rc[:], in1=rd[:], op=ALU.add),
                                     reads=[b_rc, b_rd], writes=[bdst])
                                yield
                        for t4 in range(4):
                            for kc in range(KC):
                                S.op("pe", lambda e: e.matmul(PB[7][:, :], lhsT=wsl[4][:, kc, :],
                                                              rhs=hT[:, kc, t4 * 512:(t4 + 1) * 512],
                                                              start=(kc == 0), stop=(kc == KC - 1)),
                                     reads=[b_wsl[4]] + b_hT[4 * t4:4 * t4 + 4], writes=bPB[7], inc=(kc == KC - 1))
                            yield
                            S.op("dve", lambda e: e.tensor_copy(out=vT5a[:], in_=PB[7][:, :]), reads=bPB[7], writes=[b_vT5a])
                            yield
                            pbf = PB[7][:, :].bitcast(BF16)[:, 0:512].rearrange("p (j c) -> p j c", c=128)
                            for j in range(4):
                                S.op("pe", lambda e: e.transpose(pbf[:, j, :], vT5a[:, j * 128:(j + 1) * 128], ident[:]),
                                     reads=[b_vT5a, b_const], writes=bPB[7], inc=(j == 3))
                            yield
                            S.op("dve", lambda e: e.tensor_copy(out=Vhs[hs][:, 4 * t4:4 * t4 + 4, :], in_=pbf),
                                 reads=bPB[7], writes=[b_Vhs[hs]])
                            yield
                        yield ("done", ("inA", h))

                def T_attA():
                    deferred = []
                    for h in range(4):
                        yield ("wait", ("inA", h))
                        hs = h % 2
                        qT, kT, Vh = qTs[hs], kTs[hs], Vhs[hs]
                        b_qT, b_kT, b_Vh = b_qTs[hs], b_kTs[hs], b_Vhs[hs]
                        wz5, b_wz5 = wzA[hs], b_wzA[hs]
                        jobs = []
                        for Qc in range(4):
                            for kt in range(4 * Qc + 4):
                                for c in range(2):
                                    jobs.append((Qc, kt, c))

                        def emit_pv(n):
                            Qc, kt, c = jobs[n]
                            o = max(0, kt - 4 * Qc) * 128
                            N = 512 - o
                            S.op("pe", lambda e: e.matmul(PB[2 + c][:, o:512], lhsT=Vh[:, kt, :], rhs=pt[n % 3][:, 0:N],
                                                          start=(kt == 0), stop=(kt == 4 * Qc + 3)),
                                 reads=[b_Vh, b_pt[n % 3]], writes=bPB[2 + c], inc=False)
                            S.op("pe", lambda e: e.matmul(PB[4 + c][:, o:512], lhsT=ones_b[:], rhs=pt[n % 3][:, 0:N],
                                                          start=(kt == 0), stop=(kt == 4 * Qc + 3)),
                                 reads=[b_onesb, b_pt[n % 3]], writes=bPB[4 + c])
                            if kt == 4 * Qc + 3 and c == 1:
                                emit_epilogue(Qc)

                        def emit_epilogue(Qc, h=h):
                            qsl = slice(Qc * 512, (Qc + 1) * 512)
                            for kc in range(KC):
                                S.op("pe", lambda e: e.matmul(PB[6][:, :], lhsT=wz5[:, kc, :], rhs=hT[:, kc, qsl],
                                                              start=(kc == 0), stop=(kc == KC - 1)),
                                     reads=[b_wz5] + b_hT[4 * Qc:4 * Qc + 4], writes=bPB[6], inc=(kc == KC - 1))
                            S.op("act", lambda e: e.activation(out=ta[:], in_=PB[4][:, :], func=AF.Ln), reads=bPB[4], writes=[b_ta])
                            S.op("act", lambda e: e.activation(out=ta[:], in_=ta[:], func=AF.Exp, scale=-1.0), reads=[b_ta], writes=[b_ta])
                            S.op("act", lambda e: e.activation(out=te[:], in_=PB[5][:, :], func=AF.Ln), reads=bPB[5], writes=[b_te])
                            S.op("act", lambda e: e.activation(out=te[:], in_=te[:], func=AF.Exp, scale=-1.0), reads=[b_te], writes=[b_te])
                            S.op("dve", lambda e: e.tensor_tensor(out=tb[:], in0=PB[2][:, :], in1=ta[:], op=ALU.mult),
                                 reads=bPB[2] + [b_ta], writes=[b_tb])
                            S.op("dve", lambda e: e.tensor_tensor(out=tc[:], in0=PB[3][:, :], in1=te[:], op=ALU.mult),
                                 reads=bPB[3] + [b_te], writes=[b_tc])

                            def s1():
                                S.op("dve", lambda e: e.scalar_tensor_tensor(out=tb[:], in0=tc[:], scalar=neglam[:, 0:1], in1=tb[:],
                                                                              op0=ALU.mult, op1=ALU.add),
                                     reads=[b_tc, b_tb, b_neglam], writes=[b_tb])
                                S.op("act", lambda e: e.activation(out=ta[:], in_=PB[6][:, :], func=AF.Exp, scale=-1.0),
                                     reads=bPB[6] + [b_ta], writes=[b_ta])
                                S.op("act", lambda e: e.activation(out=ta[:], in_=ta[:], func=AF.Ln, bias=1.0), reads=[b_ta], writes=[b_ta])
                                S.op("act", lambda e: e.activation(out=ta[:], in_=ta[:], func=AF.Exp, scale=-1.0), reads=[b_ta], writes=[b_ta])

                            def s2():
                                S.op("act", lambda e: e.activation(out=tc[:], in_=tb[:], func=AF.Square), reads=[b_tb], writes=[b_tc])
                                S.op("dve", lambda e: e.tensor_tensor(out=ta[:], in0=PB[6][:, :], in1=ta[:], op=ALU.mult),
                                     reads=bPB[6] + [b_ta], writes=[b_ta])

                            def s3():
                                S.op("pe", lambda e: e.matmul(PB[6][:, :], lhsT=ones_f[:], rhs=tc[:], start=True, stop=True),
                                     reads=[b_const, b_tc], writes=bPB[6])

                            def s4():
                                S.op("act", lambda e: e.activation(out=td[:], in_=PB[6][:, :], func=AF.Ln, scale=1.0 / 128, bias=EPS),
                                     reads=bPB[6], writes=[b_td])
                                S.op("act", lambda e: e.activation(out=td[:], in_=td[:], func=AF.Exp, scale=-0.5), reads=[b_td], writes=[b_td])

                            def s5():
                                S.op("pool", lambda e: e.tensor_tensor(out=tb[:], in0=tb[:], in1=td[:], op=ALU.mult),
                                     reads=[b_tb, b_td], writes=[b_tb])

                            def s6():
                                S.op("dve", lambda e: e.scalar_tensor_tensor(out=og[:, h, qsl], in0=tb[:], scalar=subcol[:, 0:1], in1=ta[:],
                                                                             op0=ALU.mult, op1=ALU.mult),
                                     reads=[b_tb, b_ta, b_subcol], writes=[b_og[h][Qc]])
                            for dl, fn in ((1, s1), (2, s2), (3, s3), (5, s4), (6, s5), (7, s6)):
                                deferred.append([dl, fn])

                        for n, (Qc, kt, c) in enumerate(jobs):
                            o = max(0, kt - 4 * Qc) * 128
                            N = 512 - o
                            q0 = Qc * 512 + o
                            sbk = n % 2
                            S.op("pe", lambda e: e.matmul(PB[sbk][:, 0:N], lhsT=kT[c * 64:(c + 1) * 64, kt * 128:(kt + 1) * 128],
                                                          rhs=qT[c * 64:(c + 1) * 64, q0:q0 + N], start=True, stop=True),
                                 reads=[b_kT, b_qT], writes=bPB[sbk])
                            S.op("act", lambda e: e.activation(out=pt[n % 3][:, 0:N], in_=PB[sbk][:, 0:N], func=AF.Exp, scale=0.125),
                                 reads=bPB[sbk], writes=[b_pt[n % 3]])
                            if kt >= 4 * Qc:
                                S.op("pool", lambda e: e.tensor_tensor(out=pt[n % 3][:, 0:128], in0=pt[n % 3][:, 0:128],
                                                                       in1=maskb[:], op=ALU.mult),
                                     reads=[b_pt[n % 3], b_const], writes=[b_pt[n % 3]])
                            if n >= 1:
                                emit_pv(n - 1)
                            for dfr in list(deferred):
                                dfr[0] -= 1
                                if dfr[0] <= 0:
                                    deferred.remove(dfr)
                                    dfr[1]()
                            yield
                        emit_pv(len(jobs) - 1)
                        yield ("done", ("attA", h))
                    while deferred:
                        for dfr in list(deferred):
                            dfr[0] -= 1
                            if dfr[0] <= 0:
                                deferred.remove(dfr)
                                dfr[1]()

                run_threads([T_inA(), T_attA()], None)
                S.fence()
                st2.close()
                st3 = st1.enter_context(contextlib.ExitStack())
                cur[0] = st3
                partB(seq, sl)
                cur[0] = st1
                if last:
                    outproj_residual(x_d[seq], b_xdram, out_d[seq], b_outd, wo)
                else:
                    outproj_residual(x_d[seq], b_xdram, x1s_d[seq], b_x1, wo)
                S.fence()
                st3.close()

        def run_threads(threads, weights):
            lists = []
            for g in threads:
                S.rec = []
                for r in g:
                    if isinstance(r, tuple):
                        S.rec.append((r[0], r[1]))
                lists.append(S.rec)
            S.rec = None
            ptr = [0] * len(lists)
            done = set()
            eng_free = {}
            rr = 0
            while True:
                best = None
                alive = False
                for ti in range(len(lists)):
                    L = lists[ti]
                    while ptr[ti] < len(L) and L[ptr[ti]][0] in ("wait", "done"):
                        kind, key = L[ptr[ti]]
                        if kind == "done":
                            done.add(key)
                            ptr[ti] += 1
                        elif key in done:
                            ptr[ti] += 1
                        else:
                            break
                    if ptr[ti] >= len(L):
                        continue
                    alive = True
                    it = L[ptr[ti]]
                    if it[0] in ("wait", "done"):
                        continue
                    kind, eng, call, reads, writes, inc = it
                    t0 = eng_free.get(eng, 0.0)
                    for b_ in reads:
                        if b_.tw > t0:
                            t0 = b_.tw
                        if b_.psum and b_.tr > t0:
                            t0 = b_.tr
                    for b_ in writes:
                        if b_.tw > t0:
                            t0 = b_.tw
                        if b_.tr > t0:
                            t0 = b_.tr
                    key2 = (t0, (ti - rr) % len(lists))
                    if best is None or key2 < best[0]:
                        best = (key2, ti, t0)
                if best is None:
                    assert not alive, "emission deadlock"
                    break
                _, ti, t0 = best
                kind, eng, call, reads, writes, inc = lists[ti][ptr[ti]]
                ptr[ti] += 1
                rr = (ti + 1) % len(lists)
                if kind == "op":
                    dur = _est_us(eng, call)
                    S.op(eng, lambda e: getattr(e, call[0])(*call[1], **call[2]), reads=reads, writes=writes, inc=inc)
                    eng_free[eng] = t0 + dur
                    tend = t0 + dur + 0.15
                else:
                    out_, in__, kw_ = call
                    S.dma(eng, out_, in__, reads=reads, writes=writes, **kw_)
                    eng_free[eng] = t0 + 0.1
                    nb = 1
                    for d in out_.shape:
                        nb *= d
                    tend = t0 + 2.0 + nb * 4 / 150e3
                for b_ in reads:
                    if tend > b_.tr:
                        b_.tr = tend
                for b_ in writes:
                    b_.tw = tend
                    b_.tr = 0.0

        def run_threads_rr(threads, weights):
            done = set()
            st_ = [{"g": g, "wait": None, "alive": True} for g in threads]
            while any(t["alive"] for t in st_):
                progressed = False
                for t, w in zip(st_, weights):
                    if not t["alive"]:
                        continue
                    for _ in range(w):
                        if t["wait"] is not None:
                            if t["wait"] in done:
                                t["wait"] = None
                            else:
                                break
                        try:
                            r = next(t["g"])
                        except StopIteration:
                            t["alive"] = False
                            progressed = True
                            break
                        progressed = True
                        if isinstance(r, tuple):
                            if r[0] == "wait":
                                if r[1] not in done:
                                    t["wait"] = r[1]
                                    break
                            elif r[0] == "done":
                                done.add(r[1])
                assert progressed, "emission deadlock"

        def partB(seq, sl):
            gm = sl("gm", [128, 6, 128], F32)
            ublk, blk, strictblk, ident_f = gm[:, 0, :], gm[:, 1, :], gm[:, 2, :], gm[:, 5, :]
            halfsel = (gm[:, 3, :], gm[:, 4, :])
            ones_b = sl("ones_bB", [128, 128], BF16)
            wbz = sl("wbz", [128, KC, 512], BF16)
            wba = sl("wba", [128, KC, 8], BF16)
            wbq = [sl("wbq%d" % i, [128, KC, 128], BF16) for i in range(2)]
            cw = sl("cw", [128, 12, 4], F32)
            negA = sl("negA", [128, 4], F32)
            dtb = sl("dtb", [128, 4], F32)
            hn_bc = sl("hn_bc", [128, 128], F32)
            G = {n: sl("g_" + n, [128, NT, 4], F32) for n in ("xa", "g", "beta", "negb", "G", "GL", "eG", "bG", "dG", "eGL0", "eGL1")}
            cr = sl("cr", [128, 12, 3], F32)
            xc = [sl("xc%d" % i, [128, 515], F32) for i in range(2)]
            yc = sl("yc", [128, 512], F32)
            sc = sl("sc", [128, 512], F32)
            t1 = sl("t1", [128, 512], F32)
            sqb = sl("sqb", [128, 512], BF16)
            qnT = [sl("qnT%d" % i, [128, 4, 512], BF16) for i in range(2)]
            knT = [sl("knT%d" % i, [128, 4, 512], BF16) for i in range(2)]
            vsT = [sl("vsT%d" % i, [128, 4, 512], BF16) for i in range(2)]
            Dm = sl("Dm", [128, 4, 128], F32)
            DTm = sl("DTm", [128, 4, 128], F32)
            Pm1 = sl("Pm", [128, 4, 128], F32)
            Qm1 = sl("Qm", [128, 4, 128], F32)
            TT = sl("TT", [128, 4, 128], F32)
            gb = TT
            gU = Pm1
            TTb = sl("TTb", [128, 4, 128], BF16)
            qgT = [sl("qgT%d" % i, [128, 4, 128], BF16) for i in range(2)]
            kbg = [sl("kbg%d" % i, [128, 4, 128], BF16) for i in range(2)]
            kdec = [sl("kdec%d" % i, [128, 4, 128], BF16) for i in range(2)]
            vb = [sl("vb%d" % i, [128, 4, 128], BF16) for i in range(2)]
            qkDT = [sl("qkDT%d" % i, [128, 4, 128], BF16) for i in range(2)]
            wT = [sl("wT%d" % i, [128, 4, 128], BF16) for i in range(2)]
            u_t = [sl("u_t%d" % i, [128, 512], F32) for i in range(2)]
            vnew = sl("vnew", [128, 512], BF16)
            St = sl("Sst", [128, 4, 128], F32)
            Sdec = sl("Sdec", [128, 4, 128], F32)
            Sb = sl("Sb", [128, 4, 128], BF16)
            zs = sl("zs", [128, 512], F32)
            t1r = sl("t1r", [128, 512], F32)
            sqr = sl("sqr", [128, 128], BF16)
            sso = sl("sso", [128, 4], F32)
            ot = sl("ot", [128, 512], BF16)
            B = {n: Buf(n) for n in ("gm", "onesb", "wbz", "wba", "cw", "negA", "dtb", "hn", "gates", "cr", "yc", "sc", "t1", "sqb",
                                    "gb", "gU", "Dm", "DTm", "Pm", "Qm", "TT", "TTb",
                                    "vnew", "S", "Sdec", "Sb", "zs", "t1r", "sqr", "sso", "ot")}
            D2 = {n: [Buf(n + "0"), Buf(n + "1")] for n in ("qnT", "knT", "vsT", "qgT", "kbg", "kdec", "vb", "qkDT", "wT", "u")}
            B["gb"] = B["TT"]
            B["gU"] = B["Pm"]
            b_wbq = [Buf(), Buf()]
            b_xc = [Buf(), Buf()]
            I_A, I_B = 0, 1
            P_A, P_B, P_C = 2, 3, 4
            R_A, R_B, R_C = 5, 6, 7

            S.dma("sp", gm[:], gdnc_d[:, :, :], writes=[B["gm"]])
            S.op("pool", lambda e: e.memset(ones_b[:], 1.0), writes=[B["onesb"]])
            S.dma("pool", wbz[:], wbz_d[:, :, :], writes=[B["wbz"]])
            S.dma("pool", wba[:], wba_d[:, :, :], writes=[B["wba"]])
            S.dma("sp", cw[:], cw_d[:, :, :], writes=[B["cw"]])
            S.dma("sp", negA[:], bass.AP(alog_d.tensor, 0, [[0, 128], [1, 4]]), writes=[B["negA"]])
            S.dma("sp", dtb[:], bass.AP(dtb_d.tensor, 0, [[0, 128], [1, 4]]), writes=[B["dtb"]])
            S.dma("sp", hn_bc[:], bass.AP(hn_d.tensor, 0, [[0, 128], [1, 128]]), writes=[B["hn"]])
            S.op("act", lambda e: e.activation(out=negA[:], in_=negA[:], func=AF.Exp), reads=[B["negA"]], writes=[B["negA"]])
            S.op("dve", lambda e: e.tensor_scalar(out=negA[:], in0=negA[:], scalar1=-1.0, scalar2=None, op0=ALU.mult),
                 reads=[B["negA"]], writes=[B["negA"]])
            S.op("pool", lambda e: e.memset(cr[:], 0.0), writes=[B["cr"]])
            S.op("pool", lambda e: e.memset(St[:], 0.0), writes=[B["S"]])
            S.op("pool", lambda e: e.memset(Sb[:], 0.0), writes=[B["Sb"]])

            ba_ps = PB[0][:, 0:NT * 8].rearrange("p (t c) -> p t c", c=8)
            for i in range(NT):
                for kc in range(KC):
                    S.op("pe", lambda e: e.matmul(ba_ps[:, i, :], lhsT=hT[:, kc, i * 128:(i + 1) * 128], rhs=wba[:, kc, :],
                                                  start=(kc == 0), stop=(kc == KC - 1)),
                         reads=[b_hT[i], B["wba"]], writes=bPB[0], inc=(kc == KC - 1))
            bg = [B["gates"]]
            S.op("dve", lambda e: e.tensor_tensor(out=G["xa"][:], in0=ba_ps[:, :, 4:8], in1=dtb[:, None, :].to_broadcast([128, NT, 4]),
                                                  op=ALU.add), reads=bPB[0] + [B["dtb"]], writes=bg)
            S.op("act", lambda e: e.activation(out=G["xa"][:], in_=G["xa"][:], func=AF.Exp), reads=bg, writes=bg)
            S.op("act", lambda e: e.activation(out=G["xa"][:], in_=G["xa"][:], func=AF.Ln, bias=1.0), reads=bg, writes=bg)
            S.op("dve", lambda e: e.tensor_tensor(out=G["g"][:], in0=G["xa"][:], in1=negA[:, None, :].to_broadcast([128, NT, 4]),
                                                  op=ALU.mult), reads=bg + [B["negA"]], writes=bg)
            S.op("act", lambda e: e.activation(out=G["beta"][:], in_=ba_ps[:, :, 0:4], func=AF.Exp, scale=-1.0),
                 reads=bPB[0] + bg, writes=bg)
            S.op("act", lambda e: e.activation(out=G["beta"][:], in_=G["beta"][:], func=AF.Ln, bias=1.0), reads=bg, writes=bg)
            S.op("act", lambda e: e.activation(out=G["beta"][:], in_=G["beta"][:], func=AF.Exp, scale=-1.0), reads=bg, writes=bg)
            S.op("dve", lambda e: e.tensor_scalar(out=G["negb"][:], in0=G["beta"][:], scalar1=-1.0, scalar2=None, op0=ALU.mult),
                 reads=bg, writes=bg)
            gflat = G["g"][:].rearrange("p t c -> p (t c)")
            S.op("pe", lambda e: e.matmul(PB[1][:, 0:64], lhsT=ublk, rhs=gflat, start=True, stop=True),
                 reads=bg + [B["gm"]], writes=bPB[1], inc=False)
            S.op("pe", lambda e: e.matmul(PB[1][:, 64:128], lhsT=blk, rhs=gflat, start=True, stop=True),
                 reads=bg + [B["gm"]], writes=bPB[1], inc=False)
            S.op("pe", lambda e: e.matmul(PB[1][:, 128:192], lhsT=halfsel[0], rhs=gflat, start=True, stop=True),
                 reads=bg + [B["gm"]], writes=bPB[1], inc=False)
            S.op("pe", lambda e: e.matmul(PB[1][:, 192:256], lhsT=halfsel[1], rhs=gflat, start=True, stop=True),
                 reads=bg + [B["gm"]], writes=bPB[1])
            v3 = lambda ap: ap.rearrange("p (t c) -> p t c", c=4)
            S.op("dve", lambda e: e.tensor_copy(out=G["G"][:], in_=v3(PB[1][:, 0:64])), reads=bPB[1] + bg, writes=bg)
            S.op("dve", lambda e: e.tensor_copy(out=G["GL"][:], in_=v3(PB[1][:, 64:128])), reads=bPB[1] + bg, writes=bg)
            S.op("act", lambda e: e.activation(out=G["eGL0"][:], in_=v3(PB[1][:, 128:192]), func=AF.Exp), reads=bPB[1] + bg, writes=bg)
            S.op("act", lambda e: e.activation(out=G["eGL1"][:], in_=v3(PB[1][:, 192:256]), func=AF.Exp), reads=bPB[1] + bg, writes=bg)
            S.op("act", lambda e: e.activation(out=G["eG"][:], in_=G["G"][:], func=AF.Exp), reads=bg, writes=bg)
            S.op("dve", lambda e: e.tensor_tensor(out=G["bG"][:], in0=G["beta"][:], in1=G["eG"][:], op=ALU.mult), reads=bg, writes=bg)
            S.op("dve", lambda e: e.tensor_tensor(out=G["dG"][:], in0=G["GL"][:], in1=G["G"][:], op=ALU.subtract), reads=bg, writes=bg)
            S.op("act", lambda e: e.activation(out=G["dG"][:], in_=G["dG"][:], func=AF.Exp), reads=bg, writes=bg)
            eGLb = (G["eGL0"], G["eGL1"])
            bgr = [Buf("gates_ro")]
            bgr[0].w = B["gates"].w

            bc4 = lambda ap2: ap2[:, :, None].to_broadcast([128, 4, 128])
            hb4 = lambda ap2: ap2[:, None, :].to_broadcast([128, 4, 128])
            h4 = lambda pb: PB[pb][:, :].rearrange("p (h c) -> p h c", c=128)

            def T_I():
                n_w = 0
                for blkI in range(4):
                    if blkI >= 2:
                        yield ("wait", ("Rblk", blkI - 2))
                    par = blkI % 2
                    bsl = slice(blkI * 512, (blkI + 1) * 512)
                    for jc in range(12):
                        wt = wbq[n_w % 2]; bwt = b_wbq[n_w % 2]
                        xcb = xc[n_w % 2]; bxc = b_xc[n_w % 2]
                        n_w += 1
                        S.dma("pool", wt[:], wbq_d[jc], writes=[bwt])
                        for kc in range(KC):
                            S.op("pe", lambda e: e.matmul(PB[I_A][:, :], lhsT=wt[:, kc, :], rhs=hT[:, kc, bsl],
                                                          start=(kc == 0), stop=(kc == KC - 1)),
                                 reads=[bwt] + b_hT[4 * blkI:4 * blkI + 4], writes=bPB[I_A], inc=(kc == KC - 1))
                        yield
                        S.op("pool", lambda e: e.tensor_copy(out=xcb[:, 0:3], in_=cr[:, jc, :]), reads=[B["cr"]], writes=[bxc])
                        S.op("act", lambda e: e.activation(out=xcb[:, 3:515], in_=PB[I_A][:, :], func=AF.Copy),
                             reads=bPB[I_A], writes=[bxc])
                        S.op("pool", lambda e: e.tensor_copy(out=cr[:, jc, :], in_=xcb[:, 512:515]), reads=[bxc], writes=[B["cr"]])
                        yield
                        S.op("dve", lambda e: e.tensor_scalar(out=yc[:], in0=xcb[:, 0:512], scalar1=cw[:, jc, 0:1], scalar2=None,
                                                              op0=ALU.mult), reads=[bxc, B["cw"]], writes=[B["yc"]])
                        for tap in range(1, 4):
                            S.op("dve", lambda e: e.scalar_tensor_tensor(out=yc[:], in0=xcb[:, tap:tap + 512], scalar=cw[:, jc, tap:tap + 1],
                                                                         in1=yc[:], op0=ALU.mult, op1=ALU.add),
                                 reads=[bxc, B["cw"], B["yc"]], writes=[B["yc"]])
                            yield
                        S.op("act", lambda e: e.activation(out=t1[:], in_=yc[:], func=AF.Exp, scale=-1.0), reads=[B["yc"]], writes=[B["t1"]])
                        S.op("act", lambda e: e.activation(out=t1[:], in_=t1[:], func=AF.Ln, bias=1.0), reads=[B["t1"]], writes=[B["t1"]])
                        S.op("act", lambda e: e.activation(out=t1[:], in_=t1[:], func=AF.Exp, scale=-1.0), reads=[B["t1"]], writes=[B["t1"]])
                        yield
                        if jc >= 8:
                            S.op("dve", lambda e: e.tensor_tensor(out=vsT[par][:, jc - 8, :], in0=yc[:], in1=t1[:], op=ALU.mult),
                                 reads=[B["yc"], B["t1"]], writes=[D2["vsT"][par]])
                            yield
                            continue
                        S.op("dve", lambda e: e.tensor_tensor(out=sc[:], in0=yc[:], in1=t1[:], op=ALU.mult),
                             reads=[B["yc"], B["t1"]], writes=[B["sc"]])
                        S.op("act", lambda e: e.activation(out=sqb[:], in_=sc[:], func=AF.Square), reads=[B["sc"]], writes=[B["sqb"]])
                        yield
                        S.op("pe", lambda e: e.matmul(PB[I_B][:, :], lhsT=ones_b[:], rhs=sqb[:], start=True, stop=True),
                             reads=[B["onesb"], B["sqb"]], writes=bPB[I_B])
                        S.op("act", lambda e: e.activation(out=t1[:], in_=PB[I_B][:, :], func=AF.Ln, bias=EPS),
                             reads=bPB[I_B] + [B["t1"]], writes=[B["t1"]])
                        isq = jc < 4
                        S.op("act", lambda e: e.activation(out=t1[:], in_=t1[:], func=AF.Exp, scale=-0.5,
                                                           bias=(-0.5 * math.log(128.0) if isq else 0.0)),
                             reads=[B["t1"]], writes=[B["t1"]])
                        yield
                        dstT = qnT[par] if isq else knT[par]
                        bd = D2["qnT"][par] if isq else D2["knT"][par]
                        S.op("dve", lambda e: e.tensor_tensor(out=dstT[:, jc % 4, :], in0=sc[:], in1=t1[:], op=ALU.mult),
                             reads=[B["sc"], B["t1"]], writes=[bd])
                        yield
                    yield ("done", ("I", blkI))

            def T_P():
                for i in range(NT):
                    blkI, tl = divmod(i, 4)
                    par = blkI % 2
                    tp = i % 2
                    yield ("wait", ("I", blkI))
                    if i >= 2:
                        yield ("wait", ("R", i - 2))
                    csl = slice(tl * 128, (tl + 1) * 128)
                    qn, kn, vs = qnT[par], knT[par], vsT[par]
                    bqn, bkn, bvs = D2["qnT"][par], D2["knT"][par], D2["vsT"][par]
                    pa, pb_, pc = h4(P_A), h4(P_B), h4(P_C)
                    S.op("dve", lambda e: e.tensor_copy(out=gb[:], in_=bc4(G["g"][:, i, :])), reads=bgr, writes=[B["gb"]])
                    for h in range(4):
                        S.op("pe", lambda e: e.matmul(pa[:, h, :], lhsT=gb[:, h, :], rhs=ublk, start=True, stop=True),
                             reads=[B["gb"], B["gm"]], writes=bPB[P_A], inc=(h == 3))
                    yield
                    S.op("act", lambda e: e.activation(out=Dm[:], in_=pa, func=AF.Exp), reads=bPB[P_A], writes=[B["Dm"]])
                    S.op("dve", lambda e: e.tensor_tensor(out=qgT[tp][:], in0=qn[:, :, csl], in1=Dm[:], op=ALU.mult),
                         reads=[bqn, B["Dm"]], writes=[D2["qgT"][tp]])
                    yield
                    p3b = PB[P_B][:, :].bitcast(BF16).rearrange("p (a h c) -> p a h c", a=2, h=4)
                    for h in range(4):
                        S.op("pe", lambda e: e.transpose(p3b[:, 0, h, :], kn[:, h, csl], ident[:]),
                             reads=[bkn, b_const], writes=bPB[P_B], inc=False)
                    for h in range(4):
                        S.op("pe", lambda e: e.transpose(p3b[:, 1, h, :], vs[:, h, csl], ident[:]),
                             reads=[bvs, b_const], writes=bPB[P_B], inc=(h == 3))
                    yield
                    S.op("dve", lambda e: e.tensor_tensor(out=kbg[tp][:], in0=p3b[:, 0], in1=bc4(G["bG"][:, i, :]), op=ALU.mult),
                         reads=bPB[P_B] + bgr, writes=[D2["kbg"][tp]])
                    S.op("dve", lambda e: e.tensor_tensor(out=kdec[tp][:], in0=p3b[:, 0], in1=bc4(G["dG"][:, i, :]), op=ALU.mult),
                         reads=bPB[P_B] + bgr, writes=[D2["kdec"][tp]])
                    yield
                    S.op("dve", lambda e: e.tensor_tensor(out=vb[tp][:], in0=p3b[:, 1], in1=bc4(G["beta"][:, i, :]), op=ALU.mult),
                         reads=bPB[P_B] + bgr, writes=[D2["vb"][tp]])
                    S.op("pool", lambda e: e.tensor_tensor(out=gU[:], in0=hb4(ublk), in1=bc4(G["g"][:, i, :]), op=ALU.mult),
                         reads=bgr + [B["gm"]], writes=[B["gU"]])
                    yield
                    for h in range(4):
                        S.op("pe", lambda e: e.matmul(pa[:, h, :], lhsT=gU[:, h, :], rhs=strictblk, start=True, stop=True),
                             reads=[B["gU"], B["gm"]], writes=bPB[P_A], inc=(h == 3))
                    for h in range(4):
                        S.op("pe", lambda e: e.matmul(pb_[:, h, :], lhsT=strictblk, rhs=gU[:, h, :], start=True, stop=True),
                             reads=[B["gU"], B["gm"]], writes=bPB[P_B], inc=(h == 3))
                    yield
                    S.op("act", lambda e: e.activation(out=Dm[:], in_=pa, func=AF.Exp), reads=bPB[P_A] + [B["Dm"]], writes=[B["Dm"]])
                    S.op("act", lambda e: e.activation(out=DTm[:], in_=pb_, func=AF.Exp), reads=bPB[P_B], writes=[B["DTm"]])
                    yield
                    for h in range(4):
                        S.op("pe", lambda e: e.matmul(pa[:, h, :], lhsT=kn[:, h, csl], rhs=kn[:, h, csl], start=True, stop=True),
                             reads=[bkn], writes=bPB[P_A], inc=(h == 3))
                    for h in range(4):
                        S.op("pe", lambda e: e.matmul(pb_[:, h, :], lhsT=kn[:, h, csl], rhs=qn[:, h, csl], start=True, stop=True),
                             reads=[bkn, bqn], writes=bPB[P_B], inc=(h == 3))
                    yield
                    S.op("pool", lambda e: e.tensor_tensor(out=Dm[:], in0=Dm[:], in1=hb4(strictblk), op=ALU.mult),
                         reads=[B["Dm"], B["gm"]], writes=[B["Dm"]])
                    S.op("pool", lambda e: e.tensor_tensor(out=Dm[:], in0=Dm[:], in1=bc4(G["negb"][:, i, :]), op=ALU.mult),
                         reads=[B["Dm"]] + bgr, writes=[B["Dm"]])
                    yield
                    S.op("dve", lambda e: e.tensor_tensor(out=Pm1[:], in0=pa, in1=Dm[:], op=ALU.mult),
                         reads=bPB[P_A] + [B["Dm"]], writes=[B["Pm"]])
                    S.op("pool", lambda e: e.tensor_tensor(out=DTm[:], in0=DTm[:], in1=hb4(ublk), op=ALU.mult),
                         reads=[B["DTm"], B["gm"]], writes=[B["DTm"]])
                    yield
                    S.op("dve", lambda e: e.tensor_tensor(out=qkDT[tp][:], in0=pb_, in1=DTm[:], op=ALU.mult),
                         reads=bPB[P_B] + [B["DTm"]], writes=[D2["qkDT"][tp]])
                    for h in range(4):
                        S.op("pe", lambda e: e.matmul(pc[:, h, :], lhsT=Pm1[:, h, :], rhs=ident_f, start=True, stop=True),
                             reads=[B["Pm"], B["gm"]], writes=bPB[P_C], inc=(h == 3))
                    yield
                    S.op("act", lambda e: e.activation(out=Qm1[:], in_=pc, func=AF.Copy), reads=bPB[P_C], writes=[B["Qm"]])
                    S.op("dve", lambda e: e.tensor_tensor(out=TT[:], in0=pc, in1=hb4(ident_f), op=ALU.add),
                         reads=bPB[P_C] + [B["gm"]], writes=[B["TT"]])
                    yield
                    for lvl in range(5):
                        for h in range(4):
                            S.op("pe", lambda e: e.matmul(pa[:, h, :], lhsT=Qm1[:, h, :], rhs=Pm1[:, h, :], start=True, stop=True),
                                 reads=[B["Qm"], B["Pm"]], writes=bPB[P_A], inc=(h == 3))
                        if lvl < 4:
                            for h in range(4):
                                S.op("pe", lambda e: e.matmul(pb_[:, h, :], lhsT=Pm1[:, h, :], rhs=Qm1[:, h, :], start=True, stop=True),
                                     reads=[B["Qm"], B["Pm"]], writes=bPB[P_B], inc=(h == 3))
                        yield
                        S.op("act", lambda e: e.activation(out=Pm1[:], in_=pa, func=AF.Copy), reads=bPB[P_A], writes=[B["Pm"]])
                        if lvl < 4:
                            S.op("dve", lambda e: e.tensor_copy(out=Qm1[:], in_=pb_), reads=bPB[P_B], writes=[B["Qm"]])
                        yield
                        for h in range(4):
                            S.op("pe", lambda e: e.matmul(pc[:, h, :], lhsT=Pm1[:, h, :], rhs=TT[:, h, :], start=True, stop=True),
                                 reads=[B["Pm"], B["TT"]], writes=bPB[P_C], inc=(h == 3))
                        yield
                        S.op("dve", lambda e: e.tensor_tensor(out=TT[:], in0=pc, in1=TT[:], op=ALU.add),
                             reads=bPB[P_C] + [B["TT"]], writes=[B["TT"]])
                        yield
                    S.op("act", lambda e: e.activation(out=TTb[:], in_=TT[:], func=AF.Copy), reads=[B["TT"]], writes=[B["TTb"]])
                    yield
                    for h in range(4):
                        S.op("pe", lambda e: e.matmul(pa[:, h, :], lhsT=TTb[:, h, :], rhs=vb[tp][:, h, :], start=True, stop=True),
                             reads=[B["TTb"], D2["vb"][tp]], writes=bPB[P_A], inc=(h == 3))
                    for h in range(4):
                        S.op("pe", lambda e: e.matmul(pb_[:, h, :], lhsT=kbg[tp][:, h, :], rhs=TTb[:, h, :], start=True, stop=True),
                             reads=[B["TTb"], D2["kbg"][tp]], writes=bPB[P_B], inc=(h == 3))
                    yield
                    S.op("act", lambda e: e.activation(out=u_t[tp][:], in_=PB[P_A][:, :], func=AF.Copy), reads=bPB[P_A], writes=[D2["u"][tp]])
                    S.op("dve", lambda e: e.tensor_copy(out=wT[tp][:], in_=pb_), reads=bPB[P_B], writes=[D2["wT"][tp]])
                    yield ("done", ("P", i))

            def T_R():
                for i in range(NT):
                    tp = i % 2
                    yield ("wait", ("P", i))
                    for ch in range(2):
                        pr = slice(64 * ch, 64 * ch + 64)
                        pc_ = slice(64 * ch, 64 * ch + 64)
                        S.op("pool", lambda e: e.tensor_tensor(out=Sdec[:], in0=St[:], in1=bc4(eGLb[ch][:, i, :]), op=ALU.mult),
                             reads=[B["S"]] + bgr, writes=[B["Sdec"]])
                        for h in range(4):
                            S.op("pe", lambda e: e.matmul(PB[R_A][pr, h * 128:(h + 1) * 128], lhsT=wT[tp][:, h, pc_], rhs=Sb[:, h, :],
                                                          start=True, stop=True),
                                 reads=[D2["wT"][tp], B["Sb"]], writes=bPB[R_A], inc=(h == 3))
                        yield
                        S.op("dve", lambda e: e.tensor_tensor(out=vnew[pr, :], in0=u_t[tp][pr, :], in1=PB[R_A][pr, :], op=ALU.subtract),
                             reads=bPB[R_A] + [D2["u"][tp]], writes=[B["vnew"]])
                        yield
                        for h in range(4):
                            S.op("pe", lambda e: e.matmul(PB[R_B][pr, h * 128:(h + 1) * 128], lhsT=qgT[tp][:, h, pc_], rhs=Sb[:, h, :],
                                                          start=True, stop=False),
                                 reads=[D2["qgT"][tp], B["Sb"]], writes=bPB[R_B], inc=False)
                            S.op("pe", lambda e: e.matmul(PB[R_B][pr, h * 128:(h + 1) * 128], lhsT=qkDT[tp][pr, h, pc_],
                                                          rhs=vnew[pr, h * 128:(h + 1) * 128], start=False, stop=True),
                                 reads=[D2["qkDT"][tp], B["vnew"]], writes=bPB[R_B], inc=(h == 3))
                        yield
                        for h in range(4):
                            S.op("pe", lambda e: e.matmul(PB[R_C][:, h * 128:(h + 1) * 128], lhsT=kdec[tp][pr, h, :],
                                                          rhs=vnew[pr, h * 128:(h + 1) * 128], start=True, stop=True),
                                 reads=[D2["kdec"][tp], B["vnew"]], writes=bPB[R_C], inc=(h == 3))
                        yield
                        Sf = St[:].rearrange("p h c -> p (h c)")
                        Sdf = Sdec[:].rearrange("p h c -> p (h c)")
                        Sbf = Sb[:].rearrange("p h c -> p (h c)")
                        S.op("dve", lambda e: e.tensor_tensor(out=Sbf, in0=PB[R_C][:, :], in1=Sdf, op=ALU.add),
                             reads=bPB[R_C] + [B["Sdec"]], writes=[B["Sb"]])
                        S.op("dve", lambda e: e.tensor_tensor(out=Sf, in0=PB[R_C][:, :], in1=Sdf, op=ALU.add),
                             reads=bPB[R_C] + [B["Sdec"]], writes=[B["S"]])
                        yield
                    for kc in range(KC):
                        S.op("pe", lambda e: e.matmul(PB[R_A][:, :], lhsT=hT[:, kc, i * 128:(i + 1) * 128], rhs=wbz[:, kc, :],
                                                      start=(kc == 0), stop=(kc == KC - 1)),
                             reads=[b_hT[i], B["wbz"]], writes=bPB[R_A], inc=(kc == KC - 1))
                    yield
                    S.op("act", lambda e: e.activation(out=zs[:], in_=PB[R_A][:, :], func=AF.Exp, scale=-1.0), reads=bPB[R_A], writes=[B["zs"]])
                    S.op("act", lambda e: e.activation(out=zs[:], in_=zs[:], func=AF.Ln, bias=1.0), reads=[B["zs"]], writes=[B["zs"]])
                    S.op("act", lambda e: e.activation(out=zs[:], in_=zs[:], func=AF.Exp, scale=-1.0), reads=[B["zs"]], writes=[B["zs"]])
                    yield
                    S.op("dve", lambda e: e.tensor_tensor(out=zs[:], in0=PB[R_A][:, :], in1=zs[:], op=ALU.mult),
                         reads=bPB[R_A] + [B["zs"]], writes=[B["zs"]])
                    zs3 = zs[:].rearrange("p (h c) -> p h c", c=128)
                    S.op("pool", lambda e: e.tensor_tensor(out=zs3, in0=zs3, in1=hb4(hn_bc[:]), op=ALU.mult),
                         reads=[B["zs"], B["hn"]], writes=[B["zs"]])
                    yield
                    for h in range(4):
                        S.op("act", lambda e: e.activation(out=sqr[:], in_=PB[R_B][:, h * 128:(h + 1) * 128], func=AF.Square,
                                                           accum_out=sso[:, h:h + 1]),
                             reads=bPB[R_B], writes=[B["sqr"], B["sso"]])
                    S.op("act", lambda e: e.activation(out=sso[:], in_=sso[:], func=AF.Ln, scale=1.0 / 128, bias=EPS),
                         reads=[B["sso"]], writes=[B["sso"]])
                    S.op("act", lambda e: e.activation(out=sso[:], in_=sso[:], func=AF.Exp, scale=-0.5), reads=[B["sso"]], writes=[B["sso"]])
                    yield
                    t13 = t1r[:].rearrange("p (h c) -> p h c", c=128)
                    S.op("dve", lambda e: e.tensor_tensor(out=t13, in0=h4(R_B), in1=bc4(sso[:]), op=ALU.mult),
                         reads=bPB[R_B] + [B["sso"], B["t1r"]], writes=[B["t1r"]])
                    S.op("dve", lambda e: e.tensor_tensor(out=ot[:], in0=t1r[:], in1=zs[:], op=ALU.mult),
                         reads=[B["t1r"], B["zs"]], writes=[B["ot"]])
                    yield
                    p3o = PB[R_C][:, :].bitcast(BF16)[:, 0:512].rearrange("p (h c) -> p h c", c=128)
                    for h in range(4):
                        S.op("pe", lambda e: e.transpose(p3o[:, h, :], ot[:, h * 128:(h + 1) * 128], ident[:]),
                             reads=[B["ot"], b_const], writes=bPB[R_C], inc=(h == 3))
                    S.op("act", lambda e: e.activation(out=og[:, 4:8, i * 128:(i + 1) * 128], in_=p3o, func=AF.Copy),
                         reads=bPB[R_C], writes=[b_og[4 + hh][i // 4] for hh in range(4)])
                    yield ("done", ("R", i))
                    if i % 4 == 3:
                        yield ("done", ("Rblk", i // 4))

            run_threads([T_I(), T_P(), T_R()], BW)

        def layer1(seq, last):
            nonlocal sb_l, b_l
            with contextlib.ExitStack() as st1:
                st2 = st1.enter_context(contextlib.ExitStack())
                cur = [st1]

                def sl(name, shape, dt):
                    uid[0] += 1
                    return cur[0].enter_context(nc.sbuf_tensor("t%d_%s" % (uid[0], name), list(shape), dt))
                sb_l = {}
                b_l = {}
                sb_l["ss2"] = sl("ss2", [128, 2 * NT], F32); b_l["ss2"] = Buf()
                sb_l["rs2"] = sl("rs2", [128, NT], F32); b_l["rs2"] = Buf()
                sb_l["tmpf"] = [sl("tmpf%d" % i, [128, 512], F32) for i in range(2)]; b_l["tmpf"] = [Buf(), Buf()]
                sb_l["tmpf2"] = [sl("tmpf2_%d" % i, [128, 512], F32) for i in range(2)]; b_l["tmpf2"] = [Buf(), Buf()]
                wo = sl("wo", [128, 8, D], BF16)
                cur[0] = st2
                qT = [[sl("qT%d_%d" % (s_, i), [128, SEQ], BF16) for i in range(2)] for s_ in range(2)]
                kT = [[sl("kT%d_%d" % (s_, i), [128, SEQ], BF16) for i in range(2)] for s_ in range(2)]
                Vx = [[sl("Vx%d_%d" % (s_, i), [128, NT, 128], BF16) for i in range(2)] for s_ in range(2)]
                vT5 = sl("vT5", [128, 512], BF16)
                b_vT5 = Buf()
                wq = sl("wq", [128, KC, 128], BF16)
                wk = sl("wk", [128, KC, 128], BF16)
                wv = sl("wv", [128, KC, 128], BF16)
                wz = [sl("wz%d" % i, [128, KC, 128], BF16) for i in range(2)]
                wf = sl("wf", [128, KC, 16], BF16)
                fb_bc = sl("fb_bc", [128, 16], F32)
                flb = sl("flb", [128, NT, 16], F32)
                nlf = sl("nlf", [128, NT, 16], F32)
                NC_ = sl("NC", [128, NT, 16], F32)
                carry = sl("carry", [128, 16], F32)
                carryT = sl("carryT", [16, 1], F32)
                cT = sl("cT", [16, SEQ], F32)
                cHL = sl("cHL", [16, 2, SEQ], BF16)
                pt = [sl("pt%d" % i, [128, 512], BF16) for i in range(3)]
                e_t = sb_l["tmpf"][0]
                sums = sb_l["tmpf"][1]
                den = sl("den", [128, 512], F32)
                tt = den
                b_qT = [[Buf(), Buf()] for _ in range(2)]; b_kT = [[Buf(), Buf()] for _ in range(2)]
                b_Vx = [[Buf(), Buf()] for _ in range(2)]
                b_qaug = [[Buf(), Buf()] for _ in range(2)]
                b_wq, b_wk, b_wv = Buf(), Buf(), Buf()
                b_wz = [Buf(), Buf()]
                b_wf, b_fb, b_flb, b_nlf, b_NC, b_carry, b_carryT, b_cT, b_cTt, b_cHL = [Buf() for _ in range(10)]
                b_pt = [Buf() for _ in range(3)]
                b_e, b_sums, b_den = b_l["tmpf"][0], b_l["tmpf"][1], Buf()
                b_tt = b_den

                try:
                    load_norms(1)
                    S.dma("pool", wo[:], woc_d[:, :, :], writes=[b_wo])
                    S.dma("pool", wf[:], wf_d[:, :, :], writes=[b_wf])
                    S.dma("sp", fb_bc[:], bass.AP(fb_d.tensor, 0, [[0, 128], [1, 16]]), writes=[b_fb])
                    for s_ in range(2):
                        for i in range(2):
                            S.op("pool", lambda e: e.memset(kT[s_][i][64:66, :], 1.0), writes=[b_kT[s_][i]])
                        S.op("pool", lambda e: e.memset(Vx[s_][0][:, :, 64:128], 1.0), writes=[b_Vx[s_][0]])
                        S.op("pool", lambda e: e.memset(Vx[s_][1][:, :, 0:64], 1.0), writes=[b_Vx[s_][1]])
                    S.op("pool", lambda e: e.memset(carry[:], 0.0), writes=[b_carry])
                    S.op("pool", lambda e: e.memset(carryT[:], 0.0), writes=[b_carryT])

                    l1src = x1s_d[seq] if 0 in layers else x_d[seq]
                    l1b = b_x1 if 0 in layers else b_xdram
                    prenorm(l1src, l1b)
                    if STAGE <= 1:
                        raise StopStage()

                    fl_ps = PB[0][:, 0:NT * 16].rearrange("p (t h) -> p t h", h=16)
                    for i in range(NT):
                        for kc in range(KC):
                            S.op("pe", lambda e: e.matmul(fl_ps[:, i, :], lhsT=hT[:, kc, i * 128:(i + 1) * 128],
                                                          rhs=wf[:, kc, :], start=(kc == 0), stop=(kc == KC - 1)),
                                 reads=[b_hT[i], b_wf], writes=bPB[0], inc=(kc == KC - 1))
                    S.op("dve", lambda e: e.tensor_tensor(out=flb[:], in0=fl_ps, in1=fb_bc[:, None, :].to_broadcast([128, NT, 16]),
                                                          op=ALU.add),
                         reads=bPB[0] + [b_fb], writes=[b_flb])
                    S.op("act", lambda e: e.activation(out=flb[:], in_=flb[:], func=AF.Exp, scale=-1.0),
                         reads=[b_flb], writes=[b_flb])
                    S.op("act", lambda e: e.activation(out=nlf[:], in_=flb[:], func=AF.Ln, bias=1.0),
                         reads=[b_flb], writes=[b_nlf])
                    for i in range(NT):
                        bk = 1 + (i % 2)
                        c1 = PB[bk][:, 0:16]
                        c2 = PB[bk][:, 16:32]
                        c3 = PB[bk][0:16, 32:32 + 129]
                        S.op("pe", lambda e: e.matmul(c1, lhsT=uext[:, 0:128], rhs=nlf[:, i, :], start=True, stop=True),
                             reads=[b_nlf, b_const], writes=bPB[bk], inc=False)
                        S.op("pe", lambda e: e.matmul(c2, lhsT=ones_f[:], rhs=nlf[:, i, :], start=True, stop=True),
                             reads=[b_nlf, b_const], writes=bPB[bk], inc=False)
                        S.op("pe", lambda e: e.matmul(c3, lhsT=nlf[:, i, :], rhs=uext[:, :], start=True, stop=True),
                             reads=[b_nlf, b_const], writes=bPB[bk])
                        S.op("dve", lambda e: e.tensor_tensor(out=NC_[:, i, :], in0=c1, in1=carry[:], op=ALU.add),
                             reads=bPB[bk] + [b_carry], writes=[b_NC])
                        S.op("dve", lambda e: e.tensor_tensor(out=carry[:], in0=c2, in1=carry[:], op=ALU.add),
                             reads=bPB[bk] + [b_carry], writes=[b_carry])
                        S.op("dve", lambda e: e.tensor_scalar(out=cT[:, i * 128:(i + 1) * 128], in0=c3[:, 0:128],
                                                              scalar1=carryT[:, 0:1], scalar2=-8.0,
                                                              op0=ALU.add, op1=ALU.mult),
                             reads=bPB[bk] + [b_carryT], writes=[b_cT])
                        S.op("dve", lambda e: e.tensor_tensor(out=carryT[:], in0=c3[:, 128:129], in1=carryT[:], op=ALU.add),
                             reads=bPB[bk] + [b_carryT], writes=[b_carryT])
                    S.op("dve", lambda e: e.tensor_copy(out=cHL[:, 0, :], in_=cT[:]), reads=[b_cT], writes=[b_cHL])
                    S.op("dve", lambda e: e.tensor_tensor(out=cT[:], in0=cT[:], in1=cHL[:, 0, :], op=ALU.subtract),
                         reads=[b_cT, b_cHL], writes=[b_cT])
                    S.op("dve", lambda e: e.tensor_copy(out=cHL[:, 1, :], in_=cT[:]), reads=[b_cT, b_cHL], writes=[b_cHL])

                    if STAGE <= 2:
                        raise StopStage()
                    IB = 7

                    def T_in():
                        for p in range(8):
                            if p >= 2:
                                yield ("wait", ("att", p - 2))
                            sp_ = p % 2
                            qTp, kTp, Vxp = qT[sp_], kT[sp_], Vx[sp_]
                            bq, bk_, bV, bqa = b_qT[sp_], b_kT[sp_], b_Vx[sp_], b_qaug[sp_]
                            S.dma("pool", wq[:], wc_d[p, 0], writes=[b_wq])
                            S.dma("pool", wk[:], wc_d[p, 1], writes=[b_wk])
                            S.dma("pool", wv[:], wc_d[p, 2], writes=[b_wv])
                            S.dma("pool", wz[p % 2][:], wc_d[p, 3], writes=[b_wz[p % 2]])
                            for hh in range(2):
                                for r in range(2):
                                    S.dma("sp", qTp[hh][64 + r:65 + r, :], cHL[2 * p + hh:2 * p + hh + 1, r, :],
                                          reads=[b_cHL], writes=[bqa[hh]])
                            yield
                            for (wt, bw, dstT, bdst) in ((wq, b_wq, qTp, bq), (wk, b_wk, kTp, bk_)):
                                for t4 in range(4):
                                    for kc in range(KC):
                                        S.op("pe", lambda e: e.matmul(PB[IB][:, :], lhsT=wt[:, kc, :],
                                                                      rhs=hT[:, kc, t4 * 512:(t4 + 1) * 512],
                                                                      start=(kc == 0), stop=(kc == KC - 1)),
                                             reads=[bw] + b_hT[4 * t4:4 * t4 + 4], writes=bPB[IB], inc=(kc == KC - 1))
                                    yield
                                    S.op("dve", lambda e: e.tensor_copy(out=dstT[0][0:64, t4 * 512:(t4 + 1) * 512],
                                                                        in_=PB[IB][0:64, :]),
                                         reads=bPB[IB][0:1], writes=[bdst[0]])
                                    S.op("dve", lambda e: e.tensor_copy(out=dstT[1][0:64, t4 * 512:(t4 + 1) * 512],
                                                                        in_=PB[IB][64:128, :]),
                                         reads=bPB[IB][1:2], writes=[bdst[1]])
                                    yield
                            for t4 in range(4):
                                for kc in range(KC):
                                    S.op("pe", lambda e: e.matmul(PB[IB][:, :], lhsT=wv[:, kc, :],
                                                                  rhs=hT[:, kc, t4 * 512:(t4 + 1) * 512],
                                                                  start=(kc == 0), stop=(kc == KC - 1)),
                                         reads=[b_wv] + b_hT[4 * t4:4 * t4 + 4], writes=bPB[IB], inc=(kc == KC - 1))
                                yield
                                S.op("dve", lambda e: e.tensor_copy(out=vT5[:], in_=PB[IB][:, :]), reads=bPB[IB], writes=[b_vT5])
                                yield
                                pbf = PB[IB][:, :].bitcast(BF16)[:, 0:512].rearrange("p (j c) -> p j c", c=128)
                                for j in range(4):
                                    S.op("pe", lambda e: e.transpose(pbf[:, j, :], vT5[:, j * 128:(j + 1) * 128], ident[:]),
                                         reads=[b_vT5, b_const], writes=bPB[IB], inc=(j == 3))
                                yield
                                S.op("dve", lambda e: e.tensor_copy(out=Vxp[0][:, 4 * t4:4 * t4 + 4, 0:64], in_=pbf[:, :, 0:64]),
                                     reads=bPB[IB], writes=[bV[0]])
                                S.op("dve", lambda e: e.tensor_copy(out=Vxp[1][:, 4 * t4:4 * t4 + 4, 64:128], in_=pbf[:, :, 64:128]),
                                     reads=bPB[IB], writes=[bV[1]])
                                yield
                            yield ("done", ("in", p))

                    def T_att():
                        deferred = []
                        for p in range(8):
                            yield ("wait", ("in", p))
                            sp_ = p % 2
                            qTp, kTp, Vxp = qT[sp_], kT[sp_], Vx[sp_]
                            bq, bk_, bV, bqa = b_qT[sp_], b_kT[sp_], b_Vx[sp_], b_qaug[sp_]
                            jobs = []
                            for Qc in range(4):
                                for kt in range(4 * Qc + 4):
                                    for hh in range(2):
                                        jobs.append((Qc, kt, hh))

                            def emit_pv(n):
                                Qc, kt, hh = jobs[n]
                                o = max(0, kt - 4 * Qc) * 128
                                N = 512 - o
                                abk = 2 + 2 * (Qc % 2) + hh
                                S.op("pe", lambda e: e.matmul(PB[abk][:, o:512], lhsT=Vxp[hh][:, kt, :], rhs=pt[n % 3][:, 0:N],
                                                              start=(kt == 0), stop=(kt == 4 * Qc + 3)),
                                     reads=[bV[hh], b_pt[n % 3]], writes=bPB[abk])
                                if kt == 4 * Qc + 3 and hh == 1:
                                    emit_epilogue(Qc)

                            def emit_epilogue(Qc, p=p):
                                zb = 6
                                a0 = 2 + 2 * (Qc % 2)
                                a1 = a0 + 1
                                qs = slice(Qc * 512, (Qc + 1) * 512)
                                for kc in range(KC):
                                    S.op("pe", lambda e: e.matmul(PB[zb][:, :], lhsT=wz[p % 2][:, kc, :], rhs=hT[:, kc, qs],
                                                                  start=(kc == 0), stop=(kc == KC - 1)),
                                         reads=[b_wz[p % 2]] + b_hT[4 * Qc:4 * Qc + 4], writes=bPB[zb], inc=(kc == KC - 1))

                                def s1():
                                    S.op("act", lambda e: e.activation(out=e_t[:], in_=PB[zb][:, :], func=AF.Exp, scale=-1.0),
                                         reads=bPB[zb], writes=[b_e])
                                    S.op("dve", lambda e: e.tensor_copy(out=sums[0:64, :], in_=PB[a0][64:128, :]),
                                         reads=bPB[a0][1:2], writes=[b_sums])
                                    S.op("dve", lambda e: e.tensor_copy(out=sums[64:128, :], in_=PB[a1][0:64, :]),
                                         reads=bPB[a1][0:1], writes=[b_sums])

                                def s2():
                                    S.op("dve", lambda e: e.scalar_tensor_tensor(out=den[:], in0=e_t[:], scalar=1.0, in1=sums[:],
                                                                                 op0=ALU.add, op1=ALU.mult),
                                         reads=[b_e, b_sums], writes=[b_den])

                                def s3():
                                    S.op("act", lambda e: e.activation(out=den[:], in_=den[:], func=AF.Ln), reads=[b_den], writes=[b_den])
                                    S.op("act", lambda e: e.activation(out=den[:], in_=den[:], func=AF.Exp, scale=-1.0),
                                         reads=[b_den], writes=[b_den])

                                def s4():
                                    S.op("dve", lambda e: e.tensor_tensor(out=tt[:], in0=PB[zb][:, :], in1=den[:], op=ALU.mult),
                                         reads=bPB[zb] + [b_den], writes=[b_tt])
                                    S.op("dve", lambda e: e.tensor_tensor(out=og[0:64, p, qs], in0=PB[a0][0:64, :], in1=tt[0:64, :],
                                                                          op=ALU.mult),
                                         reads=bPB[a0][0:1] + [b_tt], writes=[b_og[p][Qc]])
                                    S.op("dve", lambda e: e.tensor_tensor(out=og[64:128, p, qs], in0=PB[a1][64:128, :], in1=tt[64:128, :],
                                                                          op=ALU.mult),
                                         reads=bPB[a1][1:2] + [b_tt], writes=[b_og[p][Qc]])
                                for dl, fn in ((2, s1), (4, s2), (6, s3), (8, s4)):
                                    deferred.append([dl, fn])

                            for n, (Qc, kt, hh) in enumerate(jobs):
                                o = max(0, kt - 4 * Qc) * 128
                                N = 512 - o
                                q0 = Qc * 512 + o
                                h = 2 * p + hh
                                sbk = n % 2
                                S.op("pe", lambda e: e.matmul(PB[sbk][:, 0:N], lhsT=kTp[hh][0:66, kt * 128:(kt + 1) * 128],
                                                              rhs=qTp[hh][0:66, q0:q0 + N], start=True, stop=True),
                                     reads=[bk_[hh], bq[hh], bqa[hh]], writes=bPB[sbk])
                                S.op("act", lambda e: e.activation(out=pt[n % 3][:, 0:N], in_=PB[sbk][:, 0:N], func=AF.Exp,
                                                                   scale=0.125, bias=NC_[:, kt, h:h + 1]),
                                     reads=bPB[sbk] + [b_NC], writes=[b_pt[n % 3]])
                                if kt >= 4 * Qc:
                                    S.op("pool", lambda e: e.tensor_tensor(out=pt[n % 3][:, 0:128], in0=pt[n % 3][:, 0:128],
                                                                           in1=maskb[:], op=ALU.mult),
                                         reads=[b_pt[n % 3], b_const], writes=[b_pt[n % 3]])
                                if n >= 1:
                                    emit_pv(n - 1)
                                for dfr in list(deferred):
                                    dfr[0] -= 1
                                    if dfr[0] <= 0:
                                        deferred.remove(dfr)
                                        dfr[1]()
                                yield
                            emit_pv(len(jobs) - 1)
                            yield ("done", ("att", p))
                        while deferred:
                            for dfr in list(deferred):
                                dfr[0] -= 1
                                if dfr[0] <= 0:
                                    deferred.remove(dfr)
                                    dfr[1]()

                    run_threads([T_in(), T_att()], L1W)
                except StopStage:
                    pass
                cur[0] = st1
                l1src = x1s_d[seq] if 0 in layers else x_d[seq]
                l1b = b_x1 if 0 in layers else b_xdram
                outproj_residual(l1src, l1b, out_d[seq], b_outd, wo)
                S.fence()
                st2.close()

        sb_l = None
        b_l = None
        for seq in range(nseq):
            if 0 in layers:
                layer0(seq, 1 not in layers)
            if 1 in layers:
                layer1(seq, True)
        S.finish(b_outd, "sp")
        print("n_ins", S.n_ins, "n_wait", S.n_wait)
    return nc


def prep_shared(inp):
    m = dict(host_consts())
    m["pre_norm"] = np.ascontiguousarray(inp["pre_norm"], dtype=np.float32)
    m["post_norm"] = np.ascontiguousarray(inp["post_norm"], dtype=np.float32)
    wc = np.asarray(inp["w_in_c"], dtype=np.float32)
    qkvz = wc[:, :4096].reshape(KC, 128, 4, 8, 128)
    m["wc"] = np.ascontiguousarray(qkvz.transpose(3, 2, 1, 0, 4))
    m["wf"] = np.ascontiguousarray(wc[:, 4096:4112].reshape(KC, 128, 16).transpose(1, 0, 2))
    m["woc"] = np.ascontiguousarray(np.asarray(inp["w_out_c"], dtype=np.float32).reshape(8, 128, D).transpose(1, 0, 2))
    m["c_forget_bias"] = np.ascontiguousarray(inp["c_forget_bias"], dtype=np.float32).reshape(1, 16)
    wab = np.asarray(inp["w_in_ab"], dtype=np.float32)
    mcol = np.arange(128)
    dd = mcol % 64
    dperm = np.where(dd < 8, dd + 8, np.where(dd < 16, dd - 8, dd))
    permcol = (mcol // 64) * 64 + dperm
    slabs = []
    for h in range(4):
        qc = wab[:, 0 * 512 + h * 128:0 * 512 + (h + 1) * 128]
        kc_ = wab[:, 1 * 512 + h * 128:1 * 512 + (h + 1) * 128]
        vc = wab[:, 2 * 512 + h * 128:2 * 512 + (h + 1) * 128]
        zc = wab[:, 3 * 512 + h * 128:3 * 512 + (h + 1) * 128]
        slabs.append(np.stack([qc, qc[:, permcol], kc_, kc_[:, permcol], vc, zc], axis=0))
    wa = np.stack(slabs, axis=0).reshape(4, 6, KC, 128, 128).transpose(0, 1, 3, 2, 4)
    m["wa"] = np.ascontiguousarray(wa)
    m["woab"] = np.ascontiguousarray(np.asarray(inp["w_out_ab"], dtype=np.float32).reshape(8, 128, D).transpose(1, 0, 2))
    m["lam4"] = np.ascontiguousarray(np.stack([inp["a_lambda_q1"], inp["a_lambda_k1"], inp["a_lambda_q2"], inp["a_lambda_k2"]]).astype(np.float32))
    m["a_subln"] = np.ascontiguousarray(np.asarray(inp["a_subln"], dtype=np.float32).reshape(128, 1))
    m["wbq"] = np.ascontiguousarray(wab[:, 2048:3584].reshape(KC, 128, 12, 128).transpose(2, 1, 0, 3))
    m["cw"] = np.ascontiguousarray(np.asarray(inp["b_conv_w"], dtype=np.float32).reshape(4, 12, 128).transpose(2, 1, 0))
    m["wbz"] = np.ascontiguousarray(wab[:, 3584:4096].reshape(KC, 128, 512).transpose(1, 0, 2))
    m["wba"] = np.ascontiguousarray(wab[:, 4096:4104].reshape(KC, 128, 8).transpose(1, 0, 2))
    m["b_a_log"] = np.ascontiguousarray(inp["b_a_log"], dtype=np.float32).reshape(1, 4)
    m["b_dt_bias"] = np.ascontiguousarray(inp["b_dt_bias"], dtype=np.float32).reshape(1, 4)
    m["b_head_norm"] = np.ascontiguousarray(inp["b_head_norm"], dtype=np.float32).reshape(1, 128)
    return m


def kernel(**inp):
    x = np.asarray(inp["x"], dtype=np.float32)
    B = x.shape[0]
    nseq = B // NCORES
    shared = prep_shared(inp)
    nc = build(nseq, LAYERS)
    in_maps = []
    for c in range(NCORES):
        m = dict(shared)
        m["x"] = np.ascontiguousarray(x[c * nseq:(c + 1) * nseq])
        m["positions"] = np.ascontiguousarray(np.asarray(inp["positions"], dtype=np.int32)[c * nseq:(c + 1) * nseq])
        in_maps.append(m)
    res = run_bass_kernel_spmd(nc, in_maps, core_ids=list(range(NCORES)), **RUN_KW)
    LAST['res'] = res
    return np.concatenate([r["out"] for r in res.results], axis=0)
```

```python
import contextlib
import os
import math
import numpy as np
import concourse.bass as bass
import concourse.mybir as mybir
from concourse.bass_utils import run_bass_kernel_spmd

F32 = mybir.dt.float32
BF16 = mybir.dt.bfloat16
I32 = mybir.dt.int32
AF = mybir.ActivationFunctionType
ALU = mybir.AluOpType
AX = mybir.AxisListType

D = 1024
SEQ = 2048
NT = 16
KC = 8
EPS = 1e-6
NCORES = 8
LAYERS = (0, 1)
STAGE = 99
BW = tuple(int(v) for v in os.environ.get('BW', '2,4,3').split(','))
L1W = (1, 4)
PE_RATE = float(os.environ.get('PE_RATE', '2000'))
SEM_LAT = float(os.environ.get('SEM_LAT', '0.15'))
RUN_KW = {}
LAST = {}
VVAR = int(os.environ.get('VVAR', '3'))


class StopStage(Exception):
    pass


class Buf:
    __slots__ = ("name", "w", "r", "psum", "tw", "tr")

    def __init__(self, name="", psum=False):
        self.name = name
        self.w = None
        self.r = {}
        self.psum = psum
        self.tw = 0.0
        self.tr = 0.0


class _Rec:
    def __getattr__(self, name):
        def call(*a, **k):
            return (name, a, k)
        return call


_REC = _Rec()


def _est_us(eng, call):
    name, a, k = call
    out = k.get("out", a[0] if a else None)
    F = 1
    for d in out.shape[1:]:
        F *= d
    if eng == "pe":
        lhsT = k.get("lhsT", a[1] if len(a) > 1 else None)
        passes = 4 if (lhsT is not None and lhsT.dtype == F32) else 1
        return passes * max(F, 64) / PE_RATE + 0.03
    if eng == "act":
        return 0.22 + F / 1400.0
    if eng == "dve":
        return 0.12 + F / 960.0
    return 0.25 + F / 480.0


class Sched:
    ENGS = ("pe", "act", "dve", "pool", "sp")

    def __init__(self, nc, stack, n_dma_sems=6):
        self.nc = nc
        self.e = {"pe": nc.tensor, "act": nc.scalar, "dve": nc.vector,
                  "pool": nc.gpsimd, "sp": nc.sync}
        self.semh = {}
        self.cnt = {}
        for k in self.ENGS:
            self.semh[k] = stack.enter_context(nc.semaphore("s_" + k))
            self.cnt[k] = 0
        self.dq = {}
        for q in ("sp", "act", "pool"):
            slots = []
            for i in range(n_dma_sems):
                key = "d_%s%d" % (q, i)
                self.semh[key] = stack.enter_context(nc.semaphore(key))
                self.cnt[key] = 0
                slots.append(key)
            self.dq[q] = [slots, 0]
        self.seen = {k: {} for k in self.ENGS}
        self.rec = None
        self.n_ins = {k: 0 for k in self.ENGS}
        self.n_wait = {k: 0 for k in self.ENGS}

    def _wait(self, eng, deps):
        need = {}
        seen = self.seen[eng]
        for (k, v) in deps:
            if eng == "pe" and k == "pe":
                continue
            if seen.get(k, 0) < v and need.get(k, 0) < v:
                need[k] = v
        for k, v in need.items():
            self.e[eng].wait_ge(self.semh[k], v)
            seen[k] = v
            self.n_wait[eng] += 1

    @staticmethod
    def _deps(reads, writes):
        deps = []
        for b in reads:
            if b.w is not None:
                deps.append(b.w)
            if b.psum:
                deps.extend(b.r.items())
        for b in writes:
            if b.w is not None:
                deps.append(b.w)
            deps.extend(b.r.items())
        return deps

    @staticmethod
    def _mark(tok, reads, writes):
        for b in reads:
            if b.r.get(tok[0], 0) < tok[1]:
                b.r[tok[0]] = tok[1]
        for b in writes:
            b.w = tok
            b.r = {}

    def op(self, eng, fn, reads=(), writes=(), inc=True):
        if self.rec is not None:
            self.rec.append(("op", eng, fn(_REC), list(reads), list(writes), inc))
            return None
        self._wait(eng, self._deps(reads, writes))
        ins = fn(self.e[eng])
        self.n_ins[eng] += 1
        if inc:
            self.cnt[eng] += 1
            ins.then_inc(self.semh[eng], 1)
            tok = (eng, self.cnt[eng])
        else:
            tok = (eng, self.cnt[eng] + 1)
        self._mark(tok, reads, writes)
        return ins

    def dma(self, q, out, in_, reads=(), writes=(), **kw):
        if self.rec is not None:
            self.rec.append(("dma", q, (out, in_, kw), list(reads), list(writes), True))
            return None
        slots, idx = self.dq[q]
        key = slots[idx % len(slots)]
        self.dq[q][1] = idx + 1
        deps = self._deps(reads, writes)
        if self.cnt[key] > 0:
            deps.append((key, self.cnt[key]))
        self._wait(q, deps)
        ins = self.e[q].dma_start(out=out, in_=in_, **kw)
        self.cnt[key] += 16
        ins.then_inc(self.semh[key], 16)
        self._mark((key, self.cnt[key]), reads, writes)
        return ins

    def fence(self):
        allc = [(k, v) for k, v in self.cnt.items() if v > 0]
        for eng in self.ENGS:
            self._wait(eng, allc)

    def finish(self, bufs, eng="sp"):
        deps = []
        for b in bufs:
            if b.w is not None:
                deps.append(b.w)
            deps.extend(b.r.items())
        self._wait(eng, deps)


def host_consts():
    c = {}
    j = np.arange(128)
    U = (j[:, None] <= j[None, :]).astype(np.float32)
    c["c_uext"] = np.concatenate([U, np.ones((128, 1), np.float32)], axis=1)
    c["c_ident"] = np.eye(128, dtype=np.float32)
    p = np.arange(128)
    d = p % 64
    half = 8
    inv_freq = (np.float32(500000.0) ** (-(np.arange(half, dtype=np.float32) * np.float32(2.0)) / np.float32(16.0))).astype(np.float32)
    freq = np.where(d < 16, inv_freq[d % 8], 0.0).astype(np.float32)
    sign = np.where(d < 8, -1.0, np.where(d < 16, 1.0, 0.0)).astype(np.float32)
    c["c_rope"] = np.stack([freq, sign], axis=1).astype(np.float32)
    same = (j[:, None] // 64) == (j[None, :] // 64)
    ublk = (same & (j[:, None] <= j[None, :])).astype(np.float32)
    blk = same.astype(np.float32)
    strictblk = (same & (j[:, None] > j[None, :])).astype(np.float32)
    half0 = np.repeat((j < 64).astype(np.float32)[:, None], 128, axis=1)
    half1 = np.repeat((j >= 64).astype(np.float32)[:, None], 128, axis=1)
    c["c_gdn"] = np.stack([ublk, blk, strictblk, half0, half1, np.eye(128, dtype=np.float32)], axis=1)
    return c


def build(nseq, layers=(0, 1)):
    nc = bass.Bass("TRN2", target_bir_lowering=False)
    dt_in = lambda name, shape, dt=F32: nc.dram_tensor(name, list(shape), dt, kind="ExternalInput").ap()
    x_d = dt_in("x", [nseq, SEQ, D])
    out_d = nc.dram_tensor("out", [nseq, SEQ, D], F32, kind="ExternalOutput").ap()
    x1s_d = nc.dram_tensor("x1s", [nseq, SEQ, D], F32, kind="Internal").ap()
    pre_d = dt_in("pre_norm", [2, D])
    post_d = dt_in("post_norm", [2, D])
    uext_d = dt_in("c_uext", [128, 129])
    ident_d = dt_in("c_ident", [128, 128])
    wc_d = dt_in("wc", [8, 4, 128, KC, 128])
    wf_d = dt_in("wf", [128, KC, 16])
    woc_d = dt_in("woc", [128, 8, D])
    fb_d = dt_in("c_forget_bias", [1, 16])
    pos_d = dt_in("positions", [nseq, SEQ], I32)
    rope_d = dt_in("c_rope", [128, 2])
    wa_d = dt_in("wa", [4, 6, 128, KC, 128])
    woab_d = dt_in("woab", [128, 8, D])
    lam_d = dt_in("lam4", [4, 64])
    subln_d = dt_in("a_subln", [128, 1])
    gdnc_d = dt_in("c_gdn", [128, 6, 128])
    wbq_d = dt_in("wbq", [12, 128, KC, 128])
    cw_d = dt_in("cw", [128, 12, 4])
    wbz_d = dt_in("wbz", [128, KC, 512])
    wba_d = dt_in("wba", [128, KC, 8])
    alog_d = dt_in("b_a_log", [1, 4])
    dtb_d = dt_in("b_dt_bias", [1, 4])
    hn_d = dt_in("b_head_norm", [1, 128])

    with contextlib.ExitStack() as st:
        S = Sched(nc, st)
        uid = [0]
        def sb(name, shape, dt):
            uid[0] += 1
            return st.enter_context(nc.sbuf_tensor("t%d_%s" % (uid[0], name), list(shape), dt))
        xt = [sb("xt%d" % i, [128, D], F32) for i in range(3)]
        b_xt = [Buf() for _ in range(3)]
        junk2_ = sb("junk2", [128, D], BF16)
        junk2 = [junk2_, junk2_]
        b_junk2_ = Buf()
        b_junk2 = [b_junk2_, b_junk2_]
        hT = sb("hT", [128, KC, SEQ], BF16)
        og = sb("og", [128, 8, SEQ], BF16)
        pre_bc = sb("pre_bc", [128, D], F32)
        post_bc = sb("post_bc", [128, D], F32)
        ident = sb("ident", [128, 128], BF16)
        uext = sb("uext", [128, 129], F32)
        ones_f = sb("ones_f", [128, 128], F32)
        maskb = sb("maskb", [128, 128], BF16)
        ss = sb("ss", [128, NT], F32)
        rstd = sb("rstd", [128, NT], F32)
        junk = sb("junk", [128, 512], BF16)
        PB = [st.enter_context(nc.psum_tensor("pb%d" % i, [128, 512], F32)) for i in range(8)]
        bPB = [[Buf("pb%d_0" % i, True), Buf("pb%d_1" % i, True)] for i in range(8)]

        b_xdram = [Buf("xd%d" % i) for i in range(NT)]
        b_x1 = [Buf("x1_%d" % i) for i in range(NT)]
        b_outd = [Buf("od%d" % i) for i in range(NT)]
        b_ssi = [Buf() for _ in range(NT)]
        b_rsi = [Buf() for _ in range(NT)]
        b_hT = [Buf("hT%d" % i) for i in range(NT)]
        b_og = [[Buf() for _ in range(4)] for _ in range(8)]
        b_const = Buf("const")
        b_norm = Buf("normbc")
        b_ss = Buf("ss")
        b_rstd = Buf("rstd")
        b_junk = Buf("junk")
        b_xn = [Buf(), Buf()]
        b_wo = Buf("wo")
        b_out = Buf("out")

        S.dma("sp", uext[:], uext_d[:, :], writes=[b_const])
        S.dma("pool", ident[:], ident_d[:, :], writes=[b_const])
        S.dma("pool", maskb[:], uext_d[:, 0:128], writes=[b_const])
        S.op("pool", lambda e: e.memset(ones_f[:], 1.0), writes=[b_const])

        cur_layer = [0]

        b_pre = Buf("pre_bc")
        b_post = Buf("post_bc")

        def load_pre(layer):
            S.dma("sp", pre_bc[:], bass.AP(pre_d.tensor, layer * D, [[0, 128], [1, D]]), writes=[b_pre])

        def load_post(layer):
            S.dma("sp", post_bc[:], bass.AP(post_d.tensor, layer * D, [[0, 128], [1, D]]), writes=[b_post])

        def load_norms(layer):
            load_pre(layer)
            load_post(layer)

        def prenorm(src, bsrc):
            xn = [sb_l["tmpf"][k][:].bitcast(BF16) for k in range(2)]
            b_xn = b_l["tmpf"]

            def T(k):
                for j, i in enumerate(range(k, NT, 2)):
                    xb_ = xt[(j % 2) if k == 0 else 2]
                    bx = b_xt[(j % 2) if k == 0 else 2]
                    S.dma("sp", xb_[:], src[i * 128:(i + 1) * 128, :], reads=[bsrc[i]], writes=[bx])
                    S.op("act", lambda e: e.activation(out=junk2[k][:], in_=xb_[:], func=AF.Square, accum_out=ss[:, i:i + 1]),
                         reads=[bx], writes=[b_junk2[k], b_ssi[i]])
                    S.op("act", lambda e: e.activation(out=rstd[:, i:i + 1], in_=ss[:, i:i + 1], func=AF.Ln, scale=1.0 / D, bias=EPS),
                         reads=[b_ssi[i]], writes=[b_rsi[i]])
                    S.op("act", lambda e: e.activation(out=rstd[:, i:i + 1], in_=rstd[:, i:i + 1], func=AF.Exp, scale=-0.5),
                         reads=[b_rsi[i]], writes=[b_rsi[i]])
                    xb = xn[k]
                    bxb = b_xn[k]
                    S.op("dve", lambda e: e.scalar_tensor_tensor(out=xb, in0=xb_[:], scalar=rstd[:, i:i + 1],
                                                                 in1=pre_bc[:], op0=ALU.mult, op1=ALU.mult),
                         reads=[bx, b_rsi[i], b_pre], writes=[bxb])
                    bank = 6 + k
                    pv = PB[bank][:].bitcast(BF16)
                    for kc in range(KC):
                        S.op("pe", lambda e: e.transpose(pv[:, kc * 128:(kc + 1) * 128], xb[:, kc * 128:(kc + 1) * 128],
                                                         ident[:]),
                             reads=[bxb, b_const], writes=bPB[bank], inc=(kc == KC - 1))
                    srcp = pv.rearrange("p (k t) -> p k t", k=KC)
                    dst = hT[:, :, i * 128:(i + 1) * 128]
                    if k == 0:
                        S.op("act", lambda e: e.activation(out=dst, in_=srcp, func=AF.Copy),
                             reads=bPB[bank], writes=[b_hT[i]])
                    else:
                        S.op("dve", lambda e: e.tensor_copy(out=dst, in_=srcp),
                             reads=bPB[bank], writes=[b_hT[i]])
                    yield
            run_threads([T(0), T(1)], None)

        def outproj_residual(src, bsrc, dst, b_dst, wo, fuse=False):
            ss2 = sb_l["ss2"]; rs2 = sb_l["rs2"]
            b_s2 = [Buf() for _ in range(NT)]
            b_r2 = [Buf() for _ in range(NT)]

            def T(k):
                tmpf = sb_l["tmpf"] if k == 0 else sb_l["tmpf2"]
                b_tmpf = b_l["tmpf"] if k == 0 else b_l["tmpf2"]
                for j, i in enumerate(range(k, NT, 2)):
                    xb_ = xt[(j % 2) if k == 0 else 2]
                    bx = b_xt[(j % 2) if k == 0 else 2]
                    S.dma("sp", xb_[:], src[i * 128:(i + 1) * 128, :], reads=[bsrc[i]], writes=[bx])
                    banks = (4 + 2 * k, 5 + 2 * k)
                    for hf in range(2):
                        bk = banks[hf]
                        for p in range(8):
                            S.op("pe", lambda e: e.matmul(PB[bk][:, :], lhsT=og[:, p, i * 128:(i + 1) * 128],
                                                          rhs=wo[:, p, hf * 512:(hf + 1) * 512],
                                                          start=(p == 0), stop=(p == 7)),
                                 reads=[b_og[p][i // 4], b_wo], writes=bPB[bk], inc=(p == 7))
                        S.op("act", lambda e: e.activation(out=junk2[k][:, 0:512], in_=PB[bk][:, :], func=AF.Square,
                                                           accum_out=ss2[:, 2 * i + hf:2 * i + hf + 1]),
                             reads=bPB[bk], writes=[b_junk2[k], b_s2[i]])
                    yield
                    S.op("dve", lambda e: e.tensor_tensor(out=rs2[:, i:i + 1], in0=ss2[:, 2 * i:2 * i + 1],
                                                          in1=ss2[:, 2 * i + 1:2 * i + 2], op=ALU.add),
                         reads=[b_s2[i]], writes=[b_r2[i]])
                    S.op("act", lambda e: e.activation(out=rs2[:, i:i + 1], in_=rs2[:, i:i + 1], func=AF.Ln,
                                                       scale=1.0 / D, bias=EPS),
                         reads=[b_r2[i]], writes=[b_r2[i]])
                    S.op("act", lambda e: e.activation(out=rs2[:, i:i + 1], in_=rs2[:, i:i + 1], func=AF.Exp, scale=-0.5),
                         reads=[b_r2[i]], writes=[b_r2[i]])
                    for hf in range(2):
                        bk = banks[hf]
                        tf = tmpf[hf]
                        S.op("dve", lambda e: e.scalar_tensor_tensor(out=tf[:], in0=PB[bk][:, :], scalar=rs2[:, i:i + 1],
                                                                     in1=post_bc[:, hf * 512:(hf + 1) * 512],
                                                                     op0=ALU.mult, op1=ALU.mult),
                             reads=bPB[bk] + [b_r2[i], b_post], writes=[b_tmpf[hf]])
                        S.op("pool", lambda e: e.tensor_tensor(out=xb_[:, hf * 512:(hf + 1) * 512],
                                                               in0=xb_[:, hf * 512:(hf + 1) * 512], in1=tf[:],
                                                               op=ALU.add),
                             reads=[b_tmpf[hf], bx], writes=[bx])
                    S.dma("sp", dst[i * 128:(i + 1) * 128, :], xb_[:], reads=[bx], writes=[b_dst[i]])
                    yield
                    if fuse:
                        xnv = tmpf[0][:].bitcast(BF16)
                        S.op("act", lambda e: e.activation(out=junk2[k][:], in_=xb_[:], func=AF.Square, accum_out=ss[:, i:i + 1]),
                             reads=[bx], writes=[b_junk2[k], b_ssi[i]])
                        S.op("act", lambda e: e.activation(out=rstd[:, i:i + 1], in_=ss[:, i:i + 1], func=AF.Ln, scale=1.0 / D, bias=EPS),
                             reads=[b_ssi[i]], writes=[b_rsi[i]])
                        S.op("act", lambda e: e.activation(out=rstd[:, i:i + 1], in_=rstd[:, i:i + 1], func=AF.Exp, scale=-0.5),
                             reads=[b_rsi[i]], writes=[b_rsi[i]])
                        yield
                        S.op("dve", lambda e: e.scalar_tensor_tensor(out=xnv, in0=xb_[:], scalar=rstd[:, i:i + 1],
                                                                     in1=pre_bc[:], op0=ALU.mult, op1=ALU.mult),
                             reads=[bx, b_rsi[i], b_pre], writes=[b_tmpf[0]])
                        yield
                        bank = 2 + k
                        pv = PB[bank][:].bitcast(BF16)
                        for kc in range(KC):
                            S.op("pe", lambda e: e.transpose(pv[:, kc * 128:(kc + 1) * 128], xnv[:, kc * 128:(kc + 1) * 128],
                                                             ident[:]),
                                 reads=[b_tmpf[0], b_const], writes=bPB[bank], inc=(kc == KC - 1))
                        yield
                        srcp = pv.rearrange("p (k t) -> p k t", k=KC)
                        dsth = hT[:, :, i * 128:(i + 1) * 128]
                        if k == 0:
                            S.op("act", lambda e: e.activation(out=dsth, in_=srcp, func=AF.Copy),
                                 reads=bPB[bank], writes=[b_hT[i]])
                        else:
                            S.op("dve", lambda e: e.tensor_copy(out=dsth, in_=srcp),
                                 reads=bPB[bank], writes=[b_hT[i]])
                        yield
            run_threads([T(0), T(1)], None)

        PI = 3.141592653589793
        LAMBDA_INIT = 0.8 - 0.6 * math.exp(-0.3 * 0)

        def layer0(seq, last):
            nonlocal sb_l, b_l
            with contextlib.ExitStack() as st1:
                st2 = st1.enter_context(contextlib.ExitStack())
                cur = [st1]

                def sl(name, shape, dt):
                    uid[0] += 1
                    return cur[0].enter_context(nc.sbuf_tensor("t%d_%s" % (uid[0], name), list(shape), dt))
                sb_l = {}
                b_l = {}
                sb_l["ss2"] = sl("ss2", [128, 2 * NT], F32); b_l["ss2"] = Buf()
                sb_l["rs2"] = sl("rs2", [128, NT], F32); b_l["rs2"] = Buf()
                sb_l["tmpf"] = [sl("tmpf%d" % i, [128, 512], F32) for i in range(2)]; b_l["tmpf"] = [Buf(), Buf()]
                sb_l["tmpf2"] = [sl("tmpf2_%d" % i, [128, 512], F32) for i in range(2)]; b_l["tmpf2"] = [Buf(), Buf()]
                load_norms(0)
                wo = sl("wo", [128, 8, D], BF16)
                S.dma("pool", wo[:], woab_d[:, :, :], writes=[b_wo])
                prenorm(x_d[seq], b_xdram)
                cur[0] = st2
                ropec = sl("ropec", [128, 2], F32)
                lamt = sl("lamt", [128, 4, 64], F32)
                lamp = sl("lamp", [128, 2, 64], F32)
                lams = sl("lams", [128, 2], F32)
                neglam = sl("neglam", [128, 1], F32)
                subcol = sl("subcol", [128, 1], F32)
                ones_b = sl("ones_b", [128, 128], BF16)
                posi = sl("posi", [128, SEQ], I32)
                Ct = sl("Ct", [128, SEQ], F32)
                St = sl("St", [128, SEQ], F32)
                qTs = [sl("qTa%d" % i, [128, SEQ], BF16) for i in range(2)]
                kTs = [sl("kTa%d" % i, [128, SEQ], BF16) for i in range(2)]
                Vhs = [sl("Vh%d" % i, [128, NT, 128], BF16) for i in range(2)]
                wsl = [sl("wa%d" % i, [128, KC, 128], BF16) for i in range(5)]
                wzA = [sl("wzA%d" % i, [128, KC, 128], BF16) for i in range(2)]
                rc = sl("rc", [128, 512], F32); rd = sl("rd", [128, 512], F32)
                vT5a = sl("vT5a", [128, 512], BF16)
                b_qTs = [Buf(), Buf()]; b_kTs = [Buf(), Buf()]; b_Vhs = [Buf(), Buf()]
                b_wzA = [Buf(), Buf()]
                b_rc, b_rd, b_vT5a = Buf(), Buf(), Buf()
                pt = [sl("pt%d" % i, [128, 512], BF16) for i in range(3)]
                ta = sb_l["tmpf"][0]; tb = sb_l["tmpf"][1]
                tc = sl("tc", [128, 512], F32); td = sl("td", [128, 512], F32)
                te = sl("te", [128, 512], F32); b_te = Buf()
                b_ropec, b_lam, b_neglam, b_subcol, b_onesb, b_posi, b_Ct, b_St = [Buf() for _ in range(8)]
                b_wsl = [Buf() for _ in range(5)]
                b_pt = [Buf() for _ in range(3)]
                b_ta, b_tb, b_tc, b_td = b_l["tmpf"][0], b_l["tmpf"][1], Buf(), Buf()

                S.dma("sp", ropec[:], rope_d[:, :], writes=[b_ropec])
                S.op("pool", lambda e: e.memset(ones_b[:], 1.0), writes=[b_onesb])
                S.dma("sp", lamt[:], bass.AP(lam_d.tensor, 0, [[0, 128], [64, 4], [1, 64]]), writes=[b_lam])
                S.op("dve", lambda e: e.tensor_tensor(out=lamp[:, 0, :], in0=lamt[:, 0, :], in1=lamt[:, 1, :], op=ALU.mult),
                     reads=[b_lam], writes=[b_lam])
                S.op("dve", lambda e: e.tensor_tensor(out=lamp[:, 1, :], in0=lamt[:, 2, :], in1=lamt[:, 3, :], op=ALU.mult),
                     reads=[b_lam], writes=[b_lam])
                S.op("dve", lambda e: e.tensor_reduce(out=lams[:], in_=lamp[:], axis=AX.X, op=ALU.add),
                     reads=[b_lam], writes=[b_lam])
                S.op("act", lambda e: e.activation(out=lams[:], in_=lams[:], func=AF.Exp), reads=[b_lam], writes=[b_lam])
                S.op("dve", lambda e: e.tensor_tensor(out=neglam[:], in0=lams[:, 1:2], in1=lams[:, 0:1], op=ALU.subtract),
                     reads=[b_lam], writes=[b_neglam])
                S.op("dve", lambda e: e.tensor_scalar(out=neglam[:], in0=neglam[:], scalar1=-LAMBDA_INIT, scalar2=None, op0=ALU.add),
                     reads=[b_neglam], writes=[b_neglam])
                S.dma("sp", subcol[:], subln_d[:, :], writes=[b_subcol])
                S.op("dve", lambda e: e.tensor_scalar(out=subcol[:], in0=subcol[:], scalar1=1.0 - LAMBDA_INIT, scalar2=None, op0=ALU.mult),
                     reads=[b_subcol], writes=[b_subcol])
                S.dma("sp", posi[:], bass.AP(pos_d.tensor, seq * SEQ, [[0, 128], [1, SEQ]]), writes=[b_posi])

                def sin_table(dst, bdst, phase, signed):
                    S.op("dve", lambda e: e.tensor_copy(out=dst[:], in_=posi[:]), reads=[b_posi], writes=[bdst])
                    S.op("dve", lambda e: e.tensor_scalar(out=dst[:], in0=dst[:], scalar1=ropec[:, 0:1], scalar2=phase,
                                                          op0=ALU.mult, op1=ALU.add), reads=[bdst, b_ropec], writes=[bdst])
                    for c4 in range(4):
                        sl_ = slice(c4 * 512, (c4 + 1) * 512)
                        tI = tc[:].bitcast(I32)
                        S.op("dve", lambda e: e.tensor_scalar(out=td[:], in0=dst[:, sl_], scalar1=1.0 / (2 * PI), scalar2=None,
                                                              op0=ALU.mult), reads=[bdst], writes=[b_td])
                        S.op("dve", lambda e: e.tensor_copy(out=tI, in_=td[:]), reads=[b_td], writes=[b_tc])
                        S.op("dve", lambda e: e.tensor_copy(out=td[:], in_=tI), reads=[b_tc], writes=[b_td])
                        S.op("dve", lambda e: e.scalar_tensor_tensor(out=td[:], in0=td[:], scalar=-2 * PI, in1=dst[:, sl_],
                                                                     op0=ALU.mult, op1=ALU.add),
                             reads=[b_td, bdst], writes=[b_td])
                        S.op("dve", lambda e: e.tensor_scalar(out=tc[:], in0=td[:], scalar1=PI, scalar2=-2 * PI,
                                                              op0=ALU.is_gt, op1=ALU.mult), reads=[b_td], writes=[b_tc])
                        S.op("dve", lambda e: e.tensor_tensor(out=td[:], in0=td[:], in1=tc[:], op=ALU.add),
                             reads=[b_td, b_tc], writes=[b_td])
                        S.op("dve", lambda e: e.tensor_scalar(out=td[:], in0=td[:], scalar1=-PI, scalar2=PI,
                                                              op0=ALU.max, op1=ALU.min), reads=[b_td], writes=[b_td])
                        S.op("act", lambda e: e.activation(out=dst[:, sl_], in_=td[:], func=AF.Sin),
                             reads=[b_td], writes=[bdst])
                    if signed:
                        S.op("dve", lambda e: e.tensor_scalar(out=dst[:], in0=dst[:], scalar1=ropec[:, 1:2], scalar2=None,
                                                              op0=ALU.mult), reads=[bdst, b_ropec], writes=[bdst])
                sin_table(Ct, b_Ct, PI / 2, False)
                sin_table(St, b_St, 0.0, True)

                def T_inA():
                    for h in range(4):
                        if h >= 2:
                            yield ("wait", ("attA", h - 2))
                        hs = h % 2
                        for i6 in range(5):
                            S.dma("pool", wsl[i6][:], wa_d[h, i6], writes=[b_wsl[i6]])
                        S.dma("pool", wzA[hs][:], wa_d[h, 5], writes=[b_wzA[hs]])
                        yield
                        for (i_w, dstT, bdst) in ((0, qTs[hs], b_qTs[hs]), (2, kTs[hs], b_kTs[hs])):
                            for t4 in range(4):
                                tsl = slice(t4 * 512, (t4 + 1) * 512)
                                for jj in range(2):
                                    for kc in range(KC):
                                        S.op("pe", lambda e: e.matmul(PB[7][:, :], lhsT=wsl[i_w + jj][:, kc, :],
                                                                      rhs=hT[:, kc, tsl], start=(kc == 0), stop=(kc == KC - 1)),
                                             reads=[b_wsl[i_w + jj]] + b_hT[4 * t4:4 * t4 + 4], writes=bPB[7],
                                             inc=(kc == KC - 1))
                                    yield
                                    tab, tabb, rt = ((Ct, b_Ct, (rc, b_rc)) if jj == 0 else (St, b_St, (rd, b_rd)))
                                    S.op("dve", lambda e: e.tensor_tensor(out=rt[0][:], in0=PB[7][:, :], in1=tab[:, tsl], op=ALU.mult),
                                         reads=bPB[7] + [tabb], writes=[rt[1]])
                                    yield
                                S.op("pool", lambda e: e.tensor_tensor(out=dstT[:, tsl], in0=rc[:], in1=rd[:], op=ALU.add),
                                     reads=[b_rc, b_rd], writes=[bdst])
                                yield
                        for t4 in range(4):
                            for kc in range(KC):
                                S.op("pe", lambda e: e.matmul(PB[7][:, :], lhsT=wsl[4][:, kc, :],
                                                              rhs=hT[:, kc, t4 * 512:(t4 + 1) * 512],
                                                              start=(kc == 0), stop=(kc == KC - 1)),
                                     reads=[b_wsl[4]] + b_hT[4 * t4:4 * t4 + 4], writes=bPB[7], inc=(kc == KC - 1))
                            yield
                            S.op("dve", lambda e: e.tensor_copy(out=vT5a[:], in_=PB[7][:, :]), reads=bPB[7], writes=[b_vT5a])
                            yield
                            pbf = PB[7][:, :].bitcast(BF16)[:, 0:512].rearrange("p (j c) -> p j c", c=128)
                            for j in range(4):
                                S.op("pe", lambda e: e.transpose(pbf[:, j, :], vT5a[:, j * 128:(j + 1) * 128], ident[:]),
                                     reads=[b_vT5a, b_const], writes=bPB[7], inc=(j == 3))
                            yield
                            S.op("dve", lambda e: e.tensor_copy(out=Vhs[hs][:, 4 * t4:4 * t4 + 4, :], in_=pbf),
                                 reads=bPB[7], writes=[b_Vhs[hs]])
                            yield
                        yield ("done", ("inA", h))

                def T_attA():
                    deferred = []
                    for h in range(4):
                        yield ("wait", ("inA", h))
                        hs = h % 2
                        qT, kT, Vh = qTs[hs], kTs[hs], Vhs[hs]
                        b_qT, b_kT, b_Vh = b_qTs[hs], b_kTs[hs], b_Vhs[hs]
                        wz5, b_wz5 = wzA[hs], b_wzA[hs]
                        jobs = []
                        for Qc in range(4):
                            for kt in range(4 * Qc + 4):
                                for c in range(2):
                                    jobs.append((Qc, kt, c))

                        def emit_pv(n):
                            Qc, kt, c = jobs[n]
                            o = max(0, kt - 4 * Qc) * 128
                            N = 512 - o
                            S.op("pe", lambda e: e.matmul(PB[2 + c][:, o:512], lhsT=Vh[:, kt, :], rhs=pt[n % 3][:, 0:N],
                                                          start=(kt == 0), stop=(kt == 4 * Qc + 3)),
                                 reads=[b_Vh, b_pt[n % 3]], writes=bPB[2 + c], inc=False)
                            S.op("pe", lambda e: e.matmul(PB[4 + c][:, o:512], lhsT=ones_b[:], rhs=pt[n % 3][:, 0:N],
                                                          start=(kt == 0), stop=(kt == 4 * Qc + 3)),
                                 reads=[b_onesb, b_pt[n % 3]], writes=bPB[4 + c])
                            if kt == 4 * Qc + 3 and c == 1:
                                emit_epilogue(Qc)

                        def emit_epilogue(Qc, h=h):
                            qsl = slice(Qc * 512, (Qc + 1) * 512)
                            for kc in range(KC):
                                S.op("pe", lambda e: e.matmul(PB[6][:, :], lhsT=wz5[:, kc, :], rhs=hT[:, kc, qsl],
                                                              start=(kc == 0), stop=(kc == KC - 1)),
                                     reads=[b_wz5] + b_hT[4 * Qc:4 * Qc + 4], writes=bPB[6], inc=(kc == KC - 1))
                            S.op("act", lambda e: e.activation(out=ta[:], in_=PB[4][:, :], func=AF.Ln), reads=bPB[4], writes=[b_ta])
                            S.op("act", lambda e: e.activation(out=ta[:], in_=ta[:], func=AF.Exp, scale=-1.0), reads=[b_ta], writes=[b_ta])
                            S.op("act", lambda e: e.activation(out=te[:], in_=PB[5][:, :], func=AF.Ln), reads=bPB[5], writes=[b_te])
                            S.op("act", lambda e: e.activation(out=te[:], in_=te[:], func=AF.Exp, scale=-1.0), reads=[b_te], writes=[b_te])
                            S.op("dve", lambda e: e.tensor_tensor(out=tb[:], in0=PB[2][:, :], in1=ta[:], op=ALU.mult),
                                 reads=bPB[2] + [b_ta], writes=[b_tb])
                            S.op("dve", lambda e: e.tensor_tensor(out=tc[:], in0=PB[3][:, :], in1=te[:], op=ALU.mult),
                                 reads=bPB[3] + [b_te], writes=[b_tc])

                            def s1():
                                S.op("dve", lambda e: e.scalar_tensor_tensor(out=tb[:], in0=tc[:], scalar=neglam[:, 0:1], in1=tb[:],
                                                                              op0=ALU.mult, op1=ALU.add),
                                     reads=[b_tc, b_tb, b_neglam], writes=[b_tb])
                                S.op("act", lambda e: e.activation(out=ta[:], in_=PB[6][:, :], func=AF.Exp, scale=-1.0),
                                     reads=bPB[6] + [b_ta], writes=[b_ta])
                                S.op("act", lambda e: e.activation(out=ta[:], in_=ta[:], func=AF.Ln, bias=1.0), reads=[b_ta], writes=[b_ta])
                                S.op("act", lambda e: e.activation(out=ta[:], in_=ta[:], func=AF.Exp, scale=-1.0), reads=[b_ta], writes=[b_ta])

                            def s2():
                                S.op("act", lambda e: e.activation(out=tc[:], in_=tb[:], func=AF.Square), reads=[b_tb], writes=[b_tc])
                                S.op("dve", lambda e: e.tensor_tensor(out=ta[:], in0=PB[6][:, :], in1=ta[:], op=ALU.mult),
                                     reads=bPB[6] + [b_ta], writes=[b_ta])

                            def s3():
                                S.op("pe", lambda e: e.matmul(PB[6][:, :], lhsT=ones_f[:], rhs=tc[:], start=True, stop=True),
                                     reads=[b_const, b_tc], writes=bPB[6])

                            def s4():
                                S.op("act", lambda e: e.activation(out=td[:], in_=PB[6][:, :], func=AF.Ln, scale=1.0 / 128, bias=EPS),
                                     reads=bPB[6], writes=[b_td])
                                S.op("act", lambda e: e.activation(out=td[:], in_=td[:], func=AF.Exp, scale=-0.5), reads=[b_td], writes=[b_td])

                            def s5():
                                S.op("pool", lambda e: e.tensor_tensor(out=tb[:], in0=tb[:], in1=td[:], op=ALU.mult),
                                     reads=[b_tb, b_td], writes=[b_tb])

                            def s6():
                                S.op("dve", lambda e: e.scalar_tensor_tensor(out=og[:, h, qsl], in0=tb[:], scalar=subcol[:, 0:1], in1=ta[:],
                                                                             op0=ALU.mult, op1=ALU.mult),
                                     reads=[b_tb, b_ta, b_subcol], writes=[b_og[h][Qc]])
                            for dl, fn in ((1, s1), (2, s2), (3, s3), (5, s4), (6, s5), (7, s6)):
                                deferred.append([dl, fn])

                        for n, (Qc, kt, c) in enumerate(jobs):
                            o = max(0, kt - 4 * Qc) * 128
                            N = 512 - o
                            q0 = Qc * 512 + o
                            sbk = n % 2
                            S.op("pe", lambda e: e.matmul(PB[sbk][:, 0:N], lhsT=kT[c * 64:(c + 1) * 64, kt * 128:(kt + 1) * 128],
                                                          rhs=qT[c * 64:(c + 1) * 64, q0:q0 + N], start=True, stop=True),
                                 reads=[b_kT, b_qT], writes=bPB[sbk])
                            S.op("act", lambda e: e.activation(out=pt[n % 3][:, 0:N], in_=PB[sbk][:, 0:N], func=AF.Exp, scale=0.125),
                                 reads=bPB[sbk], writes=[b_pt[n % 3]])
                            if kt >= 4 * Qc:
                                S.op("pool", lambda e: e.tensor_tensor(out=pt[n % 3][:, 0:128], in0=pt[n % 3][:, 0:128],
                                                                       in1=maskb[:], op=ALU.mult),
                                     reads=[b_pt[n % 3], b_const], writes=[b_pt[n % 3]])
                            if n >= 1:
                                emit_pv(n - 1)
                            for dfr in list(deferred):
                                dfr[0] -= 1
                                if dfr[0] <= 0:
                                    deferred.remove(dfr)
                                    dfr[1]()
                            yield
                        emit_pv(len(jobs) - 1)
                        yield ("done", ("attA", h))
                    while deferred:
                        for dfr in list(deferred):
                            dfr[0] -= 1
                            if dfr[0] <= 0:
                                deferred.remove(dfr)
                                dfr[1]()

                run_threads([T_inA(), T_attA()], None)
                S.fence()
                st2.close()
                st3 = st1.enter_context(contextlib.ExitStack())
                cur[0] = st3
                partB(seq, sl)
                cur[0] = st1
                if last:
                    outproj_residual(x_d[seq], b_xdram, out_d[seq], b_outd, wo)
                else:
                    load_pre(1)
                    outproj_residual(x_d[seq], b_xdram, x1s_d[seq], b_x1, wo, fuse=True)
                S.fence()
                st3.close()

        def run_threads(threads, weights):
            lists = []
            for g in threads:
                S.rec = []
                for r in g:
                    if isinstance(r, tuple):
                        S.rec.append((r[0], r[1]))
                lists.append(S.rec)
            S.rec = None
            ptr = [0] * len(lists)
            done = set()
            eng_free = {}
            rr = 0
            while True:
                best = None
                alive = False
                for ti in range(len(lists)):
                    L = lists[ti]
                    while ptr[ti] < len(L) and L[ptr[ti]][0] in ("wait", "done"):
                        kind, key = L[ptr[ti]]
                        if kind == "done":
                            done.add(key)
                            ptr[ti] += 1
                        elif key in done:
                            ptr[ti] += 1
                        else:
                            break
                    if ptr[ti] >= len(L):
                        continue
                    alive = True
                    it = L[ptr[ti]]
                    if it[0] in ("wait", "done"):
                        continue
                    kind, eng, call, reads, writes, inc = it
                    t0 = eng_free.get(eng, 0.0)
                    for b_ in reads:
                        if b_.tw > t0:
                            t0 = b_.tw
                        if b_.psum and b_.tr > t0:
                            t0 = b_.tr
                    for b_ in writes:
                        if b_.tw > t0:
                            t0 = b_.tw
                        if b_.tr > t0:
                            t0 = b_.tr
                    key2 = (t0, (ti - rr) % len(lists))
                    if best is None or key2 < best[0]:
                        best = (key2, ti, t0)
                if best is None:
                    assert not alive, "emission deadlock"
                    break
                _, ti, t0 = best
                kind, eng, call, reads, writes, inc = lists[ti][ptr[ti]]
                ptr[ti] += 1
                rr = (ti + 1) % len(lists)
                if kind == "op":
                    dur = _est_us(eng, call)
                    S.op(eng, lambda e: getattr(e, call[0])(*call[1], **call[2]), reads=reads, writes=writes, inc=inc)
                    eng_free[eng] = t0 + dur
                    tend = t0 + dur + SEM_LAT
                else:
                    out_, in__, kw_ = call
                    S.dma(eng, out_, in__, reads=reads, writes=writes, **kw_)
                    eng_free[eng] = t0 + 0.1
                    nb = 1
                    for d in out_.shape:
                        nb *= d
                    tend = t0 + 2.0 + nb * 4 / 150e3
                for b_ in reads:
                    if tend > b_.tr:
                        b_.tr = tend
                for b_ in writes:
                    b_.tw = tend
                    b_.tr = 0.0

        def run_threads_rr(threads, weights):
            done = set()
            st_ = [{"g": g, "wait": None, "alive": True} for g in threads]
            while any(t["alive"] for t in st_):
                progressed = False
                for t, w in zip(st_, weights):
                    if not t["alive"]:
                        continue
                    for _ in range(w):
                        if t["wait"] is not None:
                            if t["wait"] in done:
                                t["wait"] = None
                            else:
                                break
                        try:
                            r = next(t["g"])
                        except StopIteration:
                            t["alive"] = False
                            progressed = True
                            break
                        progressed = True
                        if isinstance(r, tuple):
                            if r[0] == "wait":
                                if r[1] not in done:
                                    t["wait"] = r[1]
                                    break
                            elif r[0] == "done":
                                done.add(r[1])
                assert progressed, "emission deadlock"

        def partB(seq, sl):
            gm = sl("gm", [128, 6, 128], F32)
            ublk, blk, strictblk, ident_f = gm[:, 0, :], gm[:, 1, :], gm[:, 2, :], gm[:, 5, :]
            halfsel = (gm[:, 3, :], gm[:, 4, :])
            ones_b = sl("ones_bB", [128, 128], BF16)
            wbz = sl("wbz", [128, KC, 512], BF16)
            wba = sl("wba", [128, KC, 8], BF16)
            wbq = [sl("wbq%d" % i, [128, KC, 128], BF16) for i in range(2)]
            cw = sl("cw", [128, 12, 4], F32)
            negA = sl("negA", [128, 4], F32)
            dtb = sl("dtb", [128, 4], F32)
            hn_bc = sl("hn_bc", [128, 128], F32)
            G = {n: sl("g_" + n, [128, NT, 4], F32) for n in ("xa", "g", "beta", "negb", "G", "GL", "eG", "bG", "dG", "eGL0", "eGL1")}
            cr = sl("cr", [128, 12, 3], F32)
            xc = [sl("xc%d" % i, [128, 515], F32) for i in range(2)]
            yc = sl("yc", [128, 512], F32)
            sc = sl("sc", [128, 512], F32)
            t1 = sl("t1", [128, 512], F32)
            sqb = sl("sqb", [128, 512], BF16)
            qnT = [sl("qnT%d" % i, [128, 4, 512], BF16) for i in range(2)]
            knT = [sl("knT%d" % i, [128, 4, 512], BF16) for i in range(2)]
            vsT = [sl("vsT%d" % i, [128, 4, 512], BF16) for i in range(2)]
            Dm = sl("Dm", [128, 4, 128], F32)
            DTm = sl("DTm", [128, 4, 128], F32)
            Pm1 = sl("Pm", [128, 4, 128], F32)
            Qm1 = sl("Qm", [128, 4, 128], F32)
            TT = sl("TT", [128, 4, 128], F32)
            gb = TT
            gU = Pm1
            TTb = sl("TTb", [128, 4, 128], BF16)
            qgT = [sl("qgT%d" % i, [128, 4, 128], BF16) for i in range(2)]
            kbg = [sl("kbg%d" % i, [128, 4, 128], BF16) for i in range(2)]
            kdec = [sl("kdec%d" % i, [128, 4, 128], BF16) for i in range(2)]
            vb = [sl("vb%d" % i, [128, 4, 128], BF16) for i in range(2)]
            qkDT = [sl("qkDT%d" % i, [128, 4, 128], BF16) for i in range(2)]
            wT = [sl("wT%d" % i, [128, 4, 128], BF16) for i in range(2)]
            u_t = [sl("u_t%d" % i, [128, 512], F32) for i in range(2)]
            vnew = sl("vnew", [128, 512], BF16)
            St = sl("Sst", [128, 4, 128], F32)
            Sdec = sl("Sdec", [128, 4, 128], F32)
            Sb = sl("Sb", [128, 4, 128], BF16)
            zs = sl("zs", [128, 512], F32)
            t1r = sl("t1r", [128, 512], F32)
            sqr = sl("sqr", [128, 128], BF16)
            sso = sl("sso", [128, 4], F32)
            ot = sl("ot", [128, 512], BF16)
            B = {n: Buf(n) for n in ("gm", "onesb", "wbz", "wba", "cw", "negA", "dtb", "hn", "gates", "cr", "yc", "sc", "t1", "sqb",
                                    "gb", "gU", "Dm", "DTm", "Pm", "Qm", "TT", "TTb",
                                    "vnew", "S", "Sdec", "Sb", "zs", "t1r", "sqr", "sso", "ot")}
            D2 = {n: [Buf(n + "0"), Buf(n + "1")] for n in ("qnT", "knT", "vsT", "qgT", "kbg", "kdec", "vb", "qkDT", "wT", "u")}
            B["gb"] = B["TT"]
            B["gU"] = B["Pm"]
            b_wbq = [Buf(), Buf()]
            b_xc = [Buf(), Buf()]
            I_A, I_B = 0, 1
            P_A, P_B, P_C = 2, 3, 4
            R_A, R_B, R_C = 5, 6, 7

            S.dma("sp", gm[:], gdnc_d[:, :, :], writes=[B["gm"]])
            S.op("pool", lambda e: e.memset(ones_b[:], 1.0), writes=[B["onesb"]])
            S.dma("pool", wbz[:], wbz_d[:, :, :], writes=[B["wbz"]])
            S.dma("pool", wba[:], wba_d[:, :, :], writes=[B["wba"]])
            S.dma("sp", cw[:], cw_d[:, :, :], writes=[B["cw"]])
            S.dma("sp", negA[:], bass.AP(alog_d.tensor, 0, [[0, 128], [1, 4]]), writes=[B["negA"]])
            S.dma("sp", dtb[:], bass.AP(dtb_d.tensor, 0, [[0, 128], [1, 4]]), writes=[B["dtb"]])
            S.dma("sp", hn_bc[:], bass.AP(hn_d.tensor, 0, [[0, 128], [1, 128]]), writes=[B["hn"]])
            S.op("act", lambda e: e.activation(out=negA[:], in_=negA[:], func=AF.Exp), reads=[B["negA"]], writes=[B["negA"]])
            S.op("dve", lambda e: e.tensor_scalar(out=negA[:], in0=negA[:], scalar1=-1.0, scalar2=None, op0=ALU.mult),
                 reads=[B["negA"]], writes=[B["negA"]])
            S.op("pool", lambda e: e.memset(cr[:], 0.0), writes=[B["cr"]])
            S.op("pool", lambda e: e.memset(St[:], 0.0), writes=[B["S"]])
            S.op("pool", lambda e: e.memset(Sb[:], 0.0), writes=[B["Sb"]])

            ba_ps = PB[0][:, 0:NT * 8].rearrange("p (t c) -> p t c", c=8)
            for i in range(NT):
                for kc in range(KC):
                    S.op("pe", lambda e: e.matmul(ba_ps[:, i, :], lhsT=hT[:, kc, i * 128:(i + 1) * 128], rhs=wba[:, kc, :],
                                                  start=(kc == 0), stop=(kc == KC - 1)),
                         reads=[b_hT[i], B["wba"]], writes=bPB[0], inc=(kc == KC - 1))
            bg = [B["gates"]]
            S.op("dve", lambda e: e.tensor_tensor(out=G["xa"][:], in0=ba_ps[:, :, 4:8], in1=dtb[:, None, :].to_broadcast([128, NT, 4]),
                                                  op=ALU.add), reads=bPB[0] + [B["dtb"]], writes=bg)
            S.op("act", lambda e: e.activation(out=G["xa"][:], in_=G["xa"][:], func=AF.Exp), reads=bg, writes=bg)
            S.op("act", lambda e: e.activation(out=G["xa"][:], in_=G["xa"][:], func=AF.Ln, bias=1.0), reads=bg, writes=bg)
            S.op("dve", lambda e: e.tensor_tensor(out=G["g"][:], in0=G["xa"][:], in1=negA[:, None, :].to_broadcast([128, NT, 4]),
                                                  op=ALU.mult), reads=bg + [B["negA"]], writes=bg)
            S.op("act", lambda e: e.activation(out=G["beta"][:], in_=ba_ps[:, :, 0:4], func=AF.Exp, scale=-1.0),
                 reads=bPB[0] + bg, writes=bg)
            S.op("act", lambda e: e.activation(out=G["beta"][:], in_=G["beta"][:], func=AF.Ln, bias=1.0), reads=bg, writes=bg)
            S.op("act", lambda e: e.activation(out=G["beta"][:], in_=G["beta"][:], func=AF.Exp, scale=-1.0), reads=bg, writes=bg)
            S.op("dve", lambda e: e.tensor_scalar(out=G["negb"][:], in0=G["beta"][:], scalar1=-1.0, scalar2=None, op0=ALU.mult),
                 reads=bg, writes=bg)
            gflat = G["g"][:].rearrange("p t c -> p (t c)")
            S.op("pe", lambda e: e.matmul(PB[1][:, 0:64], lhsT=ublk, rhs=gflat, start=True, stop=True),
                 reads=bg + [B["gm"]], writes=bPB[1], inc=False)
            S.op("pe", lambda e: e.matmul(PB[1][:, 64:128], lhsT=blk, rhs=gflat, start=True, stop=True),
                 reads=bg + [B["gm"]], writes=bPB[1], inc=False)
            S.op("pe", lambda e: e.matmul(PB[1][:, 128:192], lhsT=halfsel[0], rhs=gflat, start=True, stop=True),
                 reads=bg + [B["gm"]], writes=bPB[1], inc=False)
            S.op("pe", lambda e: e.matmul(PB[1][:, 192:256], lhsT=halfsel[1], rhs=gflat, start=True, stop=True),
                 reads=bg + [B["gm"]], writes=bPB[1])
            v3 = lambda ap: ap.rearrange("p (t c) -> p t c", c=4)
            S.op("dve", lambda e: e.tensor_copy(out=G["G"][:], in_=v3(PB[1][:, 0:64])), reads=bPB[1] + bg, writes=bg)
            S.op("dve", lambda e: e.tensor_copy(out=G["GL"][:], in_=v3(PB[1][:, 64:128])), reads=bPB[1] + bg, writes=bg)
            S.op("act", lambda e: e.activation(out=G["eGL0"][:], in_=v3(PB[1][:, 128:192]), func=AF.Exp), reads=bPB[1] + bg, writes=bg)
            S.op("act", lambda e: e.activation(out=G["eGL1"][:], in_=v3(PB[1][:, 192:256]), func=AF.Exp), reads=bPB[1] + bg, writes=bg)
            S.op("act", lambda e: e.activation(out=G["eG"][:], in_=G["G"][:], func=AF.Exp), reads=bg, writes=bg)
            S.op("dve", lambda e: e.tensor_tensor(out=G["bG"][:], in0=G["beta"][:], in1=G["eG"][:], op=ALU.mult), reads=bg, writes=bg)
            S.op("dve", lambda e: e.tensor_tensor(out=G["dG"][:], in0=G["GL"][:], in1=G["G"][:], op=ALU.subtract), reads=bg, writes=bg)
            S.op("act", lambda e: e.activation(out=G["dG"][:], in_=G["dG"][:], func=AF.Exp), reads=bg, writes=bg)
            eGLb = (G["eGL0"], G["eGL1"])
            bgr = [Buf("gates_ro")]
            bgr[0].w = B["gates"].w

            bc4 = lambda ap2: ap2[:, :, None].to_broadcast([128, 4, 128])
            hb4 = lambda ap2: ap2[:, None, :].to_broadcast([128, 4, 128])
            h4 = lambda pb: PB[pb][:, :].rearrange("p (h c) -> p h c", c=128)

            def T_I():
                n_w = 0
                for blkI in range(4):
                    if blkI >= 2:
                        yield ("wait", ("Rblk", blkI - 2))
                    par = blkI % 2
                    bsl = slice(blkI * 512, (blkI + 1) * 512)
                    for jc in range(12):
                        wt = wbq[n_w % 2]; bwt = b_wbq[n_w % 2]
                        xcb = xc[n_w % 2]; bxc = b_xc[n_w % 2]
                        n_w += 1
                        S.dma("pool", wt[:], wbq_d[jc], writes=[bwt])
                        for kc in range(KC):
                            S.op("pe", lambda e: e.matmul(PB[I_A][:, :], lhsT=wt[:, kc, :], rhs=hT[:, kc, bsl],
                                                          start=(kc == 0), stop=(kc == KC - 1)),
                                 reads=[bwt] + b_hT[4 * blkI:4 * blkI + 4], writes=bPB[I_A], inc=(kc == KC - 1))
                        yield
                        S.op("pool", lambda e: e.tensor_copy(out=xcb[:, 0:3], in_=cr[:, jc, :]), reads=[B["cr"]], writes=[bxc])
                        S.op("act", lambda e: e.activation(out=xcb[:, 3:515], in_=PB[I_A][:, :], func=AF.Copy),
                             reads=bPB[I_A], writes=[bxc])
                        S.op("pool", lambda e: e.tensor_copy(out=cr[:, jc, :], in_=xcb[:, 512:515]), reads=[bxc], writes=[B["cr"]])
                        yield
                        S.op("dve", lambda e: e.tensor_scalar(out=yc[:], in0=xcb[:, 0:512], scalar1=cw[:, jc, 0:1], scalar2=None,
                                                              op0=ALU.mult), reads=[bxc, B["cw"]], writes=[B["yc"]])
                        for tap in range(1, 4):
                            S.op("dve", lambda e: e.scalar_tensor_tensor(out=yc[:], in0=xcb[:, tap:tap + 512], scalar=cw[:, jc, tap:tap + 1],
                                                                         in1=yc[:], op0=ALU.mult, op1=ALU.add),
                                 reads=[bxc, B["cw"], B["yc"]], writes=[B["yc"]])
                            yield
                        S.op("act", lambda e: e.activation(out=t1[:], in_=yc[:], func=AF.Exp, scale=-1.0), reads=[B["yc"]], writes=[B["t1"]])
                        S.op("act", lambda e: e.activation(out=t1[:], in_=t1[:], func=AF.Ln, bias=1.0), reads=[B["t1"]], writes=[B["t1"]])
                        S.op("act", lambda e: e.activation(out=t1[:], in_=t1[:], func=AF.Exp, scale=-1.0), reads=[B["t1"]], writes=[B["t1"]])
                        yield
                        if jc >= 8:
                            S.op("dve", lambda e: e.tensor_tensor(out=vsT[par][:, jc - 8, :], in0=yc[:], in1=t1[:], op=ALU.mult),
                                 reads=[B["yc"], B["t1"]], writes=[D2["vsT"][par]])
                            yield
                            continue
                        S.op("dve", lambda e: e.tensor_tensor(out=sc[:], in0=yc[:], in1=t1[:], op=ALU.mult),
                             reads=[B["yc"], B["t1"]], writes=[B["sc"]])
                        S.op("act", lambda e: e.activation(out=sqb[:], in_=sc[:], func=AF.Square), reads=[B["sc"]], writes=[B["sqb"]])
                        yield
                        S.op("pe", lambda e: e.matmul(PB[I_B][:, :], lhsT=ones_b[:], rhs=sqb[:], start=True, stop=True),
                             reads=[B["onesb"], B["sqb"]], writes=bPB[I_B])
                        S.op("act", lambda e: e.activation(out=t1[:], in_=PB[I_B][:, :], func=AF.Ln, bias=EPS),
                             reads=bPB[I_B] + [B["t1"]], writes=[B["t1"]])
                        isq = jc < 4
                        S.op("act", lambda e: e.activation(out=t1[:], in_=t1[:], func=AF.Exp, scale=-0.5,
                                                           bias=(-0.5 * math.log(128.0) if isq else 0.0)),
                             reads=[B["t1"]], writes=[B["t1"]])
                        yield
                        dstT = qnT[par] if isq else knT[par]
                        bd = D2["qnT"][par] if isq else D2["knT"][par]
                        S.op("dve", lambda e: e.tensor_tensor(out=dstT[:, jc % 4, :], in0=sc[:], in1=t1[:], op=ALU.mult),
                             reads=[B["sc"], B["t1"]], writes=[bd])
                        yield
                    yield ("done", ("I", blkI))

            def T_P():
                for i in range(NT):
                    blkI, tl = divmod(i, 4)
                    par = blkI % 2
                    tp = i % 2
                    yield ("wait", ("I", blkI))
                    if i >= 2:
                        yield ("wait", ("R", i - 2))
                    csl = slice(tl * 128, (tl + 1) * 128)
                    qn, kn, vs = qnT[par], knT[par], vsT[par]
                    bqn, bkn, bvs = D2["qnT"][par], D2["knT"][par], D2["vsT"][par]
                    pa, pb_, pc = h4(P_A), h4(P_B), h4(P_C)
                    S.op("dve", lambda e: e.tensor_copy(out=gb[:], in_=bc4(G["g"][:, i, :])), reads=bgr, writes=[B["gb"]])
                    for h in range(4):
                        S.op("pe", lambda e: e.matmul(pa[:, h, :], lhsT=gb[:, h, :], rhs=ublk, start=True, stop=True),
                             reads=[B["gb"], B["gm"]], writes=bPB[P_A], inc=(h == 3))
                    yield
                    S.op("act", lambda e: e.activation(out=Dm[:], in_=pa, func=AF.Exp), reads=bPB[P_A], writes=[B["Dm"]])
                    S.op("dve", lambda e: e.tensor_tensor(out=qgT[tp][:], in0=qn[:, :, csl], in1=Dm[:], op=ALU.mult),
                         reads=[bqn, B["Dm"]], writes=[D2["qgT"][tp]])
                    yield
                    p3b = PB[P_B][:, :].bitcast(BF16).rearrange("p (a h c) -> p a h c", a=2, h=4)
                    for h in range(4):
                        S.op("pe", lambda e: e.transpose(p3b[:, 0, h, :], kn[:, h, csl], ident[:]),
                             reads=[bkn, b_const], writes=bPB[P_B], inc=False)
                    for h in range(4):
                        S.op("pe", lambda e: e.transpose(p3b[:, 1, h, :], vs[:, h, csl], ident[:]),
                             reads=[bvs, b_const], writes=bPB[P_B], inc=(h == 3))
                    yield
                    S.op("dve", lambda e: e.tensor_tensor(out=kbg[tp][:], in0=p3b[:, 0], in1=bc4(G["bG"][:, i, :]), op=ALU.mult),
                         reads=bPB[P_B] + bgr, writes=[D2["kbg"][tp]])
                    S.op("dve", lambda e: e.tensor_tensor(out=kdec[tp][:], in0=p3b[:, 0], in1=bc4(G["dG"][:, i, :]), op=ALU.mult),
                         reads=bPB[P_B] + bgr, writes=[D2["kdec"][tp]])
                    yield
                    S.op("dve", lambda e: e.tensor_tensor(out=vb[tp][:], in0=p3b[:, 1], in1=bc4(G["beta"][:, i, :]), op=ALU.mult),
                         reads=bPB[P_B] + bgr, writes=[D2["vb"][tp]])
                    S.op("pool", lambda e: e.tensor_tensor(out=gU[:], in0=hb4(ublk), in1=bc4(G["g"][:, i, :]), op=ALU.mult),
                         reads=bgr + [B["gm"]], writes=[B["gU"]])
                    yield
                    for h in range(4):
                        S.op("pe", lambda e: e.matmul(pa[:, h, :], lhsT=gU[:, h, :], rhs=strictblk, start=True, stop=True),
                             reads=[B["gU"], B["gm"]], writes=bPB[P_A], inc=(h == 3))
                    for h in range(4):
                        S.op("pe", lambda e: e.matmul(pb_[:, h, :], lhsT=strictblk, rhs=gU[:, h, :], start=True, stop=True),
                             reads=[B["gU"], B["gm"]], writes=bPB[P_B], inc=(h == 3))
                    yield
                    S.op("act", lambda e: e.activation(out=Dm[:], in_=pa, func=AF.Exp), reads=bPB[P_A] + [B["Dm"]], writes=[B["Dm"]])
                    S.op("act", lambda e: e.activation(out=DTm[:], in_=pb_, func=AF.Exp), reads=bPB[P_B], writes=[B["DTm"]])
                    yield
                    for h in range(4):
                        S.op("pe", lambda e: e.matmul(pa[:, h, :], lhsT=kn[:, h, csl], rhs=kn[:, h, csl], start=True, stop=True),
                             reads=[bkn], writes=bPB[P_A], inc=(h == 3))
                    for h in range(4):
                        S.op("pe", lambda e: e.matmul(pb_[:, h, :], lhsT=kn[:, h, csl], rhs=qn[:, h, csl], start=True, stop=True),
                             reads=[bkn, bqn], writes=bPB[P_B], inc=(h == 3))
                    yield
                    S.op("pool", lambda e: e.tensor_tensor(out=Dm[:], in0=Dm[:], in1=hb4(strictblk), op=ALU.mult),
                         reads=[B["Dm"], B["gm"]], writes=[B["Dm"]])
                    S.op("pool", lambda e: e.tensor_tensor(out=Dm[:], in0=Dm[:], in1=bc4(G["negb"][:, i, :]), op=ALU.mult),
                         reads=[B["Dm"]] + bgr, writes=[B["Dm"]])
                    yield
                    S.op("dve", lambda e: e.tensor_tensor(out=Pm1[:], in0=pa, in1=Dm[:], op=ALU.mult),
                         reads=bPB[P_A] + [B["Dm"]], writes=[B["Pm"]])
                    S.op("pool", lambda e: e.tensor_tensor(out=DTm[:], in0=DTm[:], in1=hb4(ublk), op=ALU.mult),
                         reads=[B["DTm"], B["gm"]], writes=[B["DTm"]])
                    yield
                    S.op("dve", lambda e: e.tensor_tensor(out=qkDT[tp][:], in0=pb_, in1=DTm[:], op=ALU.mult),
                         reads=bPB[P_B] + [B["DTm"]], writes=[D2["qkDT"][tp]])
                    for h in range(4):
                        S.op("pe", lambda e: e.matmul(pc[:, h, :], lhsT=Pm1[:, h, :], rhs=ident_f, start=True, stop=True),
                             reads=[B["Pm"], B["gm"]], writes=bPB[P_C], inc=(h == 3))
                    yield
                    S.op("act", lambda e: e.activation(out=Qm1[:], in_=pc, func=AF.Copy), reads=bPB[P_C], writes=[B["Qm"]])
                    S.op("dve", lambda e: e.tensor_tensor(out=TT[:], in0=pc, in1=hb4(ident_f), op=ALU.add),
                         reads=bPB[P_C] + [B["gm"]], writes=[B["TT"]])
                    yield
                    for lvl in range(5):
                        for h in range(4):
                            S.op("pe", lambda e: e.matmul(pa[:, h, :], lhsT=Qm1[:, h, :], rhs=Pm1[:, h, :], start=True, stop=True),
                                 reads=[B["Qm"], B["Pm"]], writes=bPB[P_A], inc=(h == 3))
                        if lvl < 4:
                            for h in range(4):
                                S.op("pe", lambda e: e.matmul(pb_[:, h, :], lhsT=Pm1[:, h, :], rhs=Qm1[:, h, :], start=True, stop=True),
                                     reads=[B["Qm"], B["Pm"]], writes=bPB[P_B], inc=(h == 3))
                        yield
                        S.op("act", lambda e: e.activation(out=Pm1[:], in_=pa, func=AF.Copy), reads=bPB[P_A], writes=[B["Pm"]])
                        if lvl < 4:
                            S.op("dve", lambda e: e.tensor_copy(out=Qm1[:], in_=pb_), reads=bPB[P_B], writes=[B["Qm"]])
                        yield
                        for h in range(4):
                            S.op("pe", lambda e: e.matmul(pc[:, h, :], lhsT=Pm1[:, h, :], rhs=TT[:, h, :], start=True, stop=True),
                                 reads=[B["Pm"], B["TT"]], writes=bPB[P_C], inc=(h == 3))
                        yield
                        S.op("dve", lambda e: e.tensor_tensor(out=TT[:], in0=pc, in1=TT[:], op=ALU.add),
                             reads=bPB[P_C] + [B["TT"]], writes=[B["TT"]])
                        yield
                    S.op("act", lambda e: e.activation(out=TTb[:], in_=TT[:], func=AF.Copy), reads=[B["TT"]], writes=[B["TTb"]])
                    yield
                    for h in range(4):
                        S.op("pe", lambda e: e.matmul(pa[:, h, :], lhsT=TTb[:, h, :], rhs=vb[tp][:, h, :], start=True, stop=True),
                             reads=[B["TTb"], D2["vb"][tp]], writes=bPB[P_A], inc=(h == 3))
                    for h in range(4):
                        S.op("pe", lambda e: e.matmul(pb_[:, h, :], lhsT=kbg[tp][:, h, :], rhs=TTb[:, h, :], start=True, stop=True),
                             reads=[B["TTb"], D2["kbg"][tp]], writes=bPB[P_B], inc=(h == 3))
                    yield
                    S.op("act", lambda e: e.activation(out=u_t[tp][:], in_=PB[P_A][:, :], func=AF.Copy), reads=bPB[P_A], writes=[D2["u"][tp]])
                    S.op("dve", lambda e: e.tensor_copy(out=wT[tp][:], in_=pb_), reads=bPB[P_B], writes=[D2["wT"][tp]])
                    yield ("done", ("P", i))

            def T_R():
                for i in range(NT):
                    tp = i % 2
                    yield ("wait", ("P", i))
                    for ch in range(2):
                        pr = slice(64 * ch, 64 * ch + 64)
                        pc_ = slice(64 * ch, 64 * ch + 64)
                        S.op("pool", lambda e: e.tensor_tensor(out=Sdec[:], in0=St[:], in1=bc4(eGLb[ch][:, i, :]), op=ALU.mult),
                             reads=[B["S"]] + bgr, writes=[B["Sdec"]])
                        for h in range(4):
                            S.op("pe", lambda e: e.matmul(PB[R_A][pr, h * 128:(h + 1) * 128], lhsT=wT[tp][:, h, pc_], rhs=Sb[:, h, :],
                                                          start=True, stop=True),
                                 reads=[D2["wT"][tp], B["Sb"]], writes=bPB[R_A], inc=(h == 3))
                        yield
                        S.op("dve", lambda e: e.tensor_tensor(out=vnew[pr, :], in0=u_t[tp][pr, :], in1=PB[R_A][pr, :], op=ALU.subtract),
                             reads=bPB[R_A] + [D2["u"][tp]], writes=[B["vnew"]])
                        yield
                        for h in range(4):
                            S.op("pe", lambda e: e.matmul(PB[R_B][pr, h * 128:(h + 1) * 128], lhsT=qgT[tp][:, h, pc_], rhs=Sb[:, h, :],
                                                          start=True, stop=False),
                                 reads=[D2["qgT"][tp], B["Sb"]], writes=bPB[R_B], inc=False)
                            S.op("pe", lambda e: e.matmul(PB[R_B][pr, h * 128:(h + 1) * 128], lhsT=qkDT[tp][pr, h, pc_],
                                                          rhs=vnew[pr, h * 128:(h + 1) * 128], start=False, stop=True),
                                 reads=[D2["qkDT"][tp], B["vnew"]], writes=bPB[R_B], inc=(h == 3))
                        yield
                        for h in range(4):
                            S.op("pe", lambda e: e.matmul(PB[R_C][:, h * 128:(h + 1) * 128], lhsT=kdec[tp][pr, h, :],
                                                          rhs=vnew[pr, h * 128:(h + 1) * 128], start=True, stop=True),
                                 reads=[D2["kdec"][tp], B["vnew"]], writes=bPB[R_C], inc=(h == 3))
                        yield
                        Sf = St[:].rearrange("p h c -> p (h c)")
                        Sdf = Sdec[:].rearrange("p h c -> p (h c)")
                        Sbf = Sb[:].rearrange("p h c -> p (h c)")
                        S.op("dve", lambda e: e.tensor_tensor(out=Sbf, in0=PB[R_C][:, :], in1=Sdf, op=ALU.add),
                             reads=bPB[R_C] + [B["Sdec"]], writes=[B["Sb"]])
                        S.op("dve", lambda e: e.tensor_tensor(out=Sf, in0=PB[R_C][:, :], in1=Sdf, op=ALU.add),
                             reads=bPB[R_C] + [B["Sdec"]], writes=[B["S"]])
                        yield
                    for kc in range(KC):
                        S.op("pe", lambda e: e.matmul(PB[R_A][:, :], lhsT=hT[:, kc, i * 128:(i + 1) * 128], rhs=wbz[:, kc, :],
                                                      start=(kc == 0), stop=(kc == KC - 1)),
                             reads=[b_hT[i], B["wbz"]], writes=bPB[R_A], inc=(kc == KC - 1))
                    yield
                    S.op("act", lambda e: e.activation(out=zs[:], in_=PB[R_A][:, :], func=AF.Exp, scale=-1.0), reads=bPB[R_A], writes=[B["zs"]])
                    S.op("act", lambda e: e.activation(out=zs[:], in_=zs[:], func=AF.Ln, bias=1.0), reads=[B["zs"]], writes=[B["zs"]])
                    S.op("act", lambda e: e.activation(out=zs[:], in_=zs[:], func=AF.Exp, scale=-1.0), reads=[B["zs"]], writes=[B["zs"]])
                    yield
                    S.op("dve", lambda e: e.tensor_tensor(out=zs[:], in0=PB[R_A][:, :], in1=zs[:], op=ALU.mult),
                         reads=bPB[R_A] + [B["zs"]], writes=[B["zs"]])
                    zs3 = zs[:].rearrange("p (h c) -> p h c", c=128)
                    S.op("pool", lambda e: e.tensor_tensor(out=zs3, in0=zs3, in1=hb4(hn_bc[:]), op=ALU.mult),
                         reads=[B["zs"], B["hn"]], writes=[B["zs"]])
                    yield
                    for h in range(4):
                        S.op("act", lambda e: e.activation(out=sqr[:], in_=PB[R_B][:, h * 128:(h + 1) * 128], func=AF.Square,
                                                           accum_out=sso[:, h:h + 1]),
                             reads=bPB[R_B], writes=[B["sqr"], B["sso"]])
                    S.op("act", lambda e: e.activation(out=sso[:], in_=sso[:], func=AF.Ln, scale=1.0 / 128, bias=EPS),
                         reads=[B["sso"]], writes=[B["sso"]])
                    S.op("act", lambda e: e.activation(out=sso[:], in_=sso[:], func=AF.Exp, scale=-0.5), reads=[B["sso"]], writes=[B["sso"]])
                    yield
                    t13 = t1r[:].rearrange("p (h c) -> p h c", c=128)
                    S.op("dve", lambda e: e.tensor_tensor(out=t13, in0=h4(R_B), in1=bc4(sso[:]), op=ALU.mult),
                         reads=bPB[R_B] + [B["sso"], B["t1r"]], writes=[B["t1r"]])
                    S.op("dve", lambda e: e.tensor_tensor(out=ot[:], in0=t1r[:], in1=zs[:], op=ALU.mult),
                         reads=[B["t1r"], B["zs"]], writes=[B["ot"]])
                    yield
                    p3o = PB[R_C][:, :].bitcast(BF16)[:, 0:512].rearrange("p (h c) -> p h c", c=128)
                    for h in range(4):
                        S.op("pe", lambda e: e.transpose(p3o[:, h, :], ot[:, h * 128:(h + 1) * 128], ident[:]),
                             reads=[B["ot"], b_const], writes=bPB[R_C], inc=(h == 3))
                    S.op("act", lambda e: e.activation(out=og[:, 4:8, i * 128:(i + 1) * 128], in_=p3o, func=AF.Copy),
                         reads=bPB[R_C], writes=[b_og[4 + hh][i // 4] for hh in range(4)])
                    yield ("done", ("R", i))
                    if i % 4 == 3:
                        yield ("done", ("Rblk", i // 4))

            run_threads([T_I(), T_P(), T_R()], BW)

        def layer1(seq, last):
            nonlocal sb_l, b_l
            with contextlib.ExitStack() as st1:
                st2 = st1.enter_context(contextlib.ExitStack())
                cur = [st1]

                def sl(name, shape, dt):
                    uid[0] += 1
                    return cur[0].enter_context(nc.sbuf_tensor("t%d_%s" % (uid[0], name), list(shape), dt))
                sb_l = {}
                b_l = {}
                sb_l["ss2"] = sl("ss2", [128, 2 * NT], F32); b_l["ss2"] = Buf()
                sb_l["rs2"] = sl("rs2", [128, NT], F32); b_l["rs2"] = Buf()
                sb_l["tmpf"] = [sl("tmpf%d" % i, [128, 512], F32) for i in range(2)]; b_l["tmpf"] = [Buf(), Buf()]
                sb_l["tmpf2"] = [sl("tmpf2_%d" % i, [128, 512], F32) for i in range(2)]; b_l["tmpf2"] = [Buf(), Buf()]
                wo = sl("wo", [128, 8, D], BF16)
                cur[0] = st2
                qT = [[sl("qT%d_%d" % (s_, i), [128, SEQ], BF16) for i in range(2)] for s_ in range(2)]
                kT = [[sl("kT%d_%d" % (s_, i), [128, SEQ], BF16) for i in range(2)] for s_ in range(2)]
                Vx = [[sl("Vx%d_%d" % (s_, i), [128, NT, 128], BF16) for i in range(2)] for s_ in range(2)]
                vT5 = sl("vT5", [128, 512], BF16)
                b_vT5 = Buf()
                wq = sl("wq", [128, KC, 128], BF16)
                wk = sl("wk", [128, KC, 128], BF16)
                wv = sl("wv", [128, KC, 128], BF16)
                wz = [sl("wz%d" % i, [128, KC, 128], BF16) for i in range(2)]
                wf = sl("wf", [128, KC, 16], BF16)
                fb_bc = sl("fb_bc", [128, 16], F32)
                flb = sl("flb", [128, NT, 16], F32)
                nlf = sl("nlf", [128, NT, 16], F32)
                NC_ = sl("NC", [128, NT, 16], F32)
                carry = sl("carry", [128, 16], F32)
                carryT = sl("carryT", [16, 1], F32)
                cT = sl("cT", [16, SEQ], F32)
                cHL = sl("cHL", [16, 2, SEQ], BF16)
                pt = [sl("pt%d" % i, [128, 512], BF16) for i in range(3)]
                e_t = sb_l["tmpf"][0]
                sums = sb_l["tmpf"][1]
                den = sl("den", [128, 512], F32)
                tt = den
                b_qT = [[Buf(), Buf()] for _ in range(2)]; b_kT = [[Buf(), Buf()] for _ in range(2)]
                b_Vx = [[Buf(), Buf()] for _ in range(2)]
                b_qaug = [[Buf(), Buf()] for _ in range(2)]
                b_wq, b_wk, b_wv = Buf(), Buf(), Buf()
                b_wz = [Buf(), Buf()]
                b_wf, b_fb, b_flb, b_nlf, b_NC, b_carry, b_carryT, b_cT, b_cTt, b_cHL = [Buf() for _ in range(10)]
                b_pt = [Buf() for _ in range(3)]
                b_e, b_sums, b_den = b_l["tmpf"][0], b_l["tmpf"][1], Buf()
                b_tt = b_den

                try:
                    if 0 in layers:
                        load_post(1)
                    else:
                        load_norms(1)
                    S.dma("pool", wo[:], woc_d[:, :, :], writes=[b_wo])
                    S.dma("pool", wf[:], wf_d[:, :, :], writes=[b_wf])
                    S.dma("sp", fb_bc[:], bass.AP(fb_d.tensor, 0, [[0, 128], [1, 16]]), writes=[b_fb])
                    for s_ in range(2):
                        for i in range(2):
                            S.op("pool", lambda e: e.memset(kT[s_][i][64:66, :], 1.0), writes=[b_kT[s_][i]])
                        S.op("pool", lambda e: e.memset(Vx[s_][0][:, :, 64:128], 1.0), writes=[b_Vx[s_][0]])
                        S.op("pool", lambda e: e.memset(Vx[s_][1][:, :, 0:64], 1.0), writes=[b_Vx[s_][1]])
                    S.op("pool", lambda e: e.memset(carry[:], 0.0), writes=[b_carry])
                    S.op("pool", lambda e: e.memset(carryT[:], 0.0), writes=[b_carryT])

                    l1src = x1s_d[seq] if 0 in layers else x_d[seq]
                    l1b = b_x1 if 0 in layers else b_xdram
                    if 0 not in layers:
                        prenorm(l1src, l1b)
                    if STAGE <= 1:
                        raise StopStage()

                    fl_ps = PB[0][:, 0:NT * 16].rearrange("p (t h) -> p t h", h=16)
                    for i in range(NT):
                        for kc in range(KC):
                            S.op("pe", lambda e: e.matmul(fl_ps[:, i, :], lhsT=hT[:, kc, i * 128:(i + 1) * 128],
                                                          rhs=wf[:, kc, :], start=(kc == 0), stop=(kc == KC - 1)),
                                 reads=[b_hT[i], b_wf], writes=bPB[0], inc=(kc == KC - 1))
                    S.op("dve", lambda e: e.tensor_tensor(out=flb[:], in0=fl_ps, in1=fb_bc[:, None, :].to_broadcast([128, NT, 16]),
                                                          op=ALU.add),
                         reads=bPB[0] + [b_fb], writes=[b_flb])
                    S.op("act", lambda e: e.activation(out=flb[:], in_=flb[:], func=AF.Exp, scale=-1.0),
                         reads=[b_flb], writes=[b_flb])
                    S.op("act", lambda e: e.activation(out=nlf[:], in_=flb[:], func=AF.Ln, bias=1.0),
                         reads=[b_flb], writes=[b_nlf])
                    for i in range(NT):
                        bk = 1 + (i % 2)
                        c1 = PB[bk][:, 0:16]
                        c2 = PB[bk][:, 16:32]
                        c3 = PB[bk][0:16, 32:32 + 129]
                        S.op("pe", lambda e: e.matmul(c1, lhsT=uext[:, 0:128], rhs=nlf[:, i, :], start=True, stop=True),
                             reads=[b_nlf, b_const], writes=bPB[bk], inc=False)
                        S.op("pe", lambda e: e.matmul(c2, lhsT=ones_f[:], rhs=nlf[:, i, :], start=True, stop=True),
                             reads=[b_nlf, b_const], writes=bPB[bk], inc=False)
                        S.op("pe", lambda e: e.matmul(c3, lhsT=nlf[:, i, :], rhs=uext[:, :], start=True, stop=True),
                             reads=[b_nlf, b_const], writes=bPB[bk])
                        S.op("dve", lambda e: e.tensor_tensor(out=NC_[:, i, :], in0=c1, in1=carry[:], op=ALU.add),
                             reads=bPB[bk] + [b_carry], writes=[b_NC])
                        S.op("dve", lambda e: e.tensor_tensor(out=carry[:], in0=c2, in1=carry[:], op=ALU.add),
                             reads=bPB[bk] + [b_carry], writes=[b_carry])
                        S.op("dve", lambda e: e.tensor_scalar(out=cT[:, i * 128:(i + 1) * 128], in0=c3[:, 0:128],
                                                              scalar1=carryT[:, 0:1], scalar2=-8.0,
                                                              op0=ALU.add, op1=ALU.mult),
                             reads=bPB[bk] + [b_carryT], writes=[b_cT])
                        S.op("dve", lambda e: e.tensor_tensor(out=carryT[:], in0=c3[:, 128:129], in1=carryT[:], op=ALU.add),
                             reads=bPB[bk] + [b_carryT], writes=[b_carryT])
                    S.op("dve", lambda e: e.tensor_copy(out=cHL[:, 0, :], in_=cT[:]), reads=[b_cT], writes=[b_cHL])
                    S.op("dve", lambda e: e.tensor_tensor(out=cT[:], in0=cT[:], in1=cHL[:, 0, :], op=ALU.subtract),
                         reads=[b_cT, b_cHL], writes=[b_cT])
                    S.op("dve", lambda e: e.tensor_copy(out=cHL[:, 1, :], in_=cT[:]), reads=[b_cT, b_cHL], writes=[b_cHL])

                    if STAGE <= 2:
                        raise StopStage()
                    IB = 7

                    def T_in():
                        for p in range(8):
                            if p >= 2:
                                yield ("wait", ("att", p - 2))
                            sp_ = p % 2
                            qTp, kTp, Vxp = qT[sp_], kT[sp_], Vx[sp_]
                            bq, bk_, bV, bqa = b_qT[sp_], b_kT[sp_], b_Vx[sp_], b_qaug[sp_]
                            S.dma("pool", wq[:], wc_d[p, 0], writes=[b_wq])
                            S.dma("pool", wk[:], wc_d[p, 1], writes=[b_wk])
                            S.dma("pool", wv[:], wc_d[p, 2], writes=[b_wv])
                            S.dma("pool", wz[p % 2][:], wc_d[p, 3], writes=[b_wz[p % 2]])
                            for hh in range(2):
                                for r in range(2):
                                    S.dma("sp", qTp[hh][64 + r:65 + r, :], cHL[2 * p + hh:2 * p + hh + 1, r, :],
                                          reads=[b_cHL], writes=[bqa[hh]])
                            yield
                            for (wt, bw, dstT, bdst) in ((wq, b_wq, qTp, bq), (wk, b_wk, kTp, bk_)):
                                for t4 in range(4):
                                    for kc in range(KC):
                                        S.op("pe", lambda e: e.matmul(PB[IB][:, :], lhsT=wt[:, kc, :],
                                                                      rhs=hT[:, kc, t4 * 512:(t4 + 1) * 512],
                                                                      start=(kc == 0), stop=(kc == KC - 1)),
                                             reads=[bw] + b_hT[4 * t4:4 * t4 + 4], writes=bPB[IB], inc=(kc == KC - 1))
                                    yield
                                    S.op("dve", lambda e: e.tensor_copy(out=dstT[0][0:64, t4 * 512:(t4 + 1) * 512],
                                                                        in_=PB[IB][0:64, :]),
                                         reads=bPB[IB][0:1], writes=[bdst[0]])
                                    S.op("dve", lambda e: e.tensor_copy(out=dstT[1][0:64, t4 * 512:(t4 + 1) * 512],
                                                                        in_=PB[IB][64:128, :]),
                                         reads=bPB[IB][1:2], writes=[bdst[1]])
                                    yield
                            for t4 in range(4):
                                for kc in range(KC):
                                    S.op("pe", lambda e: e.matmul(PB[IB][:, :], lhsT=wv[:, kc, :],
                                                                  rhs=hT[:, kc, t4 * 512:(t4 + 1) * 512],
                                                                  start=(kc == 0), stop=(kc == KC - 1)),
                                         reads=[b_wv] + b_hT[4 * t4:4 * t4 + 4], writes=bPB[IB], inc=(kc == KC - 1))
                                yield
                                S.op("dve", lambda e: e.tensor_copy(out=vT5[:], in_=PB[IB][:, :]), reads=bPB[IB], writes=[b_vT5])
                                yield
                                pbf = PB[IB][:, :].bitcast(BF16)[:, 0:512].rearrange("p (j c) -> p j c", c=128)
                                for j in range(4):
                                    S.op("pe", lambda e: e.transpose(pbf[:, j, :], vT5[:, j * 128:(j + 1) * 128], ident[:]),
                                         reads=[b_vT5, b_const], writes=bPB[IB], inc=(j == 3))
                                yield
                                S.op("dve", lambda e: e.tensor_copy(out=Vxp[0][:, 4 * t4:4 * t4 + 4, 0:64], in_=pbf[:, :, 0:64]),
                                     reads=bPB[IB], writes=[bV[0]])
                                S.op("dve", lambda e: e.tensor_copy(out=Vxp[1][:, 4 * t4:4 * t4 + 4, 64:128], in_=pbf[:, :, 64:128]),
                                     reads=bPB[IB], writes=[bV[1]])
                                yield
                            yield ("done", ("in", p))

                    def T_att():
                        deferred = []
                        for p in range(8):
                            yield ("wait", ("in", p))
                            sp_ = p % 2
                            qTp, kTp, Vxp = qT[sp_], kT[sp_], Vx[sp_]
                            bq, bk_, bV, bqa = b_qT[sp_], b_kT[sp_], b_Vx[sp_], b_qaug[sp_]
                            jobs = []
                            for Qc in range(4):
                                for kt in range(4 * Qc + 4):
                                    for hh in range(2):
                                        jobs.append((Qc, kt, hh))

                            def emit_pv(n):
                                Qc, kt, hh = jobs[n]
                                o = max(0, kt - 4 * Qc) * 128
                                N = 512 - o
                                abk = 2 + 2 * (Qc % 2) + hh
                                S.op("pe", lambda e: e.matmul(PB[abk][:, o:512], lhsT=Vxp[hh][:, kt, :], rhs=pt[n % 3][:, 0:N],
                                                              start=(kt == 0), stop=(kt == 4 * Qc + 3)),
                                     reads=[bV[hh], b_pt[n % 3]], writes=bPB[abk])
                                if kt == 4 * Qc + 3 and hh == 1:
                                    emit_epilogue(Qc)

                            def emit_epilogue(Qc, p=p):
                                zb = 6
                                a0 = 2 + 2 * (Qc % 2)
                                a1 = a0 + 1
                                qs = slice(Qc * 512, (Qc + 1) * 512)
                                for kc in range(KC):
                                    S.op("pe", lambda e: e.matmul(PB[zb][:, :], lhsT=wz[p % 2][:, kc, :], rhs=hT[:, kc, qs],
                                                                  start=(kc == 0), stop=(kc == KC - 1)),
                                         reads=[b_wz[p % 2]] + b_hT[4 * Qc:4 * Qc + 4], writes=bPB[zb], inc=(kc == KC - 1))

                                def s1():
                                    S.op("act", lambda e: e.activation(out=e_t[:], in_=PB[zb][:, :], func=AF.Exp, scale=-1.0),
                                         reads=bPB[zb], writes=[b_e])
                                    S.op("dve", lambda e: e.tensor_copy(out=sums[0:64, :], in_=PB[a0][64:128, :]),
                                         reads=bPB[a0][1:2], writes=[b_sums])
                                    S.op("dve", lambda e: e.tensor_copy(out=sums[64:128, :], in_=PB[a1][0:64, :]),
                                         reads=bPB[a1][0:1], writes=[b_sums])

                                def s2():
                                    S.op("dve", lambda e: e.scalar_tensor_tensor(out=den[:], in0=e_t[:], scalar=1.0, in1=sums[:],
                                                                                 op0=ALU.add, op1=ALU.mult),
                                         reads=[b_e, b_sums], writes=[b_den])

                                def s3():
                                    S.op("act", lambda e: e.activation(out=den[:], in_=den[:], func=AF.Ln), reads=[b_den], writes=[b_den])
                                    S.op("act", lambda e: e.activation(out=den[:], in_=den[:], func=AF.Exp, scale=-1.0),
                                         reads=[b_den], writes=[b_den])

                                def s4():
                                    S.op("dve", lambda e: e.tensor_tensor(out=tt[:], in0=PB[zb][:, :], in1=den[:], op=ALU.mult),
                                         reads=bPB[zb] + [b_den], writes=[b_tt])
                                    S.op("dve", lambda e: e.tensor_tensor(out=og[0:64, p, qs], in0=PB[a0][0:64, :], in1=tt[0:64, :],
                                                                          op=ALU.mult),
                                         reads=bPB[a0][0:1] + [b_tt], writes=[b_og[p][Qc]])
                                    S.op("dve", lambda e: e.tensor_tensor(out=og[64:128, p, qs], in0=PB[a1][64:128, :], in1=tt[64:128, :],
                                                                          op=ALU.mult),
                                         reads=bPB[a1][1:2] + [b_tt], writes=[b_og[p][Qc]])
                                for dl, fn in ((2, s1), (4, s2), (6, s3), (8, s4)):
                                    deferred.append([dl, fn])

                            for n, (Qc, kt, hh) in enumerate(jobs):
                                o = max(0, kt - 4 * Qc) * 128
                                N = 512 - o
                                q0 = Qc * 512 + o
                                h = 2 * p + hh
                                sbk = n % 2
                                S.op("pe", lambda e: e.matmul(PB[sbk][:, 0:N], lhsT=kTp[hh][0:66, kt * 128:(kt + 1) * 128],
                                                              rhs=qTp[hh][0:66, q0:q0 + N], start=True, stop=True),
                                     reads=[bk_[hh], bq[hh], bqa[hh]], writes=bPB[sbk])
                                S.op("act", lambda e: e.activation(out=pt[n % 3][:, 0:N], in_=PB[sbk][:, 0:N], func=AF.Exp,
                                                                   scale=0.125, bias=NC_[:, kt, h:h + 1]),
                                     reads=bPB[sbk] + [b_NC], writes=[b_pt[n % 3]])
                                if kt >= 4 * Qc:
                                    S.op("pool", lambda e: e.tensor_tensor(out=pt[n % 3][:, 0:128], in0=pt[n % 3][:, 0:128],
                                                                           in1=maskb[:], op=ALU.mult),
                                         reads=[b_pt[n % 3], b_const], writes=[b_pt[n % 3]])
                                if n >= 1:
                                    emit_pv(n - 1)
                                for dfr in list(deferred):
                                    dfr[0] -= 1
                                    if dfr[0] <= 0:
                                        deferred.remove(dfr)
                                        dfr[1]()
                                yield
                            emit_pv(len(jobs) - 1)
                            yield ("done", ("att", p))
                        while deferred:
                            for dfr in list(deferred):
                                dfr[0] -= 1
                                if dfr[0] <= 0:
                                    deferred.remove(dfr)
                                    dfr[1]()

                    run_threads([T_in(), T_att()], L1W)
                except StopStage:
                    pass
                cur[0] = st1
                l1src = x1s_d[seq] if 0 in layers else x_d[seq]
                l1b = b_x1 if 0 in layers else b_xdram
                outproj_residual(l1src, l1b, out_d[seq], b_outd, wo)
                S.fence()
                st2.close()

        sb_l = None
        b_l = None
        for seq in range(nseq):
            if 0 in layers:
                layer0(seq, 1 not in layers)
            if 1 in layers:
                layer1(seq, True)
        S.finish(b_outd, "sp")
        print("n_ins", S.n_ins, "n_wait", S.n_wait)
    return nc


def prep_shared(inp):
    m = dict(host_consts())
    m["pre_norm"] = np.ascontiguousarray(inp["pre_norm"], dtype=np.float32)
    m["post_norm"] = np.ascontiguousarray(inp["post_norm"], dtype=np.float32)
    wc = np.asarray(inp["w_in_c"], dtype=np.float32)
    qkvz = wc[:, :4096].reshape(KC, 128, 4, 8, 128)
    m["wc"] = np.ascontiguousarray(qkvz.transpose(3, 2, 1, 0, 4))
    m["wf"] = np.ascontiguousarray(wc[:, 4096:4112].reshape(KC, 128, 16).transpose(1, 0, 2))
    m["woc"] = np.ascontiguousarray(np.asarray(inp["w_out_c"], dtype=np.float32).reshape(8, 128, D).transpose(1, 0, 2))
    m["c_forget_bias"] = np.ascontiguousarray(inp["c_forget_bias"], dtype=np.float32).reshape(1, 16)
    wab = np.asarray(inp["w_in_ab"], dtype=np.float32)
    mcol = np.arange(128)
    dd = mcol % 64
    dperm = np.where(dd < 8, dd + 8, np.where(dd < 16, dd - 8, dd))
    permcol = (mcol // 64) * 64 + dperm
    slabs = []
    for h in range(4):
        qc = wab[:, 0 * 512 + h * 128:0 * 512 + (h + 1) * 128]
        kc_ = wab[:, 1 * 512 + h * 128:1 * 512 + (h + 1) * 128]
        vc = wab[:, 2 * 512 + h * 128:2 * 512 + (h + 1) * 128]
        zc = wab[:, 3 * 512 + h * 128:3 * 512 + (h + 1) * 128]
        slabs.append(np.stack([qc, qc[:, permcol], kc_, kc_[:, permcol], vc, zc], axis=0))
    wa = np.stack(slabs, axis=0).reshape(4, 6, KC, 128, 128).transpose(0, 1, 3, 2, 4)
    m["wa"] = np.ascontiguousarray(wa)
    m["woab"] = np.ascontiguousarray(np.asarray(inp["w_out_ab"], dtype=np.float32).reshape(8, 128, D).transpose(1, 0, 2))
    m["lam4"] = np.ascontiguousarray(np.stack([inp["a_lambda_q1"], inp["a_lambda_k1"], inp["a_lambda_q2"], inp["a_lambda_k2"]]).astype(np.float32))
    m["a_subln"] = np.ascontiguousarray(np.asarray(inp["a_subln"], dtype=np.float32).reshape(128, 1))
    m["wbq"] = np.ascontiguousarray(wab[:, 2048:3584].reshape(KC, 128, 12, 128).transpose(2, 1, 0, 3))
    m["cw"] = np.ascontiguousarray(np.asarray(inp["b_conv_w"], dtype=np.float32).reshape(4, 12, 128).transpose(2, 1, 0))
    m["wbz"] = np.ascontiguousarray(wab[:, 3584:4096].reshape(KC, 128, 512).transpose(1, 0, 2))
    m["wba"] = np.ascontiguousarray(wab[:, 4096:4104].reshape(KC, 128, 8).transpose(1, 0, 2))
    m["b_a_log"] = np.ascontiguousarray(inp["b_a_log"], dtype=np.float32).reshape(1, 4)
    m["b_dt_bias"] = np.ascontiguousarray(inp["b_dt_bias"], dtype=np.float32).reshape(1, 4)
    m["b_head_norm"] = np.ascontiguousarray(inp["b_head_norm"], dtype=np.float32).reshape(1, 128)
    return m


def kernel(**inp):
    x = np.asarray(inp["x"], dtype=np.float32)
    B = x.shape[0]
    nseq = B // NCORES
    shared = prep_shared(inp)
    nc = build(nseq, LAYERS)
    in_maps = []
    for c in range(NCORES):
        m = dict(shared)
        m["x"] = np.ascontiguousarray(x[c * nseq:(c + 1) * nseq])
        m["positions"] = np.ascontiguousarray(np.asarray(inp["positions"], dtype=np.int32)[c * nseq:(c + 1) * nseq])
        in_maps.append(m)
    res = run_bass_kernel_spmd(nc, in_maps, core_ids=list(range(NCORES)), **RUN_KW)
    LAST['res'] = res
    return np.concatenate([r["out"] for r in res.results], axis=0)
```

```python
import contextlib
import os
import math
import numpy as np
import concourse.bass as bass
import concourse.mybir as mybir
from concourse.bass_utils import run_bass_kernel_spmd

F32 = mybir.dt.float32
BF16 = mybir.dt.bfloat16
I32 = mybir.dt.int32
AF = mybir.ActivationFunctionType
ALU = mybir.AluOpType
AX = mybir.AxisListType

D = 1024
SEQ = 2048
NT = 16
KC = 8
EPS = 1e-6
NCORES = 8
LAYERS = (0, 1)
STAGE = 99
BW = tuple(int(v) for v in os.environ.get('BW', '2,4,3').split(','))
L1W = (1, 4)
PE_RATE = float(os.environ.get('PE_RATE', '2000'))
SEM_LAT = float(os.environ.get('SEM_LAT', '0.15'))
RUN_KW = {}
LAST = {}
VVAR = int(os.environ.get('VVAR', '3'))


class StopStage(Exception):
    pass


class Buf:
    __slots__ = ("name", "w", "r", "psum", "tw", "tr")

    def __init__(self, name="", psum=False):
        self.name = name
        self.w = None
        self.r = {}
        self.psum = psum
        self.tw = 0.0
        self.tr = 0.0


class _Rec:
    def __getattr__(self, name):
        def call(*a, **k):
            return (name, a, k)
        return call


_REC = _Rec()


def _est_us(eng, call):
    name, a, k = call
    out = k.get("out", a[0] if a else None)
    F = 1
    for d in out.shape[1:]:
        F *= d
    if eng == "pe":
        lhsT = k.get("lhsT", a[1] if len(a) > 1 else None)
        passes = 4 if (lhsT is not None and lhsT.dtype == F32) else 1
        return passes * max(F, 64) / PE_RATE + 0.03
    if eng == "act":
        return 0.22 + F / 1400.0
    if eng == "dve":
        return 0.12 + F / 960.0
    return 0.25 + F / 480.0


class Sched:
    ENGS = ("pe", "act", "dve", "pool", "sp")

    def __init__(self, nc, stack, n_dma_sems=6):
        self.nc = nc
        self.e = {"pe": nc.tensor, "act": nc.scalar, "dve": nc.vector,
                  "pool": nc.gpsimd, "sp": nc.sync}
        self.semh = {}
        self.cnt = {}
        for k in self.ENGS:
            self.semh[k] = stack.enter_context(nc.semaphore("s_" + k))
            self.cnt[k] = 0
        self.dq = {}
        for q in ("sp", "act", "pool"):
            slots = []
            for i in range(n_dma_sems):
                key = "d_%s%d" % (q, i)
                self.semh[key] = stack.enter_context(nc.semaphore(key))
                self.cnt[key] = 0
                slots.append(key)
            self.dq[q] = [slots, 0]
        self.seen = {k: {} for k in self.ENGS}
        self.rec = None
        self.n_ins = {k: 0 for k in self.ENGS}
        self.n_wait = {k: 0 for k in self.ENGS}

    def _wait(self, eng, deps):
        need = {}
        seen = self.seen[eng]
        for (k, v) in deps:
            if eng == "pe" and k == "pe":
                continue
            if seen.get(k, 0) < v and need.get(k, 0) < v:
                need[k] = v
        for k, v in need.items():
            self.e[eng].wait_ge(self.semh[k], v)
            seen[k] = v
            self.n_wait[eng] += 1

    @staticmethod
    def _deps(reads, writes):
        deps = []
        for b in reads:
            if b.w is not None:
                deps.append(b.w)
            if b.psum:
                deps.extend(b.r.items())
        for b in writes:
            if b.w is not None:
                deps.append(b.w)
            deps.extend(b.r.items())
        return deps

    @staticmethod
    def _mark(tok, reads, writes):
        for b in reads:
            if b.r.get(tok[0], 0) < tok[1]:
                b.r[tok[0]] = tok[1]
        for b in writes:
            b.w = tok
            b.r = {}

    def op(self, eng, fn, reads=(), writes=(), inc=True):
        if self.rec is not None:
            self.rec.append(("op", eng, fn(_REC), list(reads), list(writes), inc))
            return None
        self._wait(eng, self._deps(reads, writes))
        ins = fn(self.e[eng])
        self.n_ins[eng] += 1
        if inc:
            self.cnt[eng] += 1
            ins.then_inc(self.semh[eng], 1)
            tok = (eng, self.cnt[eng])
        else:
            tok = (eng, self.cnt[eng] + 1)
        self._mark(tok, reads, writes)
        return ins

    def dma(self, q, out, in_, reads=(), writes=(), **kw):
        if self.rec is not None:
            self.rec.append(("dma", q, (out, in_, kw), list(reads), list(writes), True))
            return None
        slots, idx = self.dq[q]
        key = slots[idx % len(slots)]
        self.dq[q][1] = idx + 1
        deps = self._deps(reads, writes)
        if self.cnt[key] > 0:
            deps.append((key, self.cnt[key]))
        self._wait(q, deps)
        ins = self.e[q].dma_start(out=out, in_=in_, **kw)
        self.cnt[key] += 16
        ins.then_inc(self.semh[key], 16)
        self._mark((key, self.cnt[key]), reads, writes)
        return ins

    def fence(self):
        allc = [(k, v) for k, v in self.cnt.items() if v > 0]
        for eng in self.ENGS:
            self._wait(eng, allc)

    def finish(self, bufs, eng="sp"):
        deps = []
        for b in bufs:
            if b.w is not None:
                deps.append(b.w)
            deps.extend(b.r.items())
        self._wait(eng, deps)


def host_consts():
    c = {}
    j = np.arange(128)
    U = (j[:, None] <= j[None, :]).astype(np.float32)
    c["c_uext"] = np.concatenate([U, np.ones((128, 1), np.float32)], axis=1)
    c["c_ident"] = np.eye(128, dtype=np.float32)
    p = np.arange(128)
    d = p % 64
    half = 8
    inv_freq = (np.float32(500000.0) ** (-(np.arange(half, dtype=np.float32) * np.float32(2.0)) / np.float32(16.0))).astype(np.float32)
    freq = np.where(d < 16, inv_freq[d % 8], 0.0).astype(np.float32)
    sign = np.where(d < 8, -1.0, np.where(d < 16, 1.0, 0.0)).astype(np.float32)
    c["c_rope"] = np.stack([freq, sign], axis=1).astype(np.float32)
    same = (j[:, None] // 64) == (j[None, :] // 64)
    ublk = (same & (j[:, None] <= j[None, :])).astype(np.float32)
    blk = same.astype(np.float32)
    strictblk = (same & (j[:, None] > j[None, :])).astype(np.float32)
    half0 = np.repeat((j < 64).astype(np.float32)[:, None], 128, axis=1)
    half1 = np.repeat((j >= 64).astype(np.float32)[:, None], 128, axis=1)
    c["c_gdn"] = np.stack([ublk, blk, strictblk, half0, half1, np.eye(128, dtype=np.float32)], axis=1)
    return c


def build(nseq, layers=(0, 1)):
    nc = bass.Bass("TRN2", target_bir_lowering=False)
    dt_in = lambda name, shape, dt=F32: nc.dram_tensor(name, list(shape), dt, kind="ExternalInput").ap()
    x_d = dt_in("x", [nseq, SEQ, D])
    out_d = nc.dram_tensor("out", [nseq, SEQ, D], F32, kind="ExternalOutput").ap()
    x1s_d = nc.dram_tensor("x1s", [nseq, SEQ, D], F32, kind="Internal").ap()
    pre_d = dt_in("pre_norm", [2, D])
    post_d = dt_in("post_norm", [2, D])
    uext_d = dt_in("c_uext", [128, 129])
    ident_d = dt_in("c_ident", [128, 128])
    wc_d = dt_in("wc", [8, 4, 128, KC, 128])
    wf_d = dt_in("wf", [128, KC, 16])
    woc_d = dt_in("woc", [128, 8, D])
    fb_d = dt_in("c_forget_bias", [1, 16])
    pos_d = dt_in("positions", [nseq, SEQ], I32)
    rope_d = dt_in("c_rope", [128, 2])
    wa_d = dt_in("wa", [4, 6, 128, KC, 128])
    woab_d = dt_in("woab", [128, 8, D])
    lam_d = dt_in("lam4", [4, 64])
    subln_d = dt_in("a_subln", [128, 1])
    gdnc_d = dt_in("c_gdn", [128, 6, 128])
    wbq_d = dt_in("wbq", [12, 128, KC, 128])
    cw_d = dt_in("cw", [128, 12, 4])
    wbz_d = dt_in("wbz", [128, KC, 512])
    wba_d = dt_in("wba", [128, KC, 8])
    alog_d = dt_in("b_a_log", [1, 4])
    dtb_d = dt_in("b_dt_bias", [1, 4])
    hn_d = dt_in("b_head_norm", [1, 128])

    with contextlib.ExitStack() as st:
        S = Sched(nc, st)
        uid = [0]
        def sb(name, shape, dt):
            uid[0] += 1
            return st.enter_context(nc.sbuf_tensor("t%d_%s" % (uid[0], name), list(shape), dt))
        xt = [sb("xt%d" % i, [128, D], F32) for i in range(3)]
        b_xt = [Buf() for _ in range(3)]
        junk2_ = sb("junk2", [128, D], BF16)
        junk2 = [junk2_, junk2_]
        b_junk2_ = Buf()
        b_junk2 = [b_junk2_, b_junk2_]
        hT = sb("hT", [128, KC, SEQ], BF16)
        og = sb("og", [128, 8, SEQ], BF16)
        pre_bc = sb("pre_bc", [128, D], F32)
        post_bc = sb("post_bc", [128, D], F32)
        ident = sb("ident", [128, 128], BF16)
        uext = sb("uext", [128, 129], F32)
        ones_f = sb("ones_f", [128, 128], F32)
        maskb = sb("maskb", [128, 128], BF16)
        ss = sb("ss", [128, NT], F32)
        rstd = sb("rstd", [128, NT], F32)
        junk = sb("junk", [128, 512], BF16)
        PB = [st.enter_context(nc.psum_tensor("pb%d" % i, [128, 512], F32)) for i in range(8)]
        bPB = [[Buf("pb%d_0" % i, True), Buf("pb%d_1" % i, True)] for i in range(8)]

        b_xdram = [Buf("xd%d" % i) for i in range(NT)]
        b_x1 = [Buf("x1_%d" % i) for i in range(NT)]
        b_outd = [Buf("od%d" % i) for i in range(NT)]
        b_ssi = [Buf() for _ in range(NT)]
        b_rsi = [Buf() for _ in range(NT)]
        b_hT = [Buf("hT%d" % i) for i in range(NT)]
        b_og = [[Buf() for _ in range(4)] for _ in range(8)]
        b_const = Buf("const")
        b_norm = Buf("normbc")
        b_ss = Buf("ss")
        b_rstd = Buf("rstd")
        b_junk = Buf("junk")
        b_xn = [Buf(), Buf()]
        b_wo = Buf("wo")
        b_out = Buf("out")

        S.dma("sp", uext[:], uext_d[:, :], writes=[b_const])
        S.dma("pool", ident[:], ident_d[:, :], writes=[b_const])
        S.dma("pool", maskb[:], uext_d[:, 0:128], writes=[b_const])
        S.op("pool", lambda e: e.memset(ones_f[:], 1.0), writes=[b_const])

        cur_layer = [0]

        b_pre = Buf("pre_bc")
        b_post = Buf("post_bc")

        def load_pre(layer):
            S.dma("sp", pre_bc[:], bass.AP(pre_d.tensor, layer * D, [[0, 128], [1, D]]), writes=[b_pre])

        def load_post(layer):
            S.dma("sp", post_bc[:], bass.AP(post_d.tensor, layer * D, [[0, 128], [1, D]]), writes=[b_post])

        def load_norms(layer):
            load_pre(layer)
            load_post(layer)

        def prenorm_gens(src, bsrc, xts, bxts, xns, bxns, banks):
            def T(k):
                for j, i in enumerate(range(k, NT, 2)):
                    xb_ = xts[k]
                    bx = bxts[k]
                    S.dma("sp", xb_[:], src[i * 128:(i + 1) * 128, :], reads=[bsrc[i]], writes=[bx])
                    S.op("act", lambda e: e.activation(out=junk2[k][:], in_=xb_[:], func=AF.Square, accum_out=ss[:, i:i + 1]),
                         reads=[bx], writes=[b_junk2[k], b_ssi[i]])
                    S.op("act", lambda e: e.activation(out=rstd[:, i:i + 1], in_=ss[:, i:i + 1], func=AF.Ln, scale=1.0 / D, bias=EPS),
                         reads=[b_ssi[i]], writes=[b_rsi[i]])
                    S.op("act", lambda e: e.activation(out=rstd[:, i:i + 1], in_=rstd[:, i:i + 1], func=AF.Exp, scale=-0.5),
                         reads=[b_rsi[i]], writes=[b_rsi[i]])
                    xb = xns[k]
                    bxb = bxns[k]
                    S.op("dve", lambda e: e.scalar_tensor_tensor(out=xb, in0=xb_[:], scalar=rstd[:, i:i + 1],
                                                                 in1=pre_bc[:], op0=ALU.mult, op1=ALU.mult),
                         reads=[bx, b_rsi[i], b_pre], writes=[bxb])
                    bank = banks[k]
                    pv = PB[bank][:].bitcast(BF16)
                    for kc in range(KC):
                        S.op("pe", lambda e: e.transpose(pv[:, kc * 128:(kc + 1) * 128], xb[:, kc * 128:(kc + 1) * 128],
                                                         ident[:]),
                             reads=[bxb, b_const], writes=bPB[bank], inc=(kc == KC - 1))
                    srcp = pv.rearrange("p (k t) -> p k t", k=KC)
                    dst = hT[:, :, i * 128:(i + 1) * 128]
                    if k == 0:
                        S.op("act", lambda e: e.activation(out=dst, in_=srcp, func=AF.Copy),
                             reads=bPB[bank], writes=[b_hT[i]])
                    else:
                        S.op("dve", lambda e: e.tensor_copy(out=dst, in_=srcp),
                             reads=bPB[bank], writes=[b_hT[i]])
                    yield
            return [T(0), T(1)]

        def prenorm(src, bsrc):
            xns = [sb_l["tmpf"][k][:].bitcast(BF16) for k in range(2)]
            run_threads(prenorm_gens(src, bsrc, [xt[0], xt[2]], [b_xt[0], b_xt[2]], xns, b_l["tmpf"], (6, 7)), None)

        def outproj_residual(src, bsrc, dst, b_dst, wo, fuse=False, extra=()):
            ss2 = sb_l["ss2"]; rs2 = sb_l["rs2"]
            b_s2 = [Buf() for _ in range(NT)]
            b_r2 = [Buf() for _ in range(NT)]

            def T(k):
                tmpf = sb_l["tmpf"] if k == 0 else sb_l["tmpf2"]
                b_tmpf = b_l["tmpf"] if k == 0 else b_l["tmpf2"]
                for j, i in enumerate(range(k, NT, 2)):
                    xb_ = xt[(j % 2) if k == 0 else 2]
                    bx = b_xt[(j % 2) if k == 0 else 2]
                    S.dma("sp", xb_[:], src[i * 128:(i + 1) * 128, :], reads=[bsrc[i]], writes=[bx])
                    banks = (4 + 2 * k, 5 + 2 * k)
                    for hf in range(2):
                        bk = banks[hf]
                        for p in range(8):
                            S.op("pe", lambda e: e.matmul(PB[bk][:, :], lhsT=og[:, p, i * 128:(i + 1) * 128],
                                                          rhs=wo[:, p, hf * 512:(hf + 1) * 512],
                                                          start=(p == 0), stop=(p == 7)),
                                 reads=[b_og[p][i // 4], b_wo], writes=bPB[bk], inc=(p == 7))
                        S.op("act", lambda e: e.activation(out=junk2[k][:, 0:512], in_=PB[bk][:, :], func=AF.Square,
                                                           accum_out=ss2[:, 2 * i + hf:2 * i + hf + 1]),
                             reads=bPB[bk], writes=[b_junk2[k], b_s2[i]])
                    yield
                    S.op("dve", lambda e: e.tensor_tensor(out=rs2[:, i:i + 1], in0=ss2[:, 2 * i:2 * i + 1],
                                                          in1=ss2[:, 2 * i + 1:2 * i + 2], op=ALU.add),
                         reads=[b_s2[i]], writes=[b_r2[i]])
                    S.op("act", lambda e: e.activation(out=rs2[:, i:i + 1], in_=rs2[:, i:i + 1], func=AF.Ln,
                                                       scale=1.0 / D, bias=EPS),
                         reads=[b_r2[i]], writes=[b_r2[i]])
                    S.op("act", lambda e: e.activation(out=rs2[:, i:i + 1], in_=rs2[:, i:i + 1], func=AF.Exp, scale=-0.5),
                         reads=[b_r2[i]], writes=[b_r2[i]])
                    for hf in range(2):
                        bk = banks[hf]
                        tf = tmpf[hf]
                        S.op("dve", lambda e: e.scalar_tensor_tensor(out=tf[:], in0=PB[bk][:, :], scalar=rs2[:, i:i + 1],
                                                                     in1=post_bc[:, hf * 512:(hf + 1) * 512],
                                                                     op0=ALU.mult, op1=ALU.mult),
                             reads=bPB[bk] + [b_r2[i], b_post], writes=[b_tmpf[hf]])
                        S.op("pool", lambda e: e.tensor_tensor(out=xb_[:, hf * 512:(hf + 1) * 512],
                                                               in0=xb_[:, hf * 512:(hf + 1) * 512], in1=tf[:],
                                                               op=ALU.add),
                             reads=[b_tmpf[hf], bx], writes=[bx])
                    S.dma("sp", dst[i * 128:(i + 1) * 128, :], xb_[:], reads=[bx], writes=[b_dst[i]])
                    yield
                    if fuse:
                        xnv = tmpf[0][:].bitcast(BF16)
                        S.op("act", lambda e: e.activation(out=junk2[k][:], in_=xb_[:], func=AF.Square, accum_out=ss[:, i:i + 1]),
                             reads=[bx], writes=[b_junk2[k], b_ssi[i]])
                        S.op("act", lambda e: e.activation(out=rstd[:, i:i + 1], in_=ss[:, i:i + 1], func=AF.Ln, scale=1.0 / D, bias=EPS),
                             reads=[b_ssi[i]], writes=[b_rsi[i]])
                        S.op("act", lambda e: e.activation(out=rstd[:, i:i + 1], in_=rstd[:, i:i + 1], func=AF.Exp, scale=-0.5),
                             reads=[b_rsi[i]], writes=[b_rsi[i]])
                        yield
                        S.op("dve", lambda e: e.scalar_tensor_tensor(out=xnv, in0=xb_[:], scalar=rstd[:, i:i + 1],
                                                                     in1=pre_bc[:], op0=ALU.mult, op1=ALU.mult),
                             reads=[bx, b_rsi[i], b_pre], writes=[b_tmpf[0]])
                        yield
                        bank = 2 + k
                        pv = PB[bank][:].bitcast(BF16)
                        for kc in range(KC):
                            S.op("pe", lambda e: e.transpose(pv[:, kc * 128:(kc + 1) * 128], xnv[:, kc * 128:(kc + 1) * 128],
                                                             ident[:]),
                                 reads=[b_tmpf[0], b_const], writes=bPB[bank], inc=(kc == KC - 1))
                        yield
                        srcp = pv.rearrange("p (k t) -> p k t", k=KC)
                        dsth = hT[:, :, i * 128:(i + 1) * 128]
                        if k == 0:
                            S.op("act", lambda e: e.activation(out=dsth, in_=srcp, func=AF.Copy),
                                 reads=bPB[bank], writes=[b_hT[i]])
                        else:
                            S.op("dve", lambda e: e.tensor_copy(out=dsth, in_=srcp),
                                 reads=bPB[bank], writes=[b_hT[i]])
                        yield
            run_threads([T(0), T(1)] + list(extra), None)

        PI = 3.141592653589793
        LAMBDA_INIT = 0.8 - 0.6 * math.exp(-0.3 * 0)

        def layer0(seq, last):
            nonlocal sb_l, b_l
            with contextlib.ExitStack() as st1:
                st2 = st1.enter_context(contextlib.ExitStack())
                cur = [st1]

                def sl(name, shape, dt):
                    uid[0] += 1
                    return cur[0].enter_context(nc.sbuf_tensor("t%d_%s" % (uid[0], name), list(shape), dt))
                sb_l = {}
                b_l = {}
                sb_l["ss2"] = sl("ss2", [128, 2 * NT], F32); b_l["ss2"] = Buf()
                sb_l["rs2"] = sl("rs2", [128, NT], F32); b_l["rs2"] = Buf()
                sb_l["tmpf"] = [sl("tmpf%d" % i, [128, 512], F32) for i in range(2)]; b_l["tmpf"] = [Buf(), Buf()]
                sb_l["tmpf2"] = [sl("tmpf2_%d" % i, [128, 512], F32) for i in range(2)]; b_l["tmpf2"] = [Buf(), Buf()]
                load_norms(0)
                wo = sl("wo", [128, 8, D], BF16)
                S.dma("pool", wo[:], woab_d[:, :, :], writes=[b_wo])
                if prenorm_done[0]:
                    prenorm_done[0] = False
                else:
                    prenorm(x_d[seq], b_xdram)
                cur[0] = st2
                ropec = sl("ropec", [128, 2], F32)
                lamt = sl("lamt", [128, 4, 64], F32)
                lamp = sl("lamp", [128, 2, 64], F32)
                lams = sl("lams", [128, 2], F32)
                neglam = sl("neglam", [128, 1], F32)
                subcol = sl("subcol", [128, 1], F32)
                ones_b = sl("ones_b", [128, 128], BF16)
                posi = sl("posi", [128, SEQ], I32)
                Ct = sl("Ct", [128, SEQ], F32)
                St = sl("St", [128, SEQ], F32)
                qTs = [sl("qTa%d" % i, [128, SEQ], BF16) for i in range(2)]
                kTs = [sl("kTa%d" % i, [128, SEQ], BF16) for i in range(2)]
                Vhs = [sl("Vh%d" % i, [128, NT, 128], BF16) for i in range(2)]
                wsl = [sl("wa%d" % i, [128, KC, 128], BF16) for i in range(5)]
                wzA = [sl("wzA%d" % i, [128, KC, 128], BF16) for i in range(2)]
                rc = sl("rc", [128, 512], F32); rd = sl("rd", [128, 512], F32)
                vT5a = sl("vT5a", [128, 512], BF16)
                b_qTs = [Buf(), Buf()]; b_kTs = [Buf(), Buf()]; b_Vhs = [Buf(), Buf()]
                b_wzA = [Buf(), Buf()]
                b_rc, b_rd, b_vT5a = Buf(), Buf(), Buf()
                pt = [sl("pt%d" % i, [128, 512], BF16) for i in range(3)]
                ta = sb_l["tmpf"][0]; tb = sb_l["tmpf"][1]
                tc = sl("tc", [128, 512], F32); td = sl("td", [128, 512], F32)
                te = sl("te", [128, 512], F32); b_te = Buf()
                b_ropec, b_lam, b_neglam, b_subcol, b_onesb, b_posi, b_Ct, b_St = [Buf() for _ in range(8)]
                b_wsl = [Buf() for _ in range(5)]
                b_pt = [Buf() for _ in range(3)]
                b_ta, b_tb, b_tc, b_td = b_l["tmpf"][0], b_l["tmpf"][1], Buf(), Buf()

                S.dma("sp", ropec[:], rope_d[:, :], writes=[b_ropec])
                S.op("pool", lambda e: e.memset(ones_b[:], 1.0), writes=[b_onesb])
                S.dma("sp", lamt[:], bass.AP(lam_d.tensor, 0, [[0, 128], [64, 4], [1, 64]]), writes=[b_lam])
                S.op("dve", lambda e: e.tensor_tensor(out=lamp[:, 0, :], in0=lamt[:, 0, :], in1=lamt[:, 1, :], op=ALU.mult),
                     reads=[b_lam], writes=[b_lam])
                S.op("dve", lambda e: e.tensor_tensor(out=lamp[:, 1, :], in0=lamt[:, 2, :], in1=lamt[:, 3, :], op=ALU.mult),
                     reads=[b_lam], writes=[b_lam])
                S.op("dve", lambda e: e.tensor_reduce(out=lams[:], in_=lamp[:], axis=AX.X, op=ALU.add),
                     reads=[b_lam], writes=[b_lam])
                S.op("act", lambda e: e.activation(out=lams[:], in_=lams[:], func=AF.Exp), reads=[b_lam], writes=[b_lam])
                S.op("dve", lambda e: e.tensor_tensor(out=neglam[:], in0=lams[:, 1:2], in1=lams[:, 0:1], op=ALU.subtract),
                     reads=[b_lam], writes=[b_neglam])
                S.op("dve", lambda e: e.tensor_scalar(out=neglam[:], in0=neglam[:], scalar1=-LAMBDA_INIT, scalar2=None, op0=ALU.add),
                     reads=[b_neglam], writes=[b_neglam])
                S.dma("sp", subcol[:], subln_d[:, :], writes=[b_subcol])
                S.op("dve", lambda e: e.tensor_scalar(out=subcol[:], in0=subcol[:], scalar1=1.0 - LAMBDA_INIT, scalar2=None, op0=ALU.mult),
                     reads=[b_subcol], writes=[b_subcol])
                S.dma("sp", posi[:], bass.AP(pos_d.tensor, seq * SEQ, [[0, 128], [1, SEQ]]), writes=[b_posi])

                def sin_table(dst, bdst, phase, signed):
                    S.op("dve", lambda e: e.tensor_copy(out=dst[:], in_=posi[:]), reads=[b_posi], writes=[bdst])
                    S.op("dve", lambda e: e.tensor_scalar(out=dst[:], in0=dst[:], scalar1=ropec[:, 0:1], scalar2=phase,
                                                          op0=ALU.mult, op1=ALU.add), reads=[bdst, b_ropec], writes=[bdst])
                    for c4 in range(4):
                        sl_ = slice(c4 * 512, (c4 + 1) * 512)
                        tI = tc[:].bitcast(I32)
                        S.op("dve", lambda e: e.tensor_scalar(out=td[:], in0=dst[:, sl_], scalar1=1.0 / (2 * PI), scalar2=None,
                                                              op0=ALU.mult), reads=[bdst], writes=[b_td])
                        S.op("dve", lambda e: e.tensor_copy(out=tI, in_=td[:]), reads=[b_td], writes=[b_tc])
                        S.op("dve", lambda e: e.tensor_copy(out=td[:], in_=tI), reads=[b_tc], writes=[b_td])
                        S.op("dve", lambda e: e.scalar_tensor_tensor(out=td[:], in0=td[:], scalar=-2 * PI, in1=dst[:, sl_],
                                                                     op0=ALU.mult, op1=ALU.add),
                             reads=[b_td, bdst], writes=[b_td])
                        S.op("dve", lambda e: e.tensor_scalar(out=tc[:], in0=td[:], scalar1=PI, scalar2=-2 * PI,
                                                              op0=ALU.is_gt, op1=ALU.mult), reads=[b_td], writes=[b_tc])
                        S.op("dve", lambda e: e.tensor_tensor(out=td[:], in0=td[:], in1=tc[:], op=ALU.add),
                             reads=[b_td, b_tc], writes=[b_td])
                        S.op("dve", lambda e: e.tensor_scalar(out=td[:], in0=td[:], scalar1=-PI, scalar2=PI,
                                                              op0=ALU.max, op1=ALU.min), reads=[b_td], writes=[b_td])
                        S.op("act", lambda e: e.activation(out=dst[:, sl_], in_=td[:], func=AF.Sin),
                             reads=[b_td], writes=[bdst])
                    if signed:
                        S.op("dve", lambda e: e.tensor_scalar(out=dst[:], in0=dst[:], scalar1=ropec[:, 1:2], scalar2=None,
                                                              op0=ALU.mult), reads=[bdst, b_ropec], writes=[bdst])
                sin_table(Ct, b_Ct, PI / 2, False)
                sin_table(St, b_St, 0.0, True)

                def T_inA():
                    for h in range(4):
                        if h >= 2:
                            yield ("wait", ("attA", h - 2))
                        hs = h % 2
                        for i6 in range(5):
                            S.dma("pool", wsl[i6][:], wa_d[h, i6], writes=[b_wsl[i6]])
                        S.dma("pool", wzA[hs][:], wa_d[h, 5], writes=[b_wzA[hs]])
                        yield
                        for (i_w, dstT, bdst) in ((0, qTs[hs], b_qTs[hs]), (2, kTs[hs], b_kTs[hs])):
                            for t4 in range(4):
                                tsl = slice(t4 * 512, (t4 + 1) * 512)
                                for jj in range(2):
                                    for kc in range(KC):
                                        S.op("pe", lambda e: e.matmul(PB[7][:, :], lhsT=wsl[i_w + jj][:, kc, :],
                                                                      rhs=hT[:, kc, tsl], start=(kc == 0), stop=(kc == KC - 1)),
                                             reads=[b_wsl[i_w + jj]] + b_hT[4 * t4:4 * t4 + 4], writes=bPB[7],
                                             inc=(kc == KC - 1))
                                    yield
                                    tab, tabb, rt = ((Ct, b_Ct, (rc, b_rc)) if jj == 0 else (St, b_St, (rd, b_rd)))
                                    S.op("dve", lambda e: e.tensor_tensor(out=rt[0][:], in0=PB[7][:, :], in1=tab[:, tsl], op=ALU.mult),
                                         reads=bPB[7] + [tabb], writes=[rt[1]])
                                    yield
                                S.op("pool", lambda e: e.tensor_tensor(out=dstT[:, tsl], in0=rc[:], in1=rd[:], op=ALU.add),
                                     reads=[b_rc, b_rd], writes=[bdst])
                                yield
                        for t4 in range(4):
                            for kc in range(KC):
                                S.op("pe", lambda e: e.matmul(PB[7][:, :], lhsT=wsl[4][:, kc, :],
                                                              rhs=hT[:, kc, t4 * 512:(t4 + 1) * 512],
                                                              start=(kc == 0), stop=(kc == KC - 1)),
                                     reads=[b_wsl[4]] + b_hT[4 * t4:4 * t4 + 4], writes=bPB[7], inc=(kc == KC - 1))
                            yield
                            S.op("dve", lambda e: e.tensor_copy(out=vT5a[:], in_=PB[7][:, :]), reads=bPB[7], writes=[b_vT5a])
                            yield
                            pbf = PB[7][:, :].bitcast(BF16)[:, 0:512].rearrange("p (j c) -> p j c", c=128)
                            for j in range(4):
                                S.op("pe", lambda e: e.transpose(pbf[:, j, :], vT5a[:, j * 128:(j + 1) * 128], ident[:]),
                                     reads=[b_vT5a, b_const], writes=bPB[7], inc=(j == 3))
                            yield
                            S.op("dve", lambda e: e.tensor_copy(out=Vhs[hs][:, 4 * t4:4 * t4 + 4, :], in_=pbf),
                                 reads=bPB[7], writes=[b_Vhs[hs]])
                            yield
                        yield ("done", ("inA", h))

                def T_attA():
                    deferred = []
                    for h in range(4):
                        yield ("wait", ("inA", h))
                        hs = h % 2
                        qT, kT, Vh = qTs[hs], kTs[hs], Vhs[hs]
                        b_qT, b_kT, b_Vh = b_qTs[hs], b_kTs[hs], b_Vhs[hs]
                        wz5, b_wz5 = wzA[hs], b_wzA[hs]
                        jobs = []
                        for Qc in range(4):
                            for kt in range(4 * Qc + 4):
                                for c in range(2):
                                    jobs.append((Qc, kt, c))

                        def emit_pv(n):
                            Qc, kt, c = jobs[n]
                            o = max(0, kt - 4 * Qc) * 128
                            N = 512 - o
                            S.op("pe", lambda e: e.matmul(PB[2 + c][:, o:512], lhsT=Vh[:, kt, :], rhs=pt[n % 3][:, 0:N],
                                                          start=(kt == 0), stop=(kt == 4 * Qc + 3)),
                                 reads=[b_Vh, b_pt[n % 3]], writes=bPB[2 + c], inc=False)
                            S.op("pe", lambda e: e.matmul(PB[4 + c][:, o:512], lhsT=ones_b[:], rhs=pt[n % 3][:, 0:N],
                                                          start=(kt == 0), stop=(kt == 4 * Qc + 3)),
                                 reads=[b_onesb, b_pt[n % 3]], writes=bPB[4 + c])
                            if kt == 4 * Qc + 3 and c == 1:
                                emit_epilogue(Qc)

                        def emit_epilogue(Qc, h=h):
                            qsl = slice(Qc * 512, (Qc + 1) * 512)
                            for kc in range(KC):
                                S.op("pe", lambda e: e.matmul(PB[6][:, :], lhsT=wz5[:, kc, :], rhs=hT[:, kc, qsl],
                                                              start=(kc == 0), stop=(kc == KC - 1)),
                                     reads=[b_wz5] + b_hT[4 * Qc:4 * Qc + 4], writes=bPB[6], inc=(kc == KC - 1))
                            S.op("act", lambda e: e.activation(out=ta[:], in_=PB[4][:, :], func=AF.Ln), reads=bPB[4], writes=[b_ta])
                            S.op("act", lambda e: e.activation(out=ta[:], in_=ta[:], func=AF.Exp, scale=-1.0), reads=[b_ta], writes=[b_ta])
                            S.op("act", lambda e: e.activation(out=te[:], in_=PB[5][:, :], func=AF.Ln), reads=bPB[5], writes=[b_te])
                            S.op("act", lambda e: e.activation(out=te[:], in_=te[:], func=AF.Exp, scale=-1.0), reads=[b_te], writes=[b_te])
                            S.op("dve", lambda e: e.tensor_tensor(out=tb[:], in0=PB[2][:, :], in1=ta[:], op=ALU.mult),
                                 reads=bPB[2] + [b_ta], writes=[b_tb])
                            S.op("dve", lambda e: e.tensor_tensor(out=tc[:], in0=PB[3][:, :], in1=te[:], op=ALU.mult),
                                 reads=bPB[3] + [b_te], writes=[b_tc])

                            def s1():
                                S.op("dve", lambda e: e.scalar_tensor_tensor(out=tb[:], in0=tc[:], scalar=neglam[:, 0:1], in1=tb[:],
                                                                              op0=ALU.mult, op1=ALU.add),
                                     reads=[b_tc, b_tb, b_neglam], writes=[b_tb])
                                S.op("act", lambda e: e.activation(out=ta[:], in_=PB[6][:, :], func=AF.Exp, scale=-1.0),
                                     reads=bPB[6] + [b_ta], writes=[b_ta])
                                S.op("act", lambda e: e.activation(out=ta[:], in_=ta[:], func=AF.Ln, bias=1.0), reads=[b_ta], writes=[b_ta])
                                S.op("act", lambda e: e.activation(out=ta[:], in_=ta[:], func=AF.Exp, scale=-1.0), reads=[b_ta], writes=[b_ta])

                            def s2():
                                S.op("act", lambda e: e.activation(out=tc[:], in_=tb[:], func=AF.Square), reads=[b_tb], writes=[b_tc])
                                S.op("dve", lambda e: e.tensor_tensor(out=ta[:], in0=PB[6][:, :], in1=ta[:], op=ALU.mult),
                                     reads=bPB[6] + [b_ta], writes=[b_ta])

                            def s3():
                                S.op("pe", lambda e: e.matmul(PB[6][:, :], lhsT=ones_f[:], rhs=tc[:], start=True, stop=True),
                                     reads=[b_const, b_tc], writes=bPB[6])

                            def s4():
                                S.op("act", lambda e: e.activation(out=td[:], in_=PB[6][:, :], func=AF.Ln, scale=1.0 / 128, bias=EPS),
                                     reads=bPB[6], writes=[b_td])
                                S.op("act", lambda e: e.activation(out=td[:], in_=td[:], func=AF.Exp, scale=-0.5), reads=[b_td], writes=[b_td])

                            def s5():
                                S.op("pool", lambda e: e.tensor_tensor(out=tb[:], in0=tb[:], in1=td[:], op=ALU.mult),
                                     reads=[b_tb, b_td], writes=[b_tb])

                            def s6():
                                S.op("dve", lambda e: e.scalar_tensor_tensor(out=og[:, h, qsl], in0=tb[:], scalar=subcol[:, 0:1], in1=ta[:],
                                                                             op0=ALU.mult, op1=ALU.mult),
                                     reads=[b_tb, b_ta, b_subcol], writes=[b_og[h][Qc]])
                            for dl, fn in ((1, s1), (2, s2), (3, s3), (5, s4), (6, s5), (7, s6)):
                                deferred.append([dl, fn])

                        for n, (Qc, kt, c) in enumerate(jobs):
                            o = max(0, kt - 4 * Qc) * 128
                            N = 512 - o
                            q0 = Qc * 512 + o
                            sbk = n % 2
                            S.op("pe", lambda e: e.matmul(PB[sbk][:, 0:N], lhsT=kT[c * 64:(c + 1) * 64, kt * 128:(kt + 1) * 128],
                                                          rhs=qT[c * 64:(c + 1) * 64, q0:q0 + N], start=True, stop=True),
                                 reads=[b_kT, b_qT], writes=bPB[sbk])
                            S.op("act", lambda e: e.activation(out=pt[n % 3][:, 0:N], in_=PB[sbk][:, 0:N], func=AF.Exp, scale=0.125),
                                 reads=bPB[sbk], writes=[b_pt[n % 3]])
                            if kt >= 4 * Qc:
                                S.op("pool", lambda e: e.tensor_tensor(out=pt[n % 3][:, 0:128], in0=pt[n % 3][:, 0:128],
                                                                       in1=maskb[:], op=ALU.mult),
                                     reads=[b_pt[n % 3], b_const], writes=[b_pt[n % 3]])
                            if n >= 1:
                                emit_pv(n - 1)
                            for dfr in list(deferred):
                                dfr[0] -= 1
                                if dfr[0] <= 0:
                                    deferred.remove(dfr)
                                    dfr[1]()
                            yield
                        emit_pv(len(jobs) - 1)
                        yield ("done", ("attA", h))
                    while deferred:
                        for dfr in list(deferred):
                            dfr[0] -= 1
                            if dfr[0] <= 0:
                                deferred.remove(dfr)
                                dfr[1]()

                run_threads([T_inA(), T_attA()], None)
                S.fence()
                st2.close()
                st3 = st1.enter_context(contextlib.ExitStack())
                cur[0] = st3
                partB(seq, sl)
                cur[0] = st1
                if last:
                    outproj_residual(x_d[seq], b_xdram, out_d[seq], b_outd, wo)
                else:
                    load_pre(1)
                    outproj_residual(x_d[seq], b_xdram, x1s_d[seq], b_x1, wo, fuse=True)
                S.fence()
                st3.close()

        def run_threads(threads, weights):
            lists = []
            for g in threads:
                S.rec = []
                for r in g:
                    if isinstance(r, tuple):
                        S.rec.append((r[0], r[1]))
                lists.append(S.rec)
            S.rec = None
            ptr = [0] * len(lists)
            done = set()
            eng_free = {}
            rr = 0
            while True:
                best = None
                alive = False
                for ti in range(len(lists)):
                    L = lists[ti]
                    while ptr[ti] < len(L) and L[ptr[ti]][0] in ("wait", "done"):
                        kind, key = L[ptr[ti]]
                        if kind == "done":
                            done.add(key)
                            ptr[ti] += 1
                        elif key in done:
                            ptr[ti] += 1
                        else:
                            break
                    if ptr[ti] >= len(L):
                        continue
                    alive = True
                    it = L[ptr[ti]]
                    if it[0] in ("wait", "done"):
                        continue
                    kind, eng, call, reads, writes, inc = it
                    t0 = eng_free.get(eng, 0.0)
                    for b_ in reads:
                        if b_.tw > t0:
                            t0 = b_.tw
                        if b_.psum and b_.tr > t0:
                            t0 = b_.tr
                    for b_ in writes:
                        if b_.tw > t0:
                            t0 = b_.tw
                        if b_.tr > t0:
                            t0 = b_.tr
                    key2 = (t0, (ti - rr) % len(lists))
                    if best is None or key2 < best[0]:
                        best = (key2, ti, t0)
                if best is None:
                    assert not alive, "emission deadlock"
                    break
                _, ti, t0 = best
                kind, eng, call, reads, writes, inc = lists[ti][ptr[ti]]
                ptr[ti] += 1
                rr = (ti + 1) % len(lists)
                if kind == "op":
                    dur = _est_us(eng, call)
                    S.op(eng, lambda e: getattr(e, call[0])(*call[1], **call[2]), reads=reads, writes=writes, inc=inc)
                    eng_free[eng] = t0 + dur
                    tend = t0 + dur + SEM_LAT
                else:
                    out_, in__, kw_ = call
                    S.dma(eng, out_, in__, reads=reads, writes=writes, **kw_)
                    eng_free[eng] = t0 + 0.1
                    nb = 1
                    for d in out_.shape:
                        nb *= d
                    tend = t0 + 2.0 + nb * 4 / 150e3
                for b_ in reads:
                    if tend > b_.tr:
                        b_.tr = tend
                for b_ in writes:
                    b_.tw = tend
                    b_.tr = 0.0

        def run_threads_rr(threads, weights):
            done = set()
            st_ = [{"g": g, "wait": None, "alive": True} for g in threads]
            while any(t["alive"] for t in st_):
                progressed = False
                for t, w in zip(st_, weights):
                    if not t["alive"]:
                        continue
                    for _ in range(w):
                        if t["wait"] is not None:
                            if t["wait"] in done:
                                t["wait"] = None
                            else:
                                break
                        try:
                            r = next(t["g"])
                        except StopIteration:
                            t["alive"] = False
                            progressed = True
                            break
                        progressed = True
                        if isinstance(r, tuple):
                            if r[0] == "wait":
                                if r[1] not in done:
                                    t["wait"] = r[1]
                                    break
                            elif r[0] == "done":
                                done.add(r[1])
                assert progressed, "emission deadlock"

        def partB(seq, sl):
            gm = sl("gm", [128, 6, 128], F32)
            ublk, blk, strictblk, ident_f = gm[:, 0, :], gm[:, 1, :], gm[:, 2, :], gm[:, 5, :]
            halfsel = (gm[:, 3, :], gm[:, 4, :])
            ones_b = sl("ones_bB", [128, 128], BF16)
            wbz = sl("wbz", [128, KC, 512], BF16)
            wba = sl("wba", [128, KC, 8], BF16)
            wbq = [sl("wbq%d" % i, [128, KC, 128], BF16) for i in range(2)]
            cw = sl("cw", [128, 12, 4], F32)
            negA = sl("negA", [128, 4], F32)
            dtb = sl("dtb", [128, 4], F32)
            hn_bc = sl("hn_bc", [128, 128], F32)
            G = {n: sl("g_" + n, [128, NT, 4], F32) for n in ("xa", "g", "beta", "negb", "G", "GL", "eG", "bG", "dG", "eGL0", "eGL1")}
            cr = sl("cr", [128, 12, 3], F32)
            xc = [sl("xc%d" % i, [128, 515], F32) for i in range(2)]
            yc = sl("yc", [128, 512], F32)
            sc = sl("sc", [128, 512], F32)
            t1 = sl("t1", [128, 512], F32)
            sqb = sl("sqb", [128, 512], BF16)
            qnT = [sl("qnT%d" % i, [128, 4, 512], BF16) for i in range(2)]
            knT = [sl("knT%d" % i, [128, 4, 512], BF16) for i in range(2)]
            vsT = [sl("vsT%d" % i, [128, 4, 512], BF16) for i in range(2)]
            Dm = sl("Dm", [128, 4, 128], F32)
            DTm = sl("DTm", [128, 4, 128], F32)
            Pm1 = sl("Pm", [128, 4, 128], F32)
            Qm1 = sl("Qm", [128, 4, 128], F32)
            TT = sl("TT", [128, 4, 128], F32)
            gb = TT
            gU = Pm1
            TTb = sl("TTb", [128, 4, 128], BF16)
            qgT = [sl("qgT%d" % i, [128, 4, 128], BF16) for i in range(2)]
            kbg = [sl("kbg%d" % i, [128, 4, 128], BF16) for i in range(2)]
            kdec = [sl("kdec%d" % i, [128, 4, 128], BF16) for i in range(2)]
            vb = [sl("vb%d" % i, [128, 4, 128], BF16) for i in range(2)]
            qkDT = [sl("qkDT%d" % i, [128, 4, 128], BF16) for i in range(2)]
            wT = [sl("wT%d" % i, [128, 4, 128], BF16) for i in range(2)]
            u_t = [sl("u_t%d" % i, [128, 512], F32) for i in range(2)]
            vnew = sl("vnew", [128, 512], BF16)
            St = sl("Sst", [128, 4, 128], F32)
            Sdec = sl("Sdec", [128, 4, 128], F32)
            Sb = sl("Sb", [128, 4, 128], BF16)
            zs = sl("zs", [128, 512], F32)
            t1r = sl("t1r", [128, 512], F32)
            sqr = sl("sqr", [128, 128], BF16)
            sso = sl("sso", [128, 4], F32)
            ot = sl("ot", [128, 512], BF16)
            B = {n: Buf(n) for n in ("gm", "onesb", "wbz", "wba", "cw", "negA", "dtb", "hn", "gates", "cr", "yc", "sc", "t1", "sqb",
                                    "gb", "gU", "Dm", "DTm", "Pm", "Qm", "TT", "TTb",
                                    "vnew", "S", "Sdec", "Sb", "zs", "t1r", "sqr", "sso", "ot")}
            D2 = {n: [Buf(n + "0"), Buf(n + "1")] for n in ("qnT", "knT", "vsT", "qgT", "kbg", "kdec", "vb", "qkDT", "wT", "u")}
            B["gb"] = B["TT"]
            B["gU"] = B["Pm"]
            b_wbq = [Buf(), Buf()]
            b_xc = [Buf(), Buf()]
            I_A, I_B = 0, 1
            P_A, P_B, P_C = 2, 3, 4
            R_A, R_B, R_C = 5, 6, 7

            S.dma("sp", gm[:], gdnc_d[:, :, :], writes=[B["gm"]])
            S.op("pool", lambda e: e.memset(ones_b[:], 1.0), writes=[B["onesb"]])
            S.dma("pool", wbz[:], wbz_d[:, :, :], writes=[B["wbz"]])
            S.dma("pool", wba[:], wba_d[:, :, :], writes=[B["wba"]])
            S.dma("sp", cw[:], cw_d[:, :, :], writes=[B["cw"]])
            S.dma("sp", negA[:], bass.AP(alog_d.tensor, 0, [[0, 128], [1, 4]]), writes=[B["negA"]])
            S.dma("sp", dtb[:], bass.AP(dtb_d.tensor, 0, [[0, 128], [1, 4]]), writes=[B["dtb"]])
            S.dma("sp", hn_bc[:], bass.AP(hn_d.tensor, 0, [[0, 128], [1, 128]]), writes=[B["hn"]])
            S.op("act", lambda e: e.activation(out=negA[:], in_=negA[:], func=AF.Exp), reads=[B["negA"]], writes=[B["negA"]])
            S.op("dve", lambda e: e.tensor_scalar(out=negA[:], in0=negA[:], scalar1=-1.0, scalar2=None, op0=ALU.mult),
                 reads=[B["negA"]], writes=[B["negA"]])
            S.op("pool", lambda e: e.memset(cr[:], 0.0), writes=[B["cr"]])
            S.op("pool", lambda e: e.memset(St[:], 0.0), writes=[B["S"]])
            S.op("pool", lambda e: e.memset(Sb[:], 0.0), writes=[B["Sb"]])

            ba_ps = PB[0][:, 0:NT * 8].rearrange("p (t c) -> p t c", c=8)
            for i in range(NT):
                for kc in range(KC):
                    S.op("pe", lambda e: e.matmul(ba_ps[:, i, :], lhsT=hT[:, kc, i * 128:(i + 1) * 128], rhs=wba[:, kc, :],
                                                  start=(kc == 0), stop=(kc == KC - 1)),
                         reads=[b_hT[i], B["wba"]], writes=bPB[0], inc=(kc == KC - 1))
            bg = [B["gates"]]
            S.op("dve", lambda e: e.tensor_tensor(out=G["xa"][:], in0=ba_ps[:, :, 4:8], in1=dtb[:, None, :].to_broadcast([128, NT, 4]),
                                                  op=ALU.add), reads=bPB[0] + [B["dtb"]], writes=bg)
            S.op("act", lambda e: e.activation(out=G["xa"][:], in_=G["xa"][:], func=AF.Exp), reads=bg, writes=bg)
            S.op("act", lambda e: e.activation(out=G["xa"][:], in_=G["xa"][:], func=AF.Ln, bias=1.0), reads=bg, writes=bg)
            S.op("dve", lambda e: e.tensor_tensor(out=G["g"][:], in0=G["xa"][:], in1=negA[:, None, :].to_broadcast([128, NT, 4]),
                                                  op=ALU.mult), reads=bg + [B["negA"]], writes=bg)
            S.op("act", lambda e: e.activation(out=G["beta"][:], in_=ba_ps[:, :, 0:4], func=AF.Exp, scale=-1.0),
                 reads=bPB[0] + bg, writes=bg)
            S.op("act", lambda e: e.activation(out=G["beta"][:], in_=G["beta"][:], func=AF.Ln, bias=1.0), reads=bg, writes=bg)
            S.op("act", lambda e: e.activation(out=G["beta"][:], in_=G["beta"][:], func=AF.Exp, scale=-1.0), reads=bg, writes=bg)
            S.op("dve", lambda e: e.tensor_scalar(out=G["negb"][:], in0=G["beta"][:], scalar1=-1.0, scalar2=None, op0=ALU.mult),
                 reads=bg, writes=bg)
            gflat = G["g"][:].rearrange("p t c -> p (t c)")
            S.op("pe", lambda e: e.matmul(PB[1][:, 0:64], lhsT=ublk, rhs=gflat, start=True, stop=True),
                 reads=bg + [B["gm"]], writes=bPB[1], inc=False)
            S.op("pe", lambda e: e.matmul(PB[1][:, 64:128], lhsT=blk, rhs=gflat, start=True, stop=True),
                 reads=bg + [B["gm"]], writes=bPB[1], inc=False)
            S.op("pe", lambda e: e.matmul(PB[1][:, 128:192], lhsT=halfsel[0], rhs=gflat, start=True, stop=True),
                 reads=bg + [B["gm"]], writes=bPB[1], inc=False)
            S.op("pe", lambda e: e.matmul(PB[1][:, 192:256], lhsT=halfsel[1], rhs=gflat, start=True, stop=True),
                 reads=bg + [B["gm"]], writes=bPB[1])
            v3 = lambda ap: ap.rearrange("p (t c) -> p t c", c=4)
            S.op("dve", lambda e: e.tensor_copy(out=G["G"][:], in_=v3(PB[1][:, 0:64])), reads=bPB[1] + bg, writes=bg)
            S.op("dve", lambda e: e.tensor_copy(out=G["GL"][:], in_=v3(PB[1][:, 64:128])), reads=bPB[1] + bg, writes=bg)
            S.op("act", lambda e: e.activation(out=G["eGL0"][:], in_=v3(PB[1][:, 128:192]), func=AF.Exp), reads=bPB[1] + bg, writes=bg)
            S.op("act", lambda e: e.activation(out=G["eGL1"][:], in_=v3(PB[1][:, 192:256]), func=AF.Exp), reads=bPB[1] + bg, writes=bg)
            S.op("act", lambda e: e.activation(out=G["eG"][:], in_=G["G"][:], func=AF.Exp), reads=bg, writes=bg)
            S.op("dve", lambda e: e.tensor_tensor(out=G["bG"][:], in0=G["beta"][:], in1=G["eG"][:], op=ALU.mult), reads=bg, writes=bg)
            S.op("dve", lambda e: e.tensor_tensor(out=G["dG"][:], in0=G["GL"][:], in1=G["G"][:], op=ALU.subtract), reads=bg, writes=bg)
            S.op("act", lambda e: e.activation(out=G["dG"][:], in_=G["dG"][:], func=AF.Exp), reads=bg, writes=bg)
            eGLb = (G["eGL0"], G["eGL1"])
            bgr = [Buf("gates_ro")]
            bgr[0].w = B["gates"].w

            bc4 = lambda ap2: ap2[:, :, None].to_broadcast([128, 4, 128])
            hb4 = lambda ap2: ap2[:, None, :].to_broadcast([128, 4, 128])
            h4 = lambda pb: PB[pb][:, :].rearrange("p (h c) -> p h c", c=128)

            def T_I():
                n_w = 0
                for blkI in range(4):
                    if blkI >= 2:
                        yield ("wait", ("Rblk", blkI - 2))
                    par = blkI % 2
                    bsl = slice(blkI * 512, (blkI + 1) * 512)
                    for jc in range(12):
                        wt = wbq[n_w % 2]; bwt = b_wbq[n_w % 2]
                        xcb = xc[n_w % 2]; bxc = b_xc[n_w % 2]
                        n_w += 1
                        S.dma("pool", wt[:], wbq_d[jc], writes=[bwt])
                        for kc in range(KC):
                            S.op("pe", lambda e: e.matmul(PB[I_A][:, :], lhsT=wt[:, kc, :], rhs=hT[:, kc, bsl],
                                                          start=(kc == 0), stop=(kc == KC - 1)),
                                 reads=[bwt] + b_hT[4 * blkI:4 * blkI + 4], writes=bPB[I_A], inc=(kc == KC - 1))
                        yield
                        S.op("pool", lambda e: e.tensor_copy(out=xcb[:, 0:3], in_=cr[:, jc, :]), reads=[B["cr"]], writes=[bxc])
                        S.op("act", lambda e: e.activation(out=xcb[:, 3:515], in_=PB[I_A][:, :], func=AF.Copy),
                             reads=bPB[I_A], writes=[bxc])
                        S.op("pool", lambda e: e.tensor_copy(out=cr[:, jc, :], in_=xcb[:, 512:515]), reads=[bxc], writes=[B["cr"]])
                        yield
                        S.op("dve", lambda e: e.tensor_scalar(out=yc[:], in0=xcb[:, 0:512], scalar1=cw[:, jc, 0:1], scalar2=None,
                                                              op0=ALU.mult), reads=[bxc, B["cw"]], writes=[B["yc"]])
                        for tap in range(1, 4):
                            S.op("dve", lambda e: e.scalar_tensor_tensor(out=yc[:], in0=xcb[:, tap:tap + 512], scalar=cw[:, jc, tap:tap + 1],
                                                                         in1=yc[:], op0=ALU.mult, op1=ALU.add),
                                 reads=[bxc, B["cw"], B["yc"]], writes=[B["yc"]])
                            yield
                        S.op("act", lambda e: e.activation(out=t1[:], in_=yc[:], func=AF.Exp, scale=-1.0), reads=[B["yc"]], writes=[B["t1"]])
                        S.op("act", lambda e: e.activation(out=t1[:], in_=t1[:], func=AF.Ln, bias=1.0), reads=[B["t1"]], writes=[B["t1"]])
                        S.op("act", lambda e: e.activation(out=t1[:], in_=t1[:], func=AF.Exp, scale=-1.0), reads=[B["t1"]], writes=[B["t1"]])
                        yield
                        if jc >= 8:
                            S.op("dve", lambda e: e.tensor_tensor(out=vsT[par][:, jc - 8, :], in0=yc[:], in1=t1[:], op=ALU.mult),
                                 reads=[B["yc"], B["t1"]], writes=[D2["vsT"][par]])
                            yield
                            continue
                        S.op("dve", lambda e: e.tensor_tensor(out=sc[:], in0=yc[:], in1=t1[:], op=ALU.mult),
                             reads=[B["yc"], B["t1"]], writes=[B["sc"]])
                        S.op("act", lambda e: e.activation(out=sqb[:], in_=sc[:], func=AF.Square), reads=[B["sc"]], writes=[B["sqb"]])
                        yield
                        S.op("pe", lambda e: e.matmul(PB[I_B][:, :], lhsT=ones_b[:], rhs=sqb[:], start=True, stop=True),
                             reads=[B["onesb"], B["sqb"]], writes=bPB[I_B])
                        S.op("act", lambda e: e.activation(out=t1[:], in_=PB[I_B][:, :], func=AF.Ln, bias=EPS),
                             reads=bPB[I_B] + [B["t1"]], writes=[B["t1"]])
                        isq = jc < 4
                        S.op("act", lambda e: e.activation(out=t1[:], in_=t1[:], func=AF.Exp, scale=-0.5,
                                                           bias=(-0.5 * math.log(128.0) if isq else 0.0)),
                             reads=[B["t1"]], writes=[B["t1"]])
                        yield
                        dstT = qnT[par] if isq else knT[par]
                        bd = D2["qnT"][par] if isq else D2["knT"][par]
                        S.op("dve", lambda e: e.tensor_tensor(out=dstT[:, jc % 4, :], in0=sc[:], in1=t1[:], op=ALU.mult),
                             reads=[B["sc"], B["t1"]], writes=[bd])
                        yield
                    yield ("done", ("I", blkI))

            def T_P():
                for i in range(NT):
                    blkI, tl = divmod(i, 4)
                    par = blkI % 2
                    tp = i % 2
                    yield ("wait", ("I", blkI))
                    if i >= 2:
                        yield ("wait", ("R", i - 2))
                    csl = slice(tl * 128, (tl + 1) * 128)
                    qn, kn, vs = qnT[par], knT[par], vsT[par]
                    bqn, bkn, bvs = D2["qnT"][par], D2["knT"][par], D2["vsT"][par]
                    pa, pb_, pc = h4(P_A), h4(P_B), h4(P_C)
                    S.op("dve", lambda e: e.tensor_copy(out=gb[:], in_=bc4(G["g"][:, i, :])), reads=bgr, writes=[B["gb"]])
                    for h in range(4):
                        S.op("pe", lambda e: e.matmul(pa[:, h, :], lhsT=gb[:, h, :], rhs=ublk, start=True, stop=True),
                             reads=[B["gb"], B["gm"]], writes=bPB[P_A], inc=(h == 3))
                    yield
                    S.op("act", lambda e: e.activation(out=Dm[:], in_=pa, func=AF.Exp), reads=bPB[P_A], writes=[B["Dm"]])
                    S.op("dve", lambda e: e.tensor_tensor(out=qgT[tp][:], in0=qn[:, :, csl], in1=Dm[:], op=ALU.mult),
                         reads=[bqn, B["Dm"]], writes=[D2["qgT"][tp]])
                    yield
                    p3b = PB[P_B][:, :].bitcast(BF16).rearrange("p (a h c) -> p a h c", a=2, h=4)
                    for h in range(4):
                        S.op("pe", lambda e: e.transpose(p3b[:, 0, h, :], kn[:, h, csl], ident[:]),
                             reads=[bkn, b_const], writes=bPB[P_B], inc=False)
                    for h in range(4):
                        S.op("pe", lambda e: e.transpose(p3b[:, 1, h, :], vs[:, h, csl], ident[:]),
                             reads=[bvs, b_const], writes=bPB[P_B], inc=(h == 3))
                    yield
                    S.op("dve", lambda e: e.tensor_tensor(out=kbg[tp][:], in0=p3b[:, 0], in1=bc4(G["bG"][:, i, :]), op=ALU.mult),
                         reads=bPB[P_B] + bgr, writes=[D2["kbg"][tp]])
                    S.op("dve", lambda e: e.tensor_tensor(out=kdec[tp][:], in0=p3b[:, 0], in1=bc4(G["dG"][:, i, :]), op=ALU.mult),
                         reads=bPB[P_B] + bgr, writes=[D2["kdec"][tp]])
                    yield
                    S.op("dve", lambda e: e.tensor_tensor(out=vb[tp][:], in0=p3b[:, 1], in1=bc4(G["beta"][:, i, :]), op=ALU.mult),
                         reads=bPB[P_B] + bgr, writes=[D2["vb"][tp]])
                    S.op("pool", lambda e: e.tensor_tensor(out=gU[:], in0=hb4(ublk), in1=bc4(G["g"][:, i, :]), op=ALU.mult),
                         reads=bgr + [B["gm"]], writes=[B["gU"]])
                    yield
                    for h in range(4):
                        S.op("pe", lambda e: e.matmul(pa[:, h, :], lhsT=gU[:, h, :], rhs=strictblk, start=True, stop=True),
                             reads=[B["gU"], B["gm"]], writes=bPB[P_A], inc=(h == 3))
                    for h in range(4):
                        S.op("pe", lambda e: e.matmul(pb_[:, h, :], lhsT=strictblk, rhs=gU[:, h, :], start=True, stop=True),
                             reads=[B["gU"], B["gm"]], writes=bPB[P_B], inc=(h == 3))
                    yield
                    S.op("act", lambda e: e.activation(out=Dm[:], in_=pa, func=AF.Exp), reads=bPB[P_A] + [B["Dm"]], writes=[B["Dm"]])
                    S.op("act", lambda e: e.activation(out=DTm[:], in_=pb_, func=AF.Exp), reads=bPB[P_B], writes=[B["DTm"]])
                    yield
                    for h in range(4):
                        S.op("pe", lambda e: e.matmul(pa[:, h, :], lhsT=kn[:, h, csl], rhs=kn[:, h, csl], start=True, stop=True),
                             reads=[bkn], writes=bPB[P_A], inc=(h == 3))
                    for h in range(4):
                        S.op("pe", lambda e: e.matmul(pb_[:, h, :], lhsT=kn[:, h, csl], rhs=qn[:, h, csl], start=True, stop=True),
                             reads=[bkn, bqn], writes=bPB[P_B], inc=(h == 3))
                    yield
                    S.op("pool", lambda e: e.tensor_tensor(out=Dm[:], in0=Dm[:], in1=hb4(strictblk), op=ALU.mult),
                         reads=[B["Dm"], B["gm"]], writes=[B["Dm"]])
                    S.op("pool", lambda e: e.tensor_tensor(out=Dm[:], in0=Dm[:], in1=bc4(G["negb"][:, i, :]), op=ALU.mult),
                         reads=[B["Dm"]] + bgr, writes=[B["Dm"]])
                    yield
                    S.op("dve", lambda e: e.tensor_tensor(out=Pm1[:], in0=pa, in1=Dm[:], op=ALU.mult),
                         reads=bPB[P_A] + [B["Dm"]], writes=[B["Pm"]])
                    S.op("pool", lambda e: e.tensor_tensor(out=DTm[:], in0=DTm[:], in1=hb4(ublk), op=ALU.mult),
                         reads=[B["DTm"], B["gm"]], writes=[B["DTm"]])
                    yield
                    S.op("dve", lambda e: e.tensor_tensor(out=qkDT[tp][:], in0=pb_, in1=DTm[:], op=ALU.mult),
                         reads=bPB[P_B] + [B["DTm"]], writes=[D2["qkDT"][tp]])
                    for h in range(4):
                        S.op("pe", lambda e: e.matmul(pc[:, h, :], lhsT=Pm1[:, h, :], rhs=ident_f, start=True, stop=True),
                             reads=[B["Pm"], B["gm"]], writes=bPB[P_C], inc=(h == 3))
                    yield
                    S.op("act", lambda e: e.activation(out=Qm1[:], in_=pc, func=AF.Copy), reads=bPB[P_C], writes=[B["Qm"]])
                    S.op("dve", lambda e: e.tensor_tensor(out=TT[:], in0=pc, in1=hb4(ident_f), op=ALU.add),
                         reads=bPB[P_C] + [B["gm"]], writes=[B["TT"]])
                    yield
                    for lvl in range(5):
                        for h in range(4):
                            S.op("pe", lambda e: e.matmul(pa[:, h, :], lhsT=Qm1[:, h, :], rhs=Pm1[:, h, :], start=True, stop=True),
                                 reads=[B["Qm"], B["Pm"]], writes=bPB[P_A], inc=(h == 3))
                        if lvl < 4:
                            for h in range(4):
                                S.op("pe", lambda e: e.matmul(pb_[:, h, :], lhsT=Pm1[:, h, :], rhs=Qm1[:, h, :], start=True, stop=True),
                                     reads=[B["Qm"], B["Pm"]], writes=bPB[P_B], inc=(h == 3))
                        yield
                        S.op("act", lambda e: e.activation(out=Pm1[:], in_=pa, func=AF.Copy), reads=bPB[P_A], writes=[B["Pm"]])
                        if lvl < 4:
                            S.op("dve", lambda e: e.tensor_copy(out=Qm1[:], in_=pb_), reads=bPB[P_B], writes=[B["Qm"]])
                        yield
                        for h in range(4):
                            S.op("pe", lambda e: e.matmul(pc[:, h, :], lhsT=Pm1[:, h, :], rhs=TT[:, h, :], start=True, stop=True),
                                 reads=[B["Pm"], B["TT"]], writes=bPB[P_C], inc=(h == 3))
                        yield
                        S.op("dve", lambda e: e.tensor_tensor(out=TT[:], in0=pc, in1=TT[:], op=ALU.add),
                             reads=bPB[P_C] + [B["TT"]], writes=[B["TT"]])
                        yield
                    S.op("act", lambda e: e.activation(out=TTb[:], in_=TT[:], func=AF.Copy), reads=[B["TT"]], writes=[B["TTb"]])
                    yield
                    for h in range(4):
                        S.op("pe", lambda e: e.matmul(pa[:, h, :], lhsT=TTb[:, h, :], rhs=vb[tp][:, h, :], start=True, stop=True),
                             reads=[B["TTb"], D2["vb"][tp]], writes=bPB[P_A], inc=(h == 3))
                    for h in range(4):
                        S.op("pe", lambda e: e.matmul(pb_[:, h, :], lhsT=kbg[tp][:, h, :], rhs=TTb[:, h, :], start=True, stop=True),
                             reads=[B["TTb"], D2["kbg"][tp]], writes=bPB[P_B], inc=(h == 3))
                    yield
                    S.op("act", lambda e: e.activation(out=u_t[tp][:], in_=PB[P_A][:, :], func=AF.Copy), reads=bPB[P_A], writes=[D2["u"][tp]])
                    S.op("dve", lambda e: e.tensor_copy(out=wT[tp][:], in_=pb_), reads=bPB[P_B], writes=[D2["wT"][tp]])
                    yield ("done", ("P", i))

            def T_R():
                for i in range(NT):
                    tp = i % 2
                    yield ("wait", ("P", i))
                    for ch in range(2):
                        pr = slice(64 * ch, 64 * ch + 64)
                        pc_ = slice(64 * ch, 64 * ch + 64)
                        S.op("pool", lambda e: e.tensor_tensor(out=Sdec[:], in0=St[:], in1=bc4(eGLb[ch][:, i, :]), op=ALU.mult),
                             reads=[B["S"]] + bgr, writes=[B["Sdec"]])
                        for h in range(4):
                            S.op("pe", lambda e: e.matmul(PB[R_A][pr, h * 128:(h + 1) * 128], lhsT=wT[tp][:, h, pc_], rhs=Sb[:, h, :],
                                                          start=True, stop=True),
                                 reads=[D2["wT"][tp], B["Sb"]], writes=bPB[R_A], inc=(h == 3))
                        yield
                        S.op("dve", lambda e: e.tensor_tensor(out=vnew[pr, :], in0=u_t[tp][pr, :], in1=PB[R_A][pr, :], op=ALU.subtract),
                             reads=bPB[R_A] + [D2["u"][tp]], writes=[B["vnew"]])
                        yield
                        for h in range(4):
                            S.op("pe", lambda e: e.matmul(PB[R_B][pr, h * 128:(h + 1) * 128], lhsT=qgT[tp][:, h, pc_], rhs=Sb[:, h, :],
                                                          start=True, stop=False),
                                 reads=[D2["qgT"][tp], B["Sb"]], writes=bPB[R_B], inc=False)
                            S.op("pe", lambda e: e.matmul(PB[R_B][pr, h * 128:(h + 1) * 128], lhsT=qkDT[tp][pr, h, pc_],
                                                          rhs=vnew[pr, h * 128:(h + 1) * 128], start=False, stop=True),
                                 reads=[D2["qkDT"][tp], B["vnew"]], writes=bPB[R_B], inc=(h == 3))
                        yield
                        for h in range(4):
                            S.op("pe", lambda e: e.matmul(PB[R_C][:, h * 128:(h + 1) * 128], lhsT=kdec[tp][pr, h, :],
                                                          rhs=vnew[pr, h * 128:(h + 1) * 128], start=True, stop=True),
                                 reads=[D2["kdec"][tp], B["vnew"]], writes=bPB[R_C], inc=(h == 3))
                        yield
                        Sf = St[:].rearrange("p h c -> p (h c)")
                        Sdf = Sdec[:].rearrange("p h c -> p (h c)")
                        Sbf = Sb[:].rearrange("p h c -> p (h c)")
                        S.op("dve", lambda e: e.tensor_tensor(out=Sbf, in0=PB[R_C][:, :], in1=Sdf, op=ALU.add),
                             reads=bPB[R_C] + [B["Sdec"]], writes=[B["Sb"]])
                        S.op("dve", lambda e: e.tensor_tensor(out=Sf, in0=PB[R_C][:, :], in1=Sdf, op=ALU.add),
                             reads=bPB[R_C] + [B["Sdec"]], writes=[B["S"]])
                        yield
                    for kc in range(KC):
                        S.op("pe", lambda e: e.matmul(PB[R_A][:, :], lhsT=hT[:, kc, i * 128:(i + 1) * 128], rhs=wbz[:, kc, :],
                                                      start=(kc == 0), stop=(kc == KC - 1)),
                             reads=[b_hT[i], B["wbz"]], writes=bPB[R_A], inc=(kc == KC - 1))
                    yield
                    S.op("act", lambda e: e.activation(out=zs[:], in_=PB[R_A][:, :], func=AF.Exp, scale=-1.0), reads=bPB[R_A], writes=[B["zs"]])
                    S.op("act", lambda e: e.activation(out=zs[:], in_=zs[:], func=AF.Ln, bias=1.0), reads=[B["zs"]], writes=[B["zs"]])
                    S.op("act", lambda e: e.activation(out=zs[:], in_=zs[:], func=AF.Exp, scale=-1.0), reads=[B["zs"]], writes=[B["zs"]])
                    yield
                    S.op("dve", lambda e: e.tensor_tensor(out=zs[:], in0=PB[R_A][:, :], in1=zs[:], op=ALU.mult),
                         reads=bPB[R_A] + [B["zs"]], writes=[B["zs"]])
                    zs3 = zs[:].rearrange("p (h c) -> p h c", c=128)
                    S.op("pool", lambda e: e.tensor_tensor(out=zs3, in0=zs3, in1=hb4(hn_bc[:]), op=ALU.mult),
                         reads=[B["zs"], B["hn"]], writes=[B["zs"]])
                    yield
                    for h in range(4):
                        S.op("act", lambda e: e.activation(out=sqr[:], in_=PB[R_B][:, h * 128:(h + 1) * 128], func=AF.Square,
                                                           accum_out=sso[:, h:h + 1]),
                             reads=bPB[R_B], writes=[B["sqr"], B["sso"]])
                    S.op("act", lambda e: e.activation(out=sso[:], in_=sso[:], func=AF.Ln, scale=1.0 / 128, bias=EPS),
                         reads=[B["sso"]], writes=[B["sso"]])
                    S.op("act", lambda e: e.activation(out=sso[:], in_=sso[:], func=AF.Exp, scale=-0.5), reads=[B["sso"]], writes=[B["sso"]])
                    yield
                    t13 = t1r[:].rearrange("p (h c) -> p h c", c=128)
                    S.op("dve", lambda e: e.tensor_tensor(out=t13, in0=h4(R_B), in1=bc4(sso[:]), op=ALU.mult),
                         reads=bPB[R_B] + [B["sso"], B["t1r"]], writes=[B["t1r"]])
                    S.op("dve", lambda e: e.tensor_tensor(out=ot[:], in0=t1r[:], in1=zs[:], op=ALU.mult),
                         reads=[B["t1r"], B["zs"]], writes=[B["ot"]])
                    yield
                    p3o = PB[R_C][:, :].bitcast(BF16)[:, 0:512].rearrange("p (h c) -> p h c", c=128)
                    for h in range(4):
                        S.op("pe", lambda e: e.transpose(p3o[:, h, :], ot[:, h * 128:(h + 1) * 128], ident[:]),
                             reads=[B["ot"], b_const], writes=bPB[R_C], inc=(h == 3))
                    S.op("act", lambda e: e.activation(out=og[:, 4:8, i * 128:(i + 1) * 128], in_=p3o, func=AF.Copy),
                         reads=bPB[R_C], writes=[b_og[4 + hh][i // 4] for hh in range(4)])
                    yield ("done", ("R", i))
                    if i % 4 == 3:
                        yield ("done", ("Rblk", i // 4))

            run_threads([T_I(), T_P(), T_R()], BW)

        def layer1(seq, last):
            nonlocal sb_l, b_l
            with contextlib.ExitStack() as st1:
                st2 = st1.enter_context(contextlib.ExitStack())
                cur = [st1]

                def sl(name, shape, dt):
                    uid[0] += 1
                    return cur[0].enter_context(nc.sbuf_tensor("t%d_%s" % (uid[0], name), list(shape), dt))
                sb_l = {}
                b_l = {}
                sb_l["ss2"] = sl("ss2", [128, 2 * NT], F32); b_l["ss2"] = Buf()
                sb_l["rs2"] = sl("rs2", [128, NT], F32); b_l["rs2"] = Buf()
                sb_l["tmpf"] = [sl("tmpf%d" % i, [128, 512], F32) for i in range(2)]; b_l["tmpf"] = [Buf(), Buf()]
                sb_l["tmpf2"] = [sl("tmpf2_%d" % i, [128, 512], F32) for i in range(2)]; b_l["tmpf2"] = [Buf(), Buf()]
                wo = sl("wo", [128, 8, D], BF16)
                cur[0] = st2
                qT = [[sl("qT%d_%d" % (s_, i), [128, SEQ], BF16) for i in range(2)] for s_ in range(2)]
                kT = [[sl("kT%d_%d" % (s_, i), [128, SEQ], BF16) for i in range(2)] for s_ in range(2)]
                Vx = [[sl("Vx%d_%d" % (s_, i), [128, NT, 128], BF16) for i in range(2)] for s_ in range(2)]
                vT5 = sl("vT5", [128, 512], BF16)
                b_vT5 = Buf()
                wq = sl("wq", [128, KC, 128], BF16)
                wk = sl("wk", [128, KC, 128], BF16)
                wv = sl("wv", [128, KC, 128], BF16)
                wz = [sl("wz%d" % i, [128, KC, 128], BF16) for i in range(2)]
                wf = sl("wf", [128, KC, 16], BF16)
                fb_bc = sl("fb_bc", [128, 16], F32)
                flb = sl("flb", [128, NT, 16], F32)
                nlf = sl("nlf", [128, NT, 16], F32)
                NC_ = sl("NC", [128, NT, 16], F32)
                carry = sl("carry", [128, 16], F32)
                carryT = sl("carryT", [16, 1], F32)
                cT = sl("cT", [16, SEQ], F32)
                cHL = sl("cHL", [16, 2, SEQ], BF16)
                pt = [sl("pt%d" % i, [128, 512], BF16) for i in range(3)]
                e_t = sb_l["tmpf"][0]
                sums = sb_l["tmpf"][1]
                den = sl("den", [128, 512], F32)
                tt = den
                b_qT = [[Buf(), Buf()] for _ in range(2)]; b_kT = [[Buf(), Buf()] for _ in range(2)]
                b_Vx = [[Buf(), Buf()] for _ in range(2)]
                b_qaug = [[Buf(), Buf()] for _ in range(2)]
                b_wq, b_wk, b_wv = Buf(), Buf(), Buf()
                b_wz = [Buf(), Buf()]
                b_wf, b_fb, b_flb, b_nlf, b_NC, b_carry, b_carryT, b_cT, b_cTt, b_cHL = [Buf() for _ in range(10)]
                b_pt = [Buf() for _ in range(3)]
                b_e, b_sums, b_den = b_l["tmpf"][0], b_l["tmpf"][1], Buf()
                b_tt = b_den

                try:
                    if 0 in layers:
                        load_post(1)
                    else:
                        load_norms(1)
                    S.dma("pool", wo[:], woc_d[:, :, :], writes=[b_wo])
                    S.dma("pool", wf[:], wf_d[:, :, :], writes=[b_wf])
                    S.dma("sp", fb_bc[:], bass.AP(fb_d.tensor, 0, [[0, 128], [1, 16]]), writes=[b_fb])
                    for s_ in range(2):
                        for i in range(2):
                            S.op("pool", lambda e: e.memset(kT[s_][i][64:66, :], 1.0), writes=[b_kT[s_][i]])
                        S.op("pool", lambda e: e.memset(Vx[s_][0][:, :, 64:128], 1.0), writes=[b_Vx[s_][0]])
                        S.op("pool", lambda e: e.memset(Vx[s_][1][:, :, 0:64], 1.0), writes=[b_Vx[s_][1]])
                    S.op("pool", lambda e: e.memset(carry[:], 0.0), writes=[b_carry])
                    S.op("pool", lambda e: e.memset(carryT[:], 0.0), writes=[b_carryT])

                    l1src = x1s_d[seq] if 0 in layers else x_d[seq]
                    l1b = b_x1 if 0 in layers else b_xdram
                    if 0 not in layers:
                        prenorm(l1src, l1b)
                    if STAGE <= 1:
                        raise StopStage()

                    fl_ps = PB[0][:, 0:NT * 16].rearrange("p (t h) -> p t h", h=16)
                    for i in range(NT):
                        for kc in range(KC):
                            S.op("pe", lambda e: e.matmul(fl_ps[:, i, :], lhsT=hT[:, kc, i * 128:(i + 1) * 128],
                                                          rhs=wf[:, kc, :], start=(kc == 0), stop=(kc == KC - 1)),
                                 reads=[b_hT[i], b_wf], writes=bPB[0], inc=(kc == KC - 1))
                    S.op("dve", lambda e: e.tensor_tensor(out=flb[:], in0=fl_ps, in1=fb_bc[:, None, :].to_broadcast([128, NT, 16]),
                                                          op=ALU.add),
                         reads=bPB[0] + [b_fb], writes=[b_flb])
                    S.op("act", lambda e: e.activation(out=flb[:], in_=flb[:], func=AF.Exp, scale=-1.0),
                         reads=[b_flb], writes=[b_flb])
                    S.op("act", lambda e: e.activation(out=nlf[:], in_=flb[:], func=AF.Ln, bias=1.0),
                         reads=[b_flb], writes=[b_nlf])
                    for i in range(NT):
                        bk = 1 + (i % 2)
                        c1 = PB[bk][:, 0:16]
                        c2 = PB[bk][:, 16:32]
                        c3 = PB[bk][0:16, 32:32 + 129]
                        S.op("pe", lambda e: e.matmul(c1, lhsT=uext[:, 0:128], rhs=nlf[:, i, :], start=True, stop=True),
                             reads=[b_nlf, b_const], writes=bPB[bk], inc=False)
                        S.op("pe", lambda e: e.matmul(c2, lhsT=ones_f[:], rhs=nlf[:, i, :], start=True, stop=True),
                             reads=[b_nlf, b_const], writes=bPB[bk], inc=False)
                        S.op("pe", lambda e: e.matmul(c3, lhsT=nlf[:, i, :], rhs=uext[:, :], start=True, stop=True),
                             reads=[b_nlf, b_const], writes=bPB[bk])
                        S.op("dve", lambda e: e.tensor_tensor(out=NC_[:, i, :], in0=c1, in1=carry[:], op=ALU.add),
                             reads=bPB[bk] + [b_carry], writes=[b_NC])
                        S.op("dve", lambda e: e.tensor_tensor(out=carry[:], in0=c2, in1=carry[:], op=ALU.add),
                             reads=bPB[bk] + [b_carry], writes=[b_carry])
                        S.op("dve", lambda e: e.tensor_scalar(out=cT[:, i * 128:(i + 1) * 128], in0=c3[:, 0:128],
                                                              scalar1=carryT[:, 0:1], scalar2=-8.0,
                                                              op0=ALU.add, op1=ALU.mult),
                             reads=bPB[bk] + [b_carryT], writes=[b_cT])
                        S.op("dve", lambda e: e.tensor_tensor(out=carryT[:], in0=c3[:, 128:129], in1=carryT[:], op=ALU.add),
                             reads=bPB[bk] + [b_carryT], writes=[b_carryT])
                    S.op("dve", lambda e: e.tensor_copy(out=cHL[:, 0, :], in_=cT[:]), reads=[b_cT], writes=[b_cHL])
                    S.op("dve", lambda e: e.tensor_tensor(out=cT[:], in0=cT[:], in1=cHL[:, 0, :], op=ALU.subtract),
                         reads=[b_cT, b_cHL], writes=[b_cT])
                    S.op("dve", lambda e: e.tensor_copy(out=cHL[:, 1, :], in_=cT[:]), reads=[b_cT, b_cHL], writes=[b_cHL])

                    if STAGE <= 2:
                        raise StopStage()
                    IB = 7

                    def T_in():
                        for p in range(8):
                            if p >= 2:
                                yield ("wait", ("att", p - 2))
                            sp_ = p % 2
                            qTp, kTp, Vxp = qT[sp_], kT[sp_], Vx[sp_]
                            bq, bk_, bV, bqa = b_qT[sp_], b_kT[sp_], b_Vx[sp_], b_qaug[sp_]
                            S.dma("pool", wq[:], wc_d[p, 0], writes=[b_wq])
                            S.dma("pool", wk[:], wc_d[p, 1], writes=[b_wk])
                            S.dma("pool", wv[:], wc_d[p, 2], writes=[b_wv])
                            S.dma("pool", wz[p % 2][:], wc_d[p, 3], writes=[b_wz[p % 2]])
                            for hh in range(2):
                                for r in range(2):
                                    S.dma("sp", qTp[hh][64 + r:65 + r, :], cHL[2 * p + hh:2 * p + hh + 1, r, :],
                                          reads=[b_cHL], writes=[bqa[hh]])
                            yield
                            for (wt, bw, dstT, bdst) in ((wq, b_wq, qTp, bq), (wk, b_wk, kTp, bk_)):
                                for t4 in range(4):
                                    for kc in range(KC):
                                        S.op("pe", lambda e: e.matmul(PB[IB][:, :], lhsT=wt[:, kc, :],
                                                                      rhs=hT[:, kc, t4 * 512:(t4 + 1) * 512],
                                                                      start=(kc == 0), stop=(kc == KC - 1)),
                                             reads=[bw] + b_hT[4 * t4:4 * t4 + 4], writes=bPB[IB], inc=(kc == KC - 1))
                                    yield
                                    S.op("dve", lambda e: e.tensor_copy(out=dstT[0][0:64, t4 * 512:(t4 + 1) * 512],
                                                                        in_=PB[IB][0:64, :]),
                                         reads=bPB[IB][0:1], writes=[bdst[0]])
                                    S.op("dve", lambda e: e.tensor_copy(out=dstT[1][0:64, t4 * 512:(t4 + 1) * 512],
                                                                        in_=PB[IB][64:128, :]),
                                         reads=bPB[IB][1:2], writes=[bdst[1]])
                                    yield
                            for t4 in range(4):
                                for kc in range(KC):
                                    S.op("pe", lambda e: e.matmul(PB[IB][:, :], lhsT=wv[:, kc, :],
                                                                  rhs=hT[:, kc, t4 * 512:(t4 + 1) * 512],
                                                                  start=(kc == 0), stop=(kc == KC - 1)),
                                         reads=[b_wv] + b_hT[4 * t4:4 * t4 + 4], writes=bPB[IB], inc=(kc == KC - 1))
                                yield
                                S.op("dve", lambda e: e.tensor_copy(out=vT5[:], in_=PB[IB][:, :]), reads=bPB[IB], writes=[b_vT5])
                                yield
                                pbf = PB[IB][:, :].bitcast(BF16)[:, 0:512].rearrange("p (j c) -> p j c", c=128)
                                for j in range(4):
                                    S.op("pe", lambda e: e.transpose(pbf[:, j, :], vT5[:, j * 128:(j + 1) * 128], ident[:]),
                                         reads=[b_vT5, b_const], writes=bPB[IB], inc=(j == 3))
                                yield
                                S.op("dve", lambda e: e.tensor_copy(out=Vxp[0][:, 4 * t4:4 * t4 + 4, 0:64], in_=pbf[:, :, 0:64]),
                                     reads=bPB[IB], writes=[bV[0]])
                                S.op("dve", lambda e: e.tensor_copy(out=Vxp[1][:, 4 * t4:4 * t4 + 4, 64:128], in_=pbf[:, :, 64:128]),
                                     reads=bPB[IB], writes=[bV[1]])
                                yield
                            yield ("done", ("in", p))

                    def T_att():
                        deferred = []
                        for p in range(8):
                            yield ("wait", ("in", p))
                            sp_ = p % 2
                            qTp, kTp, Vxp = qT[sp_], kT[sp_], Vx[sp_]
                            bq, bk_, bV, bqa = b_qT[sp_], b_kT[sp_], b_Vx[sp_], b_qaug[sp_]
                            jobs = []
                            for Qc in range(4):
                                for kt in range(4 * Qc + 4):
                                    for hh in range(2):
                                        jobs.append((Qc, kt, hh))

                            def emit_pv(n):
                                Qc, kt, hh = jobs[n]
                                o = max(0, kt - 4 * Qc) * 128
                                N = 512 - o
                                abk = 2 + 2 * (Qc % 2) + hh
                                S.op("pe", lambda e: e.matmul(PB[abk][:, o:512], lhsT=Vxp[hh][:, kt, :], rhs=pt[n % 3][:, 0:N],
                                                              start=(kt == 0), stop=(kt == 4 * Qc + 3)),
                                     reads=[bV[hh], b_pt[n % 3]], writes=bPB[abk])
                                if kt == 4 * Qc + 3 and hh == 1:
                                    emit_epilogue(Qc)

                            def emit_epilogue(Qc, p=p):
                                zb = 6
                                a0 = 2 + 2 * (Qc % 2)
                                a1 = a0 + 1
                                qs = slice(Qc * 512, (Qc + 1) * 512)
                                for kc in range(KC):
                                    S.op("pe", lambda e: e.matmul(PB[zb][:, :], lhsT=wz[p % 2][:, kc, :], rhs=hT[:, kc, qs],
                                                                  start=(kc == 0), stop=(kc == KC - 1)),
                                         reads=[b_wz[p % 2]] + b_hT[4 * Qc:4 * Qc + 4], writes=bPB[zb], inc=(kc == KC - 1))

                                def s1():
                                    S.op("act", lambda e: e.activation(out=e_t[:], in_=PB[zb][:, :], func=AF.Exp, scale=-1.0),
                                         reads=bPB[zb], writes=[b_e])
                                    S.op("dve", lambda e: e.tensor_copy(out=sums[0:64, :], in_=PB[a0][64:128, :]),
                                         reads=bPB[a0][1:2], writes=[b_sums])
                                    S.op("dve", lambda e: e.tensor_copy(out=sums[64:128, :], in_=PB[a1][0:64, :]),
                                         reads=bPB[a1][0:1], writes=[b_sums])

                                def s2():
                                    S.op("dve", lambda e: e.scalar_tensor_tensor(out=den[:], in0=e_t[:], scalar=1.0, in1=sums[:],
                                                                                 op0=ALU.add, op1=ALU.mult),
                                         reads=[b_e, b_sums], writes=[b_den])

                                def s3():
                                    S.op("act", lambda e: e.activation(out=den[:], in_=den[:], func=AF.Ln), reads=[b_den], writes=[b_den])
                                    S.op("act", lambda e: e.activation(out=den[:], in_=den[:], func=AF.Exp, scale=-1.0),
                                         reads=[b_den], writes=[b_den])

                                def s4():
                                    S.op("dve", lambda e: e.tensor_tensor(out=tt[:], in0=PB[zb][:, :], in1=den[:], op=ALU.mult),
                                         reads=bPB[zb] + [b_den], writes=[b_tt])
                                    S.op("dve", lambda e: e.tensor_tensor(out=og[0:64, p, qs], in0=PB[a0][0:64, :], in1=tt[0:64, :],
                                                                          op=ALU.mult),
                                         reads=bPB[a0][0:1] + [b_tt], writes=[b_og[p][Qc]])
                                    S.op("dve", lambda e: e.tensor_tensor(out=og[64:128, p, qs], in0=PB[a1][64:128, :], in1=tt[64:128, :],
                                                                          op=ALU.mult),
                                         reads=bPB[a1][1:2] + [b_tt], writes=[b_og[p][Qc]])
                                for dl, fn in ((2, s1), (4, s2), (6, s3), (8, s4)):
                                    deferred.append([dl, fn])

                            for n, (Qc, kt, hh) in enumerate(jobs):
                                o = max(0, kt - 4 * Qc) * 128
                                N = 512 - o
                                q0 = Qc * 512 + o
                                h = 2 * p + hh
                                sbk = n % 2
                                S.op("pe", lambda e: e.matmul(PB[sbk][:, 0:N], lhsT=kTp[hh][0:66, kt * 128:(kt + 1) * 128],
                                                              rhs=qTp[hh][0:66, q0:q0 + N], start=True, stop=True),
                                     reads=[bk_[hh], bq[hh], bqa[hh]], writes=bPB[sbk])
                                S.op("act", lambda e: e.activation(out=pt[n % 3][:, 0:N], in_=PB[sbk][:, 0:N], func=AF.Exp,
                                                                   scale=0.125, bias=NC_[:, kt, h:h + 1]),
                                     reads=bPB[sbk] + [b_NC], writes=[b_pt[n % 3]])
                                if kt >= 4 * Qc:
                                    S.op("pool", lambda e: e.tensor_tensor(out=pt[n % 3][:, 0:128], in0=pt[n % 3][:, 0:128],
                                                                           in1=maskb[:], op=ALU.mult),
                                         reads=[b_pt[n % 3], b_const], writes=[b_pt[n % 3]])
                                if n >= 1:
                                    emit_pv(n - 1)
                                for dfr in list(deferred):
                                    dfr[0] -= 1
                                    if dfr[0] <= 0:
                                        deferred.remove(dfr)
                                        dfr[1]()
                                yield
                            emit_pv(len(jobs) - 1)
                            yield ("done", ("att", p))
                        while deferred:
                            for dfr in list(deferred):
                                dfr[0] -= 1
                                if dfr[0] <= 0:
                                    deferred.remove(dfr)
                                    dfr[1]()

                    run_threads([T_in(), T_att()], L1W)
                except StopStage:
                    pass
                cur[0] = st1
                l1src = x1s_d[seq] if 0 in layers else x_d[seq]
                l1b = b_x1 if 0 in layers else b_xdram
                extra = ()
                if 0 in layers and seq + 1 < nseq:
                    load_pre(0)
                    def inherit(*olds):
                        nb_ = Buf()
                        for ob in olds:
                            if ob.w is not None and (nb_.w is None or True):
                                nb_.r[ob.w[0]] = max(nb_.r.get(ob.w[0], 0), ob.w[1])
                            for kk, vv in ob.r.items():
                                nb_.r[kk] = max(nb_.r.get(kk, 0), vv)
                        return nb_
                    xe = [kT[0][0][:].bitcast(F32), kT[0][1][:].bitcast(F32)]
                    bxe = [inherit(b_kT[0][0]), inherit(b_kT[0][1])]
                    xne = [qT[0][0][:, 0:D], qT[0][0][:, D:2 * D]]
                    bxne = [inherit(b_qT[0][0], b_qaug[0][0]), inherit(b_qT[0][0], b_qaug[0][0])]
                    extra = prenorm_gens(x_d[seq + 1], b_xdram, xe, bxe, xne, bxne, (2, 3))
                    prenorm_done[0] = True
                outproj_residual(l1src, l1b, out_d[seq], b_outd, wo, extra=extra)
                S.fence()
                st2.close()

        sb_l = None
        b_l = None
        prenorm_done = [False]
        for seq in range(nseq):
            if 0 in layers:
                layer0(seq, 1 not in layers)
            if 1 in layers:
                layer1(seq, True)
        S.finish(b_outd, "sp")
        print("n_ins", S.n_ins, "n_wait", S.n_wait)
    return nc


def prep_shared(inp):
    m = dict(host_consts())
    m["pre_norm"] = np.ascontiguousarray(inp["pre_norm"], dtype=np.float32)
    m["post_norm"] = np.ascontiguousarray(inp["post_norm"], dtype=np.float32)
    wc = np.asarray(inp["w_in_c"], dtype=np.float32)
    qkvz = wc[:, :4096].reshape(KC, 128, 4, 8, 128)
    m["wc"] = np.ascontiguousarray(qkvz.transpose(3, 2, 1, 0, 4))
    m["wf"] = np.ascontiguousarray(wc[:, 4096:4112].reshape(KC, 128, 16).transpose(1, 0, 2))
    m["woc"] = np.ascontiguousarray(np.asarray(inp["w_out_c"], dtype=np.float32).reshape(8, 128, D).transpose(1, 0, 2))
    m["c_forget_bias"] = np.ascontiguousarray(inp["c_forget_bias"], dtype=np.float32).reshape(1, 16)
    wab = np.asarray(inp["w_in_ab"], dtype=np.float32)
    mcol = np.arange(128)
    dd = mcol % 64
    dperm = np.where(dd < 8, dd + 8, np.where(dd < 16, dd - 8, dd))
    permcol = (mcol // 64) * 64 + dperm
    slabs = []
    for h in range(4):
        qc = wab[:, 0 * 512 + h * 128:0 * 512 + (h + 1) * 128]
        kc_ = wab[:, 1 * 512 + h * 128:1 * 512 + (h + 1) * 128]
        vc = wab[:, 2 * 512 + h * 128:2 * 512 + (h + 1) * 128]
        zc = wab[:, 3 * 512 + h * 128:3 * 512 + (h + 1) * 128]
        slabs.append(np.stack([qc, qc[:, permcol], kc_, kc_[:, permcol], vc, zc], axis=0))
    wa = np.stack(slabs, axis=0).reshape(4, 6, KC, 128, 128).transpose(0, 1, 3, 2, 4)
    m["wa"] = np.ascontiguousarray(wa)
    m["woab"] = np.ascontiguousarray(np.asarray(inp["w_out_ab"], dtype=np.float32).reshape(8, 128, D).transpose(1, 0, 2))
    m["lam4"] = np.ascontiguousarray(np.stack([inp["a_lambda_q1"], inp["a_lambda_k1"], inp["a_lambda_q2"], inp["a_lambda_k2"]]).astype(np.float32))
    m["a_subln"] = np.ascontiguousarray(np.asarray(inp["a_subln"], dtype=np.float32).reshape(128, 1))
    m["wbq"] = np.ascontiguousarray(wab[:, 2048:3584].reshape(KC, 128, 12, 128).transpose(2, 1, 0, 3))
    m["cw"] = np.ascontiguousarray(np.asarray(inp["b_conv_w"], dtype=np.float32).reshape(4, 12, 128).transpose(2, 1, 0))
    m["wbz"] = np.ascontiguousarray(wab[:, 3584:4096].reshape(KC, 128, 512).transpose(1, 0, 2))
    m["wba"] = np.ascontiguousarray(wab[:, 4096:4104].reshape(KC, 128, 8).transpose(1, 0, 2))
    m["b_a_log"] = np.ascontiguousarray(inp["b_a_log"], dtype=np.float32).reshape(1, 4)
    m["b_dt_bias"] = np.ascontiguousarray(inp["b_dt_bias"], dtype=np.float32).reshape(1, 4)
    m["b_head_norm"] = np.ascontiguousarray(inp["b_head_norm"], dtype=np.float32).reshape(1, 128)
    return m


def kernel(**inp):
    x = np.asarray(inp["x"], dtype=np.float32)
    B = x.shape[0]
    nseq = B // NCORES
    shared = prep_shared(inp)
    nc = build(nseq, LAYERS)
    in_maps = []
    for c in range(NCORES):
        m = dict(shared)
        m["x"] = np.ascontiguousarray(x[c * nseq:(c + 1) * nseq])
        m["positions"] = np.ascontiguousarray(np.asarray(inp["positions"], dtype=np.int32)[c * nseq:(c + 1) * nseq])
        in_maps.append(m)
    res = run_bass_kernel_spmd(nc, in_maps, core_ids=list(range(NCORES)), **RUN_KW)
    LAST['res'] = res
    return np.concatenate([r["out"] for r in res.results], axis=0)
```
